# Optimizing a Trainium2 kernel written in Bass

```python
import math
import jax, jax.numpy as jnp
from jax import lax
import numpy as np

D_MODEL = 1024
BATCH = 8
SEQ = 2048
DEPTH = 2
DEC_BATCH = 128
DEC_SEQ = 1
PAST_LEN = 2048
PAGE_SIZE = 128

N_EVEN = (DEPTH + 1) // 2
N_ODD = DEPTH // 2
EPS = 1e-6
CONV_K = 4

ATT_HEADS = 8
ATT_HD = 64
ATT_W = ATT_HEADS * ATT_HD
ROT_DIM = ATT_HD // 4
ROPE_THETA = 500000.0
DILATION_PATTERNS = ((128, 1), (512, 4), (2048, 16))
WIN_MAX = max(w for w, _ in DILATION_PATTERNS)

SSD_HEADDIM = 64
SSD_INNER = D_MODEL
SSD_HEADS = SSD_INNER // SSD_HEADDIM
SSD_GROUPS = 2
SSD_STATE = 128
SSD_CHUNK = 128
SSD_CONV_CH = SSD_INNER + 2 * SSD_GROUPS * SSD_STATE

EVEN_SPLITS = (ATT_W, ATT_W, ATT_W, ATT_W, SSD_INNER, SSD_CONV_CH, SSD_HEADS)
EVEN_IN = sum(EVEN_SPLITS)
EVEN_MIX = ATT_W + SSD_INNER

MLSTM_INNER = 2 * D_MODEL
MLSTM_HEADS = 8
MLSTM_HD = MLSTM_INNER // MLSTM_HEADS
MLSTM_CHUNK = 128
ODD_SPLITS = (MLSTM_INNER, MLSTM_INNER, MLSTM_INNER, MLSTM_INNER, MLSTM_HEADS, MLSTM_HEADS, MLSTM_INNER)
ODD_IN = sum(ODD_SPLITS)

kernel_name = "dilated_ssd_mlstm_hybrid_step"


def _split(u, sizes):
    idx = np.cumsum(np.array(sizes))[:-1].tolist()
    return jnp.split(u, idx, axis=-1)


def rmsnorm(x, w):
    xf = x.astype(jnp.float32)
    y = xf * lax.rsqrt(jnp.mean(xf * xf, axis=-1, keepdims=True) + EPS)
    return (y * w.astype(jnp.float32)).astype(x.dtype)


def head_norm(h, w):
    mu = jnp.mean(h, axis=-1, keepdims=True)
    var = jnp.mean(jnp.square(h - mu), axis=-1, keepdims=True)
    return (h - mu) * lax.rsqrt(var + EPS) * w.astype(jnp.float32).reshape(h.shape[2], h.shape[3])


def rotary_partial(x, pos):
    half = ROT_DIM // 2
    inv = jnp.power(jnp.float32(ROPE_THETA), -jnp.arange(half, dtype=jnp.float32) * (2.0 / ROT_DIM))
    ang = pos.astype(jnp.float32)[:, None] * inv[None, :]
    cos = jnp.cos(ang)[None, :, None, :]
    sin = jnp.sin(ang)[None, :, None, :]
    xf = x.astype(jnp.float32)
    x1 = xf[..., :half]
    x2 = xf[..., half:ROT_DIM]
    out = jnp.concatenate([x1 * cos - x2 * sin, x2 * cos + x1 * sin, xf[..., ROT_DIM:]], axis=-1)
    return out.astype(x.dtype)


def causal_conv(xin, buf, w, b):
    L = xin.shape[1]
    xp = jnp.concatenate([buf.astype(xin.dtype), xin], axis=1)
    y = b + xp[:, 0:L] * w[0]
    for j in range(1, CONV_K):
        y = y + xp[:, j:j + L] * w[j]
    return jax.nn.silu(y), xp[:, xp.shape[1] - (CONV_K - 1):]


def combine_by_denominator(outs, lses):
    wts = jax.nn.softmax(jnp.stack(lses, axis=0), axis=0)
    o = jnp.stack(outs, axis=0).astype(jnp.float32)
    return jnp.sum(wts[..., None] * o, axis=0)


def dilated_attention_prompt(q, k, v):
    b, S, H, hd = q.shape
    outs, lses = [], []
    for window, dil in DILATION_PATTERNS:
        span = window // dil
        L = S // dil
        nb = -(-L // span)
        pad = nb * span - L

        def to_blocks(t):
            t = t.reshape(b, L, dil, H, hd).transpose(0, 2, 1, 3, 4)
            t = jnp.pad(t, ((0, 0), (0, 0), (0, pad), (0, 0), (0, 0)))
            return t.reshape(b, dil, nb, span, H, hd)

        def with_prev(t):
            prev = jnp.pad(t, ((0, 0), (0, 0), (1, 0), (0, 0), (0, 0), (0, 0)))[:, :, :nb]
            return jnp.concatenate([prev, t], axis=3)

        qb = to_blocks(q)
        kk = with_prev(to_blocks(k))
        vv = with_prev(to_blocks(v))
        s = jnp.einsum('bgnqhe,bgnkhe->bgnhqk', qb, kk, preferred_element_type=jnp.float32)
        qi = jnp.arange(span)[:, None]
        kj = jnp.arange(2 * span)[None, :]
        dist = span + qi - kj
        blk = jnp.arange(nb)[:, None, None]
        valid = (dist >= 0) & (dist <= span) & ((blk > 0) | (kj[None] >= span))
        s = jnp.where(valid[None, None, :, None], s, -jnp.inf)
        lse = jax.nn.logsumexp(s, axis=-1)
        p = jnp.exp(s - lse[..., None])
        o = jnp.einsum('bgnhqk,bgnkhe->bgnqhe', p.astype(v.dtype), vv)
        o = o.reshape(b, dil, nb * span, H, hd)[:, :, :L].transpose(0, 2, 1, 3, 4).reshape(b, S, H, hd)
        lse = lse.transpose(0, 1, 2, 4, 3).reshape(b, dil, nb * span, H)[:, :, :L]
        lse = lse.transpose(0, 2, 1, 3).reshape(b, S, H)
        outs.append(o)
        lses.append(lse)
    return combine_by_denominator(outs, lses)


def dilated_attention_sample(q, k, v, k_buf, v_buf):
    b, T, H, hd = q.shape
    WB = k_buf.shape[1]
    kk = jnp.concatenate([k_buf.astype(k.dtype), k], axis=1)
    vv = jnp.concatenate([v_buf.astype(v.dtype), v], axis=1)
    t = jnp.arange(T)
    outs, lses = [], []
    for window, dil in DILATION_PATTERNS:
        span = window // dil
        idx = WB + t[:, None] - dil * jnp.arange(span + 1)[None, :]
        valid = idx >= 0
        idx = jnp.maximum(idx, 0)
        kg = kk[:, idx]
        vg = vv[:, idx]
        s = jnp.einsum('bthe,btjhe->bthj', q, kg, preferred_element_type=jnp.float32)
        s = jnp.where(valid[None, :, None, :], s, -jnp.inf)
        lse = jax.nn.logsumexp(s, axis=-1)
        p = jnp.exp(s - lse[..., None])
        outs.append(jnp.einsum('bthj,btjhe->bthe', p.astype(v.dtype), vg))
        lses.append(lse)
    return combine_by_denominator(outs, lses)


def ssd_scan(x, dt, a, bmat, cmat, d_skip, init_state):
    b, L, H, P = x.shape
    G, N = bmat.shape[2], bmat.shape[3]
    R = H // G
    cl = min(SSD_CHUNK, L)
    nc = L // cl
    xf = x.astype(jnp.float32).reshape(b, nc, cl, G, R, P)
    dtc = dt.reshape(b, nc, cl, G, R)
    Bc = bmat.astype(jnp.float32).reshape(b, nc, cl, G, N)
    Cc = cmat.astype(jnp.float32).reshape(b, nc, cl, G, N)
    acum = jnp.cumsum(dtc * a.reshape(G, R), axis=2)
    causal = jnp.tril(jnp.ones((cl, cl), dtype=bool))
    seg = acum[:, :, :, None] - acum[:, :, None, :]
    decay = jnp.exp(jnp.where(causal[:, :, None, None], seg, -jnp.inf))
    cb = jnp.einsum('bcign,bcjgn->bcijg', Cc, Bc)
    y_intra = jnp.einsum('bcijgr,bcjgrp->bcigrp', cb[..., None] * decay * dtc[:, :, None], xf)
    last = acum[:, :, -1]
    w_end = jnp.exp(last[:, :, None] - acum) * dtc
    s_local = jnp.einsum('bcjgn,bcjgr,bcjgrp->bcgrpn', Bc, w_end, xf)

    def step(state, inp):
        dec, sl = inp
        return state * jnp.exp(dec)[..., None, None] + sl, state

    init = init_state.astype(jnp.float32).reshape(b, G, R, P, N)
    final, prev = lax.scan(step, init, (jnp.moveaxis(last, 1, 0), jnp.moveaxis(s_local, 1, 0)))
    prev = jnp.moveaxis(prev, 0, 1)
    y_inter = jnp.einsum('bcign,bcgrpn->bcigrp', Cc, prev) * jnp.exp(acum)[..., None]
    y = y_intra + y_inter + d_skip.reshape(G, R)[:, :, None] * xf
    return y.reshape(b, L, H, P), final.reshape(b, H, P, N)


def mlstm_chunked(q, k, v, i_t, logf, c0, n0, m0):
    b, L, H, dh = q.shape
    cl = min(MLSTM_CHUNK, L)
    nc = L // cl
    causal = jnp.tril(jnp.ones((cl, cl), dtype=bool))

    def chunks(t):
        return jnp.moveaxis(t.astype(jnp.float32).reshape((b, nc, cl) + t.shape[2:]), 1, 0)

    def step(carry, inp):
        c_prev, n_prev, m_prev = carry
        qc, kc, vc, ic, fc = inp
        bcum = jnp.cumsum(fc, axis=1).transpose(0, 2, 1)
        ich = ic.transpose(0, 2, 1)
        dmat = jnp.where(causal, bcum[:, :, :, None] - bcum[:, :, None, :] + ich[:, :, None, :], -jnp.inf)
        inter = bcum + m_prev[:, :, None]
        m_t = jnp.maximum(inter, jnp.max(dmat, axis=-1))
        w_intra = jnp.exp(dmat - m_t[..., None])
        w_inter = jnp.exp(inter - m_t)
        att = w_intra * jnp.einsum('bthd,bshd->bhts', qc, kc)
        num = jnp.einsum('bhts,bshe->bthe', att, vc) + w_inter.transpose(0, 2, 1)[..., None] * jnp.einsum('bthd,bhde->bthe', qc, c_prev)
        den = jnp.sum(att, axis=-1) + w_inter * jnp.einsum('bthd,bhd->bht', qc, n_prev)
        h = num / jnp.maximum(jnp.abs(den), jnp.exp(-m_t)).transpose(0, 2, 1)[..., None]
        b_last = bcum[:, :, -1]
        logw = b_last[..., None] - bcum + ich
        m_new = jnp.maximum(b_last + m_prev, jnp.max(logw, axis=-1))
        ws = jnp.exp(logw - m_new[..., None])
        scale = jnp.exp(b_last + m_prev - m_new)
        c_new = scale[..., None, None] * c_prev + jnp.einsum('bhs,bshd,bshe->bhde', ws, kc, vc)
        n_new = scale[..., None] * n_prev + jnp.einsum('bhs,bshd->bhd', ws, kc)
        return (c_new, n_new, m_new), h

    init = (c0.astype(jnp.float32), n0.astype(jnp.float32), m0.astype(jnp.float32))
    (c, n, m), hs = lax.scan(step, init, (chunks(q), chunks(k), chunks(v), chunks(i_t), chunks(logf)))
    return jnp.moveaxis(hs, 0, 1).reshape(b, L, H, dh), c, n, m


def even_layer(hn, pos, w_in, w_out, conv_w, conv_b, dt_bias, a_log, d_skip, ssd_norm_w, kv_buf, conv_buf, ssd_state):
    b, L, _ = hn.shape
    u = hn @ w_in
    q, k, v, g_att, z, xbc, dt_raw = _split(u, EVEN_SPLITS)
    q = rotary_partial(q.reshape(b, L, ATT_HEADS, ATT_HD), pos) * (ATT_HD ** -0.5)
    k = rotary_partial(k.reshape(b, L, ATT_HEADS, ATT_HD), pos)
    v = v.reshape(b, L, ATT_HEADS, ATT_HD)
    if kv_buf is None:
        o = dilated_attention_prompt(q, k, v)
        keep = min(WIN_MAX, L)
        new_k, new_v = k[:, L - keep:], v[:, L - keep:]
    else:
        o = dilated_attention_sample(q, k, v, kv_buf[0], kv_buf[1])
        new_k, new_v = k, v
    att = o.reshape(b, L, ATT_W).astype(hn.dtype) * jax.nn.silu(g_att)
    if conv_buf is None:
        conv_buf = jnp.zeros((b, CONV_K - 1, SSD_CONV_CH), hn.dtype)
    if ssd_state is None:
        ssd_state = jnp.zeros((b, SSD_HEADS, SSD_HEADDIM, SSD_STATE), jnp.float32)
    xbc, new_conv = causal_conv(xbc, conv_buf, conv_w, conv_b)
    xs, bm, cm = _split(xbc, (SSD_INNER, SSD_GROUPS * SSD_STATE, SSD_GROUPS * SSD_STATE))
    dt = jax.nn.softplus(dt_raw.astype(jnp.float32) + dt_bias.astype(jnp.float32))
    a = -jnp.exp(a_log.astype(jnp.float32))
    y, new_state = ssd_scan(xs.reshape(b, L, SSD_HEADS, SSD_HEADDIM), dt, a,
                            bm.reshape(b, L, SSD_GROUPS, SSD_STATE), cm.reshape(b, L, SSD_GROUPS, SSD_STATE),
                            d_skip.astype(jnp.float32), ssd_state)
    y = rmsnorm(y.reshape(b, L, SSD_INNER) * jax.nn.silu(z.astype(jnp.float32)), ssd_norm_w).astype(hn.dtype)
    mix = jnp.concatenate([att, y], axis=-1) @ w_out
    return mix, new_k, new_v, new_conv, new_state.astype(hn.dtype)


def odd_layer(hn, w_in, w_out, conv_w, conv_b, ig_b, fg_b, norm_w, conv_buf, c0, n0, m0):
    b, L, _ = hn.shape
    u = hn @ w_in
    q_pre, k_pre, v, o_pre, i_pre, f_pre, z = _split(u, ODD_SPLITS)
    if conv_buf is None:
        conv_buf = jnp.zeros((b, CONV_K - 1, 2 * MLSTM_INNER), hn.dtype)
        c0 = jnp.zeros((b, MLSTM_HEADS, MLSTM_HD, MLSTM_HD), jnp.float32)
        n0 = jnp.zeros((b, MLSTM_HEADS, MLSTM_HD), jnp.float32)
        m0 = jnp.full((b, MLSTM_HEADS), -jnp.inf, jnp.float32)
    qk, new_conv = causal_conv(jnp.concatenate([q_pre, k_pre], axis=-1), conv_buf, conv_w, conv_b)
    q, k = _split(qk, (MLSTM_INNER, MLSTM_INNER))
    q = q.reshape(b, L, MLSTM_HEADS, MLSTM_HD)
    k = k.reshape(b, L, MLSTM_HEADS, MLSTM_HD) * (MLSTM_HD ** -0.5)
    v = v.reshape(b, L, MLSTM_HEADS, MLSTM_HD)
    i_t = i_pre.astype(jnp.float32) + ig_b.astype(jnp.float32)
    logf = jax.nn.log_sigmoid(f_pre.astype(jnp.float32) + fg_b.astype(jnp.float32))
    h, c, n, m = mlstm_chunked(q, k, v, i_t, logf, c0, n0, m0)
    h = jax.nn.sigmoid(o_pre.astype(jnp.float32)).reshape(b, L, MLSTM_HEADS, MLSTM_HD) * h
    h = head_norm(h, norm_w).reshape(b, L, MLSTM_INNER).astype(hn.dtype)
    out = (h * jax.nn.silu(z)) @ w_out
    return out, new_conv, c.astype(hn.dtype), n.astype(hn.dtype), m.astype(hn.dtype)


def setup_inputs(seed: int = 0) -> dict:
    key = jax.random.key(seed)
    ks = jax.random.split(key, 32)
    f32 = jnp.float32
    wb = min(WIN_MAX, PAST_LEN)

    def nrm(k, shape, scale):
        return scale * jax.random.normal(k, shape, f32)

    dt0 = jnp.exp(jax.random.uniform(ks[14], (N_EVEN, SSD_HEADS), f32, math.log(1e-3), math.log(1e-1)))
    return {
        "x_prompt": nrm(ks[0], (BATCH, SEQ, D_MODEL), 1.0),
        "x_sample": nrm(ks[1], (DEC_BATCH, DEC_SEQ, D_MODEL), 1.0),
        "cache_attn_k": nrm(ks[2], (N_EVEN, DEC_BATCH, wb, ATT_HEADS, ATT_HD), 1.0),
        "cache_attn_v": nrm(ks[3], (N_EVEN, DEC_BATCH, wb, ATT_HEADS, ATT_HD), 1.0),
        "state_ssd_conv": nrm(ks[4], (N_EVEN, DEC_BATCH, CONV_K - 1, SSD_CONV_CH), 1.0),
        "state_ssd": nrm(ks[5], (N_EVEN, DEC_BATCH, SSD_HEADS, SSD_HEADDIM, SSD_STATE), 0.3),
        "state_mlstm_conv": nrm(ks[6], (N_ODD, DEC_BATCH, CONV_K - 1, 2 * MLSTM_INNER), 1.0),
        "state_mlstm_c": nrm(ks[7], (N_ODD, DEC_BATCH, MLSTM_HEADS, MLSTM_HD, MLSTM_HD), 0.1),
        "state_mlstm_n": nrm(ks[8], (N_ODD, DEC_BATCH, MLSTM_HEADS, MLSTM_HD), 0.3),
        "state_mlstm_m": nrm(ks[9], (N_ODD, DEC_BATCH, MLSTM_HEADS), 1.0),
        "norm_w": 1.0 + nrm(ks[10], (DEPTH, D_MODEL), 0.02),
        "final_norm_w": 1.0 + nrm(ks[11], (D_MODEL,), 0.02),
        "w_in_even": nrm(ks[12], (N_EVEN, D_MODEL, EVEN_IN), D_MODEL ** -0.5),
        "w_out_even": nrm(ks[13], (N_EVEN, EVEN_MIX, D_MODEL), EVEN_MIX ** -0.5),
        "ssd_conv_w": nrm(ks[15], (N_EVEN, CONV_K, SSD_CONV_CH), CONV_K ** -0.5),
        "ssd_conv_b": nrm(ks[16], (N_EVEN, SSD_CONV_CH), 0.02),
        "ssd_dt_bias": dt0 + jnp.log(-jnp.expm1(-dt0)),
        "ssd_a_log": jnp.log(jax.random.uniform(ks[17], (N_EVEN, SSD_HEADS), f32, 1.0, 16.0)),
        "ssd_d": 1.0 + nrm(ks[18], (N_EVEN, SSD_HEADS), 0.1),
        "ssd_norm_w": 1.0 + nrm(ks[19], (N_EVEN, SSD_INNER), 0.02),
        "w_in_odd": nrm(ks[20], (N_ODD, D_MODEL, ODD_IN), D_MODEL ** -0.5),
        "w_out_odd": nrm(ks[21], (N_ODD, MLSTM_INNER, D_MODEL), MLSTM_INNER ** -0.5),
        "mlstm_conv_w": nrm(ks[22], (N_ODD, CONV_K, 2 * MLSTM_INNER), CONV_K ** -0.5),
        "mlstm_conv_b": nrm(ks[23], (N_ODD, 2 * MLSTM_INNER), 0.02),
        "mlstm_igate_b": nrm(ks[24], (N_ODD, MLSTM_HEADS), 0.1),
        "mlstm_fgate_b": jnp.linspace(3.0, 6.0, MLSTM_HEADS, dtype=f32)[None, :] + nrm(ks[25], (N_ODD, MLSTM_HEADS), 0.1),
        "mlstm_norm_w": 1.0 + nrm(ks[26], (N_ODD, MLSTM_INNER), 0.02),
    }


def reference(x_prompt, x_sample, cache_attn_k, cache_attn_v, state_ssd_conv, state_ssd,
              state_mlstm_conv, state_mlstm_c, state_mlstm_n, state_mlstm_m,
              norm_w, final_norm_w, w_in_even, w_out_even, ssd_conv_w, ssd_conv_b,
              ssd_dt_bias, ssd_a_log, ssd_d, ssd_norm_w, w_in_odd, w_out_odd,
              mlstm_conv_w, mlstm_conv_b, mlstm_igate_b, mlstm_fgate_b, mlstm_norm_w):
    pos_p = jnp.arange(x_prompt.shape[1])
    pos_s = PAST_LEN + jnp.arange(x_sample.shape[1])
    hp, hs = x_prompt, x_sample
    ak_p, av_p, sc_p, ss_p, mc_p, mC_p, mn_p, mm_p = [], [], [], [], [], [], [], []
    ak_s, av_s, sc_s, ss_s, mc_s, mC_s, mn_s, mm_s = [], [], [], [], [], [], [], []
    for layer in range(DEPTH):
        j = layer // 2
        np_ = rmsnorm(hp, norm_w[layer])
        ns_ = rmsnorm(hs, norm_w[layer])
        if layer % 2 == 0:
            wts = (w_in_even[j], w_out_even[j], ssd_conv_w[j], ssd_conv_b[j], ssd_dt_bias[j],
                   ssd_a_log[j], ssd_d[j], ssd_norm_w[j])
            mp, kp, vp, cp, sp = even_layer(np_, pos_p, *wts, None, None, None)
            ms, ks_, vs, cs, sst = even_layer(ns_, pos_s, *wts, (cache_attn_k[j], cache_attn_v[j]),
                                              state_ssd_conv[j], state_ssd[j])
            ak_p.append(kp); av_p.append(vp); sc_p.append(cp); ss_p.append(sp)
            ak_s.append(ks_); av_s.append(vs); sc_s.append(cs); ss_s.append(sst)
        else:
            wts = (w_in_odd[j], w_out_odd[j], mlstm_conv_w[j], mlstm_conv_b[j], mlstm_igate_b[j],
                   mlstm_fgate_b[j], mlstm_norm_w[j])
            mp, cp, Cp, Np, Mp = odd_layer(np_, *wts, None, None, None, None)
            ms, cs, Cs, Ns, Ms = odd_layer(ns_, *wts, state_mlstm_conv[j], state_mlstm_c[j],
                                           state_mlstm_n[j], state_mlstm_m[j])
            mc_p.append(cp); mC_p.append(Cp); mn_p.append(Np); mm_p.append(Mp)
            mc_s.append(cs); mC_s.append(Cs); mn_s.append(Ns); mm_s.append(Ms)
        hp = hp + mp
        hs = hs + ms
    y_prompt = rmsnorm(hp, final_norm_w)
    y_sample = rmsnorm(hs, final_norm_w)
    return (y_prompt, y_sample,
            jnp.stack(ak_p), jnp.stack(av_p), jnp.stack(sc_p), jnp.stack(ss_p),
            jnp.stack(mc_p), jnp.stack(mC_p), jnp.stack(mn_p), jnp.stack(mm_p),
            jnp.stack(ak_s), jnp.stack(av_s), jnp.stack(sc_s), jnp.stack(ss_s),
            jnp.stack(mc_s), jnp.stack(mC_s), jnp.stack(mn_s), jnp.stack(mm_s))
```

```python
import math
import numpy as np
from contextlib import ExitStack
import concourse.bass as bass
import concourse.mybir as mybir
from concourse.bass_utils import run_bass_kernel_spmd

F32 = mybir.dt.float32
BF16 = mybir.dt.bfloat16
AF = mybir.ActivationFunctionType
ALU = mybir.AluOpType
AX = mybir.AxisListType

import os
NSKV = 1 if os.environ.get("DEV_SMALLKV") else 16
NCORES = 8
NCH = 16
NS = 16
EPS = 1e-6


class Buf:
    __slots__ = ("name", "t", "last_w", "readers", "dsem", "ndma")

    def __init__(self, name, t=None):
        self.name = name
        self.t = t
        self.last_w = None
        self.readers = []
        self.dsem = None
        self.ndma = 0

    def __getitem__(self, k):
        return self.t[k]


class DSem:
    __slots__ = ("sem", "n", "q")

    def __init__(self, sem, q):
        self.sem = sem
        self.n = 0
        self.q = q


class Sched:
    ENG = ("pe", "act", "dve", "pool", "sp")

    def __init__(self, nc, es, arena_words):
        self.nc = nc
        self.es = es
        self.sem = {e: es.enter_context(nc.semaphore("s_" + e)) for e in ("pe", "act", "dve", "pool")}
        self.cnt = {e: 0 for e in self.ENG}
        self.waited = {e: {} for e in self.ENG}
        self.ops = {e: [] for e in self.ENG}
        self.dma_bufs = []
        self.free_dsems = []
        self.arena = es.enter_context(nc.sbuf_tensor("arena", [128, arena_words], F32))
        self.arena_words = arena_words
        self.atop = 0
        self.nalloc = 0
        self.pa_bufs = []

    def sb(self, name, shape, dt):
        return Buf(name, self.es.enter_context(self.nc.sbuf_tensor(name, list(shape), dt)))

    def ps(self, name, shape, dt):
        return Buf(name, self.es.enter_context(self.nc.psum_tensor(name, list(shape), dt)))

    def pa(self, name, shape, dt):
        n = 1
        for s in shape[1:]:
            n *= s
        words = (n + 1) // 2 if dt == BF16 else n
        words = (words + 1) // 2 * 2
        off = self.atop
        self.atop += words
        assert self.atop <= self.arena_words, (name, self.atop, self.arena_words)
        v = self.arena[0:shape[0], off:off + words]
        if dt == BF16:
            v = v.bitcast(BF16)
        v = v[:, 0:n]
        if len(shape) == 3:
            v = v.rearrange("p (a b) -> p a b", a=shape[1])
        elif len(shape) == 4:
            v = v.rearrange("p (a b c) -> p a b c", a=shape[1], b=shape[2])
        self.nalloc += 1
        b = Buf("%s_%d" % (name, self.nalloc), v)
        self.pa_bufs.append((off, b))
        return b

    def ring(self, name, n, shape, dt):
        return [self.pa("%s%d" % (name, i), shape, dt) for i in range(n)]

    def pa_mark(self):
        return self.atop

    def pa_release(self, mark):
        self.barrier()
        keep = []
        for off, b in self.pa_bufs:
            if off >= mark:
                if b.dsem is not None:
                    self.free_dsems.append(b.dsem)
                    b.dsem = None
            else:
                keep.append((off, b))
        self.pa_bufs = keep
        self.atop = mark

    def _deps(self, eng, reads, writes, skip_sem=None):
        evs = []
        for b in reads:
            if b.last_w is not None:
                evs.append(b.last_w)
        for b in writes:
            if b.last_w is not None and not (skip_sem is not None and b.last_w[0] is skip_sem and b.last_w[2] == "dma"):
                evs.append(b.last_w)
            evs.extend(b.readers)
        w = self.waited[eng]
        best = {}
        for (s, v, src) in evs:
            if src == "pe" and eng == "pe":
                continue
            key = id(s)
            if w.get(key, 0) < v:
                w[key] = v
                best[key] = (s, v)
        return list(best.values())

    def op(self, eng, fns, reads=(), writes=()):
        if callable(fns):
            fns = [fns]
        deps = self._deps(eng, reads, writes)
        self.cnt[eng] += 1
        idx = self.cnt[eng]
        sem = self.sem[eng]
        self.ops[eng].append((deps, fns, (sem, 1)))
        ev = (sem, idx, eng)
        for b in writes:
            b.last_w = ev
            b.readers = []
        for b in reads:
            if b not in writes:
                b.readers.append(ev)

    def dma(self, out_ap, in_ap, rbuf=None, wbuf=None, q="sp", **kw):
        b = wbuf if wbuf is not None else rbuf
        if b.dsem is None:
            cand = [d for d in self.free_dsems if d.q == q]
            if cand:
                b.dsem = cand[-1]
                self.free_dsems.remove(cand[-1])
            else:
                b.dsem = DSem(self.es.enter_context(self.nc.semaphore("d%d" % len(self.dma_bufs))), q)
                self.dma_bufs.append(b.dsem)
        assert b.dsem.q == q, (b.name, q)
        reads = [rbuf] if rbuf is not None else []
        writes = [wbuf] if wbuf is not None else []
        deps = self._deps(q, reads, writes, skip_sem=(b.dsem.sem if wbuf is not None and rbuf is None else None))
        b.dsem.n += 1
        ev = (b.dsem.sem, 16 * b.dsem.n, "dma")
        self.ops[q].append((deps, [lambda h: h.dma_start(out=out_ap, in_=in_ap, **kw)], (b.dsem.sem, 16)))
        if wbuf is not None:
            wbuf.last_w = ev
            wbuf.readers = []
        if rbuf is not None and rbuf is not wbuf:
            rbuf.readers.append(ev)

    def barrier(self):
        for e in self.ENG:
            deps = []
            w = self.waited[e]
            for e2 in ("pe", "act", "dve", "pool"):
                s = self.sem[e2]
                v = self.cnt[e2]
                if v > 0 and w.get(id(s), 0) < v:
                    w[id(s)] = v
                    deps.append((s, v))
            for ds in self.dma_bufs:
                v = 16 * ds.n
                if w.get(id(ds.sem), 0) < v:
                    w[id(ds.sem)] = v
                    deps.append((ds.sem, v))
            if deps:
                self.ops[e].append((deps, [], None))

    def emit(self):
        with self.nc.Block() as block:
            def mk(e):
                def body(h):
                    for deps, fns, inc in self.ops[e]:
                        for s, v in deps:
                            h.wait_ge(s, v)
                        for i, f in enumerate(fns):
                            ins = f(h)
                            if i == len(fns) - 1 and inc is not None:
                                ins.then_inc(inc[0], inc[1])
                return body
            block.tensor(mk("pe"))
            block.scalar(mk("act"))
            block.vector(mk("dve"))
            block.gpsimd(mk("pool"))
            block.sync(mk("sp"))

    def tt(self, eng, out, in0, in1, op, r, w):
        self.op(eng, lambda h: h.tensor_tensor(out=out, in0=in0, in1=in1, op=op), r, w)

    def ts(self, eng, out, in0, s1, s2, op0, op1, r, w):
        if s2 is None:
            self.op(eng, lambda h: h.tensor_scalar(out=out, in0=in0, scalar1=s1, scalar2=None, op0=op0), r, w)
        else:
            self.op(eng, lambda h: h.tensor_scalar(out=out, in0=in0, scalar1=s1, scalar2=s2, op0=op0, op1=op1), r, w)

    def stt(self, out, in0, scalar, in1, op0, op1, r, w):
        self.op("dve", lambda h: h.scalar_tensor_tensor(out=out, in0=in0, scalar=scalar, in1=in1, op0=op0, op1=op1), r, w)

    def act(self, out, in_, func, r, w, bias=None, scale=None, accum=None):
        kw = {}
        if bias is not None:
            kw["bias"] = bias
        if scale is not None:
            kw["scale"] = scale
        if accum is not None:
            kw["accum_out"] = accum
        self.op("act", lambda h: h.activation(out=out, in_=in_, func=func, **kw), r, w)

    def cp(self, eng, out, in_, r, w):
        if eng == "act":
            self.op("act", lambda h: h.copy(out=out, in_=in_), r, w)
        else:
            self.op(eng, lambda h: h.tensor_copy(out=out, in_=in_), r, w)

    def mm(self, lst, r, w):
        self.op("pe", [(lambda h, a=a: h.matmul(out=a[0], lhsT=a[1], rhs=a[2], start=a[3], stop=a[4], skip_group_check=True)) for a in lst], r, w)

    def tr(self, lst, r, w):
        self.op("pe", [(lambda h, a=a: h.transpose(out=a[0], in_=a[1], identity=a[2])) for a in lst], r, w)

    def memset(self, eng, ap, val, w):
        self.op(eng, lambda h: h.memset(ap, val), (), w)


IN_SPECS = [
    ("xp", [2048, 1024]), ("xs", [NS, 1024]),
    ("ck", [NSKV, 2048, 512]), ("cv", [NSKV, 2048, 512]),
    ("sconv", [NS, 3, 1536]), ("sstate", [NS, 1024, 128]),
    ("mconv", [NS, 3, 4096]), ("mC", [NS, 8, 256, 256]), ("mn", [NS, 8, 256]), ("mmm", [NS, 8]),
    ("normw", [3, 128, 1024]),
    ("w_in_even", [1024, 4624]), ("w_out_even", [1536, 1024]),
    ("w_in_odd", [1024, 10256]), ("w_out_odd", [2048, 1024]),
    ("ident", [128, 128]), ("tri", [128, 128]), ("maskmm", [128, 19 * 128]),
    ("cct", [128, 16, 16]), ("sst", [128, 16, 16]), ("ccs", [NS, 16]), ("sss", [NS, 16]),
    ("cwT", [128, 12, 4]), ("cbT", [128, 12]), ("cw_s", [NS, 4, 1536]), ("cb_s", [NS, 1536]),
    ("dtb", [128, 16]), ("alog", [128, 16]), ("dsk", [128, 16]), ("snw", [128, 1024]),
    ("mcwT", [128, 32, 4]), ("mcbT", [128, 32]), ("mcw_s", [NS, 4, 4096]), ("mcb_s", [NS, 4096]),
    ("igb", [128, 8]), ("fgb", [128, 8]), ("mnw", [128, 2048]),
]
OUT_SPECS = [
    ("y_p", [2048, 1024]), ("y_s", [NS, 1024]),
    ("p_k", [2048, 512]), ("p_v", [2048, 512]), ("p_sconv", [3, 1536]), ("p_ssd", [1024, 128]),
    ("p_mconv", [3, 4096]), ("p_mC", [8, 256, 256]), ("p_mn", [8, 256]), ("p_mm", [1, 8]),
    ("s_k", [NS, 512]), ("s_v", [NS, 512]), ("s_sconv", [NS, 3, 1536]), ("s_ssd", [NS, 1024, 128]),
    ("s_mconv", [NS, 3, 4096]), ("s_mC", [NS, 8, 256, 256]), ("s_mn", [NS, 8, 256]), ("s_mm", [NS, 8]),
]

PHASES = dict(attn=True, ssd=True, mlstm=True, sample=True, b2=True, b4=True, b3=True)


def build_program(phases=PHASES):
    nc = bass.Bass("TRN2", target_bir_lowering=False)
    D = {}
    for name, shape in IN_SPECS:
        D[name] = nc.dram_tensor(name, list(shape), F32, kind="ExternalInput").ap()
    for name, shape in OUT_SPECS:
        D[name] = nc.dram_tensor(name, list(shape), F32, kind="ExternalOutput").ap()

    with ExitStack() as es:
        ARENA = 26000
        S = Sched(nc, es, ARENA)
        Xall = es.enter_context(nc.sbuf_tensor("Xall", [128, NCH, 1024], F32))
        X = [Buf("X%d" % c, Xall[:, c, :]) for c in range(NCH)]
        Xs = S.sb("Xs", [NS, 1024], F32)
        hnTall = es.enter_context(nc.sbuf_tensor("hnTall", [128, 8, 2048], BF16))
        hnT = [Buf("hnT%d" % c, hnTall[:, :, c * 128:(c + 1) * 128]) for c in range(NCH)]
        hnTs = S.sb("hnTs", [128, 8, NS], BF16)
        IDf = S.sb("IDf", [128, 128], F32)
        IDb = S.sb("IDb", [128, 128], BF16)
        TRI = S.sb("TRI", [128, 128], F32)
        ONES = S.sb("ONES", [128, 128], F32)
        NW = S.sb("NW", [128, 1024], F32)
        PS = [S.ps("PS%d" % i, [128, 512], F32) for i in range(8)]

        def psb(i, n=1024):
            return PS[i][:, :].bitcast(BF16)[:, 0:n]

        for c in range(NCH):
            S.dma(X[c][:, :], D["xp"][c * 128:(c + 1) * 128, :], wbuf=X[c])
        S.dma(Xs[:, :], D["xs"], wbuf=Xs)
        S.dma(IDf[:, :], D["ident"], wbuf=IDf)
        S.dma(TRI[:, :], D["tri"], wbuf=TRI)
        S.cp("dve", IDb[:, :], IDf[:, :], [IDf], [IDb])
        S.memset("pool", ONES[:, :], 1.0, [ONES])

        def rms_rows(xap, np_, ss, rstd, junk, xb, eng2="dve"):
            S.act(junk[0:np_, :], xap, AF.Square, [xb], [junk, ss], accum=ss[0:np_, :])
            S.ts("dve", rstd[0:np_, :], ss[0:np_, :], 1.0 / 1024, EPS, ALU.mult, ALU.add, [ss], [rstd])
            S.act(rstd[0:np_, :], rstd[0:np_, :], AF.Sqrt, [rstd], [rstd])
            S.op("dve", lambda h: h.reciprocal(out=rstd[0:np_, :], in_=rstd[0:np_, :]), [rstd], [rstd])

        def phase_norm(layer):
            mark = S.pa_mark()
            S.dma(NW[:, :], D["normw"][layer], wbuf=NW)
            junk = S.ring("junk", 2, [128, 1024], BF16)
            hn = S.ring("hn", 2, [128, 1024], BF16)
            ss = S.ring("ss", 2, [128, 1], F32)
            rstd = S.ring("rstd", 2, [128, 1], F32)
            for c in range(NCH + 1):
                i = c % 2
                if c < NCH:
                    xb, np_, dst = X[c], 128, hnT[c]
                    dap = hnT[c][:, :, :]
                else:
                    xb, np_, dst = Xs, NS, hnTs
                    dap = hnTs[:, :, :]
                rms_rows(xb[0:np_, :], np_, ss[i], rstd[i], junk[i], xb)
                S.stt(hn[i][0:np_, :], xb[0:np_, :], rstd[i][0:np_, :], NW[0:np_, :], ALU.mult, ALU.mult, [xb, rstd[i], NW], [hn[i]])
                pb = 6 + i
                pv = psb(pb).rearrange("p (k t) -> p k t", k=8)[:, :, 0:np_]
                S.tr([(pv[:, k, :], hn[i][0:np_, k * 128:(k + 1) * 128], IDb[0:np_, 0:np_]) for k in range(8)], [hn[i], IDb], [PS[pb]])
                S.cp("act", dap, pv, [PS[pb]], [dst])
            S.pa_release(mark)

        phase_norm(0)

        WE = D["w_in_even"]
        if phases["attn"]:
            mark_attn = S.pa_mark()
            ATGT = S.pa("ATGT", [128, 4, 2048], BF16)
            ATGTs = S.pa("ATGTs", [128, 4, NS], BF16)
            QS = S.pa("QS", [NS, 512], F32)
            KS = S.pa("KS", [NS, 512], F32)
            VS = S.pa("VS", [NS, 512], F32)
            SGS = S.pa("SGS", [NS, 512], F32)
            CCt = S.pa("CCt", [128, 16, 16], F32)
            SSt = S.pa("SSt", [128, 16, 16], F32)
            CCs = S.pa("CCs", [NS, 16], F32)
            SSs = S.pa("SSs", [NS, 16], F32)
            S.dma(CCt[:, :, :], D["cct"], wbuf=CCt)
            S.dma(SSt[:, :, :], D["sst"], wbuf=SSt)
            S.dma(CCs[:, :], D["ccs"], wbuf=CCs)
            S.dma(SSs[:, :], D["sss"], wbuf=SSs)
            mark_pairs = S.pa_mark()
            MM = S.pa("MM", [128, 19 * 128], BF16)
            for hf_ in range(2):
                S.dma(MM[:, hf_ * 1216:(hf_ + 1) * 1216], D["maskmm"][:, hf_ * 1216:(hf_ + 1) * 1216], wbuf=MM, q="pool")
            WP = S.ring("WP", 2, [128, 8, 512], BF16)
            QKT = S.pa("QKT", [128, 2, 2048], BF16)
            VA = S.pa("VA", [128, 16, 2, 65], BF16)
            SG = S.pa("SG", [128, 16, 128], BF16)
            ATG = S.pa("ATG", [128, 16, 128], BF16)
            QK = S.ring("QK", 2, [128, 256], F32)
            QSRC = S.ring("QSRC", 2, [128, 256], F32)
            TA = S.ring("TA", 2, [128, 4, 16], F32)
            TB = S.ring("TB", 2, [128, 4, 16], F32)
            VF = S.ring("VF", 2, [128, 128], F32)
            QKb = S.ring("QKb", 2, [128, 256], BF16)
            Eb = S.ring("Eb", 3, [128, 512], BF16)
            Pb = S.ring("Pb", 3, [128, 512], BF16)
            ATT = S.ring("ATT", 2, [128, 4, 64], F32)
            REC = S.ring("REC", 2, [128, 4, 1], F32)
            S.memset("pool", VA[:, :, :, 64:65], 1.0, [VA])

            def load_pair_w(p):
                wb = WP[p % 2]
                for j in range(4):
                    col = j * 512 + p * 128
                    S.dma(wb[:, :, j * 128:(j + 1) * 128], WE[:, col:col + 128].rearrange("(k p) n -> p k n", p=128), wbuf=wb, q="pool")

            def rotary(psv, np_, cc, sn, ta, tb, dst4, rd, dbuf):
                ccb = cc.unsqueeze(1).to_broadcast([np_, 4, 16])
                S.tt("dve", ta[0:np_, :, :], psv[:, :, 0:16], ccb, ALU.mult, rd, [ta])
                S.tt("dve", tb[0:np_, :, 0:8], psv[:, :, 8:16], sn[:, 0:8].unsqueeze(1).to_broadcast([np_, 4, 8]), ALU.mult, rd, [tb])
                S.tt("dve", tb[0:np_, :, 8:16], psv[:, :, 0:8], sn[:, 8:16].unsqueeze(1).to_broadcast([np_, 4, 8]), ALU.mult, rd, [tb])
                S.tt("pool" if phases.get('rotpool', True) else "dve", dst4[:, :, 0:16], ta[0:np_, :, :], tb[0:np_, :, :], ALU.add, [ta, tb], [dbuf])

            load_pair_w(0)
            for p in range(phases.get('npair', 4)):
                if p + 1 < 4:
                    load_pair_w(p + 1)
                wb = WP[p % 2]
                for c in range(NCH + 1):
                    if c >= phases.get('nchunk', 99) and c < NCH:
                        continue
                    if c == NCH and not phases.get('smpc', True):
                        continue
                    i = c % 2
                    smp = (c == NCH)
                    np_ = NS if smp else 128
                    hb = hnTs if smp else hnT[c]
                    pu = PS[i]
                    S.mm([(pu[0:np_, :], hb[:, k, :], wb[:, k, :], k == 0, k == 7) for k in range(8)], [hb, wb], [pu])
                    qk = QK[i]
                    S.cp("act", qk[0:np_, :], pu[0:np_, 0:256], [pu], [qk])
                    psv = pu[0:np_, 0:256].rearrange("p (a b) -> p a b", a=4)
                    qk4 = qk[0:np_, :].rearrange("p (a b) -> p a b", a=4)
                    rsrc = pu
                    if phases.get('rotsb', True):
                        qsrc = QSRC[i]
                        S.cp("act", qsrc[0:np_, :], pu[0:np_, 0:256], [pu], [qsrc])
                        psv = qsrc[0:np_, :].rearrange("p (a b) -> p a b", a=4)
                        rsrc = qsrc
                    if not phases.get('rot', True):
                        pass
                    elif smp:
                        rotary(psv, np_, CCs[:, :], SSs[:, :], TA[i], TB[i], qk4, [rsrc, CCs, SSs], qk)
                    else:
                        rotary(psv, np_, CCt[:, c, :], SSt[:, c, :], TA[i], TB[i], qk4, [rsrc, CCt, SSt], qk)
                    vf = VF[i]
                    S.cp("act", vf[0:np_, :], pu[0:np_, 256:384], [pu], [vf])
                    if smp and not phases.get('smp', True):
                        pass
                    elif smp:
                        S.dma(D["s_k"][:, p * 128:(p + 1) * 128], qk[0:NS, 128:256], rbuf=qk)
                        S.dma(D["s_v"][:, p * 128:(p + 1) * 128], vf[0:NS, :], rbuf=vf)
                        S.cp("pool", QS[:, p * 128:(p + 1) * 128], qk[0:NS, 0:128], [qk], [QS])
                        S.cp("pool", KS[:, p * 128:(p + 1) * 128], qk[0:NS, 128:256], [qk], [KS])
                        S.cp("pool", VS[:, p * 128:(p + 1) * 128], vf[0:NS, :], [vf], [VS])
                        S.act(SGS[:, p * 128:(p + 1) * 128], pu[0:NS, 384:512], AF.Silu, [pu], [SGS])
                    else:
                        S.dma(D["p_k"][c * 128:(c + 1) * 128, p * 128:(p + 1) * 128], qk[:, 128:256], rbuf=qk)
                        S.dma(D["p_v"][c * 128:(c + 1) * 128, p * 128:(p + 1) * 128], vf[:, :], rbuf=vf)
                        S.cp("pool", VA[:, c, :, 0:64], vf[:, :].rearrange("p (a b) -> p a b", a=2), [vf], [VA])
                        S.act(SG[:, c, :], pu[:, 384:512], AF.Silu, [pu], [SG])
                        if not phases.get('trq', True):
                            continue
                        qb = QKb[i]
                        S.cp("pool", qb[:, :], qk[:, :], [qk], [qb])
                        pt = 2 + i
                        ptv = psb(pt, 256).rearrange("p (a t) -> p a t", a=2)
                        S.tr([(ptv[:, a, :], qb[:, a * 128:(a + 1) * 128], IDb[:, :]) for a in range(2)], [qb, IDb], [PS[pt]])
                        S.cp("act", QKT[:, :, c * 128:(c + 1) * 128], ptv, [PS[pt]], [QKT])
                it = 0
                for hh in range(2 if phases.get('b2', True) else 0):
                    hs = slice(hh * 64, (hh + 1) * 64)
                    for g in range(4):
                        po = PS[6 + (g % 2)]
                        pov = po[:, 0:260].rearrange("p (j e) -> p j e", j=4)
                        nk = 4 * g + 4
                        for kc in range(nk):
                            j0 = max(0, kc - 4 * g)
                            cs = slice(j0 * 128, 512)
                            ps_ = PS[4 + (it % 2)]
                            eb = Eb[it % 3]
                            pb_ = Pb[it % 3]
                            S.mm([(ps_[:, cs], QKT[hs, 1, kc * 128:(kc + 1) * 128], QKT[hs, 0, (4 * g + j0) * 128:(4 * g + 4) * 128], True, True)], [QKT], [ps_])
                            S.act(eb[:, cs], ps_[:, cs], AF.Exp, [ps_], [eb], scale=0.125)
                            m0 = (4 * g + j0 - kc + 3) * 128
                            S.tt("dve" if it % 2 == 0 else "pool", pb_[:, cs], eb[:, cs], MM[:, m0:m0 + (4 - j0) * 128], ALU.mult, [eb, MM], [pb_])
                            S.mm([(pov[:, j, :], pb_[:, j * 128:(j + 1) * 128], VA[:, kc, hh, :], (kc == 0 and j == 0), (kc == 4 * g + j)) for j in range(j0, 4)], [pb_, VA], [po])
                            it += 1
                        rec = REC[g % 2]
                        att = ATT[g % 2]
                        S.op("dve", lambda h, rec=rec, pov=pov: h.reciprocal(out=rec[:, :, :], in_=pov[:, :, 64:65]), [po], [rec])
                        S.tt("dve", att[:, :, :], pov[:, :, 0:64], rec[:, :, :].to_broadcast([128, 4, 64]), ALU.mult, [po, rec], [att])
                        S.tt("pool", ATG[:, 4 * g:4 * g + 4, hs], att[:, :, :], SG[:, 4 * g:4 * g + 4, hs], ALU.mult, [att, SG], [ATG])
                for cg in range(4 if phases.get('b2', True) else 0):
                    pt = 2 + (cg % 2)
                    ptv = psb(pt, 512).rearrange("p (a t) -> p a t", a=4)
                    S.tr([(ptv[:, a, :], ATG[:, 4 * cg + a, :], IDb[:, :]) for a in range(4)], [ATG, IDb], [PS[pt]])
                    S.cp("act", ATGT[:, p, cg * 512:(cg + 1) * 512], psb(pt, 512), [PS[pt]], [ATGT])
            S.pa_release(mark_pairs)

            QSb = S.pa("QSb", [NS, 512], BF16)
            S.cp("dve", QSb[:, :], QS[:, :], [QS], [QSb])
            SELb = S.pa("SELb", [NS, NS * 128], BF16)
            S.memset("pool", SELb[:, :], 0.0, [SELb])
            S.tt("pool", SELb[:, :].rearrange("k (b m) -> k b m", b=NS), IDf[0:NS, 0:NS].unsqueeze(2).to_broadcast([NS, NS, 128]),
                 ONES[0:NS, :].unsqueeze(1).to_broadcast([NS, NS, 128]), ALU.mult, [IDf, ONES, SELb], [SELb])
            Kc = S.ring("Kc", 3, [128, 512], F32)
            Vc = S.ring("Vc", 3, [128, 512], F32)
            Vb = S.ring("Vb", 3, [128, 8, 65], BF16)
            PR = S.ring("PR", 2, [128, 512], F32)
            SC = S.ring("SC", 2, [128, 8], F32)
            PZ = S.ring("PZ", 3, [128, 8, NS], BF16)
            for v_ in Vb:
                S.memset("pool", v_[:, :, 64:65], 1.0, [v_])
            pos0 = PS[6][0:NS, 0:260].rearrange("p (j e) -> p j e", j=4)
            pos1 = PS[7][0:NS, 0:260].rearrange("p (j e) -> p j e", j=4)
            it = 0
            for b in range(NS if phases.get('b4', True) else 0):
                pq = PS[b % 2]
                S.mm([(pq[:, :], SELb[:, b * 128:(b + 1) * 128], QSb[:, :], True, True)], [SELb, QSb], [pq])
                for pat, dil in enumerate((1, 4, 16)):
                    r0 = 2048 - 128 * dil
                    kc_, vc_, vb_, pr, sc, pz = Kc[it % 3], Vc[it % 3], Vb[it % 3], PR[it % 2], SC[it % 2], PZ[it % 3]
                    S.dma(kc_[:, :], D["ck"][b, r0:2048:dil, :], wbuf=kc_)
                    S.dma(vc_[:, :], D["cv"][b, r0:2048:dil, :], wbuf=vc_)
                    S.tt("dve", pr[:, :], kc_[:, :], pq[:, :], ALU.mult, [kc_, pq], [pr])
                    S.op("dve", lambda h, sc=sc, pr=pr: h.tensor_reduce(out=sc[:, :], in_=pr[:, :].rearrange("p (a b) -> p a b", a=8), axis=AX.X, op=ALU.add), [pr], [sc])
                    S.memset("pool", pz[:, :, :], 0.0, [pz])
                    S.act(pz[:, :, b], sc[:, :], AF.Exp, [sc, pz], [pz], scale=0.125)
                    S.cp("pool", vb_[:, :, 0:64], vc_[:, :].rearrange("p (a b) -> p a b", a=8), [vc_], [vb_])
                    first = (b == 0 and pat == 0)
                    last = (b == NS - 1 and pat == 2)
                    S.mm([((pos0 if h_ < 4 else pos1)[:, h_ % 4, :], pz[:, h_, :], vb_[:, h_, :], first and (h_ % 4 == 0), last) for h_ in range(8)],
                         [pz, vb_], [PS[6], PS[7]])
                    it += 1
            OS = S.pa("OS", [NS, 8, 65], F32)
            S.cp("act", OS[:, 0:4, :], pos0, [PS[6]], [OS])
            S.cp("act", OS[:, 4:8, :], pos1, [PS[7]], [OS])
            PRs = S.pa("PRs", [NS, 512], F32)
            SCs = S.pa("SCs", [NS, 8], F32)
            S.tt("dve", PRs[:, :], QS[:, :], KS[:, :], ALU.mult, [QS, KS], [PRs])
            S.op("dve", lambda h: h.tensor_reduce(out=SCs[:, :], in_=PRs[:, :].rearrange("p (a b) -> p a b", a=8), axis=AX.X, op=ALU.add), [PRs], [SCs])
            S.act(SCs[:, :], SCs[:, :], AF.Exp, [SCs], [SCs], scale=0.125, bias=None)
            S.ts("dve", SCs[:, :], SCs[:, :], 3.0, None, ALU.mult, None, [SCs], [SCs])
            S.tt("dve", PRs[:, :].rearrange("p (a b) -> p a b", a=8), VS[:, :].rearrange("p (a b) -> p a b", a=8),
                 SCs[:, :].unsqueeze(2).to_broadcast([NS, 8, 64]), ALU.mult, [VS, SCs], [PRs])
            S.tt("dve", OS[:, :, 0:64], OS[:, :, 0:64], PRs[:, :].rearrange("p (a b) -> p a b", a=8), ALU.add, [OS, PRs], [OS])
            S.tt("dve", OS[:, :, 64:65], OS[:, :, 64:65], SCs[:, :].unsqueeze(2), ALU.add, [OS, SCs], [OS])
            RS = S.pa("RS", [NS, 8, 1], F32)
            S.op("dve", lambda h: h.reciprocal(out=RS[:, :, :], in_=OS[:, :, 64:65]), [OS], [RS])
            S.tt("dve", PRs[:, :].rearrange("p (a b) -> p a b", a=8), OS[:, :, 0:64], RS[:, :, :].to_broadcast([NS, 8, 64]), ALU.mult, [OS, RS], [PRs])
            ATGs = S.pa("ATGs", [NS, 512], BF16)
            S.tt("dve", ATGs[:, :], PRs[:, :], SGS[:, :], ALU.mult, [PRs, SGS], [ATGs])
            ptv = psb(2, 4 * NS).rearrange("p (a t) -> p a t", a=4)
            S.tr([(ptv[:, a, :], ATGs[:, a * 128:(a + 1) * 128], IDb[0:NS, 0:NS]) for a in range(4)], [ATGs, IDb], [PS[2]])
            S.cp("act", ATGTs[:, :, :], ptv, [PS[2]], [ATGTs])

            WOa = S.pa("WOa", [128, 4, 1024], BF16)
            S.dma(WOa[:, :, :], D["w_out_even"][0:512, :].rearrange("(k p) n -> p k n", p=128), wbuf=WOa, q="pool")
            for c in range(NCH + 1 if phases.get('b3', True) else 0):
                smp = (c == NCH)
                np_ = NS if smp else 128
                xb = Xs if smp else X[c]
                for hf in range(2):
                    pb = PS[2 * (c % 2) + hf]
                    if smp:
                        lst = [(pb[0:NS, :], ATGTs[:, k, :], WOa[:, k, hf * 512:(hf + 1) * 512], k == 0, k == 3) for k in range(4)]
                        S.mm(lst, [ATGTs, WOa], [pb])
                    else:
                        lst = [(pb[:, :], ATGT[:, k, c * 128:(c + 1) * 128], WOa[:, k, hf * 512:(hf + 1) * 512], k == 0, k == 3) for k in range(4)]
                        S.mm(lst, [ATGT, WOa], [pb])
                    S.tt("dve", xb[0:np_, hf * 512:(hf + 1) * 512], xb[0:np_, hf * 512:(hf + 1) * 512], pb[0:np_, :], ALU.add, [xb, pb], [xb])
            S.pa_release(mark_attn)

        if phases["ssd"]:
            mark_ssd = S.pa_mark()
            Wz = S.pa("Wz", [128, 8, 1024], BF16)
            Wx = S.pa("Wx", [128, 8, 1536], BF16)
            Wdt = S.pa("Wdt", [128, 8, 16], BF16)
            WOs = S.pa("WOs", [128, 8, 1024], BF16)
            S.dma(Wx[:, :, :], WE[:, 3072:4608].rearrange("(k p) n -> p k n", p=128), wbuf=Wx, q="pool")
            S.dma(Wz[:, :, :], WE[:, 2048:3072].rearrange("(k p) n -> p k n", p=128), wbuf=Wz, q="pool")
            S.dma(Wdt[:, :, :], WE[:, 4608:4624].rearrange("(k p) n -> p k n", p=128), wbuf=Wdt, q="pool")
            S.dma(WOs[:, :, :], D["w_out_even"][512:1536, :].rearrange("(k p) n -> p k n", p=128), wbuf=WOs, q="pool")
            CW = S.pa("CW", [128, 12, 4], F32)
            CB = S.pa("CB", [128, 12], F32)
            DTB = S.pa("DTB", [128, 16], F32)
            ABC = S.pa("ABC", [128, 16], F32)
            DSK = S.pa("DSK", [128, 16], F32)
            SNW = S.pa("SNW", [128, 1024], F32)
            S.dma(CW[:, :, :], D["cwT"], wbuf=CW)
            S.dma(CB[:, :], D["cbT"], wbuf=CB)
            S.dma(DTB[:, :], D["dtb"], wbuf=DTB)
            S.dma(ABC[:, :], D["alog"], wbuf=ABC)
            S.dma(DSK[:, :], D["dsk"], wbuf=DSK)
            S.dma(SNW[:, :], D["snw"], wbuf=SNW)
            S.act(ABC[:, :], ABC[:, :], AF.Exp, [ABC], [ABC])
            S.ts("dve", ABC[:, :], ABC[:, :], -1.0, None, ALU.mult, None, [ABC], [ABC])
            mark_ssdw = S.pa_mark()
            A1 = S.pa("A1", [128, 12, 131], F32)
            A2 = S.pa("A2", [128, 12, 128], F32)
            PRE = A1
            CV = A2
            Yv = A1[:, :, :].rearrange("p a b -> p (a b)")[:, 0:1024]
            SZv = A2[:, :, :].rearrange("p a b -> p (a b)")[:, 0:1024]
            CARRY = S.pa("CARRY", [128, 12, 3], F32)
            XC = S.pa("XC", [128, 12, 128], BF16)
            XT = S.pa("XT", [128, 1024], BF16)
            BTOK = S.pa("BTOK", [128, 2, 128], BF16)
            TMPD = S.pa("TMPD", [128, 1024], F32)
            YZW = S.pa("YZW", [128, 1024], BF16)
            YZWT = S.pa("YZWT", [128, 8, 128], BF16)
            S32 = S.pa("S32", [128, 2, 512], F32)
            SB16 = S.pa("SB16", [128, 2, 512], BF16)
            XW = S.pa("XW", [128, 1024], BF16)
            SEG = S.ring("SEG", 2, [128, 128], F32)
            EX = S.ring("EX", 2, [128, 128], F32)
            WT = S.ring("WT", 2, [128, 128], BF16)
            CBM = S.pa("CBM", [128, 2, 128], F32)
            DTR = S.pa("DTR", [128, 16], F32)
            DT = S.pa("DT", [128, 16], F32)
            DA = S.pa("DA", [128, 16], F32)
            ACS = S.pa("ACS", [128, 32], F32)
            EAC = S.pa("EAC", [128, 32], F32)
            WEND = S.pa("WEND", [128, 16], F32)
            ssy = S.pa("ssy", [128, 1], F32)
            rsy = S.pa("rsy", [128, 1], F32)
            P7r = [Buf("P7r%d" % r, PS[7][:, r * 128:(r + 1) * 128]) for r in range(4)]
            S.memset("pool", S32[:, :, :], 0.0, [S32])
            S.memset("pool", SB16[:, :, :], 0.0, [SB16])
            S.memset("pool", CARRY[:, :, :], 0.0, [CARRY])
            for c in range(NCH):
                hb = hnT[c]
                for b3 in range(3):
                    lst = []
                    for bq in range(4):
                        blk = b3 * 4 + bq
                        for k in range(8):
                            lst.append((PS[b3][:, bq * 128:(bq + 1) * 128], Wx[:, k, blk * 128:(blk + 1) * 128], hb[:, k, :], k == 0, k == 7))
                    S.mm(lst, [Wx, hb], [PS[b3]])
                S.cp("pool", PRE[:, :, 0:3], CARRY[:, :, :], [CARRY], [PRE])
                for b3 in range(3):
                    S.cp("act", PRE[:, 4 * b3:4 * b3 + 4, 3:131], PS[b3][:, :].rearrange("p (a b) -> p a b", a=4), [PS[b3]], [PRE])
                S.cp("pool", CARRY[:, :, :], PRE[:, :, 128:131], [PRE], [CARRY])
                if c == NCH - 1:
                    for j_ in range(3):
                        S.dma(D["p_sconv"][j_].rearrange("(b p) -> p b", p=128), PRE[:, :, 128 + j_], rbuf=PRE, allow_slow_non_contiguous=True)
                for blk in range(12):
                    S.act(CV[:, blk, :], PRE[:, blk, 0:128], AF.Identity, [PRE, CW, CB], [CV], bias=CB[:, blk:blk + 1], scale=CW[:, blk, 0:1])
                for blk in range(12):
                    for j in range(1, 4):
                        S.stt(CV[:, blk, :], PRE[:, blk, j:j + 128], CW[:, blk, j:j + 1], CV[:, blk, :], ALU.mult, ALU.add, [PRE, CW, CV], [CV])
                S.act(XC[:, :, :], CV[:, :, :], AF.Silu, [CV], [XC])
                p3v = psb(3).rearrange("p (a t) -> p a t", a=8)
                S.tr([(p3v[:, a, :], XC[:, a, :], IDb[:, :]) for a in range(8)], [XC, IDb], [PS[3]])
                S.cp("act", XT[:, :], psb(3), [PS[3]], [XT])
                S.tr([(p3v[:, a, :], XC[:, 8 + a, :], IDb[:, :]) for a in range(2)], [XC, IDb], [PS[3]])
                S.cp("act", BTOK[:, :, :], p3v[:, 0:2, :], [PS[3]], [BTOK])
                lst = []
                for k in range(8):
                    lst.append((PS[4][:, :], hb[:, k, :], Wz[:, k, 0:512], k == 0, k == 7))
                    lst.append((PS[5][:, :], hb[:, k, :], Wz[:, k, 512:1024], k == 0, k == 7))
                    lst.append((PS[6][:, 0:16], hb[:, k, :], Wdt[:, k, :], k == 0, k == 7))
                S.mm(lst, [hb, Wz, Wdt], [PS[4], PS[5], PS[6]])
                S.act(SZv[:, 0:512], PS[4][:, :], AF.Silu, [PS[4]], [A2])
                S.act(SZv[:, 512:1024], PS[5][:, :], AF.Silu, [PS[5]], [A2])
                S.tt("dve", DTR[:, :], PS[6][:, 0:16], DTB[:, :], ALU.add, [PS[6], DTB], [DTR])
                S.act(DTR[:, :], DTR[:, :], AF.Exp, [DTR], [DTR])
                S.act(DT[:, :], DTR[:, :], AF.Ln, [DTR], [DT], bias=1.0)
                S.tt("dve", DA[:, :], DT[:, :], ABC[:, :], ALU.mult, [DT, ABC], [DA])
                S.mm([(PS[6][:, 16:32], TRI[:, :], DA[:, :], True, True), (PS[6][:, 32:48], ONES[:, :], DA[:, :], True, True)], [TRI, ONES, DA], [PS[6]])
                S.cp("dve", ACS[:, :], PS[6][:, 16:48], [PS[6]], [ACS])
                S.act(EAC[:, :], ACS[:, :], AF.Exp, [ACS], [EAC])
                S.tt("dve", WEND[:, :], ACS[:, 16:32], ACS[:, 0:16], ALU.subtract, [ACS], [WEND])
                S.act(WEND[:, :], WEND[:, :], AF.Exp, [WEND], [WEND])
                S.tt("dve", WEND[:, :], WEND[:, :], DT[:, :], ALU.mult, [WEND, DT], [WEND])
                S.mm([(PS[6][:, 128 + g * 128:256 + g * 128], XC[:, 8 + g, :], XC[:, 10 + g, :], True, True) for g in range(2)], [XC], [PS[6]])
                S.tt("dve", CBM[:, :, :], PS[6][:, 128:384].rearrange("p (g i) -> p g i", g=2), TRI[:, :].unsqueeze(1).to_broadcast([128, 2, 128]), ALU.mult, [PS[6], TRI], [CBM])
                for h_ in range(16):
                    g = h_ // 8
                    pr = P7r[h_ % 4]
                    S.mm([(pr[:, :], DA[:, h_:h_ + 1].to_broadcast([128, 128]), TRI[:, :], True, True)], [DA, TRI], [pr])
                    sg, ex, wt = SEG[h_ % 2], EX[h_ % 2], WT[h_ % 2]
                    S.ts("dve", sg[:, :], pr[:, :], ACS[:, h_:h_ + 1], 0.0, ALU.subtract, ALU.min, [pr, ACS], [sg])
                    S.act(ex[:, :], sg[:, :], AF.Exp, [sg], [ex])
                    S.stt(wt[:, :], ex[:, :], DT[:, h_:h_ + 1], CBM[:, g, :], ALU.mult, ALU.mult, [ex, DT, CBM], [wt])
                    S.mm([(PS[g][:, (h_ % 8) * 64:(h_ % 8 + 1) * 64], wt[:, :], XT[:, h_ * 64:(h_ + 1) * 64], (h_ % 8 == 0), True)], [wt, XT], [PS[g]])
                S.mm([(PS[4 + g][:, :], XC[:, 10 + g, :], SB16[:, g, :], True, True) for g in range(2)], [XC, SB16], [PS[4], PS[5]])
                for g in range(2):
                    S.tt("dve", Yv[:, g * 512:(g + 1) * 512].rearrange("p (a b) -> p a b", a=8), PS[4 + g][:, :].rearrange("p (a b) -> p a b", a=8),
                         EAC[:, g * 8:(g + 1) * 8].unsqueeze(2).to_broadcast([128, 8, 64]), ALU.mult, [PS[4 + g], EAC], [A1])
                    S.tt("dve", Yv[:, g * 512:(g + 1) * 512], Yv[:, g * 512:(g + 1) * 512], PS[g][:, :], ALU.add, [A1, PS[g]], [A1])
                S.tt("pool", TMPD[:, :].rearrange("p (a b) -> p a b", a=16), XT[:, :].rearrange("p (a b) -> p a b", a=16),
                     DSK[:, :].unsqueeze(2).to_broadcast([128, 16, 64]), ALU.mult, [XT, DSK], [TMPD])
                S.tt("dve", Yv, Yv, TMPD[:, :], ALU.add, [A1, TMPD], [A1])
                S.tt("pool", Yv, Yv, SZv, ALU.mult, [A1, A2], [A1])
                S.act(TMPD[:, :], Yv, AF.Square, [A1], [TMPD, ssy], accum=ssy[:, :])
                S.ts("dve", rsy[:, :], ssy[:, :], 1.0 / 1024, EPS, ALU.mult, ALU.add, [ssy], [rsy])
                S.act(rsy[:, :], rsy[:, :], AF.Sqrt, [rsy], [rsy])
                S.op("dve", lambda h: h.reciprocal(out=rsy[:, :], in_=rsy[:, :]), [rsy], [rsy])
                S.tt("pool", YZW[:, :], Yv, SNW[:, :], ALU.mult, [A1, SNW], [YZW])
                S.tr([(p3v[:, a, :], YZW[:, a * 128:(a + 1) * 128], IDb[:, :]) for a in range(8)], [YZW, IDb], [PS[3]])
                S.cp("act", YZWT[:, :, :], p3v, [PS[3]], [YZWT])
                for hf in range(2):
                    S.mm([(PS[hf][:, :], YZWT[:, k, :], WOs[:, k, hf * 512:(hf + 1) * 512], k == 0, k == 7) for k in range(8)], [YZWT, WOs], [PS[hf]])
                    S.stt(X[c][:, hf * 512:(hf + 1) * 512], PS[hf][:, :], rsy[:, :], X[c][:, hf * 512:(hf + 1) * 512], ALU.mult, ALU.add, [PS[hf], rsy, X[c]], [X[c]])
                S.tt("pool", XW[:, :].rearrange("p (a b) -> p a b", a=16), XT[:, :].rearrange("p (a b) -> p a b", a=16),
                     WEND[:, :].unsqueeze(2).to_broadcast([128, 16, 64]), ALU.mult, [XT, WEND], [XW])
                S.mm([(PS[4 + g][:, :], BTOK[:, g, :], XW[:, g * 512:(g + 1) * 512], True, True) for g in range(2)], [BTOK, XW], [PS[4], PS[5]])
                for g in range(2):
                    S.tt("pool", S32[:, g, :].rearrange("p (a b) -> p a b", a=8), S32[:, g, :].rearrange("p (a b) -> p a b", a=8),
                         EAC[:, 16 + g * 8:16 + (g + 1) * 8].unsqueeze(2).to_broadcast([128, 8, 64]), ALU.mult, [S32, EAC], [S32])
                    S.tt("dve", S32[:, g, :], S32[:, g, :], PS[4 + g][:, :], ALU.add, [S32, PS[4 + g]], [S32])
                S.cp("pool", SB16[:, :, :], S32[:, :, :], [S32], [SB16])
            for g in range(2):
                S.tr([(PS[g][:, a * 128:(a + 1) * 128], S32[:, g, a * 128:(a + 1) * 128], IDf[:, :]) for a in range(4)], [S32, IDf], [PS[g]])
                S.cp("act", TMPD[:, g * 512:(g + 1) * 512], PS[g][:, :], [PS[g]], [TMPD])
            S.dma(D["p_ssd"].rearrange("(a q) n -> q a n", q=128), TMPD[:, :].rearrange("p (a n) -> p a n", a=8), rbuf=TMPD)
            S.pa_release(mark_ssdw)
            if phases["sample"] and phases.get("s_ssd", True):
                XPs = S.pa("XPs", [NS, 1536], F32)
                SZs = S.pa("SZs", [NS, 1024], F32)
                XCs = S.pa("XCs", [NS, 1536], F32)
                DTs = S.pa("DTs", [NS, 16], F32)
                DAs = S.pa("DAs", [NS, 16], F32)
                DECs = S.pa("DECs", [NS, 16], F32)
                lst = []
                for k in range(8):
                    for j_ in range(3):
                        lst.append((PS[j_][0:NS, :], hnTs[:, k, :], Wx[:, k, j_ * 512:(j_ + 1) * 512], k == 0, k == 7))
                    lst.append((PS[4][0:NS, :], hnTs[:, k, :], Wz[:, k, 0:512], k == 0, k == 7))
                    lst.append((PS[5][0:NS, :], hnTs[:, k, :], Wz[:, k, 512:1024], k == 0, k == 7))
                    lst.append((PS[6][0:NS, 0:16], hnTs[:, k, :], Wdt[:, k, :], k == 0, k == 7))
                S.mm(lst, [hnTs, Wx, Wz, Wdt], [PS[0], PS[1], PS[2], PS[4], PS[5], PS[6]])
                for j_ in range(3):
                    S.cp("act", XPs[:, j_ * 512:(j_ + 1) * 512], PS[j_][0:NS, :], [PS[j_]], [XPs])
                S.act(SZs[:, 0:512], PS[4][0:NS, :], AF.Silu, [PS[4]], [SZs])
                S.act(SZs[:, 512:1024], PS[5][0:NS, :], AF.Silu, [PS[5]], [SZs])
                S.tt("dve", DTs[:, :], PS[6][0:NS, 0:16], DTB[0:NS, :], ALU.add, [PS[6], DTB], [DTs])
                S.act(DTs[:, :], DTs[:, :], AF.Exp, [DTs], [DTs])
                S.act(DTs[:, :], DTs[:, :], AF.Ln, [DTs], [DTs], bias=1.0)
                S.tt("dve", DAs[:, :], DTs[:, :], ABC[0:NS, :], ALU.mult, [DTs, ABC], [DAs])
                S.act(DECs[:, :], DAs[:, :], AF.Exp, [DAs], [DECs])
                mk1 = S.pa_mark()
                CWg = S.pa("CWg", [NS, 4, 512], F32)
                SCVg = S.pa("SCVg", [NS, 3, 512], F32)
                CBs = S.pa("CBs", [NS, 1536], F32)
                S.dma(CBs[:, :], D["cb_s"], wbuf=CBs)
                for j_ in range(3):
                    cs = slice(j_ * 512, (j_ + 1) * 512)
                    S.dma(CWg[:, :, :], D["cw_s"][:, :, cs], wbuf=CWg)
                    S.dma(SCVg[:, :, :], D["sconv"][:, :, cs], wbuf=SCVg)
                    S.dma(D["s_sconv"][:, 0:2, cs], SCVg[:, 1:3, :], rbuf=SCVg)
                    S.dma(D["s_sconv"][:, 2, cs], XPs[:, cs], rbuf=XPs)
                    S.tt("pool", SCVg[:, :, :], SCVg[:, :, :], CWg[:, 0:3, :], ALU.mult, [SCVg, CWg], [SCVg])
                    S.tt("dve", XCs[:, cs], XPs[:, cs], CWg[:, 3, :], ALU.mult, [XPs, CWg], [XCs])
                    for t_ in range(3):
                        S.tt("dve", XCs[:, cs], XCs[:, cs], SCVg[:, t_, :], ALU.add, [XCs, SCVg], [XCs])
                    S.tt("dve", XCs[:, cs], XCs[:, cs], CBs[:, cs], ALU.add, [XCs, CBs], [XCs])
                S.act(XCs[:, :], XCs[:, :], AF.Silu, [XCs], [XCs])
                S.pa_release(mk1)
                TMPs = S.pa("TMPs", [NS, 1024], F32)
                YTs_ = S.pa("YTs_", [128, 8, NS], F32)
                mk2 = S.pa_mark()
                DTXT = S.pa("DTXT", [128, 8, NS], F32)
                DECT = S.pa("DECT", [128, 8, NS], F32)
                STr = S.ring("STr", 2, [128, 8, 128], F32)
                PROD = S.pa("PROD", [128, 8, 128], F32)
                S.tt("dve", TMPs[:, :].rearrange("p (a b) -> p a b", a=16), XCs[:, 0:1024].rearrange("p (a b) -> p a b", a=16),
                     DTs[:, :].unsqueeze(2).to_broadcast([NS, 16, 64]), ALU.mult, [XCs, DTs], [TMPs])
                p0v = PS[0][:, 0:8 * NS].rearrange("p (a t) -> p a t", a=8)
                S.tr([(p0v[:, a, :], TMPs[:, a * 128:(a + 1) * 128], IDf[0:NS, 0:NS]) for a in range(8)], [TMPs, IDf], [PS[0]])
                S.cp("dve", DTXT[:, :, :], p0v, [PS[0]], [DTXT])
                S.cp("dve", TMPs[:, :].rearrange("p (a b) -> p a b", a=16), DECs[:, :].unsqueeze(2).to_broadcast([NS, 16, 64]), [DECs, PS[0]], [TMPs])
                p1v = PS[1][:, 0:8 * NS].rearrange("p (a t) -> p a t", a=8)
                S.tr([(p1v[:, a, :], TMPs[:, a * 128:(a + 1) * 128], IDf[0:NS, 0:NS]) for a in range(8)], [TMPs, IDf], [PS[1]])
                S.cp("dve", DECT[:, :, :], p1v, [PS[1]], [DECT])
                for b in range(NS):
                    st = STr[b % 2]
                    pb_ = PS[2 + (b % 2)]
                    S.dma(st[:, :, :], D["sstate"][b].rearrange("(a q) n -> q a n", q=128), wbuf=st)
                    S.mm([(pb_[:, :], IDf[0:NS, b:b + 1].to_broadcast([NS, 128]), XCs[:, 1024:1536], True, True)], [IDf, XCs], [pb_])
                    S.tt("dve", st[:, :, :], st[:, :, :], DECT[:, :, b:b + 1].to_broadcast([128, 8, 128]), ALU.mult, [st, DECT], [st])
                    for g in range(2):
                        S.tt("dve", PROD[:, 4 * g:4 * g + 4, :], pb_[:, g * 128:(g + 1) * 128].unsqueeze(1).to_broadcast([128, 4, 128]),
                             DTXT[:, 4 * g:4 * g + 4, b:b + 1].to_broadcast([128, 4, 128]), ALU.mult, [pb_, DTXT], [PROD])
                    S.tt("pool", st[:, :, :], st[:, :, :], PROD[:, :, :], ALU.add, [st, PROD], [st])
                    S.dma(D["s_ssd"][b].rearrange("(a q) n -> q a n", q=128), st[:, :, :], rbuf=st)
                    for g in range(2):
                        S.tt("dve", PROD[:, 4 * g:4 * g + 4, :], st[:, 4 * g:4 * g + 4, :],
                             pb_[:, 256 + g * 128:256 + (g + 1) * 128].unsqueeze(1).to_broadcast([128, 4, 128]), ALU.mult, [st, pb_], [PROD])
                    S.op("dve", lambda h, b=b: h.tensor_reduce(out=YTs_[:, :, b], in_=PROD[:, :, :], axis=AX.X, op=ALU.add), [PROD], [YTs_])
                S.pa_release(mk2)
                for a in range(8):
                    pbk = PS[4 + a // 4]
                    S.tr([(pbk[0:NS, (a % 4) * 128:(a % 4 + 1) * 128], YTs_[:, a, :], IDf[:, :])], [YTs_, IDf], [pbk])
                Ys = S.pa("Ys", [NS, 1024], F32)
                S.tt("pool", TMPs[:, :].rearrange("p (a b) -> p a b", a=16), XCs[:, 0:1024].rearrange("p (a b) -> p a b", a=16),
                     DSK[0:NS, :].unsqueeze(2).to_broadcast([NS, 16, 64]), ALU.mult, [XCs, DSK], [TMPs])
                for hf in range(2):
                    S.tt("dve", Ys[:, hf * 512:(hf + 1) * 512], TMPs[:, hf * 512:(hf + 1) * 512], PS[4 + hf][0:NS, :], ALU.add, [TMPs, PS[4 + hf]], [Ys])
                S.tt("dve", Ys[:, :], Ys[:, :], SZs[:, :], ALU.mult, [Ys, SZs], [Ys])
                sss_ = S.pa("sss_", [NS, 1], F32)
                rss_ = S.pa("rss_", [NS, 1], F32)
                S.act(TMPs[:, :], Ys[:, :], AF.Square, [Ys], [TMPs, sss_], accum=sss_[:, :])
                S.ts("dve", rss_[:, :], sss_[:, :], 1.0 / 1024, EPS, ALU.mult, ALU.add, [sss_], [rss_])
                S.act(rss_[:, :], rss_[:, :], AF.Sqrt, [rss_], [rss_])
                S.op("dve", lambda h: h.reciprocal(out=rss_[:, :], in_=rss_[:, :]), [rss_], [rss_])
                YWs = S.pa("YWs", [NS, 1024], BF16)
                S.tt("dve", YWs[:, :], Ys[:, :], SNW[0:NS, :], ALU.mult, [Ys, SNW], [YWs])
                p3s = psb(3, 8 * NS).rearrange("p (a t) -> p a t", a=8)
                S.tr([(p3s[:, a, :], YWs[:, a * 128:(a + 1) * 128], IDb[0:NS, 0:NS]) for a in range(8)], [YWs, IDb], [PS[3]])
                YTb = S.pa("YTb", [128, 8, NS], BF16)
                S.cp("act", YTb[:, :, :], p3s, [PS[3]], [YTb])
                for hf in range(2):
                    S.mm([(PS[hf][0:NS, :], YTb[:, k, :], WOs[:, k, hf * 512:(hf + 1) * 512], k == 0, k == 7) for k in range(8)], [YTb, WOs], [PS[hf]])
                    S.stt(Xs[:, hf * 512:(hf + 1) * 512], PS[hf][0:NS, :], rss_[:, :], Xs[:, hf * 512:(hf + 1) * 512], ALU.mult, ALU.add, [PS[hf], rss_, Xs], [Xs])
            S.pa_release(mark_ssd)

        if phases["mlstm"]:
            phase_norm(1)
            WO_ = D["w_in_odd"]
            mark_ml = S.pa_mark()
            Wif = S.pa("Wif", [128, 8, 16], BF16)
            S.dma(Wif[:, :, :], WO_[:, 8192:8208].rearrange("(k p) n -> p k n", p=128), wbuf=Wif, q="pool")
            Wqk = S.ring("Wqk", 2, [128, 8, 512], BF16)
            Wvoz = S.ring("Wvoz", 2, [128, 8, 768], BF16)
            WOo = S.ring("WOo", 2, [128, 2, 1024], BF16)
            MNWh = S.ring("MNWh", 2, [128, 256], F32)

            def load_head_w(h_):
                i_ = h_ % 2
                for j_, off in enumerate((0, 2048)):
                    S.dma(Wqk[i_][:, :, j_ * 256:(j_ + 1) * 256], WO_[:, off + h_ * 256:off + (h_ + 1) * 256].rearrange("(k p) n -> p k n", p=128), wbuf=Wqk[i_], q="pool")
                for j_, off in enumerate((4096, 6144, 8208)):
                    S.dma(Wvoz[i_][:, :, j_ * 256:(j_ + 1) * 256], WO_[:, off + h_ * 256:off + (h_ + 1) * 256].rearrange("(k p) n -> p k n", p=128), wbuf=Wvoz[i_], q="pool")
                S.dma(WOo[i_][:, :, :], D["w_out_odd"][h_ * 256:(h_ + 1) * 256, :].rearrange("(k p) n -> p k n", p=128), wbuf=WOo[i_], q="pool")
                S.dma(MNWh[i_][:, :], D["mnw"][:, h_ * 256:(h_ + 1) * 256], wbuf=MNWh[i_])

            load_head_w(0)
            MCW = S.pa("MCW", [128, 32, 4], F32)
            MCB = S.pa("MCB", [128, 32], F32)
            IFB = S.pa("IFB", [128, 16], F32)
            S.dma(MCW[:, :, :], D["mcwT"], wbuf=MCW)
            S.dma(MCB[:, :], D["mcbT"], wbuf=MCB)
            S.dma(IFB[:, 0:8], D["igb"], wbuf=IFB)
            S.dma(IFB[:, 8:16], D["fgb"], wbuf=IFB)
            IFt = S.pa("IFt", [128, 16, 16], F32)
            LF = S.pa("LF", [128, 16, 8], F32)
            BCt = S.pa("BCt", [128, 16, 8], F32)
            BLB = S.pa("BLB", [128, 16, 8], F32)
            Gt = S.pa("Gt", [128, 16, 8], F32)
            At = S.pa("At", [128, 16, 8], F32)
            EBt = S.pa("EBt", [128, 16, 8], F32)
            EBL = S.pa("EBL", [128, 16, 8], F32)
            EMF = S.pa("EMF", [128, 8], F32)
            MX = S.pa("MX", [128, 1], F32)
            MXR = S.pa("MXR", [1, 128], F32)
            MF = S.pa("MF", [1, 8], F32)
            for c in range(NCH):
                S.mm([(PS[3][:, c * 16:(c + 1) * 16], hnT[c][:, k, :], Wif[:, k, :], k == 0, k == 7) for k in range(8)], [hnT[c], Wif], [PS[3]])
            S.tt("dve", IFt[:, :, :], PS[3][:, 0:256].rearrange("p (c j) -> p c j", c=16), IFB[:, :].unsqueeze(1).to_broadcast([128, 16, 16]), ALU.add, [PS[3], IFB], [IFt])
            S.act(LF[:, :, :], IFt[:, :, 8:16], AF.Exp, [IFt], [LF], scale=-1.0)
            S.act(LF[:, :, :], LF[:, :, :], AF.Ln, [LF], [LF], bias=1.0)
            S.ts("dve", LF[:, :, :], LF[:, :, :], -1.0, None, ALU.mult, None, [LF], [LF])
            lfl = LF[:, :, :].rearrange("p c h -> p (c h)")
            S.mm([(PS[3][:, 256 + c * 8:256 + (c + 1) * 8], TRI[:, :], LF[:, c, :], True, True) for c in range(NCH)] +
                 [(PS[3][:, 384:512], ONES[:, :], lfl, True, True)], [TRI, ONES, LF], [PS[3]])
            S.cp("dve", BCt[:, :, :].rearrange("p c h -> p (c h)"), PS[3][:, 256:384], [PS[3]], [BCt])
            S.cp("dve", BLB[:, :, :].rearrange("p c h -> p (c h)"), PS[3][:, 384:512], [PS[3]], [BLB])
            S.tt("dve", Gt[:, :, :], IFt[:, :, 0:8], BCt[:, :, :], ALU.subtract, [IFt, BCt], [Gt])
            S.act(At[:, :, :], Gt[:, :, :], AF.Exp, [Gt], [At], bias=float(math.log(1.0 / 16.0)))
            S.act(EBt[:, :, :], BCt[:, :, :], AF.Exp, [BCt], [EBt])
            S.act(EBL[:, :, :], BLB[:, :, :], AF.Exp, [BLB], [EBL])
            S.tr([(PS[4][:, 0:128], Gt[:, :, :].rearrange("p c h -> p (c h)"), IDf[:, :])], [Gt, IDf], [PS[4]])
            S.op("dve", lambda h: h.tensor_reduce(out=MX[:, :], in_=PS[4][:, 0:128], axis=AX.X, op=ALU.max), [PS[4]], [MX])
            S.tr([(PS[4][0:1, 128:256], MX[:, 0:1], IDf[:, :])], [MX, IDf], [PS[4]])
            S.cp("dve", MXR[:, :], PS[4][0:1, 128:256], [PS[4]], [MXR])
            S.memset("dve", MF[:, :], -1.0e30, [MF])
            for c in range(NCH):
                S.tt("dve", MF[:, :], MF[:, :], MXR[:, c * 8:(c + 1) * 8], ALU.max, [MF, MXR], [MF])
                S.tt("dve", MF[:, :], MF[:, :], BLB[0:1, c, :], ALU.add, [MF, BLB], [MF])
            S.dma(D["p_mm"], MF[:, :], rbuf=MF)
            S.mm([(PS[4][:, 256:264], ONES[0:1, :], MF[:, :], True, True)], [ONES, MF], [PS[4]])
            S.cp("dve", EMF[:, :], PS[4][:, 256:264], [PS[4]], [EMF])
            S.act(EMF[:, :], EMF[:, :], AF.Exp, [EMF], [EMF], scale=-1.0)

            if phases["sample"] and phases.get("s_ml", True):
                IFs = S.pa("IFs", [NS, 16], F32)
                LFs = S.pa("LFs", [NS, 8], F32)
                MM0 = S.pa("MM0", [NS, 8], F32)
                INTs = S.pa("INTs", [NS, 8], F32)
                MNs = S.pa("MNs", [NS, 8], F32)
                WINs = S.pa("WINs", [NS, 8], F32)
                WOUs = S.pa("WOUs", [NS, 8], F32)
                EMNs = S.pa("EMNs", [NS, 8], F32)
                BDs = S.pa("BDs", [NS, NS, 8], F32)
                WSB = S.pa("WSB", [128, NS, 8], F32)
                SCB = S.pa("SCB", [128, NS, 8], F32)
                OHr = S.pa("OHr", [NS, NS, NS], F32)
                OH = S.pa("OH", [128, NS, NS], F32)
                S.dma(MM0[:, :], D["mmm"], wbuf=MM0)
                S.mm([(PS[4][0:NS, 300:316], hnTs[:, k, :], Wif[:, k, :], k == 0, k == 7) for k in range(8)], [hnTs, Wif], [PS[4]])
                S.tt("dve", IFs[:, :], PS[4][0:NS, 300:316], IFB[0:NS, :], ALU.add, [PS[4], IFB], [IFs])
                S.act(LFs[:, :], IFs[:, 8:16], AF.Exp, [IFs], [LFs], scale=-1.0)
                S.act(LFs[:, :], LFs[:, :], AF.Ln, [LFs], [LFs], bias=1.0)
                S.ts("dve", LFs[:, :], LFs[:, :], -1.0, None, ALU.mult, None, [LFs], [LFs])
                S.tt("dve", INTs[:, :], LFs[:, :], MM0[:, :], ALU.add, [LFs, MM0], [INTs])
                S.tt("dve", MNs[:, :], INTs[:, :], IFs[:, 0:8], ALU.max, [INTs, IFs], [MNs])
                S.dma(D["s_mm"], MNs[:, :], rbuf=MNs)
                S.tt("dve", WINs[:, :], IFs[:, 0:8], MNs[:, :], ALU.subtract, [IFs, MNs], [WINs])
                S.act(WINs[:, :], WINs[:, :], AF.Exp, [WINs], [WINs])
                S.tt("dve", WOUs[:, :], INTs[:, :], MNs[:, :], ALU.subtract, [INTs, MNs], [WOUs])
                S.act(WOUs[:, :], WOUs[:, :], AF.Exp, [WOUs], [WOUs])
                S.act(EMNs[:, :], MNs[:, :], AF.Exp, [MNs], [EMNs], scale=-1.0)
                idb = IDf[0:NS, 0:NS]
                for src_, dst_ in ((WINs, WSB), (WOUs, SCB)):
                    S.tt("dve", BDs[:, :, :], src_[:, :].unsqueeze(1).to_broadcast([NS, NS, 8]), idb.unsqueeze(2).to_broadcast([NS, NS, 8]), ALU.mult, [src_, IDf], [BDs])
                    S.mm([(PS[4][:, 0:128], ONES[0:NS, :], BDs[:, :, :].rearrange("p a b -> p (a b)"), True, True)], [ONES, BDs], [PS[4]])
                    S.cp("dve", dst_[:, :, :].rearrange("p a b -> p (a b)"), PS[4][:, 0:128], [PS[4]], [dst_])
                S.tt("dve", OHr[:, :, :], idb.unsqueeze(2).to_broadcast([NS, NS, NS]), idb.unsqueeze(1).to_broadcast([NS, NS, NS]), ALU.mult, [IDf], [OHr])
                S.mm([(PS[4][:, 0:256], ONES[0:NS, :], OHr[:, :, :].rearrange("p a b -> p (a b)"), True, True)], [ONES, OHr], [PS[4]])
                S.cp("dve", OH[:, :, :].rearrange("p a b -> p (a b)"), PS[4][:, 0:256], [PS[4]], [OH])

            p2b = PS[2][:, 256:512].bitcast(BF16)
            for h_ in range(8):
                if h_ + 1 < 8:
                    load_head_w(h_ + 1)
                wqk, wvoz, woo, mnwh = Wqk[h_ % 2], Wvoz[h_ % 2], WOo[h_ % 2], MNWh[h_ % 2]
                mark_w = S.pa_mark()
                PREm = S.pa("PREm", [128, 4, 131], F32)
                CARm = S.pa("CARm", [128, 4, 3], F32)
                CVm = S.pa("CVm", [128, 4, 128], F32)
                QKc = S.ring("QKc", 2, [128, 4, 128], BF16)
                VAm = S.ring("VAm", 2, [128, 258], BF16)
                SIGO = S.pa("SIGO", [128, 256], F32)
                SZm = S.pa("SZm", [128, 256], F32)
                ATTm = S.ring("ATTm", 2, [128, 128], BF16)
                KTOK = S.ring("KTOK", 2, [128, 256], BF16)
                Hm = S.pa("Hm", [128, 256], F32)
                GZ = S.pa("GZ", [128, 256], F32)
                HG = S.pa("HG", [128, 256], BF16)
                HGT = S.pa("HGT", [128, 2, 128], BF16)
                C32 = S.pa("C32", [128, 2, 257], F32)
                C16 = S.pa("C16", [128, 2, 257], BF16)
                CO = S.pa("CO", [128, 2, 257], F32)
                DQ = S.pa("DQ", [128, 1], F32)
                RQ = S.pa("RQ", [128, 1], F32)
                BNS = S.pa("BNS", [128, 6], F32)
                MV = S.pa("MV", [128, 2], F32)
                RSD = S.pa("RSD", [128, 1], F32)
                S.memset("pool", C32[:, :, :], 0.0, [C32])
                S.memset("pool", C16[:, :, :], 0.0, [C16])
                S.memset("pool", CARm[:, :, :], 0.0, [CARm])
                cblk = [2 * h_, 2 * h_ + 1, 16 + 2 * h_, 16 + 2 * h_ + 1]
                for c in range(NCH):
                    hb = hnT[c]
                    qkc, va, att, ktok = QKc[c % 2], VAm[c % 2], ATTm[c % 2], KTOK[c % 2]
                    lst = []
                    for bq in range(4):
                        for k in range(8):
                            lst.append((PS[0][:, bq * 128:(bq + 1) * 128], wqk[:, k, bq * 128:(bq + 1) * 128], hb[:, k, :], k == 0, k == 7))
                    S.mm(lst, [wqk, hb], [PS[0]])
                    S.cp("pool", PREm[:, :, 0:3], CARm[:, :, :], [CARm], [PREm])
                    S.cp("act", PREm[:, :, 3:131], PS[0][:, :].rearrange("p (a b) -> p a b", a=4), [PS[0]], [PREm])
                    S.cp("pool", CARm[:, :, :], PREm[:, :, 128:131], [PREm], [CARm])
                    if c == NCH - 1:
                        for bq in range(4):
                            col0 = cblk[bq] * 128
                            for j_ in range(3):
                                S.dma(D["p_mconv"][j_, col0:col0 + 128].rearrange("(p o) -> p o", o=1), PREm[:, bq, 128 + j_:129 + j_], rbuf=PREm)
                    for bq in range(4):
                        S.act(CVm[:, bq, :], PREm[:, bq, 0:128], AF.Identity, [PREm, MCW, MCB], [CVm], bias=MCB[:, cblk[bq]:cblk[bq] + 1], scale=MCW[:, cblk[bq], 0:1])
                    for bq in range(4):
                        for j_ in range(1, 4):
                            S.stt(CVm[:, bq, :], PREm[:, bq, j_:j_ + 128], MCW[:, cblk[bq], j_:j_ + 1], CVm[:, bq, :], ALU.mult, ALU.add, [PREm, MCW, CVm], [CVm])
                    S.act(qkc[:, :, :], CVm[:, :, :], AF.Silu, [CVm], [qkc])
                    lst = []
                    for k in range(8):
                        lst.append((PS[1][:, :], hb[:, k, :], wvoz[:, k, 0:512], k == 0, k == 7))
                        lst.append((PS[2][:, 0:256], hb[:, k, :], wvoz[:, k, 512:768], k == 0, k == 7))
                    S.mm(lst, [hb, wvoz], [PS[1], PS[2]])
                    S.act(va[:, 0:256], PS[1][:, 0:256], AF.Copy, [PS[1], At], [va], scale=At[:, c, h_:h_ + 1])
                    S.cp("pool", va[:, 256:257], At[:, c, h_:h_ + 1], [At], [va])
                    S.act(SIGO[:, :], PS[1][:, 256:512], AF.Sigmoid, [PS[1]], [SIGO])
                    S.act(SZm[:, :], PS[2][:, 0:256], AF.Silu, [PS[2]], [SZm])
                    S.mm([(PS[3][:, 0:128], qkc[:, 2 + db, :], qkc[:, db, :], db == 0, db == 1) for db in range(2)], [qkc], [PS[3]])
                    S.tt("dve", att[:, :], PS[3][:, 0:128], TRI[:, :], ALU.mult, [PS[3], TRI], [att])
                    kv_ = p2b[:, 0:256].rearrange("p (a t) -> p a t", a=2)
                    S.tr([(kv_[:, db, :], qkc[:, 2 + db, :], IDb[:, :]) for db in range(2)], [qkc, IDb], [PS[2]])
                    S.cp("act", ktok[:, :], p2b[:, 0:256], [PS[2]], [ktok])
                    S.mm([(PS[5][:, 0:257], att[:, :], va[:, 0:257], True, False)] +
                         [(PS[5][:, 0:257], qkc[:, db, :], C16[:, db, :], False, db == 1) for db in range(2)], [att, va, qkc, C16], [PS[5]])
                    S.ts("dve", DQ[:, :], PS[5][:, 256:257], EBt[:, c, h_:h_ + 1], None, ALU.mult, None, [PS[5], EBt], [DQ])
                    S.stt(RQ[:, :], DQ[:, :], -1.0, DQ[:, :], ALU.mult, ALU.max, [DQ], [RQ])
                    S.ts("dve", DQ[:, :], RQ[:, :], 1.0, None, ALU.max, None, [RQ], [DQ])
                    S.op("dve", lambda h: h.reciprocal(out=RQ[:, :], in_=DQ[:, :]), [DQ], [RQ])
                    S.tt("dve", RQ[:, :], RQ[:, :], EBt[:, c, h_:h_ + 1], ALU.mult, [RQ, EBt], [RQ])
                    S.stt(Hm[:, :], PS[5][:, 0:256], RQ[:, :], SIGO[:, :], ALU.mult, ALU.mult, [PS[5], RQ, SIGO], [Hm])
                    S.op("dve", lambda h: h.bn_stats(out=BNS[:, :], in_=Hm[:, :]), [Hm], [BNS])
                    S.op("dve", lambda h: h.bn_aggr(out=MV[:, :], in_=BNS[:, :]), [BNS], [MV])
                    S.ts("dve", RSD[:, :], MV[:, 1:2], EPS, None, ALU.add, None, [MV], [RSD])
                    S.act(RSD[:, :], RSD[:, :], AF.Sqrt, [RSD], [RSD])
                    S.op("dve", lambda h: h.reciprocal(out=RSD[:, :], in_=RSD[:, :]), [RSD], [RSD])
                    S.ts("dve", Hm[:, :], Hm[:, :], MV[:, 0:1], RSD[:, :], ALU.subtract, ALU.mult, [Hm, MV, RSD], [Hm])
                    S.tt("pool", GZ[:, :], SZm[:, :], mnwh[:, :], ALU.mult, [SZm, mnwh], [GZ])
                    S.tt("pool", HG[:, :], Hm[:, :], GZ[:, :], ALU.mult, [Hm, GZ], [HG])
                    hv_ = p2b[:, 256:512].rearrange("p (a t) -> p a t", a=2)
                    S.tr([(hv_[:, db, :], HG[:, db * 128:(db + 1) * 128], IDb[:, :]) for db in range(2)], [HG, IDb], [PS[2]])
                    S.cp("act", HGT[:, :, :], hv_, [PS[2]], [HGT])
                    for hf in range(2):
                        S.mm([(PS[6 + hf][:, :], HGT[:, db, :], woo[:, db, hf * 512:(hf + 1) * 512], db == 0, db == 1) for db in range(2)], [HGT, woo], [PS[6 + hf]])
                        S.tt("dve", X[c][:, hf * 512:(hf + 1) * 512], X[c][:, hf * 512:(hf + 1) * 512], PS[6 + hf][:, :], ALU.add, [X[c], PS[6 + hf]], [X[c]])
                    S.mm([(PS[3][:, 128:385], ktok[:, 0:128], va[:, 0:257], True, True), (PS[4][:, 0:257], ktok[:, 128:256], va[:, 0:257], True, True)], [ktok, va], [PS[3], PS[4]])
                    S.ts("pool", C32[:, :, :], C32[:, :, :], EBL[:, c, h_:h_ + 1], None, ALU.mult, None, [C32, EBL], [C32])
                    S.stt(C32[:, 0, :], PS[3][:, 128:385], EBL[:, c, h_:h_ + 1], C32[:, 0, :], ALU.mult, ALU.add, [PS[3], EBL, C32], [C32])
                    S.stt(C32[:, 1, :], PS[4][:, 0:257], EBL[:, c, h_:h_ + 1], C32[:, 1, :], ALU.mult, ALU.add, [PS[4], EBL, C32], [C32])
                    S.cp("pool", C16[:, :, :], C32[:, :, :], [C32], [C16])
                S.ts("dve", CO[:, :, :], C32[:, :, :], EMF[:, h_:h_ + 1], None, ALU.mult, None, [C32, EMF], [CO])
                S.dma(D["p_mC"][h_].rearrange("(a p) e -> p a e", p=128), CO[:, :, 0:256], rbuf=CO)
                for db in range(2):
                    S.dma(D["p_mn"][h_, db * 128:(db + 1) * 128].rearrange("(p o) -> p o", o=1), CO[:, db, 256:257], rbuf=CO)
                S.pa_release(mark_w)
                if phases["sample"] and phases.get("s_ml", True):
                    PREs = S.pa("PREs", [NS, 512], F32)
                    QKs = S.pa("QKs", [NS, 512], F32)
                    VSs = S.pa("VSs", [NS, 256], F32)
                    SIGs = S.pa("SIGs", [NS, 256], F32)
                    SZs2 = S.pa("SZs2", [NS, 256], F32)
                    CWm = S.pa("CWm", [NS, 4, 256], F32)
                    SCVm = S.pa("SCVm", [NS, 3, 256], F32)
                    CBm = S.pa("CBm", [NS, 256], F32)
                    NSin = S.pa("NSin", [NS, 256], F32)
                    NOUT = S.pa("NOUT", [NS, 256], F32)
                    QKT = S.pa("QKTs", [128, 4, NS], F32)
                    NT = S.pa("NT", [128, 2, NS], F32)
                    WK = S.pa("WK", [128, 2, NS], F32)
                    NN = S.pa("NN", [128, 2, NS], F32)
                    QN = S.pa("QN", [128, 2, NS], F32)
                    QZ = S.ring("QZ", 2, [128, 2, NS], F32)
                    Cb_ = S.ring("Cb_", 2, [128, 2, 256], F32)
                    ADs = S.pa("ADs", [NS, 1], F32)
                    RDs = S.pa("RDs", [NS, 1], F32)
                    Hs_ = S.pa("Hs_", [NS, 256], F32)
                    GZs = S.pa("GZs", [NS, 256], F32)
                    HGs = S.pa("HGs", [NS, 256], BF16)
                    HGTs = S.pa("HGTs", [128, 2, NS], BF16)
                    BNs = S.pa("BNs", [NS, 6], F32)
                    MVs = S.pa("MVs", [NS, 2], F32)
                    RSs = S.pa("RSs", [NS, 1], F32)
                    lst = []
                    for k in range(8):
                        lst.append((PS[0][0:NS, :], hnTs[:, k, :], wqk[:, k, :], k == 0, k == 7))
                        lst.append((PS[1][0:NS, :], hnTs[:, k, :], wvoz[:, k, 0:512], k == 0, k == 7))
                        lst.append((PS[2][0:NS, 0:256], hnTs[:, k, :], wvoz[:, k, 512:768], k == 0, k == 7))
                    S.mm(lst, [hnTs, wqk, wvoz], [PS[0], PS[1], PS[2]])
                    S.cp("act", PREs[:, :], PS[0][0:NS, :], [PS[0]], [PREs])
                    S.cp("act", VSs[:, :], PS[1][0:NS, 0:256], [PS[1]], [VSs])
                    S.act(SIGs[:, :], PS[1][0:NS, 256:512], AF.Sigmoid, [PS[1]], [SIGs])
                    S.act(SZs2[:, :], PS[2][0:NS, 0:256], AF.Silu, [PS[2]], [SZs2])
                    for hq in range(2):
                        col0 = hq * 2048 + h_ * 256
                        cs = slice(col0, col0 + 256)
                        ls = slice(hq * 256, (hq + 1) * 256)
                        S.dma(CWm[:, :, :], D["mcw_s"][:, :, cs], wbuf=CWm)
                        S.dma(SCVm[:, :, :], D["mconv"][:, :, cs], wbuf=SCVm)
                        S.dma(CBm[:, :], D["mcb_s"][:, cs], wbuf=CBm)
                        S.dma(D["s_mconv"][:, 0:2, cs], SCVm[:, 1:3, :], rbuf=SCVm)
                        S.dma(D["s_mconv"][:, 2, cs], PREs[:, ls], rbuf=PREs)
                        S.tt("pool", SCVm[:, :, :], SCVm[:, :, :], CWm[:, 0:3, :], ALU.mult, [SCVm, CWm], [SCVm])
                        S.tt("dve", QKs[:, ls], PREs[:, ls], CWm[:, 3, :], ALU.mult, [PREs, CWm], [QKs])
                        for t_ in range(3):
                            S.tt("dve", QKs[:, ls], QKs[:, ls], SCVm[:, t_, :], ALU.add, [QKs, SCVm], [QKs])
                        S.tt("dve", QKs[:, ls], QKs[:, ls], CBm[:, :], ALU.add, [QKs, CBm], [QKs])
                    S.act(QKs[:, :], QKs[:, :], AF.Silu, [QKs], [QKs])
                    S.ts("dve", QKs[:, 256:512], QKs[:, 256:512], 0.0625, None, ALU.mult, None, [QKs], [QKs])
                    if phases.get("s_ml_stage", 9) < 2:
                        S.pa_release(mark_w)
                        continue
                    S.dma(NSin[:, :], D["mn"][:, h_, :], wbuf=NSin)
                    p3q = PS[3][:, 0:4 * NS].rearrange("p (a t) -> p a t", a=4)
                    p3n = PS[3][:, 64:64 + 2 * NS].rearrange("p (a t) -> p a t", a=2)
                    S.tr([(p3q[:, a, :], QKs[:, a * 128:(a + 1) * 128], IDf[0:NS, 0:NS]) for a in range(4)] +
                         [(p3n[:, a, :], NSin[:, a * 128:(a + 1) * 128], IDf[0:NS, 0:NS]) for a in range(2)], [QKs, NSin, IDf], [PS[3]])
                    S.cp("dve", QKT[:, :, :], p3q, [PS[3]], [QKT])
                    S.cp("dve", NT[:, :, :], p3n, [PS[3]], [NT])
                    S.tt("dve", WK[:, :, :], QKT[:, 2:4, :], WSB[:, :, h_].unsqueeze(1).to_broadcast([128, 2, NS]), ALU.mult, [QKT, WSB], [WK])
                    S.tt("dve", NN[:, :, :], NT[:, :, :], SCB[:, :, h_].unsqueeze(1).to_broadcast([128, 2, NS]), ALU.mult, [NT, SCB], [NN])
                    S.tt("dve", NN[:, :, :], NN[:, :, :], WK[:, :, :], ALU.add, [NN, WK], [NN])
                    S.tt("dve", QN[:, :, :], QKT[:, 0:2, :], NN[:, :, :], ALU.mult, [QKT, NN], [QN])
                    S.tr([(PS[3][0:NS, 128 + a * 128:256 + a * 128], NN[:, a, :], IDf[:, :]) for a in range(2)], [NN, IDf], [PS[3]])
                    S.mm([(PS[3][0:NS, 400:402], QN[:, db, :], ONES[:, 0:2], db == 0, db == 1) for db in range(2)], [QN, ONES], [PS[3]])
                    S.cp("dve", NOUT[:, :], PS[3][0:NS, 128:384], [PS[3]], [NOUT])
                    S.dma(D["s_mn"][:, h_, :], NOUT[:, :], rbuf=NOUT)
                    if phases.get("s_ml_stage", 9) < 3:
                        S.pa_release(mark_w)
                        continue
                    for b in range(NS):
                        cb_ = Cb_[b % 2]
                        qz = QZ[b % 2]
                        pvb = PS[4 + (b % 2)]
                        S.dma(cb_[:, :, :], D["mC"][b, h_].rearrange("(a p) e -> p a e", p=128), wbuf=cb_)
                        S.mm([(pvb[:, 0:256], IDf[0:NS, b:b + 1].to_broadcast([NS, 128]), VSs[:, :], True, True)], [IDf, VSs], [pvb])
                        S.act(cb_[:, :, :], cb_[:, :, :], AF.Copy, [cb_, SCB], [cb_], scale=SCB[:, b, h_:h_ + 1])
                        for db in range(2):
                            S.stt(cb_[:, db, :], pvb[:, 0:256], WK[:, db, b:b + 1], cb_[:, db, :], ALU.mult, ALU.add, [pvb, WK, cb_], [cb_])
                        S.dma(D["s_mC"][b, h_].rearrange("(a p) e -> p a e", p=128), cb_[:, :, :], rbuf=cb_)
                        S.tt("dve", qz[:, :, :], QKT[:, 0:2, :], OH[:, b, :].unsqueeze(1).to_broadcast([128, 2, NS]), ALU.mult, [QKT, OH], [qz])
                        S.mm([(PS[6][0:NS, 0:256], qz[:, db, :], cb_[:, db, :], (b == 0 and db == 0), (b == NS - 1 and db == 1)) for db in range(2)], [qz, cb_], [PS[6]])
                    if phases.get("s_ml_stage", 9) < 4:
                        S.pa_release(mark_w)
                        continue
                    S.cp("dve", ADs[:, :], PS[3][0:NS, 400:401], [PS[3]], [ADs])
                    S.stt(RDs[:, :], ADs[:, :], -1.0, ADs[:, :], ALU.mult, ALU.max, [ADs], [RDs])
                    S.tt("dve", RDs[:, :], RDs[:, :], EMNs[:, h_:h_ + 1], ALU.max, [RDs, EMNs], [RDs])
                    S.op("dve", lambda h, RDs=RDs: h.reciprocal(out=RDs[:, :], in_=RDs[:, :]), [RDs], [RDs])
                    S.stt(Hs_[:, :], PS[6][0:NS, 0:256], RDs[:, :], SIGs[:, :], ALU.mult, ALU.mult, [PS[6], RDs, SIGs], [Hs_])
                    if phases.get("s_ml_stage", 9) < 5:
                        S.pa_release(mark_w)
                        continue
                    S.op("dve", lambda h, BNs=BNs, Hs_=Hs_: h.bn_stats(out=BNs[:, :], in_=Hs_[:, :]), [Hs_], [BNs])
                    S.op("dve", lambda h, BNs=BNs, MVs=MVs: h.bn_aggr(out=MVs[:, :], in_=BNs[:, :]), [BNs], [MVs])
                    S.ts("dve", RSs[:, :], MVs[:, 1:2], EPS, None, ALU.add, None, [MVs], [RSs])
                    S.act(RSs[:, :], RSs[:, :], AF.Sqrt, [RSs], [RSs])
                    S.op("dve", lambda h, RSs=RSs: h.reciprocal(out=RSs[:, :], in_=RSs[:, :]), [RSs], [RSs])
                    S.ts("dve", Hs_[:, :], Hs_[:, :], MVs[:, 0:1], RSs[:, :], ALU.subtract, ALU.mult, [Hs_, MVs, RSs], [Hs_])
                    S.tt("pool", GZs[:, :], SZs2[:, :], mnwh[0:NS, :], ALU.mult, [SZs2, mnwh], [GZs])
                    S.tt("pool", HGs[:, :], Hs_[:, :], GZs[:, :], ALU.mult, [Hs_, GZs], [HGs])
                    if phases.get("s_ml_stage", 9) < 6:
                        S.pa_release(mark_w)
                        continue
                    hvs = p2b[:, 0:2 * NS].rearrange("p (a t) -> p a t", a=2)
                    S.tr([(hvs[:, db, :], HGs[:, db * 128:(db + 1) * 128], IDb[0:NS, 0:NS]) for db in range(2)], [HGs, IDb], [PS[2]])
                    S.cp("act", HGTs[:, :, :], hvs, [PS[2]], [HGTs])
                    for hf in range(2):
                        S.mm([(PS[6 + hf][0:NS, :], HGTs[:, db, :], woo[:, db, hf * 512:(hf + 1) * 512], db == 0, db == 1) for db in range(2)], [HGTs, woo], [PS[6 + hf]])
                        S.tt("dve", Xs[:, hf * 512:(hf + 1) * 512], Xs[:, hf * 512:(hf + 1) * 512], PS[6 + hf][0:NS, :], ALU.add, [Xs, PS[6 + hf]], [Xs])
                    S.pa_release(mark_w)
            S.pa_release(mark_ml)

        if phases.get("final", True):
            mark_f = S.pa_mark()
            S.dma(NW[:, :], D["normw"][2], wbuf=NW)
            junk = S.ring("fjunk", 2, [128, 1024], BF16)
            yo = S.ring("yo", 2, [128, 1024], F32)
            ss = S.ring("fss", 2, [128, 1], F32)
            rstd = S.ring("frstd", 2, [128, 1], F32)
            for c in range(NCH + 1):
                i = c % 2
                xb, np_ = (X[c], 128) if c < NCH else (Xs, NS)
                rms_rows(xb[0:np_, :], np_, ss[i], rstd[i], junk[i], xb)
                S.stt(yo[i][0:np_, :], xb[0:np_, :], rstd[i][0:np_, :], NW[0:np_, :], ALU.mult, ALU.mult, [xb, rstd[i], NW], [yo[i]])
                if c < NCH:
                    S.dma(D["y_p"][c * 128:(c + 1) * 128, :], yo[i][:, :], rbuf=yo[i])
                else:
                    S.dma(D["y_s"], yo[i][0:NS, :], rbuf=yo[i])
            S.pa_release(mark_f)
        else:
            for c in range(NCH):
                S.dma(D["y_p"][c * 128:(c + 1) * 128, :], X[c][:, :], rbuf=X[c])
            S.dma(D["y_s"], Xs[:, :], rbuf=Xs)
        S.barrier()
        S.emit()
    return nc


def _consts():
    ident = np.eye(128, dtype=np.float32)
    tri = np.triu(np.ones((128, 128), np.float32))
    k = np.arange(128)[:, None]
    q = np.arange(128)[None, :]
    blocks = []
    for Dlt in range(-3, 16):
        d = Dlt * 128 + q - k
        m = ((d >= 0) & (d <= 128)).astype(np.float32)
        m += ((d >= 0) & (d <= 512) & (d % 4 == 0)).astype(np.float32)
        m += ((d >= 0) & (d % 16 == 0)).astype(np.float32)
        blocks.append(m)
    maskmm = np.concatenate(blocks, axis=1).astype(np.float32)
    half = 8
    inv = np.power(np.float32(500000.0), -np.arange(half, dtype=np.float32) * np.float32(2.0 / 16)).astype(np.float32)
    pos = np.arange(2048, dtype=np.float32)
    ang = (pos[:, None] * inv[None, :]).astype(np.float32)
    cos = np.cos(ang).astype(np.float32)
    sin = np.sin(ang).astype(np.float32)
    cc = np.concatenate([cos, cos], axis=1).reshape(16, 128, 16).transpose(1, 0, 2)
    ss = np.concatenate([-sin, sin], axis=1).reshape(16, 128, 16).transpose(1, 0, 2)
    angs = (np.float32(2048.0) * inv).astype(np.float32)
    ccs = np.tile(np.concatenate([np.cos(angs), np.cos(angs)])[None, :], (NS, 1)).astype(np.float32)
    sss = np.tile(np.concatenate([-np.sin(angs), np.sin(angs)])[None, :], (NS, 1)).astype(np.float32)
    return dict(ident=ident, tri=tri, maskmm=maskmm, cct=np.ascontiguousarray(cc, np.float32),
                sst=np.ascontiguousarray(ss, np.float32), ccs=ccs, sss=sss)


def _rep(v, n):
    return np.ascontiguousarray(np.broadcast_to(np.asarray(v, np.float32)[None, ...], (n,) + tuple(np.shape(v))))


_NC_CACHE = {}


def kernel(x_prompt, x_sample, cache_attn_k, cache_attn_v, state_ssd_conv, state_ssd,
           state_mlstm_conv, state_mlstm_c, state_mlstm_n, state_mlstm_m,
           norm_w, final_norm_w, w_in_even, w_out_even, ssd_conv_w, ssd_conv_b,
           ssd_dt_bias, ssd_a_log, ssd_d, ssd_norm_w, w_in_odd, w_out_odd,
           mlstm_conv_w, mlstm_conv_b, mlstm_igate_b, mlstm_fgate_b, mlstm_norm_w):
    f = lambda a: np.ascontiguousarray(np.asarray(a, dtype=np.float32))
    cst = _consts()
    shared = dict(cst)
    shared["normw"] = np.stack([_rep(f(norm_w)[0], 128), _rep(f(norm_w)[1], 128), _rep(f(final_norm_w), 128)])
    shared["w_in_even"] = f(w_in_even)[0]
    shared["w_out_even"] = f(w_out_even)[0]
    shared["w_in_odd"] = f(w_in_odd)[0]
    shared["w_out_odd"] = f(w_out_odd)[0]
    cw = f(ssd_conv_w)[0]
    shared["cwT"] = np.ascontiguousarray(cw.reshape(4, 12, 128).transpose(2, 1, 0))
    shared["cbT"] = np.ascontiguousarray(f(ssd_conv_b)[0].reshape(12, 128).T)
    shared["cw_s"] = _rep(cw, NS)
    shared["cb_s"] = _rep(f(ssd_conv_b)[0], NS)
    shared["dtb"] = _rep(f(ssd_dt_bias)[0], 128)
    shared["alog"] = _rep(f(ssd_a_log)[0], 128)
    shared["dsk"] = _rep(f(ssd_d)[0], 128)
    shared["snw"] = _rep(f(ssd_norm_w)[0], 128)
    mcw = f(mlstm_conv_w)[0]
    shared["mcwT"] = np.ascontiguousarray(mcw.reshape(4, 32, 128).transpose(2, 1, 0))
    shared["mcbT"] = np.ascontiguousarray(f(mlstm_conv_b)[0].reshape(32, 128).T)
    shared["mcw_s"] = _rep(mcw, NS)
    shared["mcb_s"] = _rep(f(mlstm_conv_b)[0], NS)
    shared["igb"] = _rep(f(mlstm_igate_b)[0], 128)
    shared["fgb"] = _rep(f(mlstm_fgate_b)[0], 128)
    shared["mnw"] = _rep(f(mlstm_norm_w)[0], 128)

    xp = f(x_prompt)
    xs = f(x_sample)
    ck = np.asarray(cache_attn_k, np.float32)
    cv = np.asarray(cache_attn_v, np.float32)
    in_maps = []
    for c in range(NCORES):
        sl = slice(c * NS, (c + 1) * NS)
        m = dict(shared)
        m["xp"] = xp[c]
        m["xs"] = np.ascontiguousarray(xs[sl, 0, :])
        m["ck"] = np.ascontiguousarray(ck[0, sl].reshape(NS, 2048, 512)[:NSKV])
        m["cv"] = np.ascontiguousarray(cv[0, sl].reshape(NS, 2048, 512)[:NSKV])
        m["sconv"] = f(state_ssd_conv)[0, sl]
        m["sstate"] = np.ascontiguousarray(f(state_ssd)[0, sl].reshape(NS, 1024, 128))
        m["mconv"] = f(state_mlstm_conv)[0, sl]
        m["mC"] = f(state_mlstm_c)[0, sl]
        m["mn"] = f(state_mlstm_n)[0, sl]
        m["mmm"] = f(state_mlstm_m)[0, sl]
        in_maps.append({k: np.ascontiguousarray(v, dtype=np.float32) for k, v in m.items()})

    if "nc" not in _NC_CACHE:
        _NC_CACHE["nc"] = build_program()
    res = run_bass_kernel_spmd(_NC_CACHE["nc"], in_maps, core_ids=list(range(NCORES)))
    R = res.results

    def cat(name, shape):
        return np.stack([np.asarray(R[c][name], np.float32) for c in range(NCORES)]).reshape(shape)

    def cats(name, shape):
        return np.concatenate([np.asarray(R[c][name], np.float32) for c in range(NCORES)], axis=0).reshape(shape)

    outs = (
        cat("y_p", (8, 2048, 1024)),
        cats("y_s", (128, 1, 1024)),
        cat("p_k", (1, 8, 2048, 8, 64)), cat("p_v", (1, 8, 2048, 8, 64)),
        cat("p_sconv", (1, 8, 3, 1536)), cat("p_ssd", (1, 8, 16, 64, 128)),
        cat("p_mconv", (1, 8, 3, 4096)), cat("p_mC", (1, 8, 8, 256, 256)),
        cat("p_mn", (1, 8, 8, 256)), cat("p_mm", (1, 8, 8)),
        cats("s_k", (1, 128, 1, 8, 64)), cats("s_v", (1, 128, 1, 8, 64)),
        cats("s_sconv", (1, 128, 3, 1536)), cats("s_ssd", (1, 128, 16, 64, 128)),
        cats("s_mconv", (1, 128, 3, 4096)), cats("s_mC", (1, 128, 8, 256, 256)),
        cats("s_mn", (1, 128, 8, 256)), cats("s_mm", (1, 128, 8)),
    )
    return outs
```

```python
import math
import numpy as np
from contextlib import ExitStack
import concourse.bass as bass
import concourse.mybir as mybir
from concourse.bass_utils import run_bass_kernel_spmd

F32 = mybir.dt.float32
BF16 = mybir.dt.bfloat16
AF = mybir.ActivationFunctionType
ALU = mybir.AluOpType
AX = mybir.AxisListType

import os
NSKV = 1 if os.environ.get("DEV_SMALLKV") else 16
NCORES = 8
NCH = 16
NS = 16
EPS = 1e-6


class Buf:
    __slots__ = ("name", "t", "last_w", "readers", "dsem", "ndma")

    def __init__(self, name, t=None):
        self.name = name
        self.t = t
        self.last_w = None
        self.readers = []
        self.dsem = None
        self.ndma = 0

    def __getitem__(self, k):
        return self.t[k]


class DSem:
    __slots__ = ("sem", "n", "q")

    def __init__(self, sem, q):
        self.sem = sem
        self.n = 0
        self.q = q


class Sched:
    ENG = ("pe", "act", "dve", "pool", "sp")

    def __init__(self, nc, es, arena_words):
        self.nc = nc
        self.es = es
        self.sem = {e: es.enter_context(nc.semaphore("s_" + e)) for e in ("pe", "act", "dve", "pool")}
        self.cnt = {e: 0 for e in self.ENG}
        self.waited = {e: {} for e in self.ENG}
        self.ops = {e: [] for e in self.ENG}
        self.dma_bufs = []
        self.free_dsems = []
        self.arena = es.enter_context(nc.sbuf_tensor("arena", [128, arena_words], F32))
        self.arena_words = arena_words
        self.atop = 0
        self.nalloc = 0
        self.pa_bufs = []

    def sb(self, name, shape, dt):
        return Buf(name, self.es.enter_context(self.nc.sbuf_tensor(name, list(shape), dt)))

    def ps(self, name, shape, dt):
        return Buf(name, self.es.enter_context(self.nc.psum_tensor(name, list(shape), dt)))

    def pa(self, name, shape, dt):
        n = 1
        for s in shape[1:]:
            n *= s
        words = (n + 1) // 2 if dt == BF16 else n
        words = (words + 1) // 2 * 2
        off = self.atop
        self.atop += words
        assert self.atop <= self.arena_words, (name, self.atop, self.arena_words)
        v = self.arena[0:shape[0], off:off + words]
        if dt == BF16:
            v = v.bitcast(BF16)
        v = v[:, 0:n]
        if len(shape) == 3:
            v = v.rearrange("p (a b) -> p a b", a=shape[1])
        elif len(shape) == 4:
            v = v.rearrange("p (a b c) -> p a b c", a=shape[1], b=shape[2])
        self.nalloc += 1
        b = Buf("%s_%d" % (name, self.nalloc), v)
        self.pa_bufs.append((off, b))
        return b

    def ring(self, name, n, shape, dt):
        return [self.pa("%s%d" % (name, i), shape, dt) for i in range(n)]

    def pa_mark(self):
        return self.atop

    def pa_release(self, mark):
        self.barrier()
        keep = []
        for off, b in self.pa_bufs:
            if off >= mark:
                if b.dsem is not None:
                    self.free_dsems.append(b.dsem)
                    b.dsem = None
            else:
                keep.append((off, b))
        self.pa_bufs = keep
        self.atop = mark

    def _deps(self, eng, reads, writes, skip_sem=None):
        evs = []
        for b in reads:
            if b.last_w is not None:
                evs.append(b.last_w)
        for b in writes:
            if b.last_w is not None and not (skip_sem is not None and b.last_w[0] is skip_sem and b.last_w[2] == "dma"):
                evs.append(b.last_w)
            evs.extend(b.readers)
        w = self.waited[eng]
        best = {}
        for (s, v, src) in evs:
            if src == "pe" and eng == "pe":
                continue
            key = id(s)
            if w.get(key, 0) < v:
                w[key] = v
                best[key] = (s, v)
        return list(best.values())

    def op(self, eng, fns, reads=(), writes=()):
        if callable(fns):
            fns = [fns]
        deps = self._deps(eng, reads, writes)
        self.cnt[eng] += 1
        idx = self.cnt[eng]
        sem = self.sem[eng]
        self.ops[eng].append((deps, fns, (sem, 1)))
        ev = (sem, idx, eng)
        for b in writes:
            b.last_w = ev
            b.readers = []
        for b in reads:
            if b not in writes:
                b.readers.append(ev)

    def dma(self, out_ap, in_ap, rbuf=None, wbuf=None, q="sp", **kw):
        b = wbuf if wbuf is not None else rbuf
        if b.dsem is None:
            cand = [d for d in self.free_dsems if d.q == q]
            if cand:
                b.dsem = cand[-1]
                self.free_dsems.remove(cand[-1])
            else:
                b.dsem = DSem(self.es.enter_context(self.nc.semaphore("d%d" % len(self.dma_bufs))), q)
                self.dma_bufs.append(b.dsem)
        assert b.dsem.q == q, (b.name, q)
        reads = [rbuf] if rbuf is not None else []
        writes = [wbuf] if wbuf is not None else []
        deps = self._deps(q, reads, writes, skip_sem=(b.dsem.sem if wbuf is not None and rbuf is None else None))
        b.dsem.n += 1
        ev = (b.dsem.sem, 16 * b.dsem.n, "dma")
        self.ops[q].append((deps, [lambda h: h.dma_start(out=out_ap, in_=in_ap, **kw)], (b.dsem.sem, 16)))
        if wbuf is not None:
            wbuf.last_w = ev
            wbuf.readers = []
        if rbuf is not None and rbuf is not wbuf:
            rbuf.readers.append(ev)

    def barrier(self):
        for e in self.ENG:
            deps = []
            w = self.waited[e]
            for e2 in ("pe", "act", "dve", "pool"):
                s = self.sem[e2]
                v = self.cnt[e2]
                if v > 0 and w.get(id(s), 0) < v:
                    w[id(s)] = v
                    deps.append((s, v))
            for ds in self.dma_bufs:
                v = 16 * ds.n
                if w.get(id(ds.sem), 0) < v:
                    w[id(ds.sem)] = v
                    deps.append((ds.sem, v))
            if deps:
                self.ops[e].append((deps, [], None))

    def emit(self):
        with self.nc.Block() as block:
            def mk(e):
                def body(h):
                    for deps, fns, inc in self.ops[e]:
                        for s, v in deps:
                            h.wait_ge(s, v)
                        for i, f in enumerate(fns):
                            ins = f(h)
                            if i == len(fns) - 1 and inc is not None:
                                ins.then_inc(inc[0], inc[1])
                return body
            block.tensor(mk("pe"))
            block.scalar(mk("act"))
            block.vector(mk("dve"))
            block.gpsimd(mk("pool"))
            block.sync(mk("sp"))

    def tt(self, eng, out, in0, in1, op, r, w):
        self.op(eng, lambda h: h.tensor_tensor(out=out, in0=in0, in1=in1, op=op), r, w)

    def ts(self, eng, out, in0, s1, s2, op0, op1, r, w):
        if s2 is None:
            self.op(eng, lambda h: h.tensor_scalar(out=out, in0=in0, scalar1=s1, scalar2=None, op0=op0), r, w)
        else:
            self.op(eng, lambda h: h.tensor_scalar(out=out, in0=in0, scalar1=s1, scalar2=s2, op0=op0, op1=op1), r, w)

    def stt(self, out, in0, scalar, in1, op0, op1, r, w):
        self.op("dve", lambda h: h.scalar_tensor_tensor(out=out, in0=in0, scalar=scalar, in1=in1, op0=op0, op1=op1), r, w)

    def act(self, out, in_, func, r, w, bias=None, scale=None, accum=None):
        kw = {}
        if bias is not None:
            kw["bias"] = bias
        if scale is not None:
            kw["scale"] = scale
        if accum is not None:
            kw["accum_out"] = accum
        self.op("act", lambda h: h.activation(out=out, in_=in_, func=func, **kw), r, w)

    def cp(self, eng, out, in_, r, w):
        if eng == "act":
            self.op("act", lambda h: h.copy(out=out, in_=in_), r, w)
        else:
            self.op(eng, lambda h: h.tensor_copy(out=out, in_=in_), r, w)

    def mm(self, lst, r, w):
        self.op("pe", [(lambda h, a=a: h.matmul(out=a[0], lhsT=a[1], rhs=a[2], start=a[3], stop=a[4], skip_group_check=True)) for a in lst], r, w)

    def tr(self, lst, r, w):
        self.op("pe", [(lambda h, a=a: h.transpose(out=a[0], in_=a[1], identity=a[2])) for a in lst], r, w)

    def memset(self, eng, ap, val, w):
        self.op(eng, lambda h: h.memset(ap, val), (), w)


IN_SPECS = [
    ("xp", [2048, 1024]), ("xs", [NS, 1024]),
    ("ck", [NSKV, 2048, 512]), ("cv", [NSKV, 2048, 512]),
    ("sconv", [NS, 3, 1536]), ("sstate", [NS, 1024, 128]),
    ("mconv", [NS, 3, 4096]), ("mC", [NS, 8, 256, 256]), ("mn", [NS, 8, 256]), ("mmm", [NS, 8]),
    ("normw", [3, 128, 1024]),
    ("w_in_even", [1024, 4624]), ("w_out_even", [1536, 1024]),
    ("w_in_odd", [1024, 10256]), ("w_out_odd", [2048, 1024]),
    ("ident", [128, 128]), ("tri", [128, 128]), ("maskmm", [128, 19 * 128]),
    ("cct", [128, 16, 16]), ("sst", [128, 16, 16]), ("ccs", [NS, 16]), ("sss", [NS, 16]),
    ("cwT", [128, 12, 4]), ("cbT", [128, 12]), ("cw_s", [NS, 4, 1536]), ("cb_s", [NS, 1536]),
    ("dtb", [128, 16]), ("alog", [128, 16]), ("dsk", [128, 16]), ("snw", [128, 1024]),
    ("mcwT", [128, 32, 4]), ("mcbT", [128, 32]), ("mcw_s", [NS, 4, 4096]), ("mcb_s", [NS, 4096]),
    ("igb", [128, 8]), ("fgb", [128, 8]), ("mnw", [128, 2048]),
]
OUT_SPECS = [
    ("y_p", [2048, 1024]), ("y_s", [NS, 1024]),
    ("p_k", [2048, 512]), ("p_v", [2048, 512]), ("p_sconv", [3, 1536]), ("p_ssd", [1024, 128]),
    ("p_mconv", [3, 4096]), ("p_mC", [8, 256, 256]), ("p_mn", [8, 256]), ("p_mm", [1, 8]),
    ("s_k", [NS, 512]), ("s_v", [NS, 512]), ("s_sconv", [NS, 3, 1536]), ("s_ssd", [NS, 1024, 128]),
    ("s_mconv", [NS, 3, 4096]), ("s_mC", [NS, 8, 256, 256]), ("s_mn", [NS, 8, 256]), ("s_mm", [NS, 8]),
]

PHASES = dict(attn=True, ssd=True, mlstm=True, sample=True, b2=True, b4=True, b3=True)


def build_program(phases=PHASES):
    nc = bass.Bass("TRN2", target_bir_lowering=False)
    D = {}
    for name, shape in IN_SPECS:
        D[name] = nc.dram_tensor(name, list(shape), F32, kind="ExternalInput").ap()
    for name, shape in OUT_SPECS:
        D[name] = nc.dram_tensor(name, list(shape), F32, kind="ExternalOutput").ap()

    with ExitStack() as es:
        ARENA = 26000
        S = Sched(nc, es, ARENA)
        Xall = es.enter_context(nc.sbuf_tensor("Xall", [128, NCH, 1024], F32))
        X = [Buf("X%d" % c, Xall[:, c, :]) for c in range(NCH)]
        Xs = S.sb("Xs", [NS, 1024], F32)
        hnTall = es.enter_context(nc.sbuf_tensor("hnTall", [128, 8, 2048], BF16))
        hnT = [Buf("hnT%d" % c, hnTall[:, :, c * 128:(c + 1) * 128]) for c in range(NCH)]
        hnTs = S.sb("hnTs", [128, 8, NS], BF16)
        IDf = S.sb("IDf", [128, 128], F32)
        IDb = S.sb("IDb", [128, 128], BF16)
        TRI = S.sb("TRI", [128, 128], F32)
        ONES = S.sb("ONES", [128, 128], F32)
        NW = S.sb("NW", [128, 1024], F32)
        PS = [S.ps("PS%d" % i, [128, 512], F32) for i in range(8)]

        def psb(i, n=1024):
            return PS[i][:, :].bitcast(BF16)[:, 0:n]

        for c in range(NCH):
            S.dma(X[c][:, :], D["xp"][c * 128:(c + 1) * 128, :], wbuf=X[c])
        S.dma(Xs[:, :], D["xs"], wbuf=Xs)
        S.dma(IDf[:, :], D["ident"], wbuf=IDf)
        S.dma(TRI[:, :], D["tri"], wbuf=TRI)
        S.cp("dve", IDb[:, :], IDf[:, :], [IDf], [IDb])
        S.memset("pool", ONES[:, :], 1.0, [ONES])

        def rms_rows(xap, np_, ss, rstd, junk, xb, eng2="dve"):
            S.act(junk[0:np_, :], xap, AF.Square, [xb], [junk, ss], accum=ss[0:np_, :])
            S.ts("dve", rstd[0:np_, :], ss[0:np_, :], 1.0 / 1024, EPS, ALU.mult, ALU.add, [ss], [rstd])
            S.act(rstd[0:np_, :], rstd[0:np_, :], AF.Sqrt, [rstd], [rstd])
            S.op("dve", lambda h: h.reciprocal(out=rstd[0:np_, :], in_=rstd[0:np_, :]), [rstd], [rstd])

        def phase_norm(layer):
            mark = S.pa_mark()
            S.dma(NW[:, :], D["normw"][layer], wbuf=NW)
            junk = S.ring("junk", 2, [128, 1024], BF16)
            hn = S.ring("hn", 2, [128, 1024], BF16)
            ss = S.ring("ss", 2, [128, 1], F32)
            rstd = S.ring("rstd", 2, [128, 1], F32)
            for c in range(NCH + 1):
                i = c % 2
                if c < NCH:
                    xb, np_, dst = X[c], 128, hnT[c]
                    dap = hnT[c][:, :, :]
                else:
                    xb, np_, dst = Xs, NS, hnTs
                    dap = hnTs[:, :, :]
                rms_rows(xb[0:np_, :], np_, ss[i], rstd[i], junk[i], xb)
                S.stt(hn[i][0:np_, :], xb[0:np_, :], rstd[i][0:np_, :], NW[0:np_, :], ALU.mult, ALU.mult, [xb, rstd[i], NW], [hn[i]])
                pb = 6 + i
                pv = psb(pb).rearrange("p (k t) -> p k t", k=8)[:, :, 0:np_]
                S.tr([(pv[:, k, :], hn[i][0:np_, k * 128:(k + 1) * 128], IDb[0:np_, 0:np_]) for k in range(8)], [hn[i], IDb], [PS[pb]])
                S.cp("act", dap, pv, [PS[pb]], [dst])
            S.pa_release(mark)

        phase_norm(0)

        WE = D["w_in_even"]
        if phases["attn"]:
            mark_attn = S.pa_mark()
            ATGT = S.pa("ATGT", [128, 4, 2048], BF16)
            ATGTs = S.pa("ATGTs", [128, 4, NS], BF16)
            QS = S.pa("QS", [NS, 512], F32)
            KS = S.pa("KS", [NS, 512], F32)
            VS = S.pa("VS", [NS, 512], F32)
            SGS = S.pa("SGS", [NS, 512], F32)
            CCt = S.pa("CCt", [128, 16, 16], F32)
            SSt = S.pa("SSt", [128, 16, 16], F32)
            CCs = S.pa("CCs", [NS, 16], F32)
            SSs = S.pa("SSs", [NS, 16], F32)
            S.dma(CCt[:, :, :], D["cct"], wbuf=CCt)
            S.dma(SSt[:, :, :], D["sst"], wbuf=SSt)
            S.dma(CCs[:, :], D["ccs"], wbuf=CCs)
            S.dma(SSs[:, :], D["sss"], wbuf=SSs)
            mark_pairs = S.pa_mark()
            MM = S.pa("MM", [128, 19 * 128], BF16)
            for hf_ in range(2):
                S.dma(MM[:, hf_ * 1216:(hf_ + 1) * 1216], D["maskmm"][:, hf_ * 1216:(hf_ + 1) * 1216], wbuf=MM, q="pool")
            WP = S.ring("WP", 2, [128, 8, 512], BF16)
            QKT = S.pa("QKT", [128, 2, 2048], BF16)
            VA = S.pa("VA", [128, 16, 2, 65], BF16)
            SG = S.pa("SG", [128, 16, 128], BF16)
            ATG = S.pa("ATG", [128, 16, 128], BF16)
            QK = S.ring("QK", 2, [128, 256], F32)
            QSRC = S.ring("QSRC", 2, [128, 256], F32)
            TA = S.ring("TA", 2, [128, 4, 16], F32)
            TB = S.ring("TB", 2, [128, 4, 16], F32)
            VF = S.ring("VF", 2, [128, 128], F32)
            QKb = S.ring("QKb", 2, [128, 256], BF16)
            Eb = S.ring("Eb", 3, [128, 512], BF16)
            Pb = S.ring("Pb", 3, [128, 512], BF16)
            ATT = S.ring("ATT", 2, [128, 4, 64], F32)
            REC = S.ring("REC", 2, [128, 4, 1], F32)
            S.memset("pool", VA[:, :, :, 64:65], 1.0, [VA])

            def load_pair_w(p):
                wb = WP[p % 2]
                for j in range(4):
                    col = j * 512 + p * 128
                    S.dma(wb[:, :, j * 128:(j + 1) * 128], WE[:, col:col + 128].rearrange("(k p) n -> p k n", p=128), wbuf=wb, q="pool")

            def rotary(psv, np_, cc, sn, ta, tb, dst4, rd, dbuf):
                ccb = cc.unsqueeze(1).to_broadcast([np_, 4, 16])
                S.tt("dve", ta[0:np_, :, :], psv[:, :, 0:16], ccb, ALU.mult, rd, [ta])
                S.tt("dve", tb[0:np_, :, 0:8], psv[:, :, 8:16], sn[:, 0:8].unsqueeze(1).to_broadcast([np_, 4, 8]), ALU.mult, rd, [tb])
                S.tt("dve", tb[0:np_, :, 8:16], psv[:, :, 0:8], sn[:, 8:16].unsqueeze(1).to_broadcast([np_, 4, 8]), ALU.mult, rd, [tb])
                S.tt("pool" if phases.get('rotpool', True) else "dve", dst4[:, :, 0:16], ta[0:np_, :, :], tb[0:np_, :, :], ALU.add, [ta, tb], [dbuf])

            load_pair_w(0)
            for p in range(phases.get('npair', 4)):
                if p + 1 < 4:
                    load_pair_w(p + 1)
                wb = WP[p % 2]
                for c in range(NCH + 1):
                    if c >= phases.get('nchunk', 99) and c < NCH:
                        continue
                    if c == NCH and not phases.get('smpc', True):
                        continue
                    i = c % 2
                    smp = (c == NCH)
                    np_ = NS if smp else 128
                    hb = hnTs if smp else hnT[c]
                    pu = PS[i]
                    S.mm([(pu[0:np_, :], hb[:, k, :], wb[:, k, :], k == 0, k == 7) for k in range(8)], [hb, wb], [pu])
                    qk = QK[i]
                    S.cp("act", qk[0:np_, :], pu[0:np_, 0:256], [pu], [qk])
                    psv = pu[0:np_, 0:256].rearrange("p (a b) -> p a b", a=4)
                    qk4 = qk[0:np_, :].rearrange("p (a b) -> p a b", a=4)
                    rsrc = pu
                    if phases.get('rotsb', True):
                        qsrc = QSRC[i]
                        S.cp("act", qsrc[0:np_, :], pu[0:np_, 0:256], [pu], [qsrc])
                        psv = qsrc[0:np_, :].rearrange("p (a b) -> p a b", a=4)
                        rsrc = qsrc
                    if not phases.get('rot', True):
                        pass
                    elif smp:
                        rotary(psv, np_, CCs[:, :], SSs[:, :], TA[i], TB[i], qk4, [rsrc, CCs, SSs], qk)
                    else:
                        rotary(psv, np_, CCt[:, c, :], SSt[:, c, :], TA[i], TB[i], qk4, [rsrc, CCt, SSt], qk)
                    vf = VF[i]
                    S.cp("act", vf[0:np_, :], pu[0:np_, 256:384], [pu], [vf])
                    if smp and not phases.get('smp', True):
                        pass
                    elif smp:
                        S.dma(D["s_k"][:, p * 128:(p + 1) * 128], qk[0:NS, 128:256], rbuf=qk)
                        S.dma(D["s_v"][:, p * 128:(p + 1) * 128], vf[0:NS, :], rbuf=vf)
                        S.cp("pool", QS[:, p * 128:(p + 1) * 128], qk[0:NS, 0:128], [qk], [QS])
                        S.cp("pool", KS[:, p * 128:(p + 1) * 128], qk[0:NS, 128:256], [qk], [KS])
                        S.cp("pool", VS[:, p * 128:(p + 1) * 128], vf[0:NS, :], [vf], [VS])
                        S.act(SGS[:, p * 128:(p + 1) * 128], pu[0:NS, 384:512], AF.Silu, [pu], [SGS])
                    else:
                        S.dma(D["p_k"][c * 128:(c + 1) * 128, p * 128:(p + 1) * 128], qk[:, 128:256], rbuf=qk)
                        S.dma(D["p_v"][c * 128:(c + 1) * 128, p * 128:(p + 1) * 128], vf[:, :], rbuf=vf)
                        S.cp("pool", VA[:, c, :, 0:64], vf[:, :].rearrange("p (a b) -> p a b", a=2), [vf], [VA])
                        S.act(SG[:, c, :], pu[:, 384:512], AF.Silu, [pu], [SG])
                        if not phases.get('trq', True):
                            continue
                        qb = QKb[i]
                        S.cp("pool", qb[:, :], qk[:, :], [qk], [qb])
                        pt = 2 + i
                        ptv = psb(pt, 256).rearrange("p (a t) -> p a t", a=2)
                        S.tr([(ptv[:, a, :], qb[:, a * 128:(a + 1) * 128], IDb[:, :]) for a in range(2)], [qb, IDb], [PS[pt]])
                        S.cp("act", QKT[:, :, c * 128:(c + 1) * 128], ptv, [PS[pt]], [QKT])
                it = 0
                for hh in range(2 if phases.get('b2', True) else 0):
                    hs = slice(hh * 64, (hh + 1) * 64)
                    for g in range(4):
                        po = PS[6 + (g % 2)]
                        pov = po[:, 0:260].rearrange("p (j e) -> p j e", j=4)
                        nk = 4 * g + 4
                        for kc in range(nk):
                            j0 = max(0, kc - 4 * g)
                            cs = slice(j0 * 128, 512)
                            ps_ = PS[4 + (it % 2)]
                            eb = Eb[it % 3]
                            pb_ = Pb[it % 3]
                            S.mm([(ps_[:, cs], QKT[hs, 1, kc * 128:(kc + 1) * 128], QKT[hs, 0, (4 * g + j0) * 128:(4 * g + 4) * 128], True, True)], [QKT], [ps_])
                            S.act(eb[:, cs], ps_[:, cs], AF.Exp, [ps_], [eb], scale=0.125)
                            m0 = (4 * g + j0 - kc + 3) * 128
                            S.tt("dve" if it % 2 == 0 else "pool", pb_[:, cs], eb[:, cs], MM[:, m0:m0 + (4 - j0) * 128], ALU.mult, [eb, MM], [pb_])
                            S.mm([(pov[:, j, :], pb_[:, j * 128:(j + 1) * 128], VA[:, kc, hh, :], (kc == 0 and j == 0), (kc == 4 * g + j)) for j in range(j0, 4)], [pb_, VA], [po])
                            it += 1
                        rec = REC[g % 2]
                        att = ATT[g % 2]
                        S.op("dve", lambda h, rec=rec, pov=pov: h.reciprocal(out=rec[:, :, :], in_=pov[:, :, 64:65]), [po], [rec])
                        S.tt("dve", att[:, :, :], pov[:, :, 0:64], rec[:, :, :].to_broadcast([128, 4, 64]), ALU.mult, [po, rec], [att])
                        S.tt("pool", ATG[:, 4 * g:4 * g + 4, hs], att[:, :, :], SG[:, 4 * g:4 * g + 4, hs], ALU.mult, [att, SG], [ATG])
                for cg in range(4 if phases.get('b2', True) else 0):
                    pt = 2 + (cg % 2)
                    ptv = psb(pt, 512).rearrange("p (a t) -> p a t", a=4)
                    S.tr([(ptv[:, a, :], ATG[:, 4 * cg + a, :], IDb[:, :]) for a in range(4)], [ATG, IDb], [PS[pt]])
                    S.cp("act", ATGT[:, p, cg * 512:(cg + 1) * 512], psb(pt, 512), [PS[pt]], [ATGT])
            S.pa_release(mark_pairs)

            QSb = S.pa("QSb", [NS, 512], BF16)
            S.cp("dve", QSb[:, :], QS[:, :], [QS], [QSb])
            SELb = S.pa("SELb", [NS, NS * 128], BF16)
            S.memset("pool", SELb[:, :], 0.0, [SELb])
            S.tt("pool", SELb[:, :].rearrange("k (b m) -> k b m", b=NS), IDf[0:NS, 0:NS].unsqueeze(2).to_broadcast([NS, NS, 128]),
                 ONES[0:NS, :].unsqueeze(1).to_broadcast([NS, NS, 128]), ALU.mult, [IDf, ONES, SELb], [SELb])
            Kc = S.ring("Kc", 3, [128, 512], F32)
            Vc = S.ring("Vc", 3, [128, 512], F32)
            Vb = S.ring("Vb", 3, [128, 8, 65], BF16)
            PR = S.ring("PR", 2, [128, 512], F32)
            SC = S.ring("SC", 2, [128, 8], F32)
            PZ = S.ring("PZ", 3, [128, 8, NS], BF16)
            for v_ in Vb:
                S.memset("pool", v_[:, :, 64:65], 1.0, [v_])
            pos0 = PS[6][0:NS, 0:260].rearrange("p (j e) -> p j e", j=4)
            pos1 = PS[7][0:NS, 0:260].rearrange("p (j e) -> p j e", j=4)
            it = 0
            for b in range(NS if phases.get('b4', True) else 0):
                pq = PS[b % 2]
                S.mm([(pq[:, :], SELb[:, b * 128:(b + 1) * 128], QSb[:, :], True, True)], [SELb, QSb], [pq])
                for pat, dil in enumerate((1, 4, 16)):
                    r0 = 2048 - 128 * dil
                    kc_, vc_, vb_, pr, sc, pz = Kc[it % 3], Vc[it % 3], Vb[it % 3], PR[it % 2], SC[it % 2], PZ[it % 3]
                    S.dma(kc_[:, :], D["ck"][b, r0:2048:dil, :], wbuf=kc_)
                    S.dma(vc_[:, :], D["cv"][b, r0:2048:dil, :], wbuf=vc_)
                    S.tt("dve", pr[:, :], kc_[:, :], pq[:, :], ALU.mult, [kc_, pq], [pr])
                    S.op("dve", lambda h, sc=sc, pr=pr: h.tensor_reduce(out=sc[:, :], in_=pr[:, :].rearrange("p (a b) -> p a b", a=8), axis=AX.X, op=ALU.add), [pr], [sc])
                    S.memset("pool", pz[:, :, :], 0.0, [pz])
                    S.act(pz[:, :, b], sc[:, :], AF.Exp, [sc, pz], [pz], scale=0.125)
                    S.cp("pool", vb_[:, :, 0:64], vc_[:, :].rearrange("p (a b) -> p a b", a=8), [vc_], [vb_])
                    first = (b == 0 and pat == 0)
                    last = (b == NS - 1 and pat == 2)
                    S.mm([((pos0 if h_ < 4 else pos1)[:, h_ % 4, :], pz[:, h_, :], vb_[:, h_, :], first and (h_ % 4 == 0), last) for h_ in range(8)],
                         [pz, vb_], [PS[6], PS[7]])
                    it += 1
            OS = S.pa("OS", [NS, 8, 65], F32)
            S.cp("act", OS[:, 0:4, :], pos0, [PS[6]], [OS])
            S.cp("act", OS[:, 4:8, :], pos1, [PS[7]], [OS])
            PRs = S.pa("PRs", [NS, 512], F32)
            SCs = S.pa("SCs", [NS, 8], F32)
            S.tt("dve", PRs[:, :], QS[:, :], KS[:, :], ALU.mult, [QS, KS], [PRs])
            S.op("dve", lambda h: h.tensor_reduce(out=SCs[:, :], in_=PRs[:, :].rearrange("p (a b) -> p a b", a=8), axis=AX.X, op=ALU.add), [PRs], [SCs])
            S.act(SCs[:, :], SCs[:, :], AF.Exp, [SCs], [SCs], scale=0.125, bias=None)
            S.ts("dve", SCs[:, :], SCs[:, :], 3.0, None, ALU.mult, None, [SCs], [SCs])
            S.tt("dve", PRs[:, :].rearrange("p (a b) -> p a b", a=8), VS[:, :].rearrange("p (a b) -> p a b", a=8),
                 SCs[:, :].unsqueeze(2).to_broadcast([NS, 8, 64]), ALU.mult, [VS, SCs], [PRs])
            S.tt("dve", OS[:, :, 0:64], OS[:, :, 0:64], PRs[:, :].rearrange("p (a b) -> p a b", a=8), ALU.add, [OS, PRs], [OS])
            S.tt("dve", OS[:, :, 64:65], OS[:, :, 64:65], SCs[:, :].unsqueeze(2), ALU.add, [OS, SCs], [OS])
            RS = S.pa("RS", [NS, 8, 1], F32)
            S.op("dve", lambda h: h.reciprocal(out=RS[:, :, :], in_=OS[:, :, 64:65]), [OS], [RS])
            S.tt("dve", PRs[:, :].rearrange("p (a b) -> p a b", a=8), OS[:, :, 0:64], RS[:, :, :].to_broadcast([NS, 8, 64]), ALU.mult, [OS, RS], [PRs])
            ATGs = S.pa("ATGs", [NS, 512], BF16)
            S.tt("dve", ATGs[:, :], PRs[:, :], SGS[:, :], ALU.mult, [PRs, SGS], [ATGs])
            ptv = psb(2, 4 * NS).rearrange("p (a t) -> p a t", a=4)
            S.tr([(ptv[:, a, :], ATGs[:, a * 128:(a + 1) * 128], IDb[0:NS, 0:NS]) for a in range(4)], [ATGs, IDb], [PS[2]])
            S.cp("act", ATGTs[:, :, :], ptv, [PS[2]], [ATGTs])

            WOa = S.pa("WOa", [128, 4, 1024], BF16)
            S.dma(WOa[:, :, :], D["w_out_even"][0:512, :].rearrange("(k p) n -> p k n", p=128), wbuf=WOa, q="pool")
            for c in range(NCH + 1 if phases.get('b3', True) else 0):
                smp = (c == NCH)
                np_ = NS if smp else 128
                xb = Xs if smp else X[c]
                for hf in range(2):
                    pb = PS[2 * (c % 2) + hf]
                    if smp:
                        lst = [(pb[0:NS, :], ATGTs[:, k, :], WOa[:, k, hf * 512:(hf + 1) * 512], k == 0, k == 3) for k in range(4)]
                        S.mm(lst, [ATGTs, WOa], [pb])
                    else:
                        lst = [(pb[:, :], ATGT[:, k, c * 128:(c + 1) * 128], WOa[:, k, hf * 512:(hf + 1) * 512], k == 0, k == 3) for k in range(4)]
                        S.mm(lst, [ATGT, WOa], [pb])
                    S.tt("dve", xb[0:np_, hf * 512:(hf + 1) * 512], xb[0:np_, hf * 512:(hf + 1) * 512], pb[0:np_, :], ALU.add, [xb, pb], [xb])
            S.pa_release(mark_attn)

        if phases["ssd"]:
            mark_ssd = S.pa_mark()
            Wz = S.pa("Wz", [128, 8, 1024], BF16)
            Wx = S.pa("Wx", [128, 8, 1536], BF16)
            Wdt = S.pa("Wdt", [128, 8, 16], BF16)
            WOs = S.pa("WOs", [128, 8, 1024], BF16)
            S.dma(Wx[:, :, :], WE[:, 3072:4608].rearrange("(k p) n -> p k n", p=128), wbuf=Wx, q="pool")
            S.dma(Wz[:, :, :], WE[:, 2048:3072].rearrange("(k p) n -> p k n", p=128), wbuf=Wz, q="pool")
            S.dma(Wdt[:, :, :], WE[:, 4608:4624].rearrange("(k p) n -> p k n", p=128), wbuf=Wdt, q="pool")
            S.dma(WOs[:, :, :], D["w_out_even"][512:1536, :].rearrange("(k p) n -> p k n", p=128), wbuf=WOs, q="pool")
            CW = S.pa("CW", [128, 12, 4], F32)
            CB = S.pa("CB", [128, 12], F32)
            DTB = S.pa("DTB", [128, 16], F32)
            ABC = S.pa("ABC", [128, 16], F32)
            DSK = S.pa("DSK", [128, 16], F32)
            SNW = S.pa("SNW", [128, 1024], F32)
            S.dma(CW[:, :, :], D["cwT"], wbuf=CW)
            S.dma(CB[:, :], D["cbT"], wbuf=CB)
            S.dma(DTB[:, :], D["dtb"], wbuf=DTB)
            S.dma(ABC[:, :], D["alog"], wbuf=ABC)
            S.dma(DSK[:, :], D["dsk"], wbuf=DSK)
            S.dma(SNW[:, :], D["snw"], wbuf=SNW)
            S.act(ABC[:, :], ABC[:, :], AF.Exp, [ABC], [ABC])
            S.ts("dve", ABC[:, :], ABC[:, :], -1.0, None, ALU.mult, None, [ABC], [ABC])
            mark_ssdw = S.pa_mark()
            A1 = S.pa("A1", [128, 12, 131], F32)
            A2 = S.pa("A2", [128, 12, 128], F32)
            PRE = A1
            CV = A2
            Yv = A1[:, :, :].rearrange("p a b -> p (a b)")[:, 0:1024]
            SZv = A2[:, :, :].rearrange("p a b -> p (a b)")[:, 0:1024]
            CARRY = S.pa("CARRY", [128, 12, 3], F32)
            XC = S.pa("XC", [128, 12, 128], BF16)
            XT = S.pa("XT", [128, 1024], BF16)
            BTOK = S.pa("BTOK", [128, 2, 128], BF16)
            TMPD = S.pa("TMPD", [128, 1024], F32)
            YZW = S.pa("YZW", [128, 1024], BF16)
            YZWT = S.pa("YZWT", [128, 8, 128], BF16)
            S32 = S.pa("S32", [128, 2, 512], F32)
            SB16 = S.pa("SB16", [128, 2, 512], BF16)
            XW = S.pa("XW", [128, 1024], BF16)
            SEG = S.ring("SEG", 2, [128, 128], F32)
            EX = S.ring("EX", 2, [128, 128], F32)
            WT = S.ring("WT", 2, [128, 128], BF16)
            CBM = S.pa("CBM", [128, 2, 128], F32)
            DTR = S.pa("DTR", [128, 16], F32)
            DT = S.pa("DT", [128, 16], F32)
            DA = S.pa("DA", [128, 16], F32)
            ACS = S.pa("ACS", [128, 32], F32)
            EAC = S.pa("EAC", [128, 32], F32)
            WEND = S.pa("WEND", [128, 16], F32)
            ssy = S.pa("ssy", [128, 1], F32)
            rsy = S.pa("rsy", [128, 1], F32)
            P7r = [Buf("P7r%d" % r, PS[7][:, r * 128:(r + 1) * 128]) for r in range(4)]
            S.memset("pool", S32[:, :, :], 0.0, [S32])
            S.memset("pool", SB16[:, :, :], 0.0, [SB16])
            S.memset("pool", CARRY[:, :, :], 0.0, [CARRY])
            for c in range(NCH):
                hb = hnT[c]
                for b3 in range(3):
                    lst = []
                    for bq in range(4):
                        blk = b3 * 4 + bq
                        for k in range(8):
                            lst.append((PS[b3][:, bq * 128:(bq + 1) * 128], Wx[:, k, blk * 128:(blk + 1) * 128], hb[:, k, :], k == 0, k == 7))
                    S.mm(lst, [Wx, hb], [PS[b3]])
                S.cp("pool", PRE[:, :, 0:3], CARRY[:, :, :], [CARRY], [PRE])
                for b3 in range(3):
                    S.cp("act", PRE[:, 4 * b3:4 * b3 + 4, 3:131], PS[b3][:, :].rearrange("p (a b) -> p a b", a=4), [PS[b3]], [PRE])
                S.cp("pool", CARRY[:, :, :], PRE[:, :, 128:131], [PRE], [CARRY])
                if c == NCH - 1:
                    for j_ in range(3):
                        S.dma(D["p_sconv"][j_].rearrange("(b p) -> p b", p=128), PRE[:, :, 128 + j_], rbuf=PRE, allow_slow_non_contiguous=True)
                for blk in range(12):
                    S.act(CV[:, blk, :], PRE[:, blk, 0:128], AF.Identity, [PRE, CW, CB], [CV], bias=CB[:, blk:blk + 1], scale=CW[:, blk, 0:1])
                for blk in range(12):
                    for j in range(1, 4):
                        S.stt(CV[:, blk, :], PRE[:, blk, j:j + 128], CW[:, blk, j:j + 1], CV[:, blk, :], ALU.mult, ALU.add, [PRE, CW, CV], [CV])
                S.act(XC[:, :, :], CV[:, :, :], AF.Silu, [CV], [XC])
                p3v = psb(3).rearrange("p (a t) -> p a t", a=8)
                S.tr([(p3v[:, a, :], XC[:, a, :], IDb[:, :]) for a in range(8)], [XC, IDb], [PS[3]])
                S.cp("act", XT[:, :], psb(3), [PS[3]], [XT])
                S.tr([(p3v[:, a, :], XC[:, 8 + a, :], IDb[:, :]) for a in range(2)], [XC, IDb], [PS[3]])
                S.cp("act", BTOK[:, :, :], p3v[:, 0:2, :], [PS[3]], [BTOK])
                lst = []
                for k in range(8):
                    lst.append((PS[4][:, :], hb[:, k, :], Wz[:, k, 0:512], k == 0, k == 7))
                    lst.append((PS[5][:, :], hb[:, k, :], Wz[:, k, 512:1024], k == 0, k == 7))
                    lst.append((PS[6][:, 0:16], hb[:, k, :], Wdt[:, k, :], k == 0, k == 7))
                S.mm(lst, [hb, Wz, Wdt], [PS[4], PS[5], PS[6]])
                S.act(SZv[:, 0:512], PS[4][:, :], AF.Silu, [PS[4]], [A2])
                S.act(SZv[:, 512:1024], PS[5][:, :], AF.Silu, [PS[5]], [A2])
                S.tt("dve", DTR[:, :], PS[6][:, 0:16], DTB[:, :], ALU.add, [PS[6], DTB], [DTR])
                S.act(DTR[:, :], DTR[:, :], AF.Exp, [DTR], [DTR])
                S.act(DT[:, :], DTR[:, :], AF.Ln, [DTR], [DT], bias=1.0)
                S.tt("dve", DA[:, :], DT[:, :], ABC[:, :], ALU.mult, [DT, ABC], [DA])
                S.mm([(PS[6][:, 16:32], TRI[:, :], DA[:, :], True, True), (PS[6][:, 32:48], ONES[:, :], DA[:, :], True, True)], [TRI, ONES, DA], [PS[6]])
                S.cp("dve", ACS[:, :], PS[6][:, 16:48], [PS[6]], [ACS])
                S.act(EAC[:, :], ACS[:, :], AF.Exp, [ACS], [EAC])
                S.tt("dve", WEND[:, :], ACS[:, 16:32], ACS[:, 0:16], ALU.subtract, [ACS], [WEND])
                S.act(WEND[:, :], WEND[:, :], AF.Exp, [WEND], [WEND])
                S.tt("dve", WEND[:, :], WEND[:, :], DT[:, :], ALU.mult, [WEND, DT], [WEND])
                S.mm([(PS[6][:, 128 + g * 128:256 + g * 128], XC[:, 8 + g, :], XC[:, 10 + g, :], True, True) for g in range(2)], [XC], [PS[6]])
                S.tt("dve", CBM[:, :, :], PS[6][:, 128:384].rearrange("p (g i) -> p g i", g=2), TRI[:, :].unsqueeze(1).to_broadcast([128, 2, 128]), ALU.mult, [PS[6], TRI], [CBM])
                for h_ in range(16):
                    g = h_ // 8
                    pr = P7r[h_ % 4]
                    S.mm([(pr[:, :], DA[:, h_:h_ + 1].to_broadcast([128, 128]), TRI[:, :], True, True)], [DA, TRI], [pr])
                    sg, ex, wt = SEG[h_ % 2], EX[h_ % 2], WT[h_ % 2]
                    S.ts("dve", sg[:, :], pr[:, :], ACS[:, h_:h_ + 1], 0.0, ALU.subtract, ALU.min, [pr, ACS], [sg])
                    S.act(ex[:, :], sg[:, :], AF.Exp, [sg], [ex])
                    S.stt(wt[:, :], ex[:, :], DT[:, h_:h_ + 1], CBM[:, g, :], ALU.mult, ALU.mult, [ex, DT, CBM], [wt])
                    S.mm([(PS[g][:, (h_ % 8) * 64:(h_ % 8 + 1) * 64], wt[:, :], XT[:, h_ * 64:(h_ + 1) * 64], (h_ % 8 == 0), True)], [wt, XT], [PS[g]])
                S.mm([(PS[4 + g][:, :], XC[:, 10 + g, :], SB16[:, g, :], True, True) for g in range(2)], [XC, SB16], [PS[4], PS[5]])
                for g in range(2):
                    S.tt("dve", Yv[:, g * 512:(g + 1) * 512].rearrange("p (a b) -> p a b", a=8), PS[4 + g][:, :].rearrange("p (a b) -> p a b", a=8),
                         EAC[:, g * 8:(g + 1) * 8].unsqueeze(2).to_broadcast([128, 8, 64]), ALU.mult, [PS[4 + g], EAC], [A1])
                    S.tt("dve", Yv[:, g * 512:(g + 1) * 512], Yv[:, g * 512:(g + 1) * 512], PS[g][:, :], ALU.add, [A1, PS[g]], [A1])
                S.tt("pool", TMPD[:, :].rearrange("p (a b) -> p a b", a=16), XT[:, :].rearrange("p (a b) -> p a b", a=16),
                     DSK[:, :].unsqueeze(2).to_broadcast([128, 16, 64]), ALU.mult, [XT, DSK], [TMPD])
                S.tt("dve", Yv, Yv, TMPD[:, :], ALU.add, [A1, TMPD], [A1])
                S.tt("pool", Yv, Yv, SZv, ALU.mult, [A1, A2], [A1])
                S.act(TMPD[:, :], Yv, AF.Square, [A1], [TMPD, ssy], accum=ssy[:, :])
                S.ts("dve", rsy[:, :], ssy[:, :], 1.0 / 1024, EPS, ALU.mult, ALU.add, [ssy], [rsy])
                S.act(rsy[:, :], rsy[:, :], AF.Sqrt, [rsy], [rsy])
                S.op("dve", lambda h: h.reciprocal(out=rsy[:, :], in_=rsy[:, :]), [rsy], [rsy])
                S.tt("pool", YZW[:, :], Yv, SNW[:, :], ALU.mult, [A1, SNW], [YZW])
                S.tr([(p3v[:, a, :], YZW[:, a * 128:(a + 1) * 128], IDb[:, :]) for a in range(8)], [YZW, IDb], [PS[3]])
                S.cp("act", YZWT[:, :, :], p3v, [PS[3]], [YZWT])
                for hf in range(2):
                    S.mm([(PS[hf][:, :], YZWT[:, k, :], WOs[:, k, hf * 512:(hf + 1) * 512], k == 0, k == 7) for k in range(8)], [YZWT, WOs], [PS[hf]])
                    S.stt(X[c][:, hf * 512:(hf + 1) * 512], PS[hf][:, :], rsy[:, :], X[c][:, hf * 512:(hf + 1) * 512], ALU.mult, ALU.add, [PS[hf], rsy, X[c]], [X[c]])
                S.tt("pool", XW[:, :].rearrange("p (a b) -> p a b", a=16), XT[:, :].rearrange("p (a b) -> p a b", a=16),
                     WEND[:, :].unsqueeze(2).to_broadcast([128, 16, 64]), ALU.mult, [XT, WEND], [XW])
                S.mm([(PS[4 + g][:, :], BTOK[:, g, :], XW[:, g * 512:(g + 1) * 512], True, True) for g in range(2)], [BTOK, XW], [PS[4], PS[5]])
                for g in range(2):
                    S.tt("pool", S32[:, g, :].rearrange("p (a b) -> p a b", a=8), S32[:, g, :].rearrange("p (a b) -> p a b", a=8),
                         EAC[:, 16 + g * 8:16 + (g + 1) * 8].unsqueeze(2).to_broadcast([128, 8, 64]), ALU.mult, [S32, EAC], [S32])
                    S.tt("dve", S32[:, g, :], S32[:, g, :], PS[4 + g][:, :], ALU.add, [S32, PS[4 + g]], [S32])
                S.cp("pool", SB16[:, :, :], S32[:, :, :], [S32], [SB16])
            for g in range(2):
                S.tr([(PS[g][:, a * 128:(a + 1) * 128], S32[:, g, a * 128:(a + 1) * 128], IDf[:, :]) for a in range(4)], [S32, IDf], [PS[g]])
                S.cp("act", TMPD[:, g * 512:(g + 1) * 512], PS[g][:, :], [PS[g]], [TMPD])
            S.dma(D["p_ssd"].rearrange("(a q) n -> q a n", q=128), TMPD[:, :].rearrange("p (a n) -> p a n", a=8), rbuf=TMPD)
            S.pa_release(mark_ssdw)
            if phases["sample"] and phases.get("s_ssd", True):
                XPs = S.pa("XPs", [NS, 1536], F32)
                SZs = S.pa("SZs", [NS, 1024], F32)
                XCs = S.pa("XCs", [NS, 1536], F32)
                DTs = S.pa("DTs", [NS, 16], F32)
                DAs = S.pa("DAs", [NS, 16], F32)
                DECs = S.pa("DECs", [NS, 16], F32)
                lst = []
                for k in range(8):
                    for j_ in range(3):
                        lst.append((PS[j_][0:NS, :], hnTs[:, k, :], Wx[:, k, j_ * 512:(j_ + 1) * 512], k == 0, k == 7))
                    lst.append((PS[4][0:NS, :], hnTs[:, k, :], Wz[:, k, 0:512], k == 0, k == 7))
                    lst.append((PS[5][0:NS, :], hnTs[:, k, :], Wz[:, k, 512:1024], k == 0, k == 7))
                    lst.append((PS[6][0:NS, 0:16], hnTs[:, k, :], Wdt[:, k, :], k == 0, k == 7))
                S.mm(lst, [hnTs, Wx, Wz, Wdt], [PS[0], PS[1], PS[2], PS[4], PS[5], PS[6]])
                for j_ in range(3):
                    S.cp("act", XPs[:, j_ * 512:(j_ + 1) * 512], PS[j_][0:NS, :], [PS[j_]], [XPs])
                S.act(SZs[:, 0:512], PS[4][0:NS, :], AF.Silu, [PS[4]], [SZs])
                S.act(SZs[:, 512:1024], PS[5][0:NS, :], AF.Silu, [PS[5]], [SZs])
                S.tt("dve", DTs[:, :], PS[6][0:NS, 0:16], DTB[0:NS, :], ALU.add, [PS[6], DTB], [DTs])
                S.act(DTs[:, :], DTs[:, :], AF.Exp, [DTs], [DTs])
                S.act(DTs[:, :], DTs[:, :], AF.Ln, [DTs], [DTs], bias=1.0)
                S.tt("dve", DAs[:, :], DTs[:, :], ABC[0:NS, :], ALU.mult, [DTs, ABC], [DAs])
                S.act(DECs[:, :], DAs[:, :], AF.Exp, [DAs], [DECs])
                mk1 = S.pa_mark()
                CWg = S.pa("CWg", [NS, 4, 512], F32)
                SCVg = S.pa("SCVg", [NS, 3, 512], F32)
                CBs = S.pa("CBs", [NS, 1536], F32)
                S.dma(CBs[:, :], D["cb_s"], wbuf=CBs)
                for j_ in range(3):
                    cs = slice(j_ * 512, (j_ + 1) * 512)
                    S.dma(CWg[:, :, :], D["cw_s"][:, :, cs], wbuf=CWg)
                    S.dma(SCVg[:, :, :], D["sconv"][:, :, cs], wbuf=SCVg)
                    S.dma(D["s_sconv"][:, 0:2, cs], SCVg[:, 1:3, :], rbuf=SCVg)
                    S.dma(D["s_sconv"][:, 2, cs], XPs[:, cs], rbuf=XPs)
                    S.tt("pool", SCVg[:, :, :], SCVg[:, :, :], CWg[:, 0:3, :], ALU.mult, [SCVg, CWg], [SCVg])
                    S.tt("dve", XCs[:, cs], XPs[:, cs], CWg[:, 3, :], ALU.mult, [XPs, CWg], [XCs])
                    for t_ in range(3):
                        S.tt("dve", XCs[:, cs], XCs[:, cs], SCVg[:, t_, :], ALU.add, [XCs, SCVg], [XCs])
                    S.tt("dve", XCs[:, cs], XCs[:, cs], CBs[:, cs], ALU.add, [XCs, CBs], [XCs])
                S.act(XCs[:, :], XCs[:, :], AF.Silu, [XCs], [XCs])
                S.pa_release(mk1)
                TMPs = S.pa("TMPs", [NS, 1024], F32)
                YTs_ = S.pa("YTs_", [128, 8, NS], F32)
                mk2 = S.pa_mark()
                DTXT = S.pa("DTXT", [128, 8, NS], F32)
                DECT = S.pa("DECT", [128, 8, NS], F32)
                STr = S.ring("STr", 2, [128, 8, 128], F32)
                PROD = S.pa("PROD", [128, 8, 128], F32)
                S.tt("dve", TMPs[:, :].rearrange("p (a b) -> p a b", a=16), XCs[:, 0:1024].rearrange("p (a b) -> p a b", a=16),
                     DTs[:, :].unsqueeze(2).to_broadcast([NS, 16, 64]), ALU.mult, [XCs, DTs], [TMPs])
                p0v = PS[0][:, 0:8 * NS].rearrange("p (a t) -> p a t", a=8)
                S.tr([(p0v[:, a, :], TMPs[:, a * 128:(a + 1) * 128], IDf[0:NS, 0:NS]) for a in range(8)], [TMPs, IDf], [PS[0]])
                S.cp("dve", DTXT[:, :, :], p0v, [PS[0]], [DTXT])
                S.cp("dve", TMPs[:, :].rearrange("p (a b) -> p a b", a=16), DECs[:, :].unsqueeze(2).to_broadcast([NS, 16, 64]), [DECs, PS[0]], [TMPs])
                p1v = PS[1][:, 0:8 * NS].rearrange("p (a t) -> p a t", a=8)
                S.tr([(p1v[:, a, :], TMPs[:, a * 128:(a + 1) * 128], IDf[0:NS, 0:NS]) for a in range(8)], [TMPs, IDf], [PS[1]])
                S.cp("dve", DECT[:, :, :], p1v, [PS[1]], [DECT])
                for b in range(NS):
                    st = STr[b % 2]
                    pb_ = PS[2 + (b % 2)]
                    S.dma(st[:, :, :], D["sstate"][b].rearrange("(a q) n -> q a n", q=128), wbuf=st)
                    S.mm([(pb_[:, :], IDf[0:NS, b:b + 1].to_broadcast([NS, 128]), XCs[:, 1024:1536], True, True)], [IDf, XCs], [pb_])
                    S.tt("dve", st[:, :, :], st[:, :, :], DECT[:, :, b:b + 1].to_broadcast([128, 8, 128]), ALU.mult, [st, DECT], [st])
                    for g in range(2):
                        S.tt("dve", PROD[:, 4 * g:4 * g + 4, :], pb_[:, g * 128:(g + 1) * 128].unsqueeze(1).to_broadcast([128, 4, 128]),
                             DTXT[:, 4 * g:4 * g + 4, b:b + 1].to_broadcast([128, 4, 128]), ALU.mult, [pb_, DTXT], [PROD])
                    S.tt("pool", st[:, :, :], st[:, :, :], PROD[:, :, :], ALU.add, [st, PROD], [st])
                    S.dma(D["s_ssd"][b].rearrange("(a q) n -> q a n", q=128), st[:, :, :], rbuf=st)
                    for g in range(2):
                        S.tt("dve", PROD[:, 4 * g:4 * g + 4, :], st[:, 4 * g:4 * g + 4, :],
                             pb_[:, 256 + g * 128:256 + (g + 1) * 128].unsqueeze(1).to_broadcast([128, 4, 128]), ALU.mult, [st, pb_], [PROD])
                    S.op("dve", lambda h, b=b: h.tensor_reduce(out=YTs_[:, :, b], in_=PROD[:, :, :], axis=AX.X, op=ALU.add), [PROD], [YTs_])
                S.pa_release(mk2)
                for a in range(8):
                    pbk = PS[4 + a // 4]
                    S.tr([(pbk[0:NS, (a % 4) * 128:(a % 4 + 1) * 128], YTs_[:, a, :], IDf[:, :])], [YTs_, IDf], [pbk])
                Ys = S.pa("Ys", [NS, 1024], F32)
                S.tt("pool", TMPs[:, :].rearrange("p (a b) -> p a b", a=16), XCs[:, 0:1024].rearrange("p (a b) -> p a b", a=16),
                     DSK[0:NS, :].unsqueeze(2).to_broadcast([NS, 16, 64]), ALU.mult, [XCs, DSK], [TMPs])
                for hf in range(2):
                    S.tt("dve", Ys[:, hf * 512:(hf + 1) * 512], TMPs[:, hf * 512:(hf + 1) * 512], PS[4 + hf][0:NS, :], ALU.add, [TMPs, PS[4 + hf]], [Ys])
                S.tt("dve", Ys[:, :], Ys[:, :], SZs[:, :], ALU.mult, [Ys, SZs], [Ys])
                sss_ = S.pa("sss_", [NS, 1], F32)
                rss_ = S.pa("rss_", [NS, 1], F32)
                S.act(TMPs[:, :], Ys[:, :], AF.Square, [Ys], [TMPs, sss_], accum=sss_[:, :])
                S.ts("dve", rss_[:, :], sss_[:, :], 1.0 / 1024, EPS, ALU.mult, ALU.add, [sss_], [rss_])
                S.act(rss_[:, :], rss_[:, :], AF.Sqrt, [rss_], [rss_])
                S.op("dve", lambda h: h.reciprocal(out=rss_[:, :], in_=rss_[:, :]), [rss_], [rss_])
                YWs = S.pa("YWs", [NS, 1024], BF16)
                S.tt("dve", YWs[:, :], Ys[:, :], SNW[0:NS, :], ALU.mult, [Ys, SNW], [YWs])
                p3s = psb(3, 8 * NS).rearrange("p (a t) -> p a t", a=8)
                S.tr([(p3s[:, a, :], YWs[:, a * 128:(a + 1) * 128], IDb[0:NS, 0:NS]) for a in range(8)], [YWs, IDb], [PS[3]])
                YTb = S.pa("YTb", [128, 8, NS], BF16)
                S.cp("act", YTb[:, :, :], p3s, [PS[3]], [YTb])
                for hf in range(2):
                    S.mm([(PS[hf][0:NS, :], YTb[:, k, :], WOs[:, k, hf * 512:(hf + 1) * 512], k == 0, k == 7) for k in range(8)], [YTb, WOs], [PS[hf]])
                    S.stt(Xs[:, hf * 512:(hf + 1) * 512], PS[hf][0:NS, :], rss_[:, :], Xs[:, hf * 512:(hf + 1) * 512], ALU.mult, ALU.add, [PS[hf], rss_, Xs], [Xs])
            S.pa_release(mark_ssd)

        if phases["mlstm"]:
            phase_norm(1)
            WO_ = D["w_in_odd"]
            mark_ml = S.pa_mark()
            Wif = S.pa("Wif", [128, 8, 16], BF16)
            S.dma(Wif[:, :, :], WO_[:, 8192:8208].rearrange("(k p) n -> p k n", p=128), wbuf=Wif, q="pool")
            Wqk = S.ring("Wqk", 2, [128, 8, 512], BF16)
            Wvoz = S.ring("Wvoz", 2, [128, 8, 768], BF16)
            WOo = S.ring("WOo", 2, [128, 2, 1024], BF16)
            MNWh = S.ring("MNWh", 2, [128, 256], F32)

            def load_head_w(h_):
                i_ = h_ % 2
                for j_, off in enumerate((0, 2048)):
                    S.dma(Wqk[i_][:, :, j_ * 256:(j_ + 1) * 256], WO_[:, off + h_ * 256:off + (h_ + 1) * 256].rearrange("(k p) n -> p k n", p=128), wbuf=Wqk[i_], q="pool")
                for j_, off in enumerate((4096, 6144, 8208)):
                    S.dma(Wvoz[i_][:, :, j_ * 256:(j_ + 1) * 256], WO_[:, off + h_ * 256:off + (h_ + 1) * 256].rearrange("(k p) n -> p k n", p=128), wbuf=Wvoz[i_], q="pool")
                S.dma(WOo[i_][:, :, :], D["w_out_odd"][h_ * 256:(h_ + 1) * 256, :].rearrange("(k p) n -> p k n", p=128), wbuf=WOo[i_], q="pool")
                S.dma(MNWh[i_][:, :], D["mnw"][:, h_ * 256:(h_ + 1) * 256], wbuf=MNWh[i_])

            load_head_w(0)
            MCW = S.pa("MCW", [128, 32, 4], F32)
            MCB = S.pa("MCB", [128, 32], F32)
            IFB = S.pa("IFB", [128, 16], F32)
            S.dma(MCW[:, :, :], D["mcwT"], wbuf=MCW)
            S.dma(MCB[:, :], D["mcbT"], wbuf=MCB)
            S.dma(IFB[:, 0:8], D["igb"], wbuf=IFB)
            S.dma(IFB[:, 8:16], D["fgb"], wbuf=IFB)
            IFt = S.pa("IFt", [128, 16, 16], F32)
            LF = S.pa("LF", [128, 16, 8], F32)
            BCt = S.pa("BCt", [128, 16, 8], F32)
            BLB = S.pa("BLB", [128, 16, 8], F32)
            Gt = S.pa("Gt", [128, 16, 8], F32)
            At = S.pa("At", [128, 16, 8], F32)
            EBt = S.pa("EBt", [128, 16, 8], F32)
            EBL = S.pa("EBL", [128, 16, 8], F32)
            EMF = S.pa("EMF", [128, 8], F32)
            MX = S.pa("MX", [128, 1], F32)
            MXR = S.pa("MXR", [1, 128], F32)
            MF = S.pa("MF", [1, 8], F32)
            for c in range(NCH):
                S.mm([(PS[3][:, c * 16:(c + 1) * 16], hnT[c][:, k, :], Wif[:, k, :], k == 0, k == 7) for k in range(8)], [hnT[c], Wif], [PS[3]])
            S.tt("dve", IFt[:, :, :], PS[3][:, 0:256].rearrange("p (c j) -> p c j", c=16), IFB[:, :].unsqueeze(1).to_broadcast([128, 16, 16]), ALU.add, [PS[3], IFB], [IFt])
            S.act(LF[:, :, :], IFt[:, :, 8:16], AF.Exp, [IFt], [LF], scale=-1.0)
            S.act(LF[:, :, :], LF[:, :, :], AF.Ln, [LF], [LF], bias=1.0)
            S.ts("dve", LF[:, :, :], LF[:, :, :], -1.0, None, ALU.mult, None, [LF], [LF])
            lfl = LF[:, :, :].rearrange("p c h -> p (c h)")
            S.mm([(PS[3][:, 256 + c * 8:256 + (c + 1) * 8], TRI[:, :], LF[:, c, :], True, True) for c in range(NCH)] +
                 [(PS[3][:, 384:512], ONES[:, :], lfl, True, True)], [TRI, ONES, LF], [PS[3]])
            S.cp("dve", BCt[:, :, :].rearrange("p c h -> p (c h)"), PS[3][:, 256:384], [PS[3]], [BCt])
            S.cp("dve", BLB[:, :, :].rearrange("p c h -> p (c h)"), PS[3][:, 384:512], [PS[3]], [BLB])
            S.tt("dve", Gt[:, :, :], IFt[:, :, 0:8], BCt[:, :, :], ALU.subtract, [IFt, BCt], [Gt])
            S.act(At[:, :, :], Gt[:, :, :], AF.Exp, [Gt], [At], bias=float(math.log(1.0 / 16.0)))
            S.act(EBt[:, :, :], BCt[:, :, :], AF.Exp, [BCt], [EBt])
            S.act(EBL[:, :, :], BLB[:, :, :], AF.Exp, [BLB], [EBL])
            S.tr([(PS[4][:, 0:128], Gt[:, :, :].rearrange("p c h -> p (c h)"), IDf[:, :])], [Gt, IDf], [PS[4]])
            S.op("dve", lambda h: h.tensor_reduce(out=MX[:, :], in_=PS[4][:, 0:128], axis=AX.X, op=ALU.max), [PS[4]], [MX])
            S.tr([(PS[4][0:1, 128:256], MX[:, 0:1], IDf[:, :])], [MX, IDf], [PS[4]])
            S.cp("dve", MXR[:, :], PS[4][0:1, 128:256], [PS[4]], [MXR])
            S.memset("dve", MF[:, :], -1.0e30, [MF])
            for c in range(NCH):
                S.tt("dve", MF[:, :], MF[:, :], MXR[:, c * 8:(c + 1) * 8], ALU.max, [MF, MXR], [MF])
                S.tt("dve", MF[:, :], MF[:, :], BLB[0:1, c, :], ALU.add, [MF, BLB], [MF])
            S.dma(D["p_mm"], MF[:, :], rbuf=MF)
            S.mm([(PS[4][:, 256:264], ONES[0:1, :], MF[:, :], True, True)], [ONES, MF], [PS[4]])
            S.cp("dve", EMF[:, :], PS[4][:, 256:264], [PS[4]], [EMF])
            S.act(EMF[:, :], EMF[:, :], AF.Exp, [EMF], [EMF], scale=-1.0)

            if phases["sample"] and phases.get("s_ml", True):
                IFs = S.pa("IFs", [NS, 16], F32)
                LFs = S.pa("LFs", [NS, 8], F32)
                MM0 = S.pa("MM0", [NS, 8], F32)
                INTs = S.pa("INTs", [NS, 8], F32)
                MNs = S.pa("MNs", [NS, 8], F32)
                WINs = S.pa("WINs", [NS, 8], F32)
                WOUs = S.pa("WOUs", [NS, 8], F32)
                EMNs = S.pa("EMNs", [NS, 8], F32)
                BDs = S.pa("BDs", [NS, NS, 8], F32)
                WSB = S.pa("WSB", [128, NS, 8], F32)
                SCB = S.pa("SCB", [128, NS, 8], F32)
                OHr = S.pa("OHr", [NS, NS, NS], F32)
                OH = S.pa("OH", [128, NS, NS], F32)
                S.dma(MM0[:, :], D["mmm"], wbuf=MM0)
                S.mm([(PS[4][0:NS, 300:316], hnTs[:, k, :], Wif[:, k, :], k == 0, k == 7) for k in range(8)], [hnTs, Wif], [PS[4]])
                S.tt("dve", IFs[:, :], PS[4][0:NS, 300:316], IFB[0:NS, :], ALU.add, [PS[4], IFB], [IFs])
                S.act(LFs[:, :], IFs[:, 8:16], AF.Exp, [IFs], [LFs], scale=-1.0)
                S.act(LFs[:, :], LFs[:, :], AF.Ln, [LFs], [LFs], bias=1.0)
                S.ts("dve", LFs[:, :], LFs[:, :], -1.0, None, ALU.mult, None, [LFs], [LFs])
                S.tt("dve", INTs[:, :], LFs[:, :], MM0[:, :], ALU.add, [LFs, MM0], [INTs])
                S.tt("dve", MNs[:, :], INTs[:, :], IFs[:, 0:8], ALU.max, [INTs, IFs], [MNs])
                S.dma(D["s_mm"], MNs[:, :], rbuf=MNs)
                S.tt("dve", WINs[:, :], IFs[:, 0:8], MNs[:, :], ALU.subtract, [IFs, MNs], [WINs])
                S.act(WINs[:, :], WINs[:, :], AF.Exp, [WINs], [WINs])
                S.tt("dve", WOUs[:, :], INTs[:, :], MNs[:, :], ALU.subtract, [INTs, MNs], [WOUs])
                S.act(WOUs[:, :], WOUs[:, :], AF.Exp, [WOUs], [WOUs])
                S.act(EMNs[:, :], MNs[:, :], AF.Exp, [MNs], [EMNs], scale=-1.0)
                idb = IDf[0:NS, 0:NS]
                for src_, dst_ in ((WINs, WSB), (WOUs, SCB)):
                    S.tt("dve", BDs[:, :, :], src_[:, :].unsqueeze(1).to_broadcast([NS, NS, 8]), idb.unsqueeze(2).to_broadcast([NS, NS, 8]), ALU.mult, [src_, IDf], [BDs])
                    S.mm([(PS[4][:, 0:128], ONES[0:NS, :], BDs[:, :, :].rearrange("p a b -> p (a b)"), True, True)], [ONES, BDs], [PS[4]])
                    S.cp("dve", dst_[:, :, :].rearrange("p a b -> p (a b)"), PS[4][:, 0:128], [PS[4]], [dst_])
                S.tt("dve", OHr[:, :, :], idb.unsqueeze(2).to_broadcast([NS, NS, NS]), idb.unsqueeze(1).to_broadcast([NS, NS, NS]), ALU.mult, [IDf], [OHr])
                S.mm([(PS[4][:, 0:256], ONES[0:NS, :], OHr[:, :, :].rearrange("p a b -> p (a b)"), True, True)], [ONES, OHr], [PS[4]])
                S.cp("dve", OH[:, :, :].rearrange("p a b -> p (a b)"), PS[4][:, 0:256], [PS[4]], [OH])

            p2b = PS[2][:, 256:512].bitcast(BF16)
            R2z, R2k, R2h = Buf('R2z', None), Buf('R2k', None), Buf('R2h', None)
            R3s, R3kv = Buf('R3s', None), Buf('R3kv', None)
            for h_ in range(8):
                if h_ + 1 < 8:
                    load_head_w(h_ + 1)
                wqk, wvoz, woo, mnwh = Wqk[h_ % 2], Wvoz[h_ % 2], WOo[h_ % 2], MNWh[h_ % 2]
                mark_w = S.pa_mark()
                PREm_r = S.ring("PREm", 2, [128, 4, 131], F32)
                CARm = S.pa("CARm", [128, 4, 3], F32)
                CVm_r = S.ring("CVm", 2, [128, 4, 128], F32)
                QKc = S.ring("QKc", 2, [128, 4, 128], BF16)
                VAm = S.ring("VAm", 2, [128, 258], BF16)
                SIGO_r = S.ring("SIGO", 2, [128, 256], F32)
                SZm_r = S.ring("SZm", 2, [128, 256], F32)
                ATTm = S.ring("ATTm", 2, [128, 128], BF16)
                KTOK = S.ring("KTOK", 2, [128, 256], BF16)
                Hm_r = S.ring("Hm", 2, [128, 256], F32)
                GZ_r = S.ring("GZ", 2, [128, 256], F32)
                HG_r = S.ring("HG", 2, [128, 256], BF16)
                HGT_r = S.ring("HGT", 2, [128, 2, 128], BF16)
                C32 = S.pa("C32", [128, 2, 257], F32)
                C16 = S.pa("C16", [128, 2, 257], BF16)
                CO = S.pa("CO", [128, 2, 257], F32)
                DQ_r = S.ring("DQ", 2, [128, 1], F32)
                RQ_r = S.ring("RQ", 2, [128, 1], F32)
                BNS_r = S.ring("BNS", 2, [128, 6], F32)
                MV_r = S.ring("MV", 2, [128, 2], F32)
                RSD_r = S.ring("RSD", 2, [128, 1], F32)
                S.memset("pool", C32[:, :, :], 0.0, [C32])
                S.memset("pool", C16[:, :, :], 0.0, [C16])
                S.memset("pool", CARm[:, :, :], 0.0, [CARm])
                cblk = [2 * h_, 2 * h_ + 1, 16 + 2 * h_, 16 + 2 * h_ + 1]
                def early(c):
                    hb = hnT[c]
                    qkc, va, att, ktok = QKc[c % 2], VAm[c % 2], ATTm[c % 2], KTOK[c % 2]
                    PREm = PREm_r[c % 2]
                    CVm = CVm_r[c % 2]
                    SIGO = SIGO_r[c % 2]
                    SZm = SZm_r[c % 2]
                    Hm = Hm_r[c % 2]
                    GZ = GZ_r[c % 2]
                    HG = HG_r[c % 2]
                    HGT = HGT_r[c % 2]
                    DQ = DQ_r[c % 2]
                    RQ = RQ_r[c % 2]
                    BNS = BNS_r[c % 2]
                    MV = MV_r[c % 2]
                    RSD = RSD_r[c % 2]
                    lst = []
                    for bq in range(4):
                        for k in range(8):
                            lst.append((PS[0][:, bq * 128:(bq + 1) * 128], wqk[:, k, bq * 128:(bq + 1) * 128], hb[:, k, :], k == 0, k == 7))
                    S.mm(lst, [wqk, hb], [PS[0]])
                    S.cp("pool", PREm[:, :, 0:3], CARm[:, :, :], [CARm], [PREm])
                    S.cp("act", PREm[:, :, 3:131], PS[0][:, :].rearrange("p (a b) -> p a b", a=4), [PS[0]], [PREm])
                    S.cp("pool", CARm[:, :, :], PREm[:, :, 128:131], [PREm], [CARm])
                    if c == NCH - 1:
                        for bq in range(4):
                            col0 = cblk[bq] * 128
                            for j_ in range(3):
                                S.dma(D["p_mconv"][j_, col0:col0 + 128].rearrange("(p o) -> p o", o=1), PREm[:, bq, 128 + j_:129 + j_], rbuf=PREm)
                    for bq in range(4):
                        S.act(CVm[:, bq, :], PREm[:, bq, 0:128], AF.Identity, [PREm, MCW, MCB], [CVm], bias=MCB[:, cblk[bq]:cblk[bq] + 1], scale=MCW[:, cblk[bq], 0:1])
                    for bq in range(4):
                        for j_ in range(1, 4):
                            S.stt(CVm[:, bq, :], PREm[:, bq, j_:j_ + 128], MCW[:, cblk[bq], j_:j_ + 1], CVm[:, bq, :], ALU.mult, ALU.add, [PREm, MCW, CVm], [CVm])
                    S.act(qkc[:, :, :], CVm[:, :, :], AF.Silu, [CVm], [qkc])
                    lst = []
                    for k in range(8):
                        lst.append((PS[1][:, :], hb[:, k, :], wvoz[:, k, 0:512], k == 0, k == 7))
                        lst.append((PS[2][:, 0:256], hb[:, k, :], wvoz[:, k, 512:768], k == 0, k == 7))
                    S.mm(lst, [hb, wvoz], [PS[1], PS[2]])
                    S.act(va[:, 0:256], PS[1][:, 0:256], AF.Copy, [PS[1], At], [va], scale=At[:, c, h_:h_ + 1])
                    S.cp("pool", va[:, 256:257], At[:, c, h_:h_ + 1], [At], [va])
                    S.act(SIGO[:, :], PS[1][:, 256:512], AF.Sigmoid, [PS[1]], [SIGO])
                    S.act(SZm[:, :], PS[2][:, 0:256], AF.Silu, [PS[2]], [SZm])
                    S.mm([(PS[3][:, 0:128], qkc[:, 2 + db, :], qkc[:, db, :], db == 0, db == 1) for db in range(2)], [qkc], [PS[3]])
                    S.tt("dve", att[:, :], PS[3][:, 0:128], TRI[:, :], ALU.mult, [PS[3], TRI], [att])
                    kv_ = p2b[:, 0:256].rearrange("p (a t) -> p a t", a=2)
                    S.tr([(kv_[:, db, :], qkc[:, 2 + db, :], IDb[:, :]) for db in range(2)], [qkc, IDb], [PS[2]])
                    S.cp("act", ktok[:, :], p2b[:, 0:256], [PS[2]], [ktok])
                def late(c):
                    qkc, va, att, ktok = QKc[c % 2], VAm[c % 2], ATTm[c % 2], KTOK[c % 2]
                    PREm = PREm_r[c % 2]
                    CVm = CVm_r[c % 2]
                    SIGO = SIGO_r[c % 2]
                    SZm = SZm_r[c % 2]
                    Hm = Hm_r[c % 2]
                    GZ = GZ_r[c % 2]
                    HG = HG_r[c % 2]
                    HGT = HGT_r[c % 2]
                    DQ = DQ_r[c % 2]
                    RQ = RQ_r[c % 2]
                    BNS = BNS_r[c % 2]
                    MV = MV_r[c % 2]
                    RSD = RSD_r[c % 2]
                    S.mm([(PS[5][:, 0:257], att[:, :], va[:, 0:257], True, False)] +
                         [(PS[5][:, 0:257], qkc[:, db, :], C16[:, db, :], False, db == 1) for db in range(2)], [att, va, qkc, C16], [PS[5]])
                    S.ts("dve", DQ[:, :], PS[5][:, 256:257], EBt[:, c, h_:h_ + 1], None, ALU.mult, None, [PS[5], EBt], [DQ])
                    S.stt(RQ[:, :], DQ[:, :], -1.0, DQ[:, :], ALU.mult, ALU.max, [DQ], [RQ])
                    S.ts("dve", DQ[:, :], RQ[:, :], 1.0, None, ALU.max, None, [RQ], [DQ])
                    S.op("dve", lambda h, RQ=RQ, DQ=DQ: h.reciprocal(out=RQ[:, :], in_=DQ[:, :]), [DQ], [RQ])
                    S.tt("dve", RQ[:, :], RQ[:, :], EBt[:, c, h_:h_ + 1], ALU.mult, [RQ, EBt], [RQ])
                    S.stt(Hm[:, :], PS[5][:, 0:256], RQ[:, :], SIGO[:, :], ALU.mult, ALU.mult, [PS[5], RQ, SIGO], [Hm])
                    S.op("dve", lambda h, BNS=BNS, Hm=Hm: h.bn_stats(out=BNS[:, :], in_=Hm[:, :]), [Hm], [BNS])
                    S.op("dve", lambda h, MV=MV, BNS=BNS: h.bn_aggr(out=MV[:, :], in_=BNS[:, :]), [BNS], [MV])
                    S.ts("dve", RSD[:, :], MV[:, 1:2], EPS, None, ALU.add, None, [MV], [RSD])
                    S.act(RSD[:, :], RSD[:, :], AF.Sqrt, [RSD], [RSD])
                    S.op("dve", lambda h, RSD=RSD: h.reciprocal(out=RSD[:, :], in_=RSD[:, :]), [RSD], [RSD])
                    S.ts("dve", Hm[:, :], Hm[:, :], MV[:, 0:1], RSD[:, :], ALU.subtract, ALU.mult, [Hm, MV, RSD], [Hm])
                    S.tt("pool", GZ[:, :], SZm[:, :], mnwh[:, :], ALU.mult, [SZm, mnwh], [GZ])
                    S.tt("pool", HG[:, :], Hm[:, :], GZ[:, :], ALU.mult, [Hm, GZ], [HG])
                    hv_ = PS[4][:, 0:128].bitcast(BF16).rearrange("p (a t) -> p a t", a=2)
                    S.tr([(hv_[:, db, :], HG[:, db * 128:(db + 1) * 128], IDb[:, :]) for db in range(2)], [HG, IDb], [PS[4]])
                    S.cp("act", HGT[:, :, :], hv_, [PS[4]], [HGT])
                    for hf in range(2):
                        S.mm([(PS[6 + hf][:, :], HGT[:, db, :], woo[:, db, hf * 512:(hf + 1) * 512], db == 0, db == 1) for db in range(2)], [HGT, woo], [PS[6 + hf]])
                        S.tt("dve", X[c][:, hf * 512:(hf + 1) * 512], X[c][:, hf * 512:(hf + 1) * 512], PS[6 + hf][:, :], ALU.add, [X[c], PS[6 + hf]], [X[c]])
                    S.mm([(PS[4][:, 128:385], ktok[:, 0:128], va[:, 0:257], True, True), (PS[5][:, 0:257], ktok[:, 128:256], va[:, 0:257], True, True)], [ktok, va], [PS[4], PS[5]])
                    S.ts("pool", C32[:, :, :], C32[:, :, :], EBL[:, c, h_:h_ + 1], None, ALU.mult, None, [C32, EBL], [C32])
                    S.stt(C32[:, 0, :], PS[4][:, 128:385], EBL[:, c, h_:h_ + 1], C32[:, 0, :], ALU.mult, ALU.add, [PS[4], EBL, C32], [C32])
                    S.stt(C32[:, 1, :], PS[5][:, 0:257], EBL[:, c, h_:h_ + 1], C32[:, 1, :], ALU.mult, ALU.add, [PS[5], EBL, C32], [C32])
                    S.cp("pool", C16[:, :, :], C32[:, :, :], [C32], [C16])
                early(0)
                for c in range(NCH):
                    if c + 1 < NCH:
                        early(c + 1)
                    late(c)
                S.ts("dve", CO[:, :, :], C32[:, :, :], EMF[:, h_:h_ + 1], None, ALU.mult, None, [C32, EMF], [CO])
                S.dma(D["p_mC"][h_].rearrange("(a p) e -> p a e", p=128), CO[:, :, 0:256], rbuf=CO)
                for db in range(2):
                    S.dma(D["p_mn"][h_, db * 128:(db + 1) * 128].rearrange("(p o) -> p o", o=1), CO[:, db, 256:257], rbuf=CO)
                S.pa_release(mark_w)
                if phases["sample"] and phases.get("s_ml", True):
                    PREs = S.pa("PREs", [NS, 512], F32)
                    QKs = S.pa("QKs", [NS, 512], F32)
                    VSs = S.pa("VSs", [NS, 256], F32)
                    SIGs = S.pa("SIGs", [NS, 256], F32)
                    SZs2 = S.pa("SZs2", [NS, 256], F32)
                    CWm = S.pa("CWm", [NS, 4, 256], F32)
                    SCVm = S.pa("SCVm", [NS, 3, 256], F32)
                    CBm = S.pa("CBm", [NS, 256], F32)
                    NSin = S.pa("NSin", [NS, 256], F32)
                    NOUT = S.pa("NOUT", [NS, 256], F32)
                    QKT = S.pa("QKTs", [128, 4, NS], F32)
                    NT = S.pa("NT", [128, 2, NS], F32)
                    WK = S.pa("WK", [128, 2, NS], F32)
                    NN = S.pa("NN", [128, 2, NS], F32)
                    QN = S.pa("QN", [128, 2, NS], F32)
                    QZ = S.ring("QZ", 2, [128, 2, NS], F32)
                    Cb_ = S.ring("Cb_", 2, [128, 2, 256], F32)
                    ADs = S.pa("ADs", [NS, 1], F32)
                    RDs = S.pa("RDs", [NS, 1], F32)
                    Hs_ = S.pa("Hs_", [NS, 256], F32)
                    GZs = S.pa("GZs", [NS, 256], F32)
                    HGs = S.pa("HGs", [NS, 256], BF16)
                    HGTs = S.pa("HGTs", [128, 2, NS], BF16)
                    BNs = S.pa("BNs", [NS, 6], F32)
                    MVs = S.pa("MVs", [NS, 2], F32)
                    RSs = S.pa("RSs", [NS, 1], F32)
                    lst = []
                    for k in range(8):
                        lst.append((PS[0][0:NS, :], hnTs[:, k, :], wqk[:, k, :], k == 0, k == 7))
                        lst.append((PS[1][0:NS, :], hnTs[:, k, :], wvoz[:, k, 0:512], k == 0, k == 7))
                        lst.append((PS[2][0:NS, 0:256], hnTs[:, k, :], wvoz[:, k, 512:768], k == 0, k == 7))
                    S.mm(lst, [hnTs, wqk, wvoz], [PS[0], PS[1], PS[2]])
                    S.cp("act", PREs[:, :], PS[0][0:NS, :], [PS[0]], [PREs])
                    S.cp("act", VSs[:, :], PS[1][0:NS, 0:256], [PS[1]], [VSs])
                    S.act(SIGs[:, :], PS[1][0:NS, 256:512], AF.Sigmoid, [PS[1]], [SIGs])
                    S.act(SZs2[:, :], PS[2][0:NS, 0:256], AF.Silu, [PS[2]], [SZs2])
                    for hq in range(2):
                        col0 = hq * 2048 + h_ * 256
                        cs = slice(col0, col0 + 256)
                        ls = slice(hq * 256, (hq + 1) * 256)
                        S.dma(CWm[:, :, :], D["mcw_s"][:, :, cs], wbuf=CWm)
                        S.dma(SCVm[:, :, :], D["mconv"][:, :, cs], wbuf=SCVm)
                        S.dma(CBm[:, :], D["mcb_s"][:, cs], wbuf=CBm)
                        S.dma(D["s_mconv"][:, 0:2, cs], SCVm[:, 1:3, :], rbuf=SCVm)
                        S.dma(D["s_mconv"][:, 2, cs], PREs[:, ls], rbuf=PREs)
                        S.tt("pool", SCVm[:, :, :], SCVm[:, :, :], CWm[:, 0:3, :], ALU.mult, [SCVm, CWm], [SCVm])
                        S.tt("dve", QKs[:, ls], PREs[:, ls], CWm[:, 3, :], ALU.mult, [PREs, CWm], [QKs])
                        for t_ in range(3):
                            S.tt("dve", QKs[:, ls], QKs[:, ls], SCVm[:, t_, :], ALU.add, [QKs, SCVm], [QKs])
                        S.tt("dve", QKs[:, ls], QKs[:, ls], CBm[:, :], ALU.add, [QKs, CBm], [QKs])
                    S.act(QKs[:, :], QKs[:, :], AF.Silu, [QKs], [QKs])
                    S.ts("dve", QKs[:, 256:512], QKs[:, 256:512], 0.0625, None, ALU.mult, None, [QKs], [QKs])
                    if phases.get("s_ml_stage", 9) < 2:
                        S.pa_release(mark_w)
                        continue
                    S.dma(NSin[:, :], D["mn"][:, h_, :], wbuf=NSin)
                    p3q = PS[3][:, 0:4 * NS].rearrange("p (a t) -> p a t", a=4)
                    p3n = PS[3][:, 64:64 + 2 * NS].rearrange("p (a t) -> p a t", a=2)
                    S.tr([(p3q[:, a, :], QKs[:, a * 128:(a + 1) * 128], IDf[0:NS, 0:NS]) for a in range(4)] +
                         [(p3n[:, a, :], NSin[:, a * 128:(a + 1) * 128], IDf[0:NS, 0:NS]) for a in range(2)], [QKs, NSin, IDf], [PS[3]])
                    S.cp("dve", QKT[:, :, :], p3q, [PS[3]], [QKT])
                    S.cp("dve", NT[:, :, :], p3n, [PS[3]], [NT])
                    S.tt("dve", WK[:, :, :], QKT[:, 2:4, :], WSB[:, :, h_].unsqueeze(1).to_broadcast([128, 2, NS]), ALU.mult, [QKT, WSB], [WK])
                    S.tt("dve", NN[:, :, :], NT[:, :, :], SCB[:, :, h_].unsqueeze(1).to_broadcast([128, 2, NS]), ALU.mult, [NT, SCB], [NN])
                    S.tt("dve", NN[:, :, :], NN[:, :, :], WK[:, :, :], ALU.add, [NN, WK], [NN])
                    S.tt("dve", QN[:, :, :], QKT[:, 0:2, :], NN[:, :, :], ALU.mult, [QKT, NN], [QN])
                    S.tr([(PS[3][0:NS, 128 + a * 128:256 + a * 128], NN[:, a, :], IDf[:, :]) for a in range(2)], [NN, IDf], [PS[3]])
                    S.mm([(PS[3][0:NS, 400:402], QN[:, db, :], ONES[:, 0:2], db == 0, db == 1) for db in range(2)], [QN, ONES], [PS[3]])
                    S.cp("dve", NOUT[:, :], PS[3][0:NS, 128:384], [PS[3]], [NOUT])
                    S.dma(D["s_mn"][:, h_, :], NOUT[:, :], rbuf=NOUT)
                    if phases.get("s_ml_stage", 9) < 3:
                        S.pa_release(mark_w)
                        continue
                    for b in range(NS):
                        cb_ = Cb_[b % 2]
                        qz = QZ[b % 2]
                        pvb = PS[4 + (b % 2)]
                        S.dma(cb_[:, :, :], D["mC"][b, h_].rearrange("(a p) e -> p a e", p=128), wbuf=cb_)
                        S.mm([(pvb[:, 0:256], IDf[0:NS, b:b + 1].to_broadcast([NS, 128]), VSs[:, :], True, True)], [IDf, VSs], [pvb])
                        S.act(cb_[:, :, :], cb_[:, :, :], AF.Copy, [cb_, SCB], [cb_], scale=SCB[:, b, h_:h_ + 1])
                        for db in range(2):
                            S.stt(cb_[:, db, :], pvb[:, 0:256], WK[:, db, b:b + 1], cb_[:, db, :], ALU.mult, ALU.add, [pvb, WK, cb_], [cb_])
                        S.dma(D["s_mC"][b, h_].rearrange("(a p) e -> p a e", p=128), cb_[:, :, :], rbuf=cb_)
                        S.tt("dve", qz[:, :, :], QKT[:, 0:2, :], OH[:, b, :].unsqueeze(1).to_broadcast([128, 2, NS]), ALU.mult, [QKT, OH], [qz])
                        S.mm([(PS[6][0:NS, 0:256], qz[:, db, :], cb_[:, db, :], (b == 0 and db == 0), (b == NS - 1 and db == 1)) for db in range(2)], [qz, cb_], [PS[6]])
                    if phases.get("s_ml_stage", 9) < 4:
                        S.pa_release(mark_w)
                        continue
                    S.cp("dve", ADs[:, :], PS[3][0:NS, 400:401], [PS[3]], [ADs])
                    S.stt(RDs[:, :], ADs[:, :], -1.0, ADs[:, :], ALU.mult, ALU.max, [ADs], [RDs])
                    S.tt("dve", RDs[:, :], RDs[:, :], EMNs[:, h_:h_ + 1], ALU.max, [RDs, EMNs], [RDs])
                    S.op("dve", lambda h, RDs=RDs: h.reciprocal(out=RDs[:, :], in_=RDs[:, :]), [RDs], [RDs])
                    S.stt(Hs_[:, :], PS[6][0:NS, 0:256], RDs[:, :], SIGs[:, :], ALU.mult, ALU.mult, [PS[6], RDs, SIGs], [Hs_])
                    if phases.get("s_ml_stage", 9) < 5:
                        S.pa_release(mark_w)
                        continue
                    S.op("dve", lambda h, BNs=BNs, Hs_=Hs_: h.bn_stats(out=BNs[:, :], in_=Hs_[:, :]), [Hs_], [BNs])
                    S.op("dve", lambda h, BNs=BNs, MVs=MVs: h.bn_aggr(out=MVs[:, :], in_=BNs[:, :]), [BNs], [MVs])
                    S.ts("dve", RSs[:, :], MVs[:, 1:2], EPS, None, ALU.add, None, [MVs], [RSs])
                    S.act(RSs[:, :], RSs[:, :], AF.Sqrt, [RSs], [RSs])
                    S.op("dve", lambda h, RSs=RSs: h.reciprocal(out=RSs[:, :], in_=RSs[:, :]), [RSs], [RSs])
                    S.ts("dve", Hs_[:, :], Hs_[:, :], MVs[:, 0:1], RSs[:, :], ALU.subtract, ALU.mult, [Hs_, MVs, RSs], [Hs_])
                    S.tt("pool", GZs[:, :], SZs2[:, :], mnwh[0:NS, :], ALU.mult, [SZs2, mnwh], [GZs])
                    S.tt("pool", HGs[:, :], Hs_[:, :], GZs[:, :], ALU.mult, [Hs_, GZs], [HGs])
                    if phases.get("s_ml_stage", 9) < 6:
                        S.pa_release(mark_w)
                        continue
                    hvs = p2b[:, 0:2 * NS].rearrange("p (a t) -> p a t", a=2)
                    S.tr([(hvs[:, db, :], HGs[:, db * 128:(db + 1) * 128], IDb[0:NS, 0:NS]) for db in range(2)], [HGs, IDb], [PS[2]])
                    S.cp("act", HGTs[:, :, :], hvs, [PS[2]], [HGTs])
                    for hf in range(2):
                        S.mm([(PS[6 + hf][0:NS, :], HGTs[:, db, :], woo[:, db, hf * 512:(hf + 1) * 512], db == 0, db == 1) for db in range(2)], [HGTs, woo], [PS[6 + hf]])
                        S.tt("dve", Xs[:, hf * 512:(hf + 1) * 512], Xs[:, hf * 512:(hf + 1) * 512], PS[6 + hf][0:NS, :], ALU.add, [Xs, PS[6 + hf]], [Xs])
                    S.pa_release(mark_w)
            S.pa_release(mark_ml)

        if phases.get("final", True):
            mark_f = S.pa_mark()
            S.dma(NW[:, :], D["normw"][2], wbuf=NW)
            junk = S.ring("fjunk", 2, [128, 1024], BF16)
            yo = S.ring("yo", 2, [128, 1024], F32)
            ss = S.ring("fss", 2, [128, 1], F32)
            rstd = S.ring("frstd", 2, [128, 1], F32)
            for c in range(NCH + 1):
                i = c % 2
                xb, np_ = (X[c], 128) if c < NCH else (Xs, NS)
                rms_rows(xb[0:np_, :], np_, ss[i], rstd[i], junk[i], xb)
                S.stt(yo[i][0:np_, :], xb[0:np_, :], rstd[i][0:np_, :], NW[0:np_, :], ALU.mult, ALU.mult, [xb, rstd[i], NW], [yo[i]])
                if c < NCH:
                    S.dma(D["y_p"][c * 128:(c + 1) * 128, :], yo[i][:, :], rbuf=yo[i])
                else:
                    S.dma(D["y_s"], yo[i][0:NS, :], rbuf=yo[i])
            S.pa_release(mark_f)
        else:
            for c in range(NCH):
                S.dma(D["y_p"][c * 128:(c + 1) * 128, :], X[c][:, :], rbuf=X[c])
            S.dma(D["y_s"], Xs[:, :], rbuf=Xs)
        S.barrier()
        S.emit()
    return nc


def _consts():
    ident = np.eye(128, dtype=np.float32)
    tri = np.triu(np.ones((128, 128), np.float32))
    k = np.arange(128)[:, None]
    q = np.arange(128)[None, :]
    blocks = []
    for Dlt in range(-3, 16):
        d = Dlt * 128 + q - k
        m = ((d >= 0) & (d <= 128)).astype(np.float32)
        m += ((d >= 0) & (d <= 512) & (d % 4 == 0)).astype(np.float32)
        m += ((d >= 0) & (d % 16 == 0)).astype(np.float32)
        blocks.append(m)
    maskmm = np.concatenate(blocks, axis=1).astype(np.float32)
    half = 8
    inv = np.power(np.float32(500000.0), -np.arange(half, dtype=np.float32) * np.float32(2.0 / 16)).astype(np.float32)
    pos = np.arange(2048, dtype=np.float32)
    ang = (pos[:, None] * inv[None, :]).astype(np.float32)
    cos = np.cos(ang).astype(np.float32)
    sin = np.sin(ang).astype(np.float32)
    cc = np.concatenate([cos, cos], axis=1).reshape(16, 128, 16).transpose(1, 0, 2)
    ss = np.concatenate([-sin, sin], axis=1).reshape(16, 128, 16).transpose(1, 0, 2)
    angs = (np.float32(2048.0) * inv).astype(np.float32)
    ccs = np.tile(np.concatenate([np.cos(angs), np.cos(angs)])[None, :], (NS, 1)).astype(np.float32)
    sss = np.tile(np.concatenate([-np.sin(angs), np.sin(angs)])[None, :], (NS, 1)).astype(np.float32)
    return dict(ident=ident, tri=tri, maskmm=maskmm, cct=np.ascontiguousarray(cc, np.float32),
                sst=np.ascontiguousarray(ss, np.float32), ccs=ccs, sss=sss)


def _rep(v, n):
    return np.ascontiguousarray(np.broadcast_to(np.asarray(v, np.float32)[None, ...], (n,) + tuple(np.shape(v))))


_NC_CACHE = {}


def kernel(x_prompt, x_sample, cache_attn_k, cache_attn_v, state_ssd_conv, state_ssd,
           state_mlstm_conv, state_mlstm_c, state_mlstm_n, state_mlstm_m,
           norm_w, final_norm_w, w_in_even, w_out_even, ssd_conv_w, ssd_conv_b,
           ssd_dt_bias, ssd_a_log, ssd_d, ssd_norm_w, w_in_odd, w_out_odd,
           mlstm_conv_w, mlstm_conv_b, mlstm_igate_b, mlstm_fgate_b, mlstm_norm_w):
    f = lambda a: np.ascontiguousarray(np.asarray(a, dtype=np.float32))
    cst = _consts()
    shared = dict(cst)
    shared["normw"] = np.stack([_rep(f(norm_w)[0], 128), _rep(f(norm_w)[1], 128), _rep(f(final_norm_w), 128)])
    shared["w_in_even"] = f(w_in_even)[0]
    shared["w_out_even"] = f(w_out_even)[0]
    shared["w_in_odd"] = f(w_in_odd)[0]
    shared["w_out_odd"] = f(w_out_odd)[0]
    cw = f(ssd_conv_w)[0]
    shared["cwT"] = np.ascontiguousarray(cw.reshape(4, 12, 128).transpose(2, 1, 0))
    shared["cbT"] = np.ascontiguousarray(f(ssd_conv_b)[0].reshape(12, 128).T)
    shared["cw_s"] = _rep(cw, NS)
    shared["cb_s"] = _rep(f(ssd_conv_b)[0], NS)
    shared["dtb"] = _rep(f(ssd_dt_bias)[0], 128)
    shared["alog"] = _rep(f(ssd_a_log)[0], 128)
    shared["dsk"] = _rep(f(ssd_d)[0], 128)
    shared["snw"] = _rep(f(ssd_norm_w)[0], 128)
    mcw = f(mlstm_conv_w)[0]
    shared["mcwT"] = np.ascontiguousarray(mcw.reshape(4, 32, 128).transpose(2, 1, 0))
    shared["mcbT"] = np.ascontiguousarray(f(mlstm_conv_b)[0].reshape(32, 128).T)
    shared["mcw_s"] = _rep(mcw, NS)
    shared["mcb_s"] = _rep(f(mlstm_conv_b)[0], NS)
    shared["igb"] = _rep(f(mlstm_igate_b)[0], 128)
    shared["fgb"] = _rep(f(mlstm_fgate_b)[0], 128)
    shared["mnw"] = _rep(f(mlstm_norm_w)[0], 128)

    xp = f(x_prompt)
    xs = f(x_sample)
    ck = np.asarray(cache_attn_k, np.float32)
    cv = np.asarray(cache_attn_v, np.float32)
    in_maps = []
    for c in range(NCORES):
        sl = slice(c * NS, (c + 1) * NS)
        m = dict(shared)
        m["xp"] = xp[c]
        m["xs"] = np.ascontiguousarray(xs[sl, 0, :])
        m["ck"] = np.ascontiguousarray(ck[0, sl].reshape(NS, 2048, 512)[:NSKV])
        m["cv"] = np.ascontiguousarray(cv[0, sl].reshape(NS, 2048, 512)[:NSKV])
        m["sconv"] = f(state_ssd_conv)[0, sl]
        m["sstate"] = np.ascontiguousarray(f(state_ssd)[0, sl].reshape(NS, 1024, 128))
        m["mconv"] = f(state_mlstm_conv)[0, sl]
        m["mC"] = f(state_mlstm_c)[0, sl]
        m["mn"] = f(state_mlstm_n)[0, sl]
        m["mmm"] = f(state_mlstm_m)[0, sl]
        in_maps.append({k: np.ascontiguousarray(v, dtype=np.float32) for k, v in m.items()})

    if "nc" not in _NC_CACHE:
        _NC_CACHE["nc"] = build_program()
    res = run_bass_kernel_spmd(_NC_CACHE["nc"], in_maps, core_ids=list(range(NCORES)))
    R = res.results

    def cat(name, shape):
        return np.stack([np.asarray(R[c][name], np.float32) for c in range(NCORES)]).reshape(shape)

    def cats(name, shape):
        return np.concatenate([np.asarray(R[c][name], np.float32) for c in range(NCORES)], axis=0).reshape(shape)

    outs = (
        cat("y_p", (8, 2048, 1024)),
        cats("y_s", (128, 1, 1024)),
        cat("p_k", (1, 8, 2048, 8, 64)), cat("p_v", (1, 8, 2048, 8, 64)),
        cat("p_sconv", (1, 8, 3, 1536)), cat("p_ssd", (1, 8, 16, 64, 128)),
        cat("p_mconv", (1, 8, 3, 4096)), cat("p_mC", (1, 8, 8, 256, 256)),
        cat("p_mn", (1, 8, 8, 256)), cat("p_mm", (1, 8, 8)),
        cats("s_k", (1, 128, 1, 8, 64)), cats("s_v", (1, 128, 1, 8, 64)),
        cats("s_sconv", (1, 128, 3, 1536)), cats("s_ssd", (1, 128, 16, 64, 128)),
        cats("s_mconv", (1, 128, 3, 4096)), cats("s_mC", (1, 128, 8, 256, 256)),
        cats("s_mn", (1, 128, 8, 256)), cats("s_mm", (1, 128, 8)),
    )
    return outs
```

```python
import math
import numpy as np
from contextlib import ExitStack
import concourse.bass as bass
import concourse.mybir as mybir
from concourse.bass_utils import run_bass_kernel_spmd

F32 = mybir.dt.float32
BF16 = mybir.dt.bfloat16
AF = mybir.ActivationFunctionType
ALU = mybir.AluOpType
AX = mybir.AxisListType

import os
NSKV = 1 if os.environ.get("DEV_SMALLKV") else 16
NCORES = 8
NCH = 16
NS = 16
EPS = 1e-6


class Buf:
    __slots__ = ("name", "t", "last_w", "readers", "dsem", "ndma")

    def __init__(self, name, t=None):
        self.name = name
        self.t = t
        self.last_w = None
        self.readers = []
        self.dsem = None
        self.ndma = 0

    def __getitem__(self, k):
        return self.t[k]


class DSem:
    __slots__ = ("sem", "n", "q")

    def __init__(self, sem, q):
        self.sem = sem
        self.n = 0
        self.q = q


class Sched:
    ENG = ("pe", "act", "dve", "pool", "sp")

    def __init__(self, nc, es, arena_words):
        self.nc = nc
        self.es = es
        self.sem = {e: es.enter_context(nc.semaphore("s_" + e)) for e in ("pe", "act", "dve", "pool")}
        self.cnt = {e: 0 for e in self.ENG}
        self.waited = {e: {} for e in self.ENG}
        self.ops = {e: [] for e in self.ENG}
        self.dma_bufs = []
        self.free_dsems = []
        self.arena = es.enter_context(nc.sbuf_tensor("arena", [128, arena_words], F32))
        self.arena_words = arena_words
        self.atop = 0
        self.nalloc = 0
        self.pa_bufs = []
        self.capture = None

    def sb(self, name, shape, dt):
        return Buf(name, self.es.enter_context(self.nc.sbuf_tensor(name, list(shape), dt)))

    def ps(self, name, shape, dt):
        return Buf(name, self.es.enter_context(self.nc.psum_tensor(name, list(shape), dt)))

    def pa(self, name, shape, dt):
        n = 1
        for s in shape[1:]:
            n *= s
        words = (n + 1) // 2 if dt == BF16 else n
        words = (words + 1) // 2 * 2
        off = self.atop
        self.atop += words
        assert self.atop <= self.arena_words, (name, self.atop, self.arena_words)
        v = self.arena[0:shape[0], off:off + words]
        if dt == BF16:
            v = v.bitcast(BF16)
        v = v[:, 0:n]
        if len(shape) == 3:
            v = v.rearrange("p (a b) -> p a b", a=shape[1])
        elif len(shape) == 4:
            v = v.rearrange("p (a b c) -> p a b c", a=shape[1], b=shape[2])
        self.nalloc += 1
        b = Buf("%s_%d" % (name, self.nalloc), v)
        self.pa_bufs.append((off, b))
        return b

    def ring(self, name, n, shape, dt):
        return [self.pa("%s%d" % (name, i), shape, dt) for i in range(n)]

    def pa_mark(self):
        return self.atop

    def pa_release(self, mark):
        self.barrier()
        keep = []
        for off, b in self.pa_bufs:
            if off >= mark:
                if b.dsem is not None:
                    self.free_dsems.append(b.dsem)
                    b.dsem = None
            else:
                keep.append((off, b))
        self.pa_bufs = keep
        self.atop = mark

    def _deps(self, eng, reads, writes, skip_sem=None):
        evs = []
        for b in reads:
            if b.last_w is not None:
                evs.append(b.last_w)
        for b in writes:
            if b.last_w is not None and not (skip_sem is not None and b.last_w[0] is skip_sem and b.last_w[2] == "dma"):
                evs.append(b.last_w)
            evs.extend(b.readers)
        w = self.waited[eng]
        best = {}
        for (s, v, src) in evs:
            if src == "pe" and eng == "pe":
                continue
            key = id(s)
            if w.get(key, 0) < v:
                w[key] = v
                best[key] = (s, v)
        return list(best.values())

    def op(self, eng, fns, reads=(), writes=()):
        if self.capture is not None:
            self.capture.append(("op", (eng, fns, reads, writes), {}))
            return
        if callable(fns):
            fns = [fns]
        deps = self._deps(eng, reads, writes)
        self.cnt[eng] += 1
        idx = self.cnt[eng]
        sem = self.sem[eng]
        self.ops[eng].append((deps, fns, (sem, 1)))
        ev = (sem, idx, eng)
        for b in writes:
            b.last_w = ev
            b.readers = []
        for b in reads:
            if b not in writes:
                b.readers.append(ev)

    def dma(self, out_ap, in_ap, rbuf=None, wbuf=None, q="sp", **kw):
        if self.capture is not None:
            kw2 = dict(kw); kw2.update(rbuf=rbuf, wbuf=wbuf, q=q)
            self.capture.append(("dma", (out_ap, in_ap), kw2))
            return
        b = wbuf if wbuf is not None else rbuf
        if b.dsem is None:
            cand = [d for d in self.free_dsems if d.q == q]
            if cand:
                b.dsem = cand[-1]
                self.free_dsems.remove(cand[-1])
            else:
                b.dsem = DSem(self.es.enter_context(self.nc.semaphore("d%d" % len(self.dma_bufs))), q)
                self.dma_bufs.append(b.dsem)
        assert b.dsem.q == q, (b.name, q)
        reads = [rbuf] if rbuf is not None else []
        writes = [wbuf] if wbuf is not None else []
        deps = self._deps(q, reads, writes, skip_sem=(b.dsem.sem if wbuf is not None and rbuf is None else None))
        b.dsem.n += 1
        ev = (b.dsem.sem, 16 * b.dsem.n, "dma")
        self.ops[q].append((deps, [lambda h: h.dma_start(out=out_ap, in_=in_ap, **kw)], (b.dsem.sem, 16)))
        if wbuf is not None:
            wbuf.last_w = ev
            wbuf.readers = []
        if rbuf is not None and rbuf is not wbuf:
            rbuf.readers.append(ev)

    def captured(self, fn, *a):
        self.capture = []
        fn(*a)
        lst = self.capture
        self.capture = None
        return lst

    def replay_interleaved(self, A, B):
        i = j = 0
        while i < len(A) or j < len(B):
            for lst, k in ((A, i), (B, j)):
                if k < len(lst):
                    kind, args, kw = lst[k]
                    if kind == "op":
                        self.op(*args)
                    else:
                        self.dma(*args, **kw)
            i += 1
            j += 1

    def barrier(self):
        for e in self.ENG:
            deps = []
            w = self.waited[e]
            for e2 in ("pe", "act", "dve", "pool"):
                s = self.sem[e2]
                v = self.cnt[e2]
                if v > 0 and w.get(id(s), 0) < v:
                    w[id(s)] = v
                    deps.append((s, v))
            for ds in self.dma_bufs:
                v = 16 * ds.n
                if w.get(id(ds.sem), 0) < v:
                    w[id(ds.sem)] = v
                    deps.append((ds.sem, v))
            if deps:
                self.ops[e].append((deps, [], None))

    def emit(self):
        with self.nc.Block() as block:
            def mk(e):
                def body(h):
                    for deps, fns, inc in self.ops[e]:
                        for s, v in deps:
                            h.wait_ge(s, v)
                        for i, f in enumerate(fns):
                            ins = f(h)
                            if i == len(fns) - 1 and inc is not None:
                                ins.then_inc(inc[0], inc[1])
                return body
            block.tensor(mk("pe"))
            block.scalar(mk("act"))
            block.vector(mk("dve"))
            block.gpsimd(mk("pool"))
            block.sync(mk("sp"))

    def tt(self, eng, out, in0, in1, op, r, w):
        self.op(eng, lambda h: h.tensor_tensor(out=out, in0=in0, in1=in1, op=op), r, w)

    def ts(self, eng, out, in0, s1, s2, op0, op1, r, w):
        if s2 is None:
            self.op(eng, lambda h: h.tensor_scalar(out=out, in0=in0, scalar1=s1, scalar2=None, op0=op0), r, w)
        else:
            self.op(eng, lambda h: h.tensor_scalar(out=out, in0=in0, scalar1=s1, scalar2=s2, op0=op0, op1=op1), r, w)

    def stt(self, out, in0, scalar, in1, op0, op1, r, w):
        self.op("dve", lambda h: h.scalar_tensor_tensor(out=out, in0=in0, scalar=scalar, in1=in1, op0=op0, op1=op1), r, w)

    def act(self, out, in_, func, r, w, bias=None, scale=None, accum=None):
        kw = {}
        if bias is not None:
            kw["bias"] = bias
        if scale is not None:
            kw["scale"] = scale
        if accum is not None:
            kw["accum_out"] = accum
        self.op("act", lambda h: h.activation(out=out, in_=in_, func=func, **kw), r, w)

    def cp(self, eng, out, in_, r, w):
        if eng == "act":
            self.op("act", lambda h: h.copy(out=out, in_=in_), r, w)
        else:
            self.op(eng, lambda h: h.tensor_copy(out=out, in_=in_), r, w)

    def mm(self, lst, r, w):
        self.op("pe", [(lambda h, a=a: h.matmul(out=a[0], lhsT=a[1], rhs=a[2], start=a[3], stop=a[4], skip_group_check=True)) for a in lst], r, w)

    def tr(self, lst, r, w):
        self.op("pe", [(lambda h, a=a: h.transpose(out=a[0], in_=a[1], identity=a[2])) for a in lst], r, w)

    def memset(self, eng, ap, val, w):
        self.op(eng, lambda h: h.memset(ap, val), (), w)


IN_SPECS = [
    ("xp", [2048, 1024]), ("xs", [NS, 1024]),
    ("ck", [NSKV, 2048, 512]), ("cv", [NSKV, 2048, 512]),
    ("sconv", [NS, 3, 1536]), ("sstate", [NS, 1024, 128]),
    ("mconv", [NS, 3, 4096]), ("mC", [NS, 8, 256, 256]), ("mn", [NS, 8, 256]), ("mmm", [NS, 8]),
    ("normw", [3, 128, 1024]),
    ("w_in_even", [1024, 4624]), ("w_out_even", [1536, 1024]),
    ("w_in_odd", [1024, 10256]), ("w_out_odd", [2048, 1024]),
    ("ident", [128, 128]), ("tri", [128, 128]), ("maskmm", [128, 19 * 128]),
    ("cct", [128, 16, 16]), ("sst", [128, 16, 16]), ("ccs", [NS, 16]), ("sss", [NS, 16]),
    ("cwT", [128, 12, 4]), ("cbT", [128, 12]), ("cw_s", [NS, 4, 1536]), ("cb_s", [NS, 1536]),
    ("dtb", [128, 16]), ("alog", [128, 16]), ("dsk", [128, 16]), ("snw", [128, 1024]),
    ("mcwT", [128, 32, 4]), ("mcbT", [128, 32]), ("mcw_s", [NS, 4, 4096]), ("mcb_s", [NS, 4096]),
    ("igb", [128, 8]), ("fgb", [128, 8]), ("mnw", [128, 2048]),
]
OUT_SPECS = [
    ("y_p", [2048, 1024]), ("y_s", [NS, 1024]),
    ("p_k", [2048, 512]), ("p_v", [2048, 512]), ("p_sconv", [3, 1536]), ("p_ssd", [1024, 128]),
    ("p_mconv", [3, 4096]), ("p_mC", [8, 256, 256]), ("p_mn", [8, 256]), ("p_mm", [1, 8]),
    ("s_k", [NS, 512]), ("s_v", [NS, 512]), ("s_sconv", [NS, 3, 1536]), ("s_ssd", [NS, 1024, 128]),
    ("s_mconv", [NS, 3, 4096]), ("s_mC", [NS, 8, 256, 256]), ("s_mn", [NS, 8, 256]), ("s_mm", [NS, 8]),
]

PHASES = dict(attn=True, ssd=True, mlstm=True, sample=True, b2=True, b4=True, b3=True)


def build_program(phases=PHASES):
    nc = bass.Bass("TRN2", target_bir_lowering=False)
    D = {}
    for name, shape in IN_SPECS:
        D[name] = nc.dram_tensor(name, list(shape), F32, kind="ExternalInput").ap()
    for name, shape in OUT_SPECS:
        D[name] = nc.dram_tensor(name, list(shape), F32, kind="ExternalOutput").ap()

    with ExitStack() as es:
        ARENA = 26000
        S = Sched(nc, es, ARENA)
        Xall = es.enter_context(nc.sbuf_tensor("Xall", [128, NCH, 1024], F32))
        X = [Buf("X%d" % c, Xall[:, c, :]) for c in range(NCH)]
        Xs = S.sb("Xs", [NS, 1024], F32)
        hnTall = es.enter_context(nc.sbuf_tensor("hnTall", [128, 8, 2048], BF16))
        hnT = [Buf("hnT%d" % c, hnTall[:, :, c * 128:(c + 1) * 128]) for c in range(NCH)]
        hnTs = S.sb("hnTs", [128, 8, NS], BF16)
        IDf = S.sb("IDf", [128, 128], F32)
        IDb = S.sb("IDb", [128, 128], BF16)
        TRI = S.sb("TRI", [128, 128], F32)
        ONES = S.sb("ONES", [128, 128], F32)
        NW = S.sb("NW", [128, 1024], F32)
        PS = [S.ps("PS%d" % i, [128, 512], F32) for i in range(8)]

        def psb(i, n=1024):
            return PS[i][:, :].bitcast(BF16)[:, 0:n]

        for c in range(NCH):
            S.dma(X[c][:, :], D["xp"][c * 128:(c + 1) * 128, :], wbuf=X[c])
        S.dma(Xs[:, :], D["xs"], wbuf=Xs)
        S.dma(IDf[:, :], D["ident"], wbuf=IDf)
        S.dma(TRI[:, :], D["tri"], wbuf=TRI)
        S.cp("dve", IDb[:, :], IDf[:, :], [IDf], [IDb])
        S.memset("pool", ONES[:, :], 1.0, [ONES])

        def rms_rows(xap, np_, ss, rstd, junk, xb, eng2="dve"):
            S.act(junk[0:np_, :], xap, AF.Square, [xb], [junk, ss], accum=ss[0:np_, :])
            S.ts("dve", rstd[0:np_, :], ss[0:np_, :], 1.0 / 1024, EPS, ALU.mult, ALU.add, [ss], [rstd])
            S.act(rstd[0:np_, :], rstd[0:np_, :], AF.Sqrt, [rstd], [rstd])
            S.op("dve", lambda h: h.reciprocal(out=rstd[0:np_, :], in_=rstd[0:np_, :]), [rstd], [rstd])

        def phase_norm(layer):
            mark = S.pa_mark()
            S.dma(NW[:, :], D["normw"][layer], wbuf=NW)
            junk = S.ring("junk", 2, [128, 1024], BF16)
            hn = S.ring("hn", 2, [128, 1024], BF16)
            ss = S.ring("ss", 2, [128, 1], F32)
            rstd = S.ring("rstd", 2, [128, 1], F32)
            for c in range(NCH + 1):
                i = c % 2
                if c < NCH:
                    xb, np_, dst = X[c], 128, hnT[c]
                    dap = hnT[c][:, :, :]
                else:
                    xb, np_, dst = Xs, NS, hnTs
                    dap = hnTs[:, :, :]
                rms_rows(xb[0:np_, :], np_, ss[i], rstd[i], junk[i], xb)
                S.stt(hn[i][0:np_, :], xb[0:np_, :], rstd[i][0:np_, :], NW[0:np_, :], ALU.mult, ALU.mult, [xb, rstd[i], NW], [hn[i]])
                pb = 6 + i
                pv = psb(pb).rearrange("p (k t) -> p k t", k=8)[:, :, 0:np_]
                S.tr([(pv[:, k, :], hn[i][0:np_, k * 128:(k + 1) * 128], IDb[0:np_, 0:np_]) for k in range(8)], [hn[i], IDb], [PS[pb]])
                S.cp("act", dap, pv, [PS[pb]], [dst])
            S.pa_release(mark)

        phase_norm(0)

        WE = D["w_in_even"]
        if phases["attn"]:
            mark_attn = S.pa_mark()
            ATGT = S.pa("ATGT", [128, 4, 2048], BF16)
            ATGTs = S.pa("ATGTs", [128, 4, NS], BF16)
            QS = S.pa("QS", [NS, 512], F32)
            KS = S.pa("KS", [NS, 512], F32)
            VS = S.pa("VS", [NS, 512], F32)
            SGS = S.pa("SGS", [NS, 512], F32)
            CCt = S.pa("CCt", [128, 16, 16], F32)
            SSt = S.pa("SSt", [128, 16, 16], F32)
            CCs = S.pa("CCs", [NS, 16], F32)
            SSs = S.pa("SSs", [NS, 16], F32)
            S.dma(CCt[:, :, :], D["cct"], wbuf=CCt)
            S.dma(SSt[:, :, :], D["sst"], wbuf=SSt)
            S.dma(CCs[:, :], D["ccs"], wbuf=CCs)
            S.dma(SSs[:, :], D["sss"], wbuf=SSs)
            mark_pairs = S.pa_mark()
            MM = S.pa("MM", [128, 19 * 128], BF16)
            for hf_ in range(2):
                S.dma(MM[:, hf_ * 1216:(hf_ + 1) * 1216], D["maskmm"][:, hf_ * 1216:(hf_ + 1) * 1216], wbuf=MM, q="pool")
            WP = S.ring("WP", 2, [128, 8, 512], BF16)
            QKT = S.pa("QKT", [128, 2, 2048], BF16)
            VA = S.pa("VA", [128, 16, 2, 65], BF16)
            SG = S.pa("SG", [128, 16, 128], BF16)
            ATG = S.pa("ATG", [128, 16, 128], BF16)
            QK = S.ring("QK", 2, [128, 256], F32)
            QSRC = S.ring("QSRC", 2, [128, 256], F32)
            TA = S.ring("TA", 2, [128, 4, 16], F32)
            TB = S.ring("TB", 2, [128, 4, 16], F32)
            VF = S.ring("VF", 2, [128, 128], F32)
            QKb = S.ring("QKb", 2, [128, 256], BF16)
            Eb = S.ring("Eb", 3, [128, 512], BF16)
            Pb = S.ring("Pb", 3, [128, 512], BF16)
            ATT = S.ring("ATT", 2, [128, 4, 64], F32)
            REC = S.ring("REC", 2, [128, 4, 1], F32)
            S.memset("pool", VA[:, :, :, 64:65], 1.0, [VA])

            def load_pair_w(p):
                wb = WP[p % 2]
                for j in range(4):
                    col = j * 512 + p * 128
                    S.dma(wb[:, :, j * 128:(j + 1) * 128], WE[:, col:col + 128].rearrange("(k p) n -> p k n", p=128), wbuf=wb, q="pool")

            def rotary(psv, np_, cc, sn, ta, tb, dst4, rd, dbuf):
                ccb = cc.unsqueeze(1).to_broadcast([np_, 4, 16])
                S.tt("dve", ta[0:np_, :, :], psv[:, :, 0:16], ccb, ALU.mult, rd, [ta])
                S.tt("dve", tb[0:np_, :, 0:8], psv[:, :, 8:16], sn[:, 0:8].unsqueeze(1).to_broadcast([np_, 4, 8]), ALU.mult, rd, [tb])
                S.tt("dve", tb[0:np_, :, 8:16], psv[:, :, 0:8], sn[:, 8:16].unsqueeze(1).to_broadcast([np_, 4, 8]), ALU.mult, rd, [tb])
                S.tt("pool" if phases.get('rotpool', True) else "dve", dst4[:, :, 0:16], ta[0:np_, :, :], tb[0:np_, :, :], ALU.add, [ta, tb], [dbuf])

            load_pair_w(0)
            for p in range(phases.get('npair', 4)):
                if p + 1 < 4:
                    load_pair_w(p + 1)
                wb = WP[p % 2]
                for c in range(NCH + 1):
                    if c >= phases.get('nchunk', 99) and c < NCH:
                        continue
                    if c == NCH and not phases.get('smpc', True):
                        continue
                    i = c % 2
                    smp = (c == NCH)
                    np_ = NS if smp else 128
                    hb = hnTs if smp else hnT[c]
                    pu = PS[i]
                    S.mm([(pu[0:np_, :], hb[:, k, :], wb[:, k, :], k == 0, k == 7) for k in range(8)], [hb, wb], [pu])
                    qk = QK[i]
                    S.cp("act", qk[0:np_, :], pu[0:np_, 0:256], [pu], [qk])
                    psv = pu[0:np_, 0:256].rearrange("p (a b) -> p a b", a=4)
                    qk4 = qk[0:np_, :].rearrange("p (a b) -> p a b", a=4)
                    rsrc = pu
                    if phases.get('rotsb', True):
                        qsrc = QSRC[i]
                        S.cp("act", qsrc[0:np_, :], pu[0:np_, 0:256], [pu], [qsrc])
                        psv = qsrc[0:np_, :].rearrange("p (a b) -> p a b", a=4)
                        rsrc = qsrc
                    if not phases.get('rot', True):
                        pass
                    elif smp:
                        rotary(psv, np_, CCs[:, :], SSs[:, :], TA[i], TB[i], qk4, [rsrc, CCs, SSs], qk)
                    else:
                        rotary(psv, np_, CCt[:, c, :], SSt[:, c, :], TA[i], TB[i], qk4, [rsrc, CCt, SSt], qk)
                    vf = VF[i]
                    S.cp("act", vf[0:np_, :], pu[0:np_, 256:384], [pu], [vf])
                    if smp and not phases.get('smp', True):
                        pass
                    elif smp:
                        S.dma(D["s_k"][:, p * 128:(p + 1) * 128], qk[0:NS, 128:256], rbuf=qk)
                        S.dma(D["s_v"][:, p * 128:(p + 1) * 128], vf[0:NS, :], rbuf=vf)
                        S.cp("pool", QS[:, p * 128:(p + 1) * 128], qk[0:NS, 0:128], [qk], [QS])
                        S.cp("pool", KS[:, p * 128:(p + 1) * 128], qk[0:NS, 128:256], [qk], [KS])
                        S.cp("pool", VS[:, p * 128:(p + 1) * 128], vf[0:NS, :], [vf], [VS])
                        S.act(SGS[:, p * 128:(p + 1) * 128], pu[0:NS, 384:512], AF.Silu, [pu], [SGS])
                    else:
                        S.dma(D["p_k"][c * 128:(c + 1) * 128, p * 128:(p + 1) * 128], qk[:, 128:256], rbuf=qk)
                        S.dma(D["p_v"][c * 128:(c + 1) * 128, p * 128:(p + 1) * 128], vf[:, :], rbuf=vf)
                        S.cp("pool", VA[:, c, :, 0:64], vf[:, :].rearrange("p (a b) -> p a b", a=2), [vf], [VA])
                        S.act(SG[:, c, :], pu[:, 384:512], AF.Silu, [pu], [SG])
                        if not phases.get('trq', True):
                            continue
                        qb = QKb[i]
                        S.cp("pool", qb[:, :], qk[:, :], [qk], [qb])
                        pt = 2 + i
                        ptv = psb(pt, 256).rearrange("p (a t) -> p a t", a=2)
                        S.tr([(ptv[:, a, :], qb[:, a * 128:(a + 1) * 128], IDb[:, :]) for a in range(2)], [qb, IDb], [PS[pt]])
                        S.cp("act", QKT[:, :, c * 128:(c + 1) * 128], ptv, [PS[pt]], [QKT])
                it = 0
                for hh in range(2 if phases.get('b2', True) else 0):
                    hs = slice(hh * 64, (hh + 1) * 64)
                    for g in range(4):
                        po = PS[6 + (g % 2)]
                        pov = po[:, 0:260].rearrange("p (j e) -> p j e", j=4)
                        nk = 4 * g + 4
                        for kc in range(nk):
                            j0 = max(0, kc - 4 * g)
                            cs = slice(j0 * 128, 512)
                            ps_ = PS[4 + (it % 2)]
                            eb = Eb[it % 3]
                            pb_ = Pb[it % 3]
                            S.mm([(ps_[:, cs], QKT[hs, 1, kc * 128:(kc + 1) * 128], QKT[hs, 0, (4 * g + j0) * 128:(4 * g + 4) * 128], True, True)], [QKT], [ps_])
                            S.act(eb[:, cs], ps_[:, cs], AF.Exp, [ps_], [eb], scale=0.125)
                            m0 = (4 * g + j0 - kc + 3) * 128
                            S.tt("dve" if it % 2 == 0 else "pool", pb_[:, cs], eb[:, cs], MM[:, m0:m0 + (4 - j0) * 128], ALU.mult, [eb, MM], [pb_])
                            S.mm([(pov[:, j, :], pb_[:, j * 128:(j + 1) * 128], VA[:, kc, hh, :], (kc == 0 and j == 0), (kc == 4 * g + j)) for j in range(j0, 4)], [pb_, VA], [po])
                            it += 1
                        rec = REC[g % 2]
                        att = ATT[g % 2]
                        S.op("dve", lambda h, rec=rec, pov=pov: h.reciprocal(out=rec[:, :, :], in_=pov[:, :, 64:65]), [po], [rec])
                        S.tt("dve", att[:, :, :], pov[:, :, 0:64], rec[:, :, :].to_broadcast([128, 4, 64]), ALU.mult, [po, rec], [att])
                        S.tt("pool", ATG[:, 4 * g:4 * g + 4, hs], att[:, :, :], SG[:, 4 * g:4 * g + 4, hs], ALU.mult, [att, SG], [ATG])
                for cg in range(4 if phases.get('b2', True) else 0):
                    pt = 2 + (cg % 2)
                    ptv = psb(pt, 512).rearrange("p (a t) -> p a t", a=4)
                    S.tr([(ptv[:, a, :], ATG[:, 4 * cg + a, :], IDb[:, :]) for a in range(4)], [ATG, IDb], [PS[pt]])
                    S.cp("act", ATGT[:, p, cg * 512:(cg + 1) * 512], psb(pt, 512), [PS[pt]], [ATGT])
            S.pa_release(mark_pairs)

            QSb = S.pa("QSb", [NS, 512], BF16)
            S.cp("dve", QSb[:, :], QS[:, :], [QS], [QSb])
            SELb = S.pa("SELb", [NS, NS * 128], BF16)
            S.memset("pool", SELb[:, :], 0.0, [SELb])
            S.tt("pool", SELb[:, :].rearrange("k (b m) -> k b m", b=NS), IDf[0:NS, 0:NS].unsqueeze(2).to_broadcast([NS, NS, 128]),
                 ONES[0:NS, :].unsqueeze(1).to_broadcast([NS, NS, 128]), ALU.mult, [IDf, ONES, SELb], [SELb])
            Kc = S.ring("Kc", 3, [128, 512], F32)
            Vc = S.ring("Vc", 3, [128, 512], F32)
            Vb = S.ring("Vb", 3, [128, 8, 65], BF16)
            PR = S.ring("PR", 2, [128, 512], F32)
            SC = S.ring("SC", 2, [128, 8], F32)
            PZ = S.ring("PZ", 3, [128, 8, NS], BF16)
            for v_ in Vb:
                S.memset("pool", v_[:, :, 64:65], 1.0, [v_])
            pos0 = PS[6][0:NS, 0:260].rearrange("p (j e) -> p j e", j=4)
            pos1 = PS[7][0:NS, 0:260].rearrange("p (j e) -> p j e", j=4)
            it = 0
            for b in range(NS if phases.get('b4', True) else 0):
                pq = PS[b % 2]
                S.mm([(pq[:, :], SELb[:, b * 128:(b + 1) * 128], QSb[:, :], True, True)], [SELb, QSb], [pq])
                for pat, dil in enumerate((1, 4, 16)):
                    r0 = 2048 - 128 * dil
                    kc_, vc_, vb_, pr, sc, pz = Kc[it % 3], Vc[it % 3], Vb[it % 3], PR[it % 2], SC[it % 2], PZ[it % 3]
                    S.dma(kc_[:, :], D["ck"][b, r0:2048:dil, :], wbuf=kc_)
                    S.dma(vc_[:, :], D["cv"][b, r0:2048:dil, :], wbuf=vc_)
                    S.tt("dve", pr[:, :], kc_[:, :], pq[:, :], ALU.mult, [kc_, pq], [pr])
                    S.op("dve", lambda h, sc=sc, pr=pr: h.tensor_reduce(out=sc[:, :], in_=pr[:, :].rearrange("p (a b) -> p a b", a=8), axis=AX.X, op=ALU.add), [pr], [sc])
                    S.memset("pool", pz[:, :, :], 0.0, [pz])
                    S.act(pz[:, :, b], sc[:, :], AF.Exp, [sc, pz], [pz], scale=0.125)
                    S.cp("pool", vb_[:, :, 0:64], vc_[:, :].rearrange("p (a b) -> p a b", a=8), [vc_], [vb_])
                    first = (b == 0 and pat == 0)
                    last = (b == NS - 1 and pat == 2)
                    S.mm([((pos0 if h_ < 4 else pos1)[:, h_ % 4, :], pz[:, h_, :], vb_[:, h_, :], first and (h_ % 4 == 0), last) for h_ in range(8)],
                         [pz, vb_], [PS[6], PS[7]])
                    it += 1
            OS = S.pa("OS", [NS, 8, 65], F32)
            S.cp("act", OS[:, 0:4, :], pos0, [PS[6]], [OS])
            S.cp("act", OS[:, 4:8, :], pos1, [PS[7]], [OS])
            PRs = S.pa("PRs", [NS, 512], F32)
            SCs = S.pa("SCs", [NS, 8], F32)
            S.tt("dve", PRs[:, :], QS[:, :], KS[:, :], ALU.mult, [QS, KS], [PRs])
            S.op("dve", lambda h: h.tensor_reduce(out=SCs[:, :], in_=PRs[:, :].rearrange("p (a b) -> p a b", a=8), axis=AX.X, op=ALU.add), [PRs], [SCs])
            S.act(SCs[:, :], SCs[:, :], AF.Exp, [SCs], [SCs], scale=0.125, bias=None)
            S.ts("dve", SCs[:, :], SCs[:, :], 3.0, None, ALU.mult, None, [SCs], [SCs])
            S.tt("dve", PRs[:, :].rearrange("p (a b) -> p a b", a=8), VS[:, :].rearrange("p (a b) -> p a b", a=8),
                 SCs[:, :].unsqueeze(2).to_broadcast([NS, 8, 64]), ALU.mult, [VS, SCs], [PRs])
            S.tt("dve", OS[:, :, 0:64], OS[:, :, 0:64], PRs[:, :].rearrange("p (a b) -> p a b", a=8), ALU.add, [OS, PRs], [OS])
            S.tt("dve", OS[:, :, 64:65], OS[:, :, 64:65], SCs[:, :].unsqueeze(2), ALU.add, [OS, SCs], [OS])
            RS = S.pa("RS", [NS, 8, 1], F32)
            S.op("dve", lambda h: h.reciprocal(out=RS[:, :, :], in_=OS[:, :, 64:65]), [OS], [RS])
            S.tt("dve", PRs[:, :].rearrange("p (a b) -> p a b", a=8), OS[:, :, 0:64], RS[:, :, :].to_broadcast([NS, 8, 64]), ALU.mult, [OS, RS], [PRs])
            ATGs = S.pa("ATGs", [NS, 512], BF16)
            S.tt("dve", ATGs[:, :], PRs[:, :], SGS[:, :], ALU.mult, [PRs, SGS], [ATGs])
            ptv = psb(2, 4 * NS).rearrange("p (a t) -> p a t", a=4)
            S.tr([(ptv[:, a, :], ATGs[:, a * 128:(a + 1) * 128], IDb[0:NS, 0:NS]) for a in range(4)], [ATGs, IDb], [PS[2]])
            S.cp("act", ATGTs[:, :, :], ptv, [PS[2]], [ATGTs])

            WOa = S.pa("WOa", [128, 4, 1024], BF16)
            S.dma(WOa[:, :, :], D["w_out_even"][0:512, :].rearrange("(k p) n -> p k n", p=128), wbuf=WOa, q="pool")
            for c in range(NCH + 1 if phases.get('b3', True) else 0):
                smp = (c == NCH)
                np_ = NS if smp else 128
                xb = Xs if smp else X[c]
                for hf in range(2):
                    pb = PS[2 * (c % 2) + hf]
                    if smp:
                        lst = [(pb[0:NS, :], ATGTs[:, k, :], WOa[:, k, hf * 512:(hf + 1) * 512], k == 0, k == 3) for k in range(4)]
                        S.mm(lst, [ATGTs, WOa], [pb])
                    else:
                        lst = [(pb[:, :], ATGT[:, k, c * 128:(c + 1) * 128], WOa[:, k, hf * 512:(hf + 1) * 512], k == 0, k == 3) for k in range(4)]
                        S.mm(lst, [ATGT, WOa], [pb])
                    S.tt("dve", xb[0:np_, hf * 512:(hf + 1) * 512], xb[0:np_, hf * 512:(hf + 1) * 512], pb[0:np_, :], ALU.add, [xb, pb], [xb])
            S.pa_release(mark_attn)

        if phases["ssd"]:
            mark_ssd = S.pa_mark()
            Wz = S.pa("Wz", [128, 8, 1024], BF16)
            Wx = S.pa("Wx", [128, 8, 1536], BF16)
            Wdt = S.pa("Wdt", [128, 8, 16], BF16)
            WOs = S.pa("WOs", [128, 8, 1024], BF16)
            S.dma(Wx[:, :, :], WE[:, 3072:4608].rearrange("(k p) n -> p k n", p=128), wbuf=Wx, q="pool")
            S.dma(Wz[:, :, :], WE[:, 2048:3072].rearrange("(k p) n -> p k n", p=128), wbuf=Wz, q="pool")
            S.dma(Wdt[:, :, :], WE[:, 4608:4624].rearrange("(k p) n -> p k n", p=128), wbuf=Wdt, q="pool")
            S.dma(WOs[:, :, :], D["w_out_even"][512:1536, :].rearrange("(k p) n -> p k n", p=128), wbuf=WOs, q="pool")
            CW = S.pa("CW", [128, 12, 4], F32)
            CB = S.pa("CB", [128, 12], F32)
            DTB = S.pa("DTB", [128, 16], F32)
            ABC = S.pa("ABC", [128, 16], F32)
            DSK = S.pa("DSK", [128, 16], F32)
            SNW = S.pa("SNW", [128, 1024], F32)
            S.dma(CW[:, :, :], D["cwT"], wbuf=CW)
            S.dma(CB[:, :], D["cbT"], wbuf=CB)
            S.dma(DTB[:, :], D["dtb"], wbuf=DTB)
            S.dma(ABC[:, :], D["alog"], wbuf=ABC)
            S.dma(DSK[:, :], D["dsk"], wbuf=DSK)
            S.dma(SNW[:, :], D["snw"], wbuf=SNW)
            S.act(ABC[:, :], ABC[:, :], AF.Exp, [ABC], [ABC])
            S.ts("dve", ABC[:, :], ABC[:, :], -1.0, None, ALU.mult, None, [ABC], [ABC])
            mark_ssdw = S.pa_mark()
            A1 = S.pa("A1", [128, 12, 131], F32)
            A2 = S.pa("A2", [128, 12, 128], F32)
            PRE = A1
            CV = A2
            Yv = A1[:, :, :].rearrange("p a b -> p (a b)")[:, 0:1024]
            SZv = A2[:, :, :].rearrange("p a b -> p (a b)")[:, 0:1024]
            CARRY = S.pa("CARRY", [128, 12, 3], F32)
            XC = S.pa("XC", [128, 12, 128], BF16)
            XT = S.pa("XT", [128, 1024], BF16)
            BTOK = S.pa("BTOK", [128, 2, 128], BF16)
            TMPD = S.pa("TMPD", [128, 1024], F32)
            YZW = S.pa("YZW", [128, 1024], BF16)
            YZWT = S.pa("YZWT", [128, 8, 128], BF16)
            S32 = S.pa("S32", [128, 2, 512], F32)
            SB16 = S.pa("SB16", [128, 2, 512], BF16)
            XW = S.pa("XW", [128, 1024], BF16)
            SEG = S.ring("SEG", 2, [128, 128], F32)
            EX = S.ring("EX", 2, [128, 128], F32)
            WT = S.ring("WT", 2, [128, 128], BF16)
            CBM = S.pa("CBM", [128, 2, 128], F32)
            DTR = S.pa("DTR", [128, 16], F32)
            DT = S.pa("DT", [128, 16], F32)
            DA = S.pa("DA", [128, 16], F32)
            ACS = S.pa("ACS", [128, 32], F32)
            EAC = S.pa("EAC", [128, 32], F32)
            WEND = S.pa("WEND", [128, 16], F32)
            ssy = S.pa("ssy", [128, 1], F32)
            rsy = S.pa("rsy", [128, 1], F32)
            P7r = [Buf("P7r%d" % r, PS[7][:, r * 128:(r + 1) * 128]) for r in range(4)]
            S.memset("pool", S32[:, :, :], 0.0, [S32])
            S.memset("pool", SB16[:, :, :], 0.0, [SB16])
            S.memset("pool", CARRY[:, :, :], 0.0, [CARRY])
            for c in range(NCH):
                hb = hnT[c]
                for b3 in range(3):
                    lst = []
                    for bq in range(4):
                        blk = b3 * 4 + bq
                        for k in range(8):
                            lst.append((PS[b3][:, bq * 128:(bq + 1) * 128], Wx[:, k, blk * 128:(blk + 1) * 128], hb[:, k, :], k == 0, k == 7))
                    S.mm(lst, [Wx, hb], [PS[b3]])
                S.cp("pool", PRE[:, :, 0:3], CARRY[:, :, :], [CARRY], [PRE])
                for b3 in range(3):
                    S.cp("act", PRE[:, 4 * b3:4 * b3 + 4, 3:131], PS[b3][:, :].rearrange("p (a b) -> p a b", a=4), [PS[b3]], [PRE])
                S.cp("pool", CARRY[:, :, :], PRE[:, :, 128:131], [PRE], [CARRY])
                if c == NCH - 1:
                    for j_ in range(3):
                        S.dma(D["p_sconv"][j_].rearrange("(b p) -> p b", p=128), PRE[:, :, 128 + j_], rbuf=PRE, allow_slow_non_contiguous=True)
                for blk in range(12):
                    S.act(CV[:, blk, :], PRE[:, blk, 0:128], AF.Identity, [PRE, CW, CB], [CV], bias=CB[:, blk:blk + 1], scale=CW[:, blk, 0:1])
                for blk in range(12):
                    for j in range(1, 4):
                        S.stt(CV[:, blk, :], PRE[:, blk, j:j + 128], CW[:, blk, j:j + 1], CV[:, blk, :], ALU.mult, ALU.add, [PRE, CW, CV], [CV])
                S.act(XC[:, :, :], CV[:, :, :], AF.Silu, [CV], [XC])
                p3v = psb(3).rearrange("p (a t) -> p a t", a=8)
                S.tr([(p3v[:, a, :], XC[:, a, :], IDb[:, :]) for a in range(8)], [XC, IDb], [PS[3]])
                S.cp("act", XT[:, :], psb(3), [PS[3]], [XT])
                S.tr([(p3v[:, a, :], XC[:, 8 + a, :], IDb[:, :]) for a in range(2)], [XC, IDb], [PS[3]])
                S.cp("act", BTOK[:, :, :], p3v[:, 0:2, :], [PS[3]], [BTOK])
                lst = []
                for k in range(8):
                    lst.append((PS[4][:, :], hb[:, k, :], Wz[:, k, 0:512], k == 0, k == 7))
                    lst.append((PS[5][:, :], hb[:, k, :], Wz[:, k, 512:1024], k == 0, k == 7))
                    lst.append((PS[6][:, 0:16], hb[:, k, :], Wdt[:, k, :], k == 0, k == 7))
                S.mm(lst, [hb, Wz, Wdt], [PS[4], PS[5], PS[6]])
                S.act(SZv[:, 0:512], PS[4][:, :], AF.Silu, [PS[4]], [A2])
                S.act(SZv[:, 512:1024], PS[5][:, :], AF.Silu, [PS[5]], [A2])
                S.tt("dve", DTR[:, :], PS[6][:, 0:16], DTB[:, :], ALU.add, [PS[6], DTB], [DTR])
                S.act(DTR[:, :], DTR[:, :], AF.Exp, [DTR], [DTR])
                S.act(DT[:, :], DTR[:, :], AF.Ln, [DTR], [DT], bias=1.0)
                S.tt("dve", DA[:, :], DT[:, :], ABC[:, :], ALU.mult, [DT, ABC], [DA])
                S.mm([(PS[6][:, 16:32], TRI[:, :], DA[:, :], True, True), (PS[6][:, 32:48], ONES[:, :], DA[:, :], True, True)], [TRI, ONES, DA], [PS[6]])
                S.cp("dve", ACS[:, :], PS[6][:, 16:48], [PS[6]], [ACS])
                S.act(EAC[:, :], ACS[:, :], AF.Exp, [ACS], [EAC])
                S.tt("dve", WEND[:, :], ACS[:, 16:32], ACS[:, 0:16], ALU.subtract, [ACS], [WEND])
                S.act(WEND[:, :], WEND[:, :], AF.Exp, [WEND], [WEND])
                S.tt("dve", WEND[:, :], WEND[:, :], DT[:, :], ALU.mult, [WEND, DT], [WEND])
                S.mm([(PS[6][:, 128 + g * 128:256 + g * 128], XC[:, 8 + g, :], XC[:, 10 + g, :], True, True) for g in range(2)], [XC], [PS[6]])
                S.tt("dve", CBM[:, :, :], PS[6][:, 128:384].rearrange("p (g i) -> p g i", g=2), TRI[:, :].unsqueeze(1).to_broadcast([128, 2, 128]), ALU.mult, [PS[6], TRI], [CBM])
                for h_ in range(16):
                    g = h_ // 8
                    pr = P7r[h_ % 4]
                    S.mm([(pr[:, :], DA[:, h_:h_ + 1].to_broadcast([128, 128]), TRI[:, :], True, True)], [DA, TRI], [pr])
                    sg, ex, wt = SEG[h_ % 2], EX[h_ % 2], WT[h_ % 2]
                    S.ts("dve", sg[:, :], pr[:, :], ACS[:, h_:h_ + 1], 0.0, ALU.subtract, ALU.min, [pr, ACS], [sg])
                    S.act(ex[:, :], sg[:, :], AF.Exp, [sg], [ex])
                    S.stt(wt[:, :], ex[:, :], DT[:, h_:h_ + 1], CBM[:, g, :], ALU.mult, ALU.mult, [ex, DT, CBM], [wt])
                    S.mm([(PS[g][:, (h_ % 8) * 64:(h_ % 8 + 1) * 64], wt[:, :], XT[:, h_ * 64:(h_ + 1) * 64], (h_ % 8 == 0), True)], [wt, XT], [PS[g]])
                S.mm([(PS[4 + g][:, :], XC[:, 10 + g, :], SB16[:, g, :], True, True) for g in range(2)], [XC, SB16], [PS[4], PS[5]])
                for g in range(2):
                    S.tt("dve", Yv[:, g * 512:(g + 1) * 512].rearrange("p (a b) -> p a b", a=8), PS[4 + g][:, :].rearrange("p (a b) -> p a b", a=8),
                         EAC[:, g * 8:(g + 1) * 8].unsqueeze(2).to_broadcast([128, 8, 64]), ALU.mult, [PS[4 + g], EAC], [A1])
                    S.tt("dve", Yv[:, g * 512:(g + 1) * 512], Yv[:, g * 512:(g + 1) * 512], PS[g][:, :], ALU.add, [A1, PS[g]], [A1])
                S.tt("pool", TMPD[:, :].rearrange("p (a b) -> p a b", a=16), XT[:, :].rearrange("p (a b) -> p a b", a=16),
                     DSK[:, :].unsqueeze(2).to_broadcast([128, 16, 64]), ALU.mult, [XT, DSK], [TMPD])
                S.tt("dve", Yv, Yv, TMPD[:, :], ALU.add, [A1, TMPD], [A1])
                S.tt("pool", Yv, Yv, SZv, ALU.mult, [A1, A2], [A1])
                S.act(TMPD[:, :], Yv, AF.Square, [A1], [TMPD, ssy], accum=ssy[:, :])
                S.ts("dve", rsy[:, :], ssy[:, :], 1.0 / 1024, EPS, ALU.mult, ALU.add, [ssy], [rsy])
                S.act(rsy[:, :], rsy[:, :], AF.Sqrt, [rsy], [rsy])
                S.op("dve", lambda h: h.reciprocal(out=rsy[:, :], in_=rsy[:, :]), [rsy], [rsy])
                S.tt("pool", YZW[:, :], Yv, SNW[:, :], ALU.mult, [A1, SNW], [YZW])
                S.tr([(p3v[:, a, :], YZW[:, a * 128:(a + 1) * 128], IDb[:, :]) for a in range(8)], [YZW, IDb], [PS[3]])
                S.cp("act", YZWT[:, :, :], p3v, [PS[3]], [YZWT])
                for hf in range(2):
                    S.mm([(PS[hf][:, :], YZWT[:, k, :], WOs[:, k, hf * 512:(hf + 1) * 512], k == 0, k == 7) for k in range(8)], [YZWT, WOs], [PS[hf]])
                    S.stt(X[c][:, hf * 512:(hf + 1) * 512], PS[hf][:, :], rsy[:, :], X[c][:, hf * 512:(hf + 1) * 512], ALU.mult, ALU.add, [PS[hf], rsy, X[c]], [X[c]])
                S.tt("pool", XW[:, :].rearrange("p (a b) -> p a b", a=16), XT[:, :].rearrange("p (a b) -> p a b", a=16),
                     WEND[:, :].unsqueeze(2).to_broadcast([128, 16, 64]), ALU.mult, [XT, WEND], [XW])
                S.mm([(PS[4 + g][:, :], BTOK[:, g, :], XW[:, g * 512:(g + 1) * 512], True, True) for g in range(2)], [BTOK, XW], [PS[4], PS[5]])
                for g in range(2):
                    S.tt("pool", S32[:, g, :].rearrange("p (a b) -> p a b", a=8), S32[:, g, :].rearrange("p (a b) -> p a b", a=8),
                         EAC[:, 16 + g * 8:16 + (g + 1) * 8].unsqueeze(2).to_broadcast([128, 8, 64]), ALU.mult, [S32, EAC], [S32])
                    S.tt("dve", S32[:, g, :], S32[:, g, :], PS[4 + g][:, :], ALU.add, [S32, PS[4 + g]], [S32])
                S.cp("pool", SB16[:, :, :], S32[:, :, :], [S32], [SB16])
            for g in range(2):
                S.tr([(PS[g][:, a * 128:(a + 1) * 128], S32[:, g, a * 128:(a + 1) * 128], IDf[:, :]) for a in range(4)], [S32, IDf], [PS[g]])
                S.cp("act", TMPD[:, g * 512:(g + 1) * 512], PS[g][:, :], [PS[g]], [TMPD])
            S.dma(D["p_ssd"].rearrange("(a q) n -> q a n", q=128), TMPD[:, :].rearrange("p (a n) -> p a n", a=8), rbuf=TMPD)
            S.pa_release(mark_ssdw)
            if phases["sample"] and phases.get("s_ssd", True):
                XPs = S.pa("XPs", [NS, 1536], F32)
                SZs = S.pa("SZs", [NS, 1024], F32)
                XCs = S.pa("XCs", [NS, 1536], F32)
                DTs = S.pa("DTs", [NS, 16], F32)
                DAs = S.pa("DAs", [NS, 16], F32)
                DECs = S.pa("DECs", [NS, 16], F32)
                lst = []
                for k in range(8):
                    for j_ in range(3):
                        lst.append((PS[j_][0:NS, :], hnTs[:, k, :], Wx[:, k, j_ * 512:(j_ + 1) * 512], k == 0, k == 7))
                    lst.append((PS[4][0:NS, :], hnTs[:, k, :], Wz[:, k, 0:512], k == 0, k == 7))
                    lst.append((PS[5][0:NS, :], hnTs[:, k, :], Wz[:, k, 512:1024], k == 0, k == 7))
                    lst.append((PS[6][0:NS, 0:16], hnTs[:, k, :], Wdt[:, k, :], k == 0, k == 7))
                S.mm(lst, [hnTs, Wx, Wz, Wdt], [PS[0], PS[1], PS[2], PS[4], PS[5], PS[6]])
                for j_ in range(3):
                    S.cp("act", XPs[:, j_ * 512:(j_ + 1) * 512], PS[j_][0:NS, :], [PS[j_]], [XPs])
                S.act(SZs[:, 0:512], PS[4][0:NS, :], AF.Silu, [PS[4]], [SZs])
                S.act(SZs[:, 512:1024], PS[5][0:NS, :], AF.Silu, [PS[5]], [SZs])
                S.tt("dve", DTs[:, :], PS[6][0:NS, 0:16], DTB[0:NS, :], ALU.add, [PS[6], DTB], [DTs])
                S.act(DTs[:, :], DTs[:, :], AF.Exp, [DTs], [DTs])
                S.act(DTs[:, :], DTs[:, :], AF.Ln, [DTs], [DTs], bias=1.0)
                S.tt("dve", DAs[:, :], DTs[:, :], ABC[0:NS, :], ALU.mult, [DTs, ABC], [DAs])
                S.act(DECs[:, :], DAs[:, :], AF.Exp, [DAs], [DECs])
                mk1 = S.pa_mark()
                CWg = S.pa("CWg", [NS, 4, 512], F32)
                SCVg = S.pa("SCVg", [NS, 3, 512], F32)
                CBs = S.pa("CBs", [NS, 1536], F32)
                S.dma(CBs[:, :], D["cb_s"], wbuf=CBs)
                for j_ in range(3):
                    cs = slice(j_ * 512, (j_ + 1) * 512)
                    S.dma(CWg[:, :, :], D["cw_s"][:, :, cs], wbuf=CWg)
                    S.dma(SCVg[:, :, :], D["sconv"][:, :, cs], wbuf=SCVg)
                    S.dma(D["s_sconv"][:, 0:2, cs], SCVg[:, 1:3, :], rbuf=SCVg)
                    S.dma(D["s_sconv"][:, 2, cs], XPs[:, cs], rbuf=XPs)
                    S.tt("pool", SCVg[:, :, :], SCVg[:, :, :], CWg[:, 0:3, :], ALU.mult, [SCVg, CWg], [SCVg])
                    S.tt("dve", XCs[:, cs], XPs[:, cs], CWg[:, 3, :], ALU.mult, [XPs, CWg], [XCs])
                    for t_ in range(3):
                        S.tt("dve", XCs[:, cs], XCs[:, cs], SCVg[:, t_, :], ALU.add, [XCs, SCVg], [XCs])
                    S.tt("dve", XCs[:, cs], XCs[:, cs], CBs[:, cs], ALU.add, [XCs, CBs], [XCs])
                S.act(XCs[:, :], XCs[:, :], AF.Silu, [XCs], [XCs])
                S.pa_release(mk1)
                TMPs = S.pa("TMPs", [NS, 1024], F32)
                YTs_ = S.pa("YTs_", [128, 8, NS], F32)
                mk2 = S.pa_mark()
                DTXT = S.pa("DTXT", [128, 8, NS], F32)
                DECT = S.pa("DECT", [128, 8, NS], F32)
                STr = S.ring("STr", 2, [128, 8, 128], F32)
                PROD = S.pa("PROD", [128, 8, 128], F32)
                S.tt("dve", TMPs[:, :].rearrange("p (a b) -> p a b", a=16), XCs[:, 0:1024].rearrange("p (a b) -> p a b", a=16),
                     DTs[:, :].unsqueeze(2).to_broadcast([NS, 16, 64]), ALU.mult, [XCs, DTs], [TMPs])
                p0v = PS[0][:, 0:8 * NS].rearrange("p (a t) -> p a t", a=8)
                S.tr([(p0v[:, a, :], TMPs[:, a * 128:(a + 1) * 128], IDf[0:NS, 0:NS]) for a in range(8)], [TMPs, IDf], [PS[0]])
                S.cp("dve", DTXT[:, :, :], p0v, [PS[0]], [DTXT])
                S.cp("dve", TMPs[:, :].rearrange("p (a b) -> p a b", a=16), DECs[:, :].unsqueeze(2).to_broadcast([NS, 16, 64]), [DECs, PS[0]], [TMPs])
                p1v = PS[1][:, 0:8 * NS].rearrange("p (a t) -> p a t", a=8)
                S.tr([(p1v[:, a, :], TMPs[:, a * 128:(a + 1) * 128], IDf[0:NS, 0:NS]) for a in range(8)], [TMPs, IDf], [PS[1]])
                S.cp("dve", DECT[:, :, :], p1v, [PS[1]], [DECT])
                for b in range(NS):
                    st = STr[b % 2]
                    pb_ = PS[2 + (b % 2)]
                    S.dma(st[:, :, :], D["sstate"][b].rearrange("(a q) n -> q a n", q=128), wbuf=st)
                    S.mm([(pb_[:, :], IDf[0:NS, b:b + 1].to_broadcast([NS, 128]), XCs[:, 1024:1536], True, True)], [IDf, XCs], [pb_])
                    S.tt("dve", st[:, :, :], st[:, :, :], DECT[:, :, b:b + 1].to_broadcast([128, 8, 128]), ALU.mult, [st, DECT], [st])
                    for g in range(2):
                        S.tt("dve", PROD[:, 4 * g:4 * g + 4, :], pb_[:, g * 128:(g + 1) * 128].unsqueeze(1).to_broadcast([128, 4, 128]),
                             DTXT[:, 4 * g:4 * g + 4, b:b + 1].to_broadcast([128, 4, 128]), ALU.mult, [pb_, DTXT], [PROD])
                    S.tt("pool", st[:, :, :], st[:, :, :], PROD[:, :, :], ALU.add, [st, PROD], [st])
                    S.dma(D["s_ssd"][b].rearrange("(a q) n -> q a n", q=128), st[:, :, :], rbuf=st)
                    for g in range(2):
                        S.tt("dve", PROD[:, 4 * g:4 * g + 4, :], st[:, 4 * g:4 * g + 4, :],
                             pb_[:, 256 + g * 128:256 + (g + 1) * 128].unsqueeze(1).to_broadcast([128, 4, 128]), ALU.mult, [st, pb_], [PROD])
                    S.op("dve", lambda h, b=b: h.tensor_reduce(out=YTs_[:, :, b], in_=PROD[:, :, :], axis=AX.X, op=ALU.add), [PROD], [YTs_])
                S.pa_release(mk2)
                for a in range(8):
                    pbk = PS[4 + a // 4]
                    S.tr([(pbk[0:NS, (a % 4) * 128:(a % 4 + 1) * 128], YTs_[:, a, :], IDf[:, :])], [YTs_, IDf], [pbk])
                Ys = S.pa("Ys", [NS, 1024], F32)
                S.tt("pool", TMPs[:, :].rearrange("p (a b) -> p a b", a=16), XCs[:, 0:1024].rearrange("p (a b) -> p a b", a=16),
                     DSK[0:NS, :].unsqueeze(2).to_broadcast([NS, 16, 64]), ALU.mult, [XCs, DSK], [TMPs])
                for hf in range(2):
                    S.tt("dve", Ys[:, hf * 512:(hf + 1) * 512], TMPs[:, hf * 512:(hf + 1) * 512], PS[4 + hf][0:NS, :], ALU.add, [TMPs, PS[4 + hf]], [Ys])
                S.tt("dve", Ys[:, :], Ys[:, :], SZs[:, :], ALU.mult, [Ys, SZs], [Ys])
                sss_ = S.pa("sss_", [NS, 1], F32)
                rss_ = S.pa("rss_", [NS, 1], F32)
                S.act(TMPs[:, :], Ys[:, :], AF.Square, [Ys], [TMPs, sss_], accum=sss_[:, :])
                S.ts("dve", rss_[:, :], sss_[:, :], 1.0 / 1024, EPS, ALU.mult, ALU.add, [sss_], [rss_])
                S.act(rss_[:, :], rss_[:, :], AF.Sqrt, [rss_], [rss_])
                S.op("dve", lambda h: h.reciprocal(out=rss_[:, :], in_=rss_[:, :]), [rss_], [rss_])
                YWs = S.pa("YWs", [NS, 1024], BF16)
                S.tt("dve", YWs[:, :], Ys[:, :], SNW[0:NS, :], ALU.mult, [Ys, SNW], [YWs])
                p3s = psb(3, 8 * NS).rearrange("p (a t) -> p a t", a=8)
                S.tr([(p3s[:, a, :], YWs[:, a * 128:(a + 1) * 128], IDb[0:NS, 0:NS]) for a in range(8)], [YWs, IDb], [PS[3]])
                YTb = S.pa("YTb", [128, 8, NS], BF16)
                S.cp("act", YTb[:, :, :], p3s, [PS[3]], [YTb])
                for hf in range(2):
                    S.mm([(PS[hf][0:NS, :], YTb[:, k, :], WOs[:, k, hf * 512:(hf + 1) * 512], k == 0, k == 7) for k in range(8)], [YTb, WOs], [PS[hf]])
                    S.stt(Xs[:, hf * 512:(hf + 1) * 512], PS[hf][0:NS, :], rss_[:, :], Xs[:, hf * 512:(hf + 1) * 512], ALU.mult, ALU.add, [PS[hf], rss_, Xs], [Xs])
            S.pa_release(mark_ssd)

        if phases["mlstm"]:
            phase_norm(1)
            WO_ = D["w_in_odd"]
            mark_ml = S.pa_mark()
            Wif = S.pa("Wif", [128, 8, 16], BF16)
            S.dma(Wif[:, :, :], WO_[:, 8192:8208].rearrange("(k p) n -> p k n", p=128), wbuf=Wif, q="pool")
            Wqk = S.ring("Wqk", 2, [128, 8, 512], BF16)
            Wvoz = S.ring("Wvoz", 2, [128, 8, 768], BF16)
            WOo = S.ring("WOo", 2, [128, 2, 1024], BF16)
            MNWh = S.ring("MNWh", 2, [128, 256], F32)

            def load_head_w(h_):
                i_ = h_ % 2
                for j_, off in enumerate((0, 2048)):
                    S.dma(Wqk[i_][:, :, j_ * 256:(j_ + 1) * 256], WO_[:, off + h_ * 256:off + (h_ + 1) * 256].rearrange("(k p) n -> p k n", p=128), wbuf=Wqk[i_], q="pool")
                for j_, off in enumerate((4096, 6144, 8208)):
                    S.dma(Wvoz[i_][:, :, j_ * 256:(j_ + 1) * 256], WO_[:, off + h_ * 256:off + (h_ + 1) * 256].rearrange("(k p) n -> p k n", p=128), wbuf=Wvoz[i_], q="pool")
                S.dma(WOo[i_][:, :, :], D["w_out_odd"][h_ * 256:(h_ + 1) * 256, :].rearrange("(k p) n -> p k n", p=128), wbuf=WOo[i_], q="pool")
                S.dma(MNWh[i_][:, :], D["mnw"][:, h_ * 256:(h_ + 1) * 256], wbuf=MNWh[i_])

            load_head_w(0)
            MCW = S.pa("MCW", [128, 32, 4], F32)
            MCB = S.pa("MCB", [128, 32], F32)
            IFB = S.pa("IFB", [128, 16], F32)
            S.dma(MCW[:, :, :], D["mcwT"], wbuf=MCW)
            S.dma(MCB[:, :], D["mcbT"], wbuf=MCB)
            S.dma(IFB[:, 0:8], D["igb"], wbuf=IFB)
            S.dma(IFB[:, 8:16], D["fgb"], wbuf=IFB)
            IFt = S.pa("IFt", [128, 16, 16], F32)
            LF = S.pa("LF", [128, 16, 8], F32)
            BCt = S.pa("BCt", [128, 16, 8], F32)
            BLB = S.pa("BLB", [128, 16, 8], F32)
            Gt = S.pa("Gt", [128, 16, 8], F32)
            At = S.pa("At", [128, 16, 8], F32)
            EBt = S.pa("EBt", [128, 16, 8], F32)
            EBL = S.pa("EBL", [128, 16, 8], F32)
            EMF = S.pa("EMF", [128, 8], F32)
            MX = S.pa("MX", [128, 1], F32)
            MXR = S.pa("MXR", [1, 128], F32)
            MF = S.pa("MF", [1, 8], F32)
            for c in range(NCH):
                S.mm([(PS[3][:, c * 16:(c + 1) * 16], hnT[c][:, k, :], Wif[:, k, :], k == 0, k == 7) for k in range(8)], [hnT[c], Wif], [PS[3]])
            S.tt("dve", IFt[:, :, :], PS[3][:, 0:256].rearrange("p (c j) -> p c j", c=16), IFB[:, :].unsqueeze(1).to_broadcast([128, 16, 16]), ALU.add, [PS[3], IFB], [IFt])
            S.act(LF[:, :, :], IFt[:, :, 8:16], AF.Exp, [IFt], [LF], scale=-1.0)
            S.act(LF[:, :, :], LF[:, :, :], AF.Ln, [LF], [LF], bias=1.0)
            S.ts("dve", LF[:, :, :], LF[:, :, :], -1.0, None, ALU.mult, None, [LF], [LF])
            lfl = LF[:, :, :].rearrange("p c h -> p (c h)")
            S.mm([(PS[3][:, 256 + c * 8:256 + (c + 1) * 8], TRI[:, :], LF[:, c, :], True, True) for c in range(NCH)] +
                 [(PS[3][:, 384:512], ONES[:, :], lfl, True, True)], [TRI, ONES, LF], [PS[3]])
            S.cp("dve", BCt[:, :, :].rearrange("p c h -> p (c h)"), PS[3][:, 256:384], [PS[3]], [BCt])
            S.cp("dve", BLB[:, :, :].rearrange("p c h -> p (c h)"), PS[3][:, 384:512], [PS[3]], [BLB])
            S.tt("dve", Gt[:, :, :], IFt[:, :, 0:8], BCt[:, :, :], ALU.subtract, [IFt, BCt], [Gt])
            S.act(At[:, :, :], Gt[:, :, :], AF.Exp, [Gt], [At], bias=float(math.log(1.0 / 16.0)))
            S.act(EBt[:, :, :], BCt[:, :, :], AF.Exp, [BCt], [EBt])
            S.act(EBL[:, :, :], BLB[:, :, :], AF.Exp, [BLB], [EBL])
            S.tr([(PS[4][:, 0:128], Gt[:, :, :].rearrange("p c h -> p (c h)"), IDf[:, :])], [Gt, IDf], [PS[4]])
            S.op("dve", lambda h: h.tensor_reduce(out=MX[:, :], in_=PS[4][:, 0:128], axis=AX.X, op=ALU.max), [PS[4]], [MX])
            S.tr([(PS[4][0:1, 128:256], MX[:, 0:1], IDf[:, :])], [MX, IDf], [PS[4]])
            S.cp("dve", MXR[:, :], PS[4][0:1, 128:256], [PS[4]], [MXR])
            S.memset("dve", MF[:, :], -1.0e30, [MF])
            for c in range(NCH):
                S.tt("dve", MF[:, :], MF[:, :], MXR[:, c * 8:(c + 1) * 8], ALU.max, [MF, MXR], [MF])
                S.tt("dve", MF[:, :], MF[:, :], BLB[0:1, c, :], ALU.add, [MF, BLB], [MF])
            S.dma(D["p_mm"], MF[:, :], rbuf=MF)
            S.mm([(PS[4][:, 256:264], ONES[0:1, :], MF[:, :], True, True)], [ONES, MF], [PS[4]])
            S.cp("dve", EMF[:, :], PS[4][:, 256:264], [PS[4]], [EMF])
            S.act(EMF[:, :], EMF[:, :], AF.Exp, [EMF], [EMF], scale=-1.0)

            if phases["sample"] and phases.get("s_ml", True):
                IFs = S.pa("IFs", [NS, 16], F32)
                LFs = S.pa("LFs", [NS, 8], F32)
                MM0 = S.pa("MM0", [NS, 8], F32)
                INTs = S.pa("INTs", [NS, 8], F32)
                MNs = S.pa("MNs", [NS, 8], F32)
                WINs = S.pa("WINs", [NS, 8], F32)
                WOUs = S.pa("WOUs", [NS, 8], F32)
                EMNs = S.pa("EMNs", [NS, 8], F32)
                BDs = S.pa("BDs", [NS, NS, 8], F32)
                WSB = S.pa("WSB", [128, NS, 8], F32)
                SCB = S.pa("SCB", [128, NS, 8], F32)
                OHr = S.pa("OHr", [NS, NS, NS], F32)
                OH = S.pa("OH", [128, NS, NS], F32)
                S.dma(MM0[:, :], D["mmm"], wbuf=MM0)
                S.mm([(PS[4][0:NS, 300:316], hnTs[:, k, :], Wif[:, k, :], k == 0, k == 7) for k in range(8)], [hnTs, Wif], [PS[4]])
                S.tt("dve", IFs[:, :], PS[4][0:NS, 300:316], IFB[0:NS, :], ALU.add, [PS[4], IFB], [IFs])
                S.act(LFs[:, :], IFs[:, 8:16], AF.Exp, [IFs], [LFs], scale=-1.0)
                S.act(LFs[:, :], LFs[:, :], AF.Ln, [LFs], [LFs], bias=1.0)
                S.ts("dve", LFs[:, :], LFs[:, :], -1.0, None, ALU.mult, None, [LFs], [LFs])
                S.tt("dve", INTs[:, :], LFs[:, :], MM0[:, :], ALU.add, [LFs, MM0], [INTs])
                S.tt("dve", MNs[:, :], INTs[:, :], IFs[:, 0:8], ALU.max, [INTs, IFs], [MNs])
                S.dma(D["s_mm"], MNs[:, :], rbuf=MNs)
                S.tt("dve", WINs[:, :], IFs[:, 0:8], MNs[:, :], ALU.subtract, [IFs, MNs], [WINs])
                S.act(WINs[:, :], WINs[:, :], AF.Exp, [WINs], [WINs])
                S.tt("dve", WOUs[:, :], INTs[:, :], MNs[:, :], ALU.subtract, [INTs, MNs], [WOUs])
                S.act(WOUs[:, :], WOUs[:, :], AF.Exp, [WOUs], [WOUs])
                S.act(EMNs[:, :], MNs[:, :], AF.Exp, [MNs], [EMNs], scale=-1.0)
                idb = IDf[0:NS, 0:NS]
                for src_, dst_ in ((WINs, WSB), (WOUs, SCB)):
                    S.tt("dve", BDs[:, :, :], src_[:, :].unsqueeze(1).to_broadcast([NS, NS, 8]), idb.unsqueeze(2).to_broadcast([NS, NS, 8]), ALU.mult, [src_, IDf], [BDs])
                    S.mm([(PS[4][:, 0:128], ONES[0:NS, :], BDs[:, :, :].rearrange("p a b -> p (a b)"), True, True)], [ONES, BDs], [PS[4]])
                    S.cp("dve", dst_[:, :, :].rearrange("p a b -> p (a b)"), PS[4][:, 0:128], [PS[4]], [dst_])
                S.tt("dve", OHr[:, :, :], idb.unsqueeze(2).to_broadcast([NS, NS, NS]), idb.unsqueeze(1).to_broadcast([NS, NS, NS]), ALU.mult, [IDf], [OHr])
                S.mm([(PS[4][:, 0:256], ONES[0:NS, :], OHr[:, :, :].rearrange("p a b -> p (a b)"), True, True)], [ONES, OHr], [PS[4]])
                S.cp("dve", OH[:, :, :].rearrange("p a b -> p (a b)"), PS[4][:, 0:256], [PS[4]], [OH])

            p2b = PS[2][:, 256:512].bitcast(BF16)
            R2z, R2k, R2h = Buf('R2z', None), Buf('R2k', None), Buf('R2h', None)
            R3s, R3kv = Buf('R3s', None), Buf('R3kv', None)
            for h_ in range(8):
                if h_ + 1 < 8:
                    load_head_w(h_ + 1)
                wqk, wvoz, woo, mnwh = Wqk[h_ % 2], Wvoz[h_ % 2], WOo[h_ % 2], MNWh[h_ % 2]
                mark_w = S.pa_mark()
                PREm_r = S.ring("PREm", 2, [128, 4, 131], F32)
                CARm = S.pa("CARm", [128, 4, 3], F32)
                CVm_r = S.ring("CVm", 2, [128, 4, 128], F32)
                QKc = S.ring("QKc", 2, [128, 4, 128], BF16)
                VAm = S.ring("VAm", 2, [128, 258], BF16)
                SIGO_r = S.ring("SIGO", 2, [128, 256], F32)
                SZm_r = S.ring("SZm", 2, [128, 256], F32)
                ATTm = S.ring("ATTm", 2, [128, 128], BF16)
                KTOK = S.ring("KTOK", 2, [128, 256], BF16)
                Hm_r = S.ring("Hm", 2, [128, 256], F32)
                GZ_r = S.ring("GZ", 2, [128, 256], F32)
                HG_r = S.ring("HG", 2, [128, 256], BF16)
                HGT_r = S.ring("HGT", 2, [128, 2, 128], BF16)
                C32 = S.pa("C32", [128, 2, 257], F32)
                C16 = S.pa("C16", [128, 2, 257], BF16)
                CO = S.pa("CO", [128, 2, 257], F32)
                DQ_r = S.ring("DQ", 2, [128, 1], F32)
                RQ_r = S.ring("RQ", 2, [128, 1], F32)
                BNS_r = S.ring("BNS", 2, [128, 6], F32)
                MV_r = S.ring("MV", 2, [128, 2], F32)
                RSD_r = S.ring("RSD", 2, [128, 1], F32)
                S.memset("pool", C32[:, :, :], 0.0, [C32])
                S.memset("pool", C16[:, :, :], 0.0, [C16])
                S.memset("pool", CARm[:, :, :], 0.0, [CARm])
                cblk = [2 * h_, 2 * h_ + 1, 16 + 2 * h_, 16 + 2 * h_ + 1]
                def early(c):
                    hb = hnT[c]
                    qkc, va, att, ktok = QKc[c % 2], VAm[c % 2], ATTm[c % 2], KTOK[c % 2]
                    PREm = PREm_r[c % 2]
                    CVm = CVm_r[c % 2]
                    SIGO = SIGO_r[c % 2]
                    SZm = SZm_r[c % 2]
                    Hm = Hm_r[c % 2]
                    GZ = GZ_r[c % 2]
                    HG = HG_r[c % 2]
                    HGT = HGT_r[c % 2]
                    DQ = DQ_r[c % 2]
                    RQ = RQ_r[c % 2]
                    BNS = BNS_r[c % 2]
                    MV = MV_r[c % 2]
                    RSD = RSD_r[c % 2]
                    lst = []
                    for bq in range(4):
                        for k in range(8):
                            lst.append((PS[0][:, bq * 128:(bq + 1) * 128], wqk[:, k, bq * 128:(bq + 1) * 128], hb[:, k, :], k == 0, k == 7))
                    S.mm(lst, [wqk, hb], [PS[0]])
                    S.cp("pool", PREm[:, :, 0:3], CARm[:, :, :], [CARm], [PREm])
                    S.cp("act", PREm[:, :, 3:131], PS[0][:, :].rearrange("p (a b) -> p a b", a=4), [PS[0]], [PREm])
                    S.cp("pool", CARm[:, :, :], PREm[:, :, 128:131], [PREm], [CARm])
                    if c == NCH - 1:
                        for bq in range(4):
                            col0 = cblk[bq] * 128
                            for j_ in range(3):
                                S.dma(D["p_mconv"][j_, col0:col0 + 128].rearrange("(p o) -> p o", o=1), PREm[:, bq, 128 + j_:129 + j_], rbuf=PREm)
                    for bq in range(4):
                        S.act(CVm[:, bq, :], PREm[:, bq, 0:128], AF.Identity, [PREm, MCW, MCB], [CVm], bias=MCB[:, cblk[bq]:cblk[bq] + 1], scale=MCW[:, cblk[bq], 0:1])
                    for bq in range(4):
                        for j_ in range(1, 4):
                            S.stt(CVm[:, bq, :], PREm[:, bq, j_:j_ + 128], MCW[:, cblk[bq], j_:j_ + 1], CVm[:, bq, :], ALU.mult, ALU.add, [PREm, MCW, CVm], [CVm])
                    S.act(qkc[:, :, :], CVm[:, :, :], AF.Silu, [CVm], [qkc])
                    lst = []
                    for k in range(8):
                        lst.append((PS[1][:, :], hb[:, k, :], wvoz[:, k, 0:512], k == 0, k == 7))
                        lst.append((PS[2][:, 0:256], hb[:, k, :], wvoz[:, k, 512:768], k == 0, k == 7))
                    S.mm(lst, [hb, wvoz], [PS[1], PS[2]])
                    S.act(va[:, 0:256], PS[1][:, 0:256], AF.Copy, [PS[1], At], [va], scale=At[:, c, h_:h_ + 1])
                    S.cp("pool", va[:, 256:257], At[:, c, h_:h_ + 1], [At], [va])
                    S.act(SIGO[:, :], PS[1][:, 256:512], AF.Sigmoid, [PS[1]], [SIGO])
                    S.act(SZm[:, :], PS[2][:, 0:256], AF.Silu, [PS[2]], [SZm])
                    S.mm([(PS[3][:, 0:128], qkc[:, 2 + db, :], qkc[:, db, :], db == 0, db == 1) for db in range(2)], [qkc], [PS[3]])
                    S.tt("dve", att[:, :], PS[3][:, 0:128], TRI[:, :], ALU.mult, [PS[3], TRI], [att])
                    kv_ = p2b[:, 0:256].rearrange("p (a t) -> p a t", a=2)
                    S.tr([(kv_[:, db, :], qkc[:, 2 + db, :], IDb[:, :]) for db in range(2)], [qkc, IDb], [PS[2]])
                    S.cp("act", ktok[:, :], p2b[:, 0:256], [PS[2]], [ktok])
                def late(c):
                    qkc, va, att, ktok = QKc[c % 2], VAm[c % 2], ATTm[c % 2], KTOK[c % 2]
                    PREm = PREm_r[c % 2]
                    CVm = CVm_r[c % 2]
                    SIGO = SIGO_r[c % 2]
                    SZm = SZm_r[c % 2]
                    Hm = Hm_r[c % 2]
                    GZ = GZ_r[c % 2]
                    HG = HG_r[c % 2]
                    HGT = HGT_r[c % 2]
                    DQ = DQ_r[c % 2]
                    RQ = RQ_r[c % 2]
                    BNS = BNS_r[c % 2]
                    MV = MV_r[c % 2]
                    RSD = RSD_r[c % 2]
                    S.mm([(PS[5][:, 0:257], att[:, :], va[:, 0:257], True, False)] +
                         [(PS[5][:, 0:257], qkc[:, db, :], C16[:, db, :], False, db == 1) for db in range(2)], [att, va, qkc, C16], [PS[5]])
                    S.ts("dve", DQ[:, :], PS[5][:, 256:257], EBt[:, c, h_:h_ + 1], None, ALU.mult, None, [PS[5], EBt], [DQ])
                    S.stt(RQ[:, :], DQ[:, :], -1.0, DQ[:, :], ALU.mult, ALU.max, [DQ], [RQ])
                    S.ts("dve", DQ[:, :], RQ[:, :], 1.0, None, ALU.max, None, [RQ], [DQ])
                    S.op("dve", lambda h, RQ=RQ, DQ=DQ: h.reciprocal(out=RQ[:, :], in_=DQ[:, :]), [DQ], [RQ])
                    S.tt("dve", RQ[:, :], RQ[:, :], EBt[:, c, h_:h_ + 1], ALU.mult, [RQ, EBt], [RQ])
                    S.stt(Hm[:, :], PS[5][:, 0:256], RQ[:, :], SIGO[:, :], ALU.mult, ALU.mult, [PS[5], RQ, SIGO], [Hm])
                    S.op("dve", lambda h, BNS=BNS, Hm=Hm: h.bn_stats(out=BNS[:, :], in_=Hm[:, :]), [Hm], [BNS])
                    S.op("dve", lambda h, MV=MV, BNS=BNS: h.bn_aggr(out=MV[:, :], in_=BNS[:, :]), [BNS], [MV])
                    S.ts("dve", RSD[:, :], MV[:, 1:2], EPS, None, ALU.add, None, [MV], [RSD])
                    S.act(RSD[:, :], RSD[:, :], AF.Sqrt, [RSD], [RSD])
                    S.op("dve", lambda h, RSD=RSD: h.reciprocal(out=RSD[:, :], in_=RSD[:, :]), [RSD], [RSD])
                    S.ts("dve", Hm[:, :], Hm[:, :], MV[:, 0:1], RSD[:, :], ALU.subtract, ALU.mult, [Hm, MV, RSD], [Hm])
                    S.tt("pool", GZ[:, :], SZm[:, :], mnwh[:, :], ALU.mult, [SZm, mnwh], [GZ])
                    S.tt("pool", HG[:, :], Hm[:, :], GZ[:, :], ALU.mult, [Hm, GZ], [HG])
                    hv_ = PS[4][:, 0:128].bitcast(BF16).rearrange("p (a t) -> p a t", a=2)
                    S.tr([(hv_[:, db, :], HG[:, db * 128:(db + 1) * 128], IDb[:, :]) for db in range(2)], [HG, IDb], [PS[4]])
                    S.cp("act", HGT[:, :, :], hv_, [PS[4]], [HGT])
                    for hf in range(2):
                        S.mm([(PS[6 + hf][:, :], HGT[:, db, :], woo[:, db, hf * 512:(hf + 1) * 512], db == 0, db == 1) for db in range(2)], [HGT, woo], [PS[6 + hf]])
                        S.tt("dve", X[c][:, hf * 512:(hf + 1) * 512], X[c][:, hf * 512:(hf + 1) * 512], PS[6 + hf][:, :], ALU.add, [X[c], PS[6 + hf]], [X[c]])
                    S.mm([(PS[4][:, 128:385], ktok[:, 0:128], va[:, 0:257], True, True), (PS[5][:, 0:257], ktok[:, 128:256], va[:, 0:257], True, True)], [ktok, va], [PS[4], PS[5]])
                    S.ts("pool", C32[:, :, :], C32[:, :, :], EBL[:, c, h_:h_ + 1], None, ALU.mult, None, [C32, EBL], [C32])
                    S.stt(C32[:, 0, :], PS[4][:, 128:385], EBL[:, c, h_:h_ + 1], C32[:, 0, :], ALU.mult, ALU.add, [PS[4], EBL, C32], [C32])
                    S.stt(C32[:, 1, :], PS[5][:, 0:257], EBL[:, c, h_:h_ + 1], C32[:, 1, :], ALU.mult, ALU.add, [PS[5], EBL, C32], [C32])
                    S.cp("pool", C16[:, :, :], C32[:, :, :], [C32], [C16])
                early(0)
                for c in range(NCH):
                    la = S.captured(late, c)
                    ea = S.captured(early, c + 1) if c + 1 < NCH else []
                    S.replay_interleaved(ea, la)
                S.ts("dve", CO[:, :, :], C32[:, :, :], EMF[:, h_:h_ + 1], None, ALU.mult, None, [C32, EMF], [CO])
                S.dma(D["p_mC"][h_].rearrange("(a p) e -> p a e", p=128), CO[:, :, 0:256], rbuf=CO)
                for db in range(2):
                    S.dma(D["p_mn"][h_, db * 128:(db + 1) * 128].rearrange("(p o) -> p o", o=1), CO[:, db, 256:257], rbuf=CO)
                S.pa_release(mark_w)
                if phases["sample"] and phases.get("s_ml", True):
                    PREs = S.pa("PREs", [NS, 512], F32)
                    QKs = S.pa("QKs", [NS, 512], F32)
                    VSs = S.pa("VSs", [NS, 256], F32)
                    SIGs = S.pa("SIGs", [NS, 256], F32)
                    SZs2 = S.pa("SZs2", [NS, 256], F32)
                    CWm = S.pa("CWm", [NS, 4, 256], F32)
                    SCVm = S.pa("SCVm", [NS, 3, 256], F32)
                    CBm = S.pa("CBm", [NS, 256], F32)
                    NSin = S.pa("NSin", [NS, 256], F32)
                    NOUT = S.pa("NOUT", [NS, 256], F32)
                    QKT = S.pa("QKTs", [128, 4, NS], F32)
                    NT = S.pa("NT", [128, 2, NS], F32)
                    WK = S.pa("WK", [128, 2, NS], F32)
                    NN = S.pa("NN", [128, 2, NS], F32)
                    QN = S.pa("QN", [128, 2, NS], F32)
                    QZ = S.ring("QZ", 2, [128, 2, NS], F32)
                    Cb_ = S.ring("Cb_", 2, [128, 2, 256], F32)
                    ADs = S.pa("ADs", [NS, 1], F32)
                    RDs = S.pa("RDs", [NS, 1], F32)
                    Hs_ = S.pa("Hs_", [NS, 256], F32)
                    GZs = S.pa("GZs", [NS, 256], F32)
                    HGs = S.pa("HGs", [NS, 256], BF16)
                    HGTs = S.pa("HGTs", [128, 2, NS], BF16)
                    BNs = S.pa("BNs", [NS, 6], F32)
                    MVs = S.pa("MVs", [NS, 2], F32)
                    RSs = S.pa("RSs", [NS, 1], F32)
                    lst = []
                    for k in range(8):
                        lst.append((PS[0][0:NS, :], hnTs[:, k, :], wqk[:, k, :], k == 0, k == 7))
                        lst.append((PS[1][0:NS, :], hnTs[:, k, :], wvoz[:, k, 0:512], k == 0, k == 7))
                        lst.append((PS[2][0:NS, 0:256], hnTs[:, k, :], wvoz[:, k, 512:768], k == 0, k == 7))
                    S.mm(lst, [hnTs, wqk, wvoz], [PS[0], PS[1], PS[2]])
                    S.cp("act", PREs[:, :], PS[0][0:NS, :], [PS[0]], [PREs])
                    S.cp("act", VSs[:, :], PS[1][0:NS, 0:256], [PS[1]], [VSs])
                    S.act(SIGs[:, :], PS[1][0:NS, 256:512], AF.Sigmoid, [PS[1]], [SIGs])
                    S.act(SZs2[:, :], PS[2][0:NS, 0:256], AF.Silu, [PS[2]], [SZs2])
                    for hq in range(2):
                        col0 = hq * 2048 + h_ * 256
                        cs = slice(col0, col0 + 256)
                        ls = slice(hq * 256, (hq + 1) * 256)
                        S.dma(CWm[:, :, :], D["mcw_s"][:, :, cs], wbuf=CWm)
                        S.dma(SCVm[:, :, :], D["mconv"][:, :, cs], wbuf=SCVm)
                        S.dma(CBm[:, :], D["mcb_s"][:, cs], wbuf=CBm)
                        S.dma(D["s_mconv"][:, 0:2, cs], SCVm[:, 1:3, :], rbuf=SCVm)
                        S.dma(D["s_mconv"][:, 2, cs], PREs[:, ls], rbuf=PREs)
                        S.tt("pool", SCVm[:, :, :], SCVm[:, :, :], CWm[:, 0:3, :], ALU.mult, [SCVm, CWm], [SCVm])
                        S.tt("dve", QKs[:, ls], PREs[:, ls], CWm[:, 3, :], ALU.mult, [PREs, CWm], [QKs])
                        for t_ in range(3):
                            S.tt("dve", QKs[:, ls], QKs[:, ls], SCVm[:, t_, :], ALU.add, [QKs, SCVm], [QKs])
                        S.tt("dve", QKs[:, ls], QKs[:, ls], CBm[:, :], ALU.add, [QKs, CBm], [QKs])
                    S.act(QKs[:, :], QKs[:, :], AF.Silu, [QKs], [QKs])
                    S.ts("dve", QKs[:, 256:512], QKs[:, 256:512], 0.0625, None, ALU.mult, None, [QKs], [QKs])
                    if phases.get("s_ml_stage", 9) < 2:
                        S.pa_release(mark_w)
                        continue
                    S.dma(NSin[:, :], D["mn"][:, h_, :], wbuf=NSin)
                    p3q = PS[3][:, 0:4 * NS].rearrange("p (a t) -> p a t", a=4)
                    p3n = PS[3][:, 64:64 + 2 * NS].rearrange("p (a t) -> p a t", a=2)
                    S.tr([(p3q[:, a, :], QKs[:, a * 128:(a + 1) * 128], IDf[0:NS, 0:NS]) for a in range(4)] +
                         [(p3n[:, a, :], NSin[:, a * 128:(a + 1) * 128], IDf[0:NS, 0:NS]) for a in range(2)], [QKs, NSin, IDf], [PS[3]])
                    S.cp("dve", QKT[:, :, :], p3q, [PS[3]], [QKT])
                    S.cp("dve", NT[:, :, :], p3n, [PS[3]], [NT])
                    S.tt("dve", WK[:, :, :], QKT[:, 2:4, :], WSB[:, :, h_].unsqueeze(1).to_broadcast([128, 2, NS]), ALU.mult, [QKT, WSB], [WK])
                    S.tt("dve", NN[:, :, :], NT[:, :, :], SCB[:, :, h_].unsqueeze(1).to_broadcast([128, 2, NS]), ALU.mult, [NT, SCB], [NN])
                    S.tt("dve", NN[:, :, :], NN[:, :, :], WK[:, :, :], ALU.add, [NN, WK], [NN])
                    S.tt("dve", QN[:, :, :], QKT[:, 0:2, :], NN[:, :, :], ALU.mult, [QKT, NN], [QN])
                    S.tr([(PS[3][0:NS, 128 + a * 128:256 + a * 128], NN[:, a, :], IDf[:, :]) for a in range(2)], [NN, IDf], [PS[3]])
                    S.mm([(PS[3][0:NS, 400:402], QN[:, db, :], ONES[:, 0:2], db == 0, db == 1) for db in range(2)], [QN, ONES], [PS[3]])
                    S.cp("dve", NOUT[:, :], PS[3][0:NS, 128:384], [PS[3]], [NOUT])
                    S.dma(D["s_mn"][:, h_, :], NOUT[:, :], rbuf=NOUT)
                    if phases.get("s_ml_stage", 9) < 3:
                        S.pa_release(mark_w)
                        continue
                    for b in range(NS):
                        cb_ = Cb_[b % 2]
                        qz = QZ[b % 2]
                        pvb = PS[4 + (b % 2)]
                        S.dma(cb_[:, :, :], D["mC"][b, h_].rearrange("(a p) e -> p a e", p=128), wbuf=cb_)
                        S.mm([(pvb[:, 0:256], IDf[0:NS, b:b + 1].to_broadcast([NS, 128]), VSs[:, :], True, True)], [IDf, VSs], [pvb])
                        S.act(cb_[:, :, :], cb_[:, :, :], AF.Copy, [cb_, SCB], [cb_], scale=SCB[:, b, h_:h_ + 1])
                        for db in range(2):
                            S.stt(cb_[:, db, :], pvb[:, 0:256], WK[:, db, b:b + 1], cb_[:, db, :], ALU.mult, ALU.add, [pvb, WK, cb_], [cb_])
                        S.dma(D["s_mC"][b, h_].rearrange("(a p) e -> p a e", p=128), cb_[:, :, :], rbuf=cb_)
                        S.tt("dve", qz[:, :, :], QKT[:, 0:2, :], OH[:, b, :].unsqueeze(1).to_broadcast([128, 2, NS]), ALU.mult, [QKT, OH], [qz])
                        S.mm([(PS[6][0:NS, 0:256], qz[:, db, :], cb_[:, db, :], (b == 0 and db == 0), (b == NS - 1 and db == 1)) for db in range(2)], [qz, cb_], [PS[6]])
                    if phases.get("s_ml_stage", 9) < 4:
                        S.pa_release(mark_w)
                        continue
                    S.cp("dve", ADs[:, :], PS[3][0:NS, 400:401], [PS[3]], [ADs])
                    S.stt(RDs[:, :], ADs[:, :], -1.0, ADs[:, :], ALU.mult, ALU.max, [ADs], [RDs])
                    S.tt("dve", RDs[:, :], RDs[:, :], EMNs[:, h_:h_ + 1], ALU.max, [RDs, EMNs], [RDs])
                    S.op("dve", lambda h, RDs=RDs: h.reciprocal(out=RDs[:, :], in_=RDs[:, :]), [RDs], [RDs])
                    S.stt(Hs_[:, :], PS[6][0:NS, 0:256], RDs[:, :], SIGs[:, :], ALU.mult, ALU.mult, [PS[6], RDs, SIGs], [Hs_])
                    if phases.get("s_ml_stage", 9) < 5:
                        S.pa_release(mark_w)
                        continue
                    S.op("dve", lambda h, BNs=BNs, Hs_=Hs_: h.bn_stats(out=BNs[:, :], in_=Hs_[:, :]), [Hs_], [BNs])
                    S.op("dve", lambda h, BNs=BNs, MVs=MVs: h.bn_aggr(out=MVs[:, :], in_=BNs[:, :]), [BNs], [MVs])
                    S.ts("dve", RSs[:, :], MVs[:, 1:2], EPS, None, ALU.add, None, [MVs], [RSs])
                    S.act(RSs[:, :], RSs[:, :], AF.Sqrt, [RSs], [RSs])
                    S.op("dve", lambda h, RSs=RSs: h.reciprocal(out=RSs[:, :], in_=RSs[:, :]), [RSs], [RSs])
                    S.ts("dve", Hs_[:, :], Hs_[:, :], MVs[:, 0:1], RSs[:, :], ALU.subtract, ALU.mult, [Hs_, MVs, RSs], [Hs_])
                    S.tt("pool", GZs[:, :], SZs2[:, :], mnwh[0:NS, :], ALU.mult, [SZs2, mnwh], [GZs])
                    S.tt("pool", HGs[:, :], Hs_[:, :], GZs[:, :], ALU.mult, [Hs_, GZs], [HGs])
                    if phases.get("s_ml_stage", 9) < 6:
                        S.pa_release(mark_w)
                        continue
                    hvs = p2b[:, 0:2 * NS].rearrange("p (a t) -> p a t", a=2)
                    S.tr([(hvs[:, db, :], HGs[:, db * 128:(db + 1) * 128], IDb[0:NS, 0:NS]) for db in range(2)], [HGs, IDb], [PS[2]])
                    S.cp("act", HGTs[:, :, :], hvs, [PS[2]], [HGTs])
                    for hf in range(2):
                        S.mm([(PS[6 + hf][0:NS, :], HGTs[:, db, :], woo[:, db, hf * 512:(hf + 1) * 512], db == 0, db == 1) for db in range(2)], [HGTs, woo], [PS[6 + hf]])
                        S.tt("dve", Xs[:, hf * 512:(hf + 1) * 512], Xs[:, hf * 512:(hf + 1) * 512], PS[6 + hf][0:NS, :], ALU.add, [Xs, PS[6 + hf]], [Xs])
                    S.pa_release(mark_w)
            S.pa_release(mark_ml)

        if phases.get("final", True):
            mark_f = S.pa_mark()
            S.dma(NW[:, :], D["normw"][2], wbuf=NW)
            junk = S.ring("fjunk", 2, [128, 1024], BF16)
            yo = S.ring("yo", 2, [128, 1024], F32)
            ss = S.ring("fss", 2, [128, 1], F32)
            rstd = S.ring("frstd", 2, [128, 1], F32)
            for c in range(NCH + 1):
                i = c % 2
                xb, np_ = (X[c], 128) if c < NCH else (Xs, NS)
                rms_rows(xb[0:np_, :], np_, ss[i], rstd[i], junk[i], xb)
                S.stt(yo[i][0:np_, :], xb[0:np_, :], rstd[i][0:np_, :], NW[0:np_, :], ALU.mult, ALU.mult, [xb, rstd[i], NW], [yo[i]])
                if c < NCH:
                    S.dma(D["y_p"][c * 128:(c + 1) * 128, :], yo[i][:, :], rbuf=yo[i])
                else:
                    S.dma(D["y_s"], yo[i][0:NS, :], rbuf=yo[i])
            S.pa_release(mark_f)
        else:
            for c in range(NCH):
                S.dma(D["y_p"][c * 128:(c + 1) * 128, :], X[c][:, :], rbuf=X[c])
            S.dma(D["y_s"], Xs[:, :], rbuf=Xs)
        S.barrier()
        S.emit()
    return nc


def _consts():
    ident = np.eye(128, dtype=np.float32)
    tri = np.triu(np.ones((128, 128), np.float32))
    k = np.arange(128)[:, None]
    q = np.arange(128)[None, :]
    blocks = []
    for Dlt in range(-3, 16):
        d = Dlt * 128 + q - k
        m = ((d >= 0) & (d <= 128)).astype(np.float32)
        m += ((d >= 0) & (d <= 512) & (d % 4 == 0)).astype(np.float32)
        m += ((d >= 0) & (d % 16 == 0)).astype(np.float32)
        blocks.append(m)
    maskmm = np.concatenate(blocks, axis=1).astype(np.float32)
    half = 8
    inv = np.power(np.float32(500000.0), -np.arange(half, dtype=np.float32) * np.float32(2.0 / 16)).astype(np.float32)
    pos = np.arange(2048, dtype=np.float32)
    ang = (pos[:, None] * inv[None, :]).astype(np.float32)
    cos = np.cos(ang).astype(np.float32)
    sin = np.sin(ang).astype(np.float32)
    cc = np.concatenate([cos, cos], axis=1).reshape(16, 128, 16).transpose(1, 0, 2)
    ss = np.concatenate([-sin, sin], axis=1).reshape(16, 128, 16).transpose(1, 0, 2)
    angs = (np.float32(2048.0) * inv).astype(np.float32)
    ccs = np.tile(np.concatenate([np.cos(angs), np.cos(angs)])[None, :], (NS, 1)).astype(np.float32)
    sss = np.tile(np.concatenate([-np.sin(angs), np.sin(angs)])[None, :], (NS, 1)).astype(np.float32)
    return dict(ident=ident, tri=tri, maskmm=maskmm, cct=np.ascontiguousarray(cc, np.float32),
                sst=np.ascontiguousarray(ss, np.float32), ccs=ccs, sss=sss)


def _rep(v, n):
    return np.ascontiguousarray(np.broadcast_to(np.asarray(v, np.float32)[None, ...], (n,) + tuple(np.shape(v))))


_NC_CACHE = {}


def kernel(x_prompt, x_sample, cache_attn_k, cache_attn_v, state_ssd_conv, state_ssd,
           state_mlstm_conv, state_mlstm_c, state_mlstm_n, state_mlstm_m,
           norm_w, final_norm_w, w_in_even, w_out_even, ssd_conv_w, ssd_conv_b,
           ssd_dt_bias, ssd_a_log, ssd_d, ssd_norm_w, w_in_odd, w_out_odd,
           mlstm_conv_w, mlstm_conv_b, mlstm_igate_b, mlstm_fgate_b, mlstm_norm_w):
    f = lambda a: np.ascontiguousarray(np.asarray(a, dtype=np.float32))
    cst = _consts()
    shared = dict(cst)
    shared["normw"] = np.stack([_rep(f(norm_w)[0], 128), _rep(f(norm_w)[1], 128), _rep(f(final_norm_w), 128)])
    shared["w_in_even"] = f(w_in_even)[0]
    shared["w_out_even"] = f(w_out_even)[0]
    shared["w_in_odd"] = f(w_in_odd)[0]
    shared["w_out_odd"] = f(w_out_odd)[0]
    cw = f(ssd_conv_w)[0]
    shared["cwT"] = np.ascontiguousarray(cw.reshape(4, 12, 128).transpose(2, 1, 0))
    shared["cbT"] = np.ascontiguousarray(f(ssd_conv_b)[0].reshape(12, 128).T)
    shared["cw_s"] = _rep(cw, NS)
    shared["cb_s"] = _rep(f(ssd_conv_b)[0], NS)
    shared["dtb"] = _rep(f(ssd_dt_bias)[0], 128)
    shared["alog"] = _rep(f(ssd_a_log)[0], 128)
    shared["dsk"] = _rep(f(ssd_d)[0], 128)
    shared["snw"] = _rep(f(ssd_norm_w)[0], 128)
    mcw = f(mlstm_conv_w)[0]
    shared["mcwT"] = np.ascontiguousarray(mcw.reshape(4, 32, 128).transpose(2, 1, 0))
    shared["mcbT"] = np.ascontiguousarray(f(mlstm_conv_b)[0].reshape(32, 128).T)
    shared["mcw_s"] = _rep(mcw, NS)
    shared["mcb_s"] = _rep(f(mlstm_conv_b)[0], NS)
    shared["igb"] = _rep(f(mlstm_igate_b)[0], 128)
    shared["fgb"] = _rep(f(mlstm_fgate_b)[0], 128)
    shared["mnw"] = _rep(f(mlstm_norm_w)[0], 128)

    xp = f(x_prompt)
    xs = f(x_sample)
    ck = np.asarray(cache_attn_k, np.float32)
    cv = np.asarray(cache_attn_v, np.float32)
    in_maps = []
    for c in range(NCORES):
        sl = slice(c * NS, (c + 1) * NS)
        m = dict(shared)
        m["xp"] = xp[c]
        m["xs"] = np.ascontiguousarray(xs[sl, 0, :])
        m["ck"] = np.ascontiguousarray(ck[0, sl].reshape(NS, 2048, 512)[:NSKV])
        m["cv"] = np.ascontiguousarray(cv[0, sl].reshape(NS, 2048, 512)[:NSKV])
        m["sconv"] = f(state_ssd_conv)[0, sl]
        m["sstate"] = np.ascontiguousarray(f(state_ssd)[0, sl].reshape(NS, 1024, 128))
        m["mconv"] = f(state_mlstm_conv)[0, sl]
        m["mC"] = f(state_mlstm_c)[0, sl]
        m["mn"] = f(state_mlstm_n)[0, sl]
        m["mmm"] = f(state_mlstm_m)[0, sl]
        in_maps.append({k: np.ascontiguousarray(v, dtype=np.float32) for k, v in m.items()})

    if "nc" not in _NC_CACHE:
        _NC_CACHE["nc"] = build_program()
    res = run_bass_kernel_spmd(_NC_CACHE["nc"], in_maps, core_ids=list(range(NCORES)))
    R = res.results

    def cat(name, shape):
        return np.stack([np.asarray(R[c][name], np.float32) for c in range(NCORES)]).reshape(shape)

    def cats(name, shape):
        return np.concatenate([np.asarray(R[c][name], np.float32) for c in range(NCORES)], axis=0).reshape(shape)

    outs = (
        cat("y_p", (8, 2048, 1024)),
        cats("y_s", (128, 1, 1024)),
        cat("p_k", (1, 8, 2048, 8, 64)), cat("p_v", (1, 8, 2048, 8, 64)),
        cat("p_sconv", (1, 8, 3, 1536)), cat("p_ssd", (1, 8, 16, 64, 128)),
        cat("p_mconv", (1, 8, 3, 4096)), cat("p_mC", (1, 8, 8, 256, 256)),
        cat("p_mn", (1, 8, 8, 256)), cat("p_mm", (1, 8, 8)),
        cats("s_k", (1, 128, 1, 8, 64)), cats("s_v", (1, 128, 1, 8, 64)),
        cats("s_sconv", (1, 128, 3, 1536)), cats("s_ssd", (1, 128, 16, 64, 128)),
        cats("s_mconv", (1, 128, 3, 4096)), cats("s_mC", (1, 128, 8, 256, 256)),
        cats("s_mn", (1, 128, 8, 256)), cats("s_mm", (1, 128, 8)),
    )
    return outs
```

```python
import math
import numpy as np
from contextlib import ExitStack
import concourse.bass as bass
import concourse.mybir as mybir
from concourse.bass_utils import run_bass_kernel_spmd

F32 = mybir.dt.float32
BF16 = mybir.dt.bfloat16
AF = mybir.ActivationFunctionType
ALU = mybir.AluOpType
AX = mybir.AxisListType

import os
NSKV = 1 if os.environ.get("DEV_SMALLKV") else 16
NCORES = 8
NCH = 16
NS = 16
EPS = 1e-6


class Buf:
    __slots__ = ("name", "t", "last_w", "readers", "dsem", "ndma")

    def __init__(self, name, t=None):
        self.name = name
        self.t = t
        self.last_w = None
        self.readers = []
        self.dsem = None
        self.ndma = 0

    def __getitem__(self, k):
        return self.t[k]


class DSem:
    __slots__ = ("sem", "n", "q")

    def __init__(self, sem, q):
        self.sem = sem
        self.n = 0
        self.q = q


class Sched:
    ENG = ("pe", "act", "dve", "pool", "sp")

    def __init__(self, nc, es, arena_words):
        self.nc = nc
        self.es = es
        self.sem = {e: es.enter_context(nc.semaphore("s_" + e)) for e in ("pe", "act", "dve", "pool")}
        self.cnt = {e: 0 for e in self.ENG}
        self.waited = {e: {} for e in self.ENG}
        self.ops = {e: [] for e in self.ENG}
        self.dma_bufs = []
        self.free_dsems = []
        self.arena = es.enter_context(nc.sbuf_tensor("arena", [128, arena_words], F32))
        self.arena_words = arena_words
        self.atop = 0
        self.nalloc = 0
        self.pa_bufs = []
        self.capture = None

    def sb(self, name, shape, dt):
        return Buf(name, self.es.enter_context(self.nc.sbuf_tensor(name, list(shape), dt)))

    def ps(self, name, shape, dt):
        return Buf(name, self.es.enter_context(self.nc.psum_tensor(name, list(shape), dt)))

    def pa(self, name, shape, dt):
        n = 1
        for s in shape[1:]:
            n *= s
        words = (n + 1) // 2 if dt == BF16 else n
        words = (words + 1) // 2 * 2
        off = self.atop
        self.atop += words
        assert self.atop <= self.arena_words, (name, self.atop, self.arena_words)
        v = self.arena[0:shape[0], off:off + words]
        if dt == BF16:
            v = v.bitcast(BF16)
        v = v[:, 0:n]
        if len(shape) == 3:
            v = v.rearrange("p (a b) -> p a b", a=shape[1])
        elif len(shape) == 4:
            v = v.rearrange("p (a b c) -> p a b c", a=shape[1], b=shape[2])
        self.nalloc += 1
        b = Buf("%s_%d" % (name, self.nalloc), v)
        self.pa_bufs.append((off, b))
        return b

    def ring(self, name, n, shape, dt):
        return [self.pa("%s%d" % (name, i), shape, dt) for i in range(n)]

    def pa_mark(self):
        return self.atop

    def pa_release(self, mark):
        self.barrier()
        keep = []
        for off, b in self.pa_bufs:
            if off >= mark:
                if b.dsem is not None:
                    self.free_dsems.append(b.dsem)
                    b.dsem = None
            else:
                keep.append((off, b))
        self.pa_bufs = keep
        self.atop = mark

    def _deps(self, eng, reads, writes, skip_sem=None):
        evs = []
        for b in reads:
            if b.last_w is not None:
                evs.append(b.last_w)
        for b in writes:
            if b.last_w is not None and not (skip_sem is not None and b.last_w[0] is skip_sem and b.last_w[2] == "dma"):
                evs.append(b.last_w)
            evs.extend(b.readers)
        w = self.waited[eng]
        best = {}
        for (s, v, src) in evs:
            if src == "pe" and eng == "pe":
                continue
            key = id(s)
            if w.get(key, 0) < v:
                w[key] = v
                best[key] = (s, v)
        return list(best.values())

    def op(self, eng, fns, reads=(), writes=()):
        if self.capture is not None:
            self.capture.append(("op", (eng, fns, reads, writes), {}))
            return
        if callable(fns):
            fns = [fns]
        deps = self._deps(eng, reads, writes)
        self.cnt[eng] += 1
        idx = self.cnt[eng]
        sem = self.sem[eng]
        self.ops[eng].append((deps, fns, (sem, 1)))
        ev = (sem, idx, eng)
        for b in writes:
            b.last_w = ev
            b.readers = []
        for b in reads:
            if b not in writes:
                b.readers.append(ev)

    def dma(self, out_ap, in_ap, rbuf=None, wbuf=None, q="sp", **kw):
        if self.capture is not None:
            kw2 = dict(kw); kw2.update(rbuf=rbuf, wbuf=wbuf, q=q)
            self.capture.append(("dma", (out_ap, in_ap), kw2))
            return
        b = wbuf if wbuf is not None else rbuf
        if b.dsem is None:
            cand = [d for d in self.free_dsems if d.q == q]
            if cand:
                b.dsem = cand[-1]
                self.free_dsems.remove(cand[-1])
            else:
                b.dsem = DSem(self.es.enter_context(self.nc.semaphore("d%d" % len(self.dma_bufs))), q)
                self.dma_bufs.append(b.dsem)
        assert b.dsem.q == q, (b.name, q)
        reads = [rbuf] if rbuf is not None else []
        writes = [wbuf] if wbuf is not None else []
        deps = self._deps(q, reads, writes, skip_sem=(b.dsem.sem if wbuf is not None and rbuf is None else None))
        b.dsem.n += 1
        ev = (b.dsem.sem, 16 * b.dsem.n, "dma")
        self.ops[q].append((deps, [lambda h: h.dma_start(out=out_ap, in_=in_ap, **kw)], (b.dsem.sem, 16)))
        if wbuf is not None:
            wbuf.last_w = ev
            wbuf.readers = []
        if rbuf is not None and rbuf is not wbuf:
            rbuf.readers.append(ev)

    def captured(self, fn, *a):
        self.capture = []
        fn(*a)
        lst = self.capture
        self.capture = None
        return lst

    def replay_interleaved(self, A, B):
        i = j = 0
        while i < len(A) or j < len(B):
            for lst, k in ((A, i), (B, j)):
                if k < len(lst):
                    kind, args, kw = lst[k]
                    if kind == "op":
                        self.op(*args)
                    else:
                        self.dma(*args, **kw)
            i += 1
            j += 1

    def barrier(self):
        for e in self.ENG:
            deps = []
            w = self.waited[e]
            for e2 in ("pe", "act", "dve", "pool"):
                s = self.sem[e2]
                v = self.cnt[e2]
                if v > 0 and w.get(id(s), 0) < v:
                    w[id(s)] = v
                    deps.append((s, v))
            for ds in self.dma_bufs:
                v = 16 * ds.n
                if w.get(id(ds.sem), 0) < v:
                    w[id(ds.sem)] = v
                    deps.append((ds.sem, v))
            if deps:
                self.ops[e].append((deps, [], None))

    def emit(self):
        with self.nc.Block() as block:
            def mk(e):
                def body(h):
                    for deps, fns, inc in self.ops[e]:
                        for s, v in deps:
                            h.wait_ge(s, v)
                        for i, f in enumerate(fns):
                            ins = f(h)
                            if i == len(fns) - 1 and inc is not None:
                                ins.then_inc(inc[0], inc[1])
                return body
            block.tensor(mk("pe"))
            block.scalar(mk("act"))
            block.vector(mk("dve"))
            block.gpsimd(mk("pool"))
            block.sync(mk("sp"))

    def tt(self, eng, out, in0, in1, op, r, w):
        self.op(eng, lambda h: h.tensor_tensor(out=out, in0=in0, in1=in1, op=op), r, w)

    def ts(self, eng, out, in0, s1, s2, op0, op1, r, w):
        if s2 is None:
            self.op(eng, lambda h: h.tensor_scalar(out=out, in0=in0, scalar1=s1, scalar2=None, op0=op0), r, w)
        else:
            self.op(eng, lambda h: h.tensor_scalar(out=out, in0=in0, scalar1=s1, scalar2=s2, op0=op0, op1=op1), r, w)

    def stt(self, out, in0, scalar, in1, op0, op1, r, w):
        self.op("dve", lambda h: h.scalar_tensor_tensor(out=out, in0=in0, scalar=scalar, in1=in1, op0=op0, op1=op1), r, w)

    def act(self, out, in_, func, r, w, bias=None, scale=None, accum=None):
        kw = {}
        if bias is not None:
            kw["bias"] = bias
        if scale is not None:
            kw["scale"] = scale
        if accum is not None:
            kw["accum_out"] = accum
        self.op("act", lambda h: h.activation(out=out, in_=in_, func=func, **kw), r, w)

    def cp(self, eng, out, in_, r, w):
        if eng == "act":
            self.op("act", lambda h: h.copy(out=out, in_=in_), r, w)
        else:
            self.op(eng, lambda h: h.tensor_copy(out=out, in_=in_), r, w)

    def mm(self, lst, r, w):
        self.op("pe", [(lambda h, a=a: h.matmul(out=a[0], lhsT=a[1], rhs=a[2], start=a[3], stop=a[4], skip_group_check=True)) for a in lst], r, w)

    def tr(self, lst, r, w):
        self.op("pe", [(lambda h, a=a: h.transpose(out=a[0], in_=a[1], identity=a[2])) for a in lst], r, w)

    def memset(self, eng, ap, val, w):
        self.op(eng, lambda h: h.memset(ap, val), (), w)


IN_SPECS = [
    ("xp", [2048, 1024]), ("xs", [NS, 1024]),
    ("ck", [NSKV, 2048, 512]), ("cv", [NSKV, 2048, 512]),
    ("sconv", [NS, 3, 1536]), ("sstate", [NS, 1024, 128]),
    ("mconv", [NS, 3, 4096]), ("mC", [NS, 8, 256, 256]), ("mn", [NS, 8, 256]), ("mmm", [NS, 8]),
    ("normw", [3, 128, 1024]),
    ("w_in_even", [1024, 4624]), ("w_out_even", [1536, 1024]),
    ("w_in_odd", [1024, 10256]), ("w_out_odd", [2048, 1024]),
    ("ident", [128, 128]), ("tri", [128, 128]), ("maskmm", [128, 19 * 128]),
    ("cct", [128, 16, 16]), ("sst", [128, 16, 16]), ("ccs", [NS, 16]), ("sss", [NS, 16]),
    ("cwT", [128, 12, 4]), ("cbT", [128, 12]), ("cw_s", [NS, 4, 1536]), ("cb_s", [NS, 1536]),
    ("dtb", [128, 16]), ("alog", [128, 16]), ("dsk", [128, 16]), ("snw", [128, 1024]),
    ("mcwT", [128, 32, 4]), ("mcbT", [128, 32]), ("mcw_s", [NS, 4, 4096]), ("mcb_s", [NS, 4096]),
    ("igb", [128, 8]), ("fgb", [128, 8]), ("mnw", [128, 2048]),
]
OUT_SPECS = [
    ("y_p", [2048, 1024]), ("y_s", [NS, 1024]),
    ("p_k", [2048, 512]), ("p_v", [2048, 512]), ("p_sconv", [3, 1536]), ("p_ssd", [1024, 128]),
    ("p_mconv", [3, 4096]), ("p_mC", [8, 256, 256]), ("p_mn", [8, 256]), ("p_mm", [1, 8]),
    ("s_k", [NS, 512]), ("s_v", [NS, 512]), ("s_sconv", [NS, 3, 1536]), ("s_ssd", [NS, 1024, 128]),
    ("s_mconv", [NS, 3, 4096]), ("s_mC", [NS, 8, 256, 256]), ("s_mn", [NS, 8, 256]), ("s_mm", [NS, 8]),
]

PHASES = dict(attn=True, ssd=True, mlstm=True, sample=True, b2=True, b4=True, b3=True)


def build_program(phases=PHASES):
    nc = bass.Bass("TRN2", target_bir_lowering=False)
    D = {}
    for name, shape in IN_SPECS:
        D[name] = nc.dram_tensor(name, list(shape), F32, kind="ExternalInput").ap()
    for name, shape in OUT_SPECS:
        D[name] = nc.dram_tensor(name, list(shape), F32, kind="ExternalOutput").ap()

    with ExitStack() as es:
        ARENA = 26000
        S = Sched(nc, es, ARENA)
        Xall = es.enter_context(nc.sbuf_tensor("Xall", [128, NCH, 1024], F32))
        X = [Buf("X%d" % c, Xall[:, c, :]) for c in range(NCH)]
        Xs = S.sb("Xs", [NS, 1024], F32)
        hnTall = es.enter_context(nc.sbuf_tensor("hnTall", [128, 8, 2048], BF16))
        hnT = [Buf("hnT%d" % c, hnTall[:, :, c * 128:(c + 1) * 128]) for c in range(NCH)]
        hnTs = S.sb("hnTs", [128, 8, NS], BF16)
        IDf = S.sb("IDf", [128, 128], F32)
        IDb = S.sb("IDb", [128, 128], BF16)
        TRI = S.sb("TRI", [128, 128], F32)
        ONES = S.sb("ONES", [128, 128], F32)
        NW = S.sb("NW", [128, 1024], F32)
        PS = [S.ps("PS%d" % i, [128, 512], F32) for i in range(8)]

        def psb(i, n=1024):
            return PS[i][:, :].bitcast(BF16)[:, 0:n]

        for c in range(NCH):
            S.dma(X[c][:, :], D["xp"][c * 128:(c + 1) * 128, :], wbuf=X[c])
        S.dma(Xs[:, :], D["xs"], wbuf=Xs)
        S.dma(IDf[:, :], D["ident"], wbuf=IDf)
        S.dma(TRI[:, :], D["tri"], wbuf=TRI)
        S.cp("dve", IDb[:, :], IDf[:, :], [IDf], [IDb])
        S.memset("pool", ONES[:, :], 1.0, [ONES])

        def rms_rows(xap, np_, ss, rstd, junk, xb, eng2="dve"):
            S.act(junk[0:np_, :], xap, AF.Square, [xb], [junk, ss], accum=ss[0:np_, :])
            S.ts("dve", rstd[0:np_, :], ss[0:np_, :], 1.0 / 1024, EPS, ALU.mult, ALU.add, [ss], [rstd])
            S.act(rstd[0:np_, :], rstd[0:np_, :], AF.Sqrt, [rstd], [rstd])
            S.op("dve", lambda h: h.reciprocal(out=rstd[0:np_, :], in_=rstd[0:np_, :]), [rstd], [rstd])

        def phase_norm(layer):
            mark = S.pa_mark()
            S.dma(NW[:, :], D["normw"][layer], wbuf=NW)
            junk = S.ring("junk", 2, [128, 1024], BF16)
            hn = S.ring("hn", 2, [128, 1024], BF16)
            ss = S.ring("ss", 2, [128, 1], F32)
            rstd = S.ring("rstd", 2, [128, 1], F32)
            for c in range(NCH + 1):
                i = c % 2
                if c < NCH:
                    xb, np_, dst = X[c], 128, hnT[c]
                    dap = hnT[c][:, :, :]
                else:
                    xb, np_, dst = Xs, NS, hnTs
                    dap = hnTs[:, :, :]
                rms_rows(xb[0:np_, :], np_, ss[i], rstd[i], junk[i], xb)
                S.stt(hn[i][0:np_, :], xb[0:np_, :], rstd[i][0:np_, :], NW[0:np_, :], ALU.mult, ALU.mult, [xb, rstd[i], NW], [hn[i]])
                pb = 6 + i
                pv = psb(pb).rearrange("p (k t) -> p k t", k=8)[:, :, 0:np_]
                S.tr([(pv[:, k, :], hn[i][0:np_, k * 128:(k + 1) * 128], IDb[0:np_, 0:np_]) for k in range(8)], [hn[i], IDb], [PS[pb]])
                S.cp("act", dap, pv, [PS[pb]], [dst])
            S.pa_release(mark)

        phase_norm(0)

        WE = D["w_in_even"]
        if phases["attn"]:
            mark_attn = S.pa_mark()
            ATGT = S.pa("ATGT", [128, 4, 2048], BF16)
            ATGTs = S.pa("ATGTs", [128, 4, NS], BF16)
            QS = S.pa("QS", [NS, 512], F32)
            KS = S.pa("KS", [NS, 512], F32)
            VS = S.pa("VS", [NS, 512], F32)
            SGS = S.pa("SGS", [NS, 512], F32)
            CCt = S.pa("CCt", [128, 16, 16], F32)
            SSt = S.pa("SSt", [128, 16, 16], F32)
            CCs = S.pa("CCs", [NS, 16], F32)
            SSs = S.pa("SSs", [NS, 16], F32)
            S.dma(CCt[:, :, :], D["cct"], wbuf=CCt)
            S.dma(SSt[:, :, :], D["sst"], wbuf=SSt)
            S.dma(CCs[:, :], D["ccs"], wbuf=CCs)
            S.dma(SSs[:, :], D["sss"], wbuf=SSs)
            mark_pairs = S.pa_mark()
            MM = S.pa("MM", [128, 19 * 128], BF16)
            for hf_ in range(2):
                S.dma(MM[:, hf_ * 1216:(hf_ + 1) * 1216], D["maskmm"][:, hf_ * 1216:(hf_ + 1) * 1216], wbuf=MM, q="pool")
            WP = S.ring("WP", 2, [128, 8, 512], BF16)
            QKT = S.pa("QKT", [128, 2, 2048], BF16)
            VA = S.pa("VA", [128, 16, 2, 65], BF16)
            SG = S.pa("SG", [128, 16, 128], BF16)
            ATG = S.pa("ATG", [128, 16, 128], BF16)
            QK = S.ring("QK", 2, [128, 256], F32)
            QSRC = S.ring("QSRC", 2, [128, 256], F32)
            TA = S.ring("TA", 2, [128, 4, 16], F32)
            TB = S.ring("TB", 2, [128, 4, 16], F32)
            VF = S.ring("VF", 2, [128, 128], F32)
            QKb = S.ring("QKb", 2, [128, 256], BF16)
            Eb = S.ring("Eb", 3, [128, 512], BF16)
            Pb = S.ring("Pb", 3, [128, 512], BF16)
            ATT = S.ring("ATT", 2, [128, 4, 64], F32)
            REC = S.ring("REC", 2, [128, 4, 1], F32)
            S.memset("pool", VA[:, :, :, 64:65], 1.0, [VA])

            def load_pair_w(p):
                wb = WP[p % 2]
                for j in range(4):
                    col = j * 512 + p * 128
                    S.dma(wb[:, :, j * 128:(j + 1) * 128], WE[:, col:col + 128].rearrange("(k p) n -> p k n", p=128), wbuf=wb, q="pool")

            def rotary(psv, np_, cc, sn, ta, tb, dst4, rd, dbuf):
                ccb = cc.unsqueeze(1).to_broadcast([np_, 4, 16])
                S.tt("dve", ta[0:np_, :, :], psv[:, :, 0:16], ccb, ALU.mult, rd, [ta])
                S.tt("dve", tb[0:np_, :, 0:8], psv[:, :, 8:16], sn[:, 0:8].unsqueeze(1).to_broadcast([np_, 4, 8]), ALU.mult, rd, [tb])
                S.tt("dve", tb[0:np_, :, 8:16], psv[:, :, 0:8], sn[:, 8:16].unsqueeze(1).to_broadcast([np_, 4, 8]), ALU.mult, rd, [tb])
                S.tt("pool" if phases.get('rotpool', True) else "dve", dst4[:, :, 0:16], ta[0:np_, :, :], tb[0:np_, :, :], ALU.add, [ta, tb], [dbuf])

            load_pair_w(0)
            for p in range(phases.get('npair', 4)):
                if p + 1 < 4:
                    load_pair_w(p + 1)
                wb = WP[p % 2]
                for c in range(NCH + 1):
                    if c >= phases.get('nchunk', 99) and c < NCH:
                        continue
                    if c == NCH and not phases.get('smpc', True):
                        continue
                    i = c % 2
                    smp = (c == NCH)
                    np_ = NS if smp else 128
                    hb = hnTs if smp else hnT[c]
                    pu = PS[i]
                    S.mm([(pu[0:np_, :], hb[:, k, :], wb[:, k, :], k == 0, k == 7) for k in range(8)], [hb, wb], [pu])
                    qk = QK[i]
                    S.cp("act", qk[0:np_, :], pu[0:np_, 0:256], [pu], [qk])
                    psv = pu[0:np_, 0:256].rearrange("p (a b) -> p a b", a=4)
                    qk4 = qk[0:np_, :].rearrange("p (a b) -> p a b", a=4)
                    rsrc = pu
                    if phases.get('rotsb', True):
                        qsrc = QSRC[i]
                        S.cp("act", qsrc[0:np_, :], pu[0:np_, 0:256], [pu], [qsrc])
                        psv = qsrc[0:np_, :].rearrange("p (a b) -> p a b", a=4)
                        rsrc = qsrc
                    if not phases.get('rot', True):
                        pass
                    elif smp:
                        rotary(psv, np_, CCs[:, :], SSs[:, :], TA[i], TB[i], qk4, [rsrc, CCs, SSs], qk)
                    else:
                        rotary(psv, np_, CCt[:, c, :], SSt[:, c, :], TA[i], TB[i], qk4, [rsrc, CCt, SSt], qk)
                    vf = VF[i]
                    S.cp("act", vf[0:np_, :], pu[0:np_, 256:384], [pu], [vf])
                    if smp and not phases.get('smp', True):
                        pass
                    elif smp:
                        S.dma(D["s_k"][:, p * 128:(p + 1) * 128], qk[0:NS, 128:256], rbuf=qk)
                        S.dma(D["s_v"][:, p * 128:(p + 1) * 128], vf[0:NS, :], rbuf=vf)
                        S.cp("pool", QS[:, p * 128:(p + 1) * 128], qk[0:NS, 0:128], [qk], [QS])
                        S.cp("pool", KS[:, p * 128:(p + 1) * 128], qk[0:NS, 128:256], [qk], [KS])
                        S.cp("pool", VS[:, p * 128:(p + 1) * 128], vf[0:NS, :], [vf], [VS])
                        S.act(SGS[:, p * 128:(p + 1) * 128], pu[0:NS, 384:512], AF.Silu, [pu], [SGS])
                    else:
                        S.dma(D["p_k"][c * 128:(c + 1) * 128, p * 128:(p + 1) * 128], qk[:, 128:256], rbuf=qk)
                        S.dma(D["p_v"][c * 128:(c + 1) * 128, p * 128:(p + 1) * 128], vf[:, :], rbuf=vf)
                        S.cp("pool", VA[:, c, :, 0:64], vf[:, :].rearrange("p (a b) -> p a b", a=2), [vf], [VA])
                        S.act(SG[:, c, :], pu[:, 384:512], AF.Silu, [pu], [SG])
                        if not phases.get('trq', True):
                            continue
                        qb = QKb[i]
                        S.cp("pool", qb[:, :], qk[:, :], [qk], [qb])
                        pt = 2 + i
                        ptv = psb(pt, 256).rearrange("p (a t) -> p a t", a=2)
                        S.tr([(ptv[:, a, :], qb[:, a * 128:(a + 1) * 128], IDb[:, :]) for a in range(2)], [qb, IDb], [PS[pt]])
                        S.cp("act", QKT[:, :, c * 128:(c + 1) * 128], ptv, [PS[pt]], [QKT])
                it = 0
                for hh in range(2 if phases.get('b2', True) else 0):
                    hs = slice(hh * 64, (hh + 1) * 64)
                    for g in range(4):
                        po = PS[6 + (g % 2)]
                        pov = po[:, 0:260].rearrange("p (j e) -> p j e", j=4)
                        nk = 4 * g + 4
                        for kc in range(nk):
                            j0 = max(0, kc - 4 * g)
                            cs = slice(j0 * 128, 512)
                            ps_ = PS[4 + (it % 2)]
                            eb = Eb[it % 3]
                            pb_ = Pb[it % 3]
                            S.mm([(ps_[:, cs], QKT[hs, 1, kc * 128:(kc + 1) * 128], QKT[hs, 0, (4 * g + j0) * 128:(4 * g + 4) * 128], True, True)], [QKT], [ps_])
                            S.act(eb[:, cs], ps_[:, cs], AF.Exp, [ps_], [eb], scale=0.125)
                            m0 = (4 * g + j0 - kc + 3) * 128
                            S.tt("dve" if it % 2 == 0 else "pool", pb_[:, cs], eb[:, cs], MM[:, m0:m0 + (4 - j0) * 128], ALU.mult, [eb, MM], [pb_])
                            S.mm([(pov[:, j, :], pb_[:, j * 128:(j + 1) * 128], VA[:, kc, hh, :], (kc == 0 and j == 0), (kc == 4 * g + j)) for j in range(j0, 4)], [pb_, VA], [po])
                            it += 1
                        rec = REC[g % 2]
                        att = ATT[g % 2]
                        S.op("dve", lambda h, rec=rec, pov=pov: h.reciprocal(out=rec[:, :, :], in_=pov[:, :, 64:65]), [po], [rec])
                        S.tt("dve", att[:, :, :], pov[:, :, 0:64], rec[:, :, :].to_broadcast([128, 4, 64]), ALU.mult, [po, rec], [att])
                        S.tt("pool", ATG[:, 4 * g:4 * g + 4, hs], att[:, :, :], SG[:, 4 * g:4 * g + 4, hs], ALU.mult, [att, SG], [ATG])
                for cg in range(4 if phases.get('b2', True) else 0):
                    pt = 2 + (cg % 2)
                    ptv = psb(pt, 512).rearrange("p (a t) -> p a t", a=4)
                    S.tr([(ptv[:, a, :], ATG[:, 4 * cg + a, :], IDb[:, :]) for a in range(4)], [ATG, IDb], [PS[pt]])
                    S.cp("act", ATGT[:, p, cg * 512:(cg + 1) * 512], psb(pt, 512), [PS[pt]], [ATGT])
            S.pa_release(mark_pairs)

            QSb = S.pa("QSb", [NS, 512], BF16)
            S.cp("dve", QSb[:, :], QS[:, :], [QS], [QSb])
            SELb = S.pa("SELb", [NS, NS * 128], BF16)
            S.memset("pool", SELb[:, :], 0.0, [SELb])
            S.tt("pool", SELb[:, :].rearrange("k (b m) -> k b m", b=NS), IDf[0:NS, 0:NS].unsqueeze(2).to_broadcast([NS, NS, 128]),
                 ONES[0:NS, :].unsqueeze(1).to_broadcast([NS, NS, 128]), ALU.mult, [IDf, ONES, SELb], [SELb])
            Kc = S.ring("Kc", 3, [128, 512], F32)
            Vc = S.ring("Vc", 3, [128, 512], F32)
            Vb = S.ring("Vb", 3, [128, 8, 65], BF16)
            PR = S.ring("PR", 2, [128, 512], F32)
            SC = S.ring("SC", 2, [128, 8], F32)
            PZ = S.ring("PZ", 3, [128, 8, NS], BF16)
            for v_ in Vb:
                S.memset("pool", v_[:, :, 64:65], 1.0, [v_])
            pos0 = PS[6][0:NS, 0:260].rearrange("p (j e) -> p j e", j=4)
            pos1 = PS[7][0:NS, 0:260].rearrange("p (j e) -> p j e", j=4)
            it = 0
            for b in range(NS if phases.get('b4', True) else 0):
                pq = PS[b % 2]
                S.mm([(pq[:, :], SELb[:, b * 128:(b + 1) * 128], QSb[:, :], True, True)], [SELb, QSb], [pq])
                for pat, dil in enumerate((1, 4, 16)):
                    r0 = 2048 - 128 * dil
                    kc_, vc_, vb_, pr, sc, pz = Kc[it % 3], Vc[it % 3], Vb[it % 3], PR[it % 2], SC[it % 2], PZ[it % 3]
                    S.dma(kc_[:, :], D["ck"][b, r0:2048:dil, :], wbuf=kc_)
                    S.dma(vc_[:, :], D["cv"][b, r0:2048:dil, :], wbuf=vc_)
                    S.tt("dve", pr[:, :], kc_[:, :], pq[:, :], ALU.mult, [kc_, pq], [pr])
                    S.op("dve", lambda h, sc=sc, pr=pr: h.tensor_reduce(out=sc[:, :], in_=pr[:, :].rearrange("p (a b) -> p a b", a=8), axis=AX.X, op=ALU.add), [pr], [sc])
                    S.memset("pool", pz[:, :, :], 0.0, [pz])
                    S.act(pz[:, :, b], sc[:, :], AF.Exp, [sc, pz], [pz], scale=0.125)
                    S.cp("pool", vb_[:, :, 0:64], vc_[:, :].rearrange("p (a b) -> p a b", a=8), [vc_], [vb_])
                    first = (b == 0 and pat == 0)
                    last = (b == NS - 1 and pat == 2)
                    S.mm([((pos0 if h_ < 4 else pos1)[:, h_ % 4, :], pz[:, h_, :], vb_[:, h_, :], first and (h_ % 4 == 0), last) for h_ in range(8)],
                         [pz, vb_], [PS[6], PS[7]])
                    it += 1
            OS = S.pa("OS", [NS, 8, 65], F32)
            S.cp("act", OS[:, 0:4, :], pos0, [PS[6]], [OS])
            S.cp("act", OS[:, 4:8, :], pos1, [PS[7]], [OS])
            PRs = S.pa("PRs", [NS, 512], F32)
            SCs = S.pa("SCs", [NS, 8], F32)
            S.tt("dve", PRs[:, :], QS[:, :], KS[:, :], ALU.mult, [QS, KS], [PRs])
            S.op("dve", lambda h: h.tensor_reduce(out=SCs[:, :], in_=PRs[:, :].rearrange("p (a b) -> p a b", a=8), axis=AX.X, op=ALU.add), [PRs], [SCs])
            S.act(SCs[:, :], SCs[:, :], AF.Exp, [SCs], [SCs], scale=0.125, bias=None)
            S.ts("dve", SCs[:, :], SCs[:, :], 3.0, None, ALU.mult, None, [SCs], [SCs])
            S.tt("dve", PRs[:, :].rearrange("p (a b) -> p a b", a=8), VS[:, :].rearrange("p (a b) -> p a b", a=8),
                 SCs[:, :].unsqueeze(2).to_broadcast([NS, 8, 64]), ALU.mult, [VS, SCs], [PRs])
            S.tt("dve", OS[:, :, 0:64], OS[:, :, 0:64], PRs[:, :].rearrange("p (a b) -> p a b", a=8), ALU.add, [OS, PRs], [OS])
            S.tt("dve", OS[:, :, 64:65], OS[:, :, 64:65], SCs[:, :].unsqueeze(2), ALU.add, [OS, SCs], [OS])
            RS = S.pa("RS", [NS, 8, 1], F32)
            S.op("dve", lambda h: h.reciprocal(out=RS[:, :, :], in_=OS[:, :, 64:65]), [OS], [RS])
            S.tt("dve", PRs[:, :].rearrange("p (a b) -> p a b", a=8), OS[:, :, 0:64], RS[:, :, :].to_broadcast([NS, 8, 64]), ALU.mult, [OS, RS], [PRs])
            ATGs = S.pa("ATGs", [NS, 512], BF16)
            S.tt("dve", ATGs[:, :], PRs[:, :], SGS[:, :], ALU.mult, [PRs, SGS], [ATGs])
            ptv = psb(2, 4 * NS).rearrange("p (a t) -> p a t", a=4)
            S.tr([(ptv[:, a, :], ATGs[:, a * 128:(a + 1) * 128], IDb[0:NS, 0:NS]) for a in range(4)], [ATGs, IDb], [PS[2]])
            S.cp("act", ATGTs[:, :, :], ptv, [PS[2]], [ATGTs])

            WOa = S.pa("WOa", [128, 4, 1024], BF16)
            S.dma(WOa[:, :, :], D["w_out_even"][0:512, :].rearrange("(k p) n -> p k n", p=128), wbuf=WOa, q="pool")
            for c in range(NCH + 1 if phases.get('b3', True) else 0):
                smp = (c == NCH)
                np_ = NS if smp else 128
                xb = Xs if smp else X[c]
                for hf in range(2):
                    pb = PS[2 * (c % 2) + hf]
                    if smp:
                        lst = [(pb[0:NS, :], ATGTs[:, k, :], WOa[:, k, hf * 512:(hf + 1) * 512], k == 0, k == 3) for k in range(4)]
                        S.mm(lst, [ATGTs, WOa], [pb])
                    else:
                        lst = [(pb[:, :], ATGT[:, k, c * 128:(c + 1) * 128], WOa[:, k, hf * 512:(hf + 1) * 512], k == 0, k == 3) for k in range(4)]
                        S.mm(lst, [ATGT, WOa], [pb])
                    S.tt("dve", xb[0:np_, hf * 512:(hf + 1) * 512], xb[0:np_, hf * 512:(hf + 1) * 512], pb[0:np_, :], ALU.add, [xb, pb], [xb])
            S.pa_release(mark_attn)

        if phases["ssd"]:
            mark_ssd = S.pa_mark()
            Wz = S.pa("Wz", [128, 8, 1024], BF16)
            Wx = S.pa("Wx", [128, 8, 1536], BF16)
            Wdt = S.pa("Wdt", [128, 8, 16], BF16)
            WOs = S.pa("WOs", [128, 8, 1024], BF16)
            S.dma(Wx[:, :, :], WE[:, 3072:4608].rearrange("(k p) n -> p k n", p=128), wbuf=Wx, q="pool")
            S.dma(Wz[:, :, :], WE[:, 2048:3072].rearrange("(k p) n -> p k n", p=128), wbuf=Wz, q="pool")
            S.dma(Wdt[:, :, :], WE[:, 4608:4624].rearrange("(k p) n -> p k n", p=128), wbuf=Wdt, q="pool")
            S.dma(WOs[:, :, :], D["w_out_even"][512:1536, :].rearrange("(k p) n -> p k n", p=128), wbuf=WOs, q="pool")
            CW = S.pa("CW", [128, 12, 4], F32)
            CB = S.pa("CB", [128, 12], F32)
            DTB = S.pa("DTB", [128, 16], F32)
            ABC = S.pa("ABC", [128, 16], F32)
            DSK = S.pa("DSK", [128, 16], F32)
            SNW = S.pa("SNW", [128, 1024], F32)
            S.dma(CW[:, :, :], D["cwT"], wbuf=CW)
            S.dma(CB[:, :], D["cbT"], wbuf=CB)
            S.dma(DTB[:, :], D["dtb"], wbuf=DTB)
            S.dma(ABC[:, :], D["alog"], wbuf=ABC)
            S.dma(DSK[:, :], D["dsk"], wbuf=DSK)
            S.dma(SNW[:, :], D["snw"], wbuf=SNW)
            S.act(ABC[:, :], ABC[:, :], AF.Exp, [ABC], [ABC])
            S.ts("dve", ABC[:, :], ABC[:, :], -1.0, None, ALU.mult, None, [ABC], [ABC])
            mark_ssdw = S.pa_mark()
            A1 = S.pa("A1", [128, 12, 131], F32)
            A2 = S.pa("A2", [128, 12, 128], F32)
            PRE = A1
            CV = A2
            Yv = A1[:, :, :].rearrange("p a b -> p (a b)")[:, 0:1024]
            SZv = A2[:, :, :].rearrange("p a b -> p (a b)")[:, 0:1024]
            CARRY = S.pa("CARRY", [128, 12, 3], F32)
            XC = S.pa("XC", [128, 12, 128], BF16)
            XT = S.pa("XT", [128, 1024], BF16)
            BTOK = S.pa("BTOK", [128, 2, 128], BF16)
            TMPD = S.pa("TMPD", [128, 1024], F32)
            YZW = S.pa("YZW", [128, 1024], BF16)
            YZWT = S.pa("YZWT", [128, 8, 128], BF16)
            S32 = S.pa("S32", [128, 2, 512], F32)
            SB16 = S.pa("SB16", [128, 2, 512], BF16)
            XW = S.pa("XW", [128, 1024], BF16)
            SEG = S.ring("SEG", 2, [128, 128], F32)
            EX = S.ring("EX", 2, [128, 128], F32)
            WT = S.ring("WT", 2, [128, 128], BF16)
            CBM = S.pa("CBM", [128, 2, 128], F32)
            DTR = S.pa("DTR", [128, 16], F32)
            DT = S.pa("DT", [128, 16], F32)
            DA = S.pa("DA", [128, 16], F32)
            ACS = S.pa("ACS", [128, 32], F32)
            EAC = S.pa("EAC", [128, 32], F32)
            WEND = S.pa("WEND", [128, 16], F32)
            ssy = S.pa("ssy", [128, 1], F32)
            rsy = S.pa("rsy", [128, 1], F32)
            P7r = [Buf("P7r%d" % r, PS[7][:, r * 128:(r + 1) * 128]) for r in range(4)]
            S.memset("pool", S32[:, :, :], 0.0, [S32])
            S.memset("pool", SB16[:, :, :], 0.0, [SB16])
            S.memset("pool", CARRY[:, :, :], 0.0, [CARRY])
            for c in range(NCH):
                hb = hnT[c]
                for b3 in range(3):
                    lst = []
                    for bq in range(4):
                        blk = b3 * 4 + bq
                        for k in range(8):
                            lst.append((PS[b3][:, bq * 128:(bq + 1) * 128], Wx[:, k, blk * 128:(blk + 1) * 128], hb[:, k, :], k == 0, k == 7))
                    S.mm(lst, [Wx, hb], [PS[b3]])
                S.cp("pool", PRE[:, :, 0:3], CARRY[:, :, :], [CARRY], [PRE])
                for b3 in range(3):
                    S.cp("act", PRE[:, 4 * b3:4 * b3 + 4, 3:131], PS[b3][:, :].rearrange("p (a b) -> p a b", a=4), [PS[b3]], [PRE])
                S.cp("pool", CARRY[:, :, :], PRE[:, :, 128:131], [PRE], [CARRY])
                if c == NCH - 1:
                    for j_ in range(3):
                        S.dma(D["p_sconv"][j_].rearrange("(b p) -> p b", p=128), PRE[:, :, 128 + j_], rbuf=PRE, allow_slow_non_contiguous=True)
                for blk in range(12):
                    S.act(CV[:, blk, :], PRE[:, blk, 0:128], AF.Identity, [PRE, CW, CB], [CV], bias=CB[:, blk:blk + 1], scale=CW[:, blk, 0:1])
                for blk in range(12):
                    for j in range(1, 4):
                        S.stt(CV[:, blk, :], PRE[:, blk, j:j + 128], CW[:, blk, j:j + 1], CV[:, blk, :], ALU.mult, ALU.add, [PRE, CW, CV], [CV])
                S.act(XC[:, :, :], CV[:, :, :], AF.Silu, [CV], [XC])
                p3v = psb(3).rearrange("p (a t) -> p a t", a=8)
                S.tr([(p3v[:, a, :], XC[:, a, :], IDb[:, :]) for a in range(8)], [XC, IDb], [PS[3]])
                S.cp("act", XT[:, :], psb(3), [PS[3]], [XT])
                S.tr([(p3v[:, a, :], XC[:, 8 + a, :], IDb[:, :]) for a in range(2)], [XC, IDb], [PS[3]])
                S.cp("act", BTOK[:, :, :], p3v[:, 0:2, :], [PS[3]], [BTOK])
                lst = []
                for k in range(8):
                    lst.append((PS[4][:, :], hb[:, k, :], Wz[:, k, 0:512], k == 0, k == 7))
                    lst.append((PS[5][:, :], hb[:, k, :], Wz[:, k, 512:1024], k == 0, k == 7))
                    lst.append((PS[6][:, 0:16], hb[:, k, :], Wdt[:, k, :], k == 0, k == 7))
                S.mm(lst, [hb, Wz, Wdt], [PS[4], PS[5], PS[6]])
                S.act(SZv[:, 0:512], PS[4][:, :], AF.Silu, [PS[4]], [A2])
                S.act(SZv[:, 512:1024], PS[5][:, :], AF.Silu, [PS[5]], [A2])
                S.tt("dve", DTR[:, :], PS[6][:, 0:16], DTB[:, :], ALU.add, [PS[6], DTB], [DTR])
                S.act(DTR[:, :], DTR[:, :], AF.Exp, [DTR], [DTR])
                S.act(DT[:, :], DTR[:, :], AF.Ln, [DTR], [DT], bias=1.0)
                S.tt("dve", DA[:, :], DT[:, :], ABC[:, :], ALU.mult, [DT, ABC], [DA])
                S.mm([(PS[6][:, 16:32], TRI[:, :], DA[:, :], True, True), (PS[6][:, 32:48], ONES[:, :], DA[:, :], True, True)], [TRI, ONES, DA], [PS[6]])
                S.cp("dve", ACS[:, :], PS[6][:, 16:48], [PS[6]], [ACS])
                S.act(EAC[:, :], ACS[:, :], AF.Exp, [ACS], [EAC])
                S.tt("dve", WEND[:, :], ACS[:, 16:32], ACS[:, 0:16], ALU.subtract, [ACS], [WEND])
                S.act(WEND[:, :], WEND[:, :], AF.Exp, [WEND], [WEND])
                S.tt("dve", WEND[:, :], WEND[:, :], DT[:, :], ALU.mult, [WEND, DT], [WEND])
                S.mm([(PS[6][:, 128 + g * 128:256 + g * 128], XC[:, 8 + g, :], XC[:, 10 + g, :], True, True) for g in range(2)], [XC], [PS[6]])
                S.tt("dve", CBM[:, :, :], PS[6][:, 128:384].rearrange("p (g i) -> p g i", g=2), TRI[:, :].unsqueeze(1).to_broadcast([128, 2, 128]), ALU.mult, [PS[6], TRI], [CBM])
                for h_ in range(16):
                    g = h_ // 8
                    pr = P7r[h_ % 4]
                    S.mm([(pr[:, :], DA[:, h_:h_ + 1].to_broadcast([128, 128]), TRI[:, :], True, True)], [DA, TRI], [pr])
                    sg, ex, wt = SEG[h_ % 2], EX[h_ % 2], WT[h_ % 2]
                    S.ts("dve", sg[:, :], pr[:, :], ACS[:, h_:h_ + 1], 0.0, ALU.subtract, ALU.min, [pr, ACS], [sg])
                    S.act(ex[:, :], sg[:, :], AF.Exp, [sg], [ex])
                    S.stt(wt[:, :], ex[:, :], DT[:, h_:h_ + 1], CBM[:, g, :], ALU.mult, ALU.mult, [ex, DT, CBM], [wt])
                    S.mm([(PS[g][:, (h_ % 8) * 64:(h_ % 8 + 1) * 64], wt[:, :], XT[:, h_ * 64:(h_ + 1) * 64], (h_ % 8 == 0), True)], [wt, XT], [PS[g]])
                S.mm([(PS[4 + g][:, :], XC[:, 10 + g, :], SB16[:, g, :], True, True) for g in range(2)], [XC, SB16], [PS[4], PS[5]])
                for g in range(2):
                    S.tt("dve", Yv[:, g * 512:(g + 1) * 512].rearrange("p (a b) -> p a b", a=8), PS[4 + g][:, :].rearrange("p (a b) -> p a b", a=8),
                         EAC[:, g * 8:(g + 1) * 8].unsqueeze(2).to_broadcast([128, 8, 64]), ALU.mult, [PS[4 + g], EAC], [A1])
                    S.tt("dve", Yv[:, g * 512:(g + 1) * 512], Yv[:, g * 512:(g + 1) * 512], PS[g][:, :], ALU.add, [A1, PS[g]], [A1])
                S.tt("pool", TMPD[:, :].rearrange("p (a b) -> p a b", a=16), XT[:, :].rearrange("p (a b) -> p a b", a=16),
                     DSK[:, :].unsqueeze(2).to_broadcast([128, 16, 64]), ALU.mult, [XT, DSK], [TMPD])
                S.tt("dve", Yv, Yv, TMPD[:, :], ALU.add, [A1, TMPD], [A1])
                S.tt("pool", Yv, Yv, SZv, ALU.mult, [A1, A2], [A1])
                S.act(TMPD[:, :], Yv, AF.Square, [A1], [TMPD, ssy], accum=ssy[:, :])
                S.ts("dve", rsy[:, :], ssy[:, :], 1.0 / 1024, EPS, ALU.mult, ALU.add, [ssy], [rsy])
                S.act(rsy[:, :], rsy[:, :], AF.Sqrt, [rsy], [rsy])
                S.op("dve", lambda h: h.reciprocal(out=rsy[:, :], in_=rsy[:, :]), [rsy], [rsy])
                S.tt("pool", YZW[:, :], Yv, SNW[:, :], ALU.mult, [A1, SNW], [YZW])
                S.tr([(p3v[:, a, :], YZW[:, a * 128:(a + 1) * 128], IDb[:, :]) for a in range(8)], [YZW, IDb], [PS[3]])
                S.cp("act", YZWT[:, :, :], p3v, [PS[3]], [YZWT])
                for hf in range(2):
                    S.mm([(PS[hf][:, :], YZWT[:, k, :], WOs[:, k, hf * 512:(hf + 1) * 512], k == 0, k == 7) for k in range(8)], [YZWT, WOs], [PS[hf]])
                    S.stt(X[c][:, hf * 512:(hf + 1) * 512], PS[hf][:, :], rsy[:, :], X[c][:, hf * 512:(hf + 1) * 512], ALU.mult, ALU.add, [PS[hf], rsy, X[c]], [X[c]])
                S.tt("pool", XW[:, :].rearrange("p (a b) -> p a b", a=16), XT[:, :].rearrange("p (a b) -> p a b", a=16),
                     WEND[:, :].unsqueeze(2).to_broadcast([128, 16, 64]), ALU.mult, [XT, WEND], [XW])
                S.mm([(PS[4 + g][:, :], BTOK[:, g, :], XW[:, g * 512:(g + 1) * 512], True, True) for g in range(2)], [BTOK, XW], [PS[4], PS[5]])
                for g in range(2):
                    S.tt("pool", S32[:, g, :].rearrange("p (a b) -> p a b", a=8), S32[:, g, :].rearrange("p (a b) -> p a b", a=8),
                         EAC[:, 16 + g * 8:16 + (g + 1) * 8].unsqueeze(2).to_broadcast([128, 8, 64]), ALU.mult, [S32, EAC], [S32])
                    S.tt("dve", S32[:, g, :], S32[:, g, :], PS[4 + g][:, :], ALU.add, [S32, PS[4 + g]], [S32])
                S.cp("pool", SB16[:, :, :], S32[:, :, :], [S32], [SB16])
            for g in range(2):
                S.tr([(PS[g][:, a * 128:(a + 1) * 128], S32[:, g, a * 128:(a + 1) * 128], IDf[:, :]) for a in range(4)], [S32, IDf], [PS[g]])
                S.cp("act", TMPD[:, g * 512:(g + 1) * 512], PS[g][:, :], [PS[g]], [TMPD])
            S.dma(D["p_ssd"].rearrange("(a q) n -> q a n", q=128), TMPD[:, :].rearrange("p (a n) -> p a n", a=8), rbuf=TMPD)
            S.pa_release(mark_ssdw)
            if phases["sample"] and phases.get("s_ssd", True):
                XPs = S.pa("XPs", [NS, 1536], F32)
                SZs = S.pa("SZs", [NS, 1024], F32)
                XCs = S.pa("XCs", [NS, 1536], F32)
                DTs = S.pa("DTs", [NS, 16], F32)
                DAs = S.pa("DAs", [NS, 16], F32)
                DECs = S.pa("DECs", [NS, 16], F32)
                lst = []
                for k in range(8):
                    for j_ in range(3):
                        lst.append((PS[j_][0:NS, :], hnTs[:, k, :], Wx[:, k, j_ * 512:(j_ + 1) * 512], k == 0, k == 7))
                    lst.append((PS[4][0:NS, :], hnTs[:, k, :], Wz[:, k, 0:512], k == 0, k == 7))
                    lst.append((PS[5][0:NS, :], hnTs[:, k, :], Wz[:, k, 512:1024], k == 0, k == 7))
                    lst.append((PS[6][0:NS, 0:16], hnTs[:, k, :], Wdt[:, k, :], k == 0, k == 7))
                S.mm(lst, [hnTs, Wx, Wz, Wdt], [PS[0], PS[1], PS[2], PS[4], PS[5], PS[6]])
                for j_ in range(3):
                    S.cp("act", XPs[:, j_ * 512:(j_ + 1) * 512], PS[j_][0:NS, :], [PS[j_]], [XPs])
                S.act(SZs[:, 0:512], PS[4][0:NS, :], AF.Silu, [PS[4]], [SZs])
                S.act(SZs[:, 512:1024], PS[5][0:NS, :], AF.Silu, [PS[5]], [SZs])
                S.tt("dve", DTs[:, :], PS[6][0:NS, 0:16], DTB[0:NS, :], ALU.add, [PS[6], DTB], [DTs])
                S.act(DTs[:, :], DTs[:, :], AF.Exp, [DTs], [DTs])
                S.act(DTs[:, :], DTs[:, :], AF.Ln, [DTs], [DTs], bias=1.0)
                S.tt("dve", DAs[:, :], DTs[:, :], ABC[0:NS, :], ALU.mult, [DTs, ABC], [DAs])
                S.act(DECs[:, :], DAs[:, :], AF.Exp, [DAs], [DECs])
                mk1 = S.pa_mark()
                CWg = S.pa("CWg", [NS, 4, 512], F32)
                SCVg = S.pa("SCVg", [NS, 3, 512], F32)
                CBs = S.pa("CBs", [NS, 1536], F32)
                S.dma(CBs[:, :], D["cb_s"], wbuf=CBs)
                for j_ in range(3):
                    cs = slice(j_ * 512, (j_ + 1) * 512)
                    S.dma(CWg[:, :, :], D["cw_s"][:, :, cs], wbuf=CWg)
                    S.dma(SCVg[:, :, :], D["sconv"][:, :, cs], wbuf=SCVg)
                    S.dma(D["s_sconv"][:, 0:2, cs], SCVg[:, 1:3, :], rbuf=SCVg)
                    S.dma(D["s_sconv"][:, 2, cs], XPs[:, cs], rbuf=XPs)
                    S.tt("pool", SCVg[:, :, :], SCVg[:, :, :], CWg[:, 0:3, :], ALU.mult, [SCVg, CWg], [SCVg])
                    S.tt("dve", XCs[:, cs], XPs[:, cs], CWg[:, 3, :], ALU.mult, [XPs, CWg], [XCs])
                    for t_ in range(3):
                        S.tt("dve", XCs[:, cs], XCs[:, cs], SCVg[:, t_, :], ALU.add, [XCs, SCVg], [XCs])
                    S.tt("dve", XCs[:, cs], XCs[:, cs], CBs[:, cs], ALU.add, [XCs, CBs], [XCs])
                S.act(XCs[:, :], XCs[:, :], AF.Silu, [XCs], [XCs])
                S.pa_release(mk1)
                TMPs = S.pa("TMPs", [NS, 1024], F32)
                YTs_ = S.pa("YTs_", [128, 8, NS], F32)
                mk2 = S.pa_mark()
                DTXT = S.pa("DTXT", [128, 8, NS], F32)
                DECT = S.pa("DECT", [128, 8, NS], F32)
                STr = S.ring("STr", 2, [128, 8, 128], F32)
                PROD = S.pa("PROD", [128, 8, 128], F32)
                S.tt("dve", TMPs[:, :].rearrange("p (a b) -> p a b", a=16), XCs[:, 0:1024].rearrange("p (a b) -> p a b", a=16),
                     DTs[:, :].unsqueeze(2).to_broadcast([NS, 16, 64]), ALU.mult, [XCs, DTs], [TMPs])
                p0v = PS[0][:, 0:8 * NS].rearrange("p (a t) -> p a t", a=8)
                S.tr([(p0v[:, a, :], TMPs[:, a * 128:(a + 1) * 128], IDf[0:NS, 0:NS]) for a in range(8)], [TMPs, IDf], [PS[0]])
                S.cp("dve", DTXT[:, :, :], p0v, [PS[0]], [DTXT])
                S.cp("dve", TMPs[:, :].rearrange("p (a b) -> p a b", a=16), DECs[:, :].unsqueeze(2).to_broadcast([NS, 16, 64]), [DECs, PS[0]], [TMPs])
                p1v = PS[1][:, 0:8 * NS].rearrange("p (a t) -> p a t", a=8)
                S.tr([(p1v[:, a, :], TMPs[:, a * 128:(a + 1) * 128], IDf[0:NS, 0:NS]) for a in range(8)], [TMPs, IDf], [PS[1]])
                S.cp("dve", DECT[:, :, :], p1v, [PS[1]], [DECT])
                for b in range(NS):
                    st = STr[b % 2]
                    pb_ = PS[2 + (b % 2)]
                    S.dma(st[:, :, :], D["sstate"][b].rearrange("(a q) n -> q a n", q=128), wbuf=st)
                    S.mm([(pb_[:, :], IDf[0:NS, b:b + 1].to_broadcast([NS, 128]), XCs[:, 1024:1536], True, True)], [IDf, XCs], [pb_])
                    S.tt("dve", st[:, :, :], st[:, :, :], DECT[:, :, b:b + 1].to_broadcast([128, 8, 128]), ALU.mult, [st, DECT], [st])
                    for g in range(2):
                        S.tt("dve", PROD[:, 4 * g:4 * g + 4, :], pb_[:, g * 128:(g + 1) * 128].unsqueeze(1).to_broadcast([128, 4, 128]),
                             DTXT[:, 4 * g:4 * g + 4, b:b + 1].to_broadcast([128, 4, 128]), ALU.mult, [pb_, DTXT], [PROD])
                    S.tt("pool", st[:, :, :], st[:, :, :], PROD[:, :, :], ALU.add, [st, PROD], [st])
                    S.dma(D["s_ssd"][b].rearrange("(a q) n -> q a n", q=128), st[:, :, :], rbuf=st)
                    for g in range(2):
                        S.tt("dve", PROD[:, 4 * g:4 * g + 4, :], st[:, 4 * g:4 * g + 4, :],
                             pb_[:, 256 + g * 128:256 + (g + 1) * 128].unsqueeze(1).to_broadcast([128, 4, 128]), ALU.mult, [st, pb_], [PROD])
                    S.op("dve", lambda h, b=b: h.tensor_reduce(out=YTs_[:, :, b], in_=PROD[:, :, :], axis=AX.X, op=ALU.add), [PROD], [YTs_])
                S.pa_release(mk2)
                for a in range(8):
                    pbk = PS[4 + a // 4]
                    S.tr([(pbk[0:NS, (a % 4) * 128:(a % 4 + 1) * 128], YTs_[:, a, :], IDf[:, :])], [YTs_, IDf], [pbk])
                Ys = S.pa("Ys", [NS, 1024], F32)
                S.tt("pool", TMPs[:, :].rearrange("p (a b) -> p a b", a=16), XCs[:, 0:1024].rearrange("p (a b) -> p a b", a=16),
                     DSK[0:NS, :].unsqueeze(2).to_broadcast([NS, 16, 64]), ALU.mult, [XCs, DSK], [TMPs])
                for hf in range(2):
                    S.tt("dve", Ys[:, hf * 512:(hf + 1) * 512], TMPs[:, hf * 512:(hf + 1) * 512], PS[4 + hf][0:NS, :], ALU.add, [TMPs, PS[4 + hf]], [Ys])
                S.tt("dve", Ys[:, :], Ys[:, :], SZs[:, :], ALU.mult, [Ys, SZs], [Ys])
                sss_ = S.pa("sss_", [NS, 1], F32)
                rss_ = S.pa("rss_", [NS, 1], F32)
                S.act(TMPs[:, :], Ys[:, :], AF.Square, [Ys], [TMPs, sss_], accum=sss_[:, :])
                S.ts("dve", rss_[:, :], sss_[:, :], 1.0 / 1024, EPS, ALU.mult, ALU.add, [sss_], [rss_])
                S.act(rss_[:, :], rss_[:, :], AF.Sqrt, [rss_], [rss_])
                S.op("dve", lambda h: h.reciprocal(out=rss_[:, :], in_=rss_[:, :]), [rss_], [rss_])
                YWs = S.pa("YWs", [NS, 1024], BF16)
                S.tt("dve", YWs[:, :], Ys[:, :], SNW[0:NS, :], ALU.mult, [Ys, SNW], [YWs])
                p3s = psb(3, 8 * NS).rearrange("p (a t) -> p a t", a=8)
                S.tr([(p3s[:, a, :], YWs[:, a * 128:(a + 1) * 128], IDb[0:NS, 0:NS]) for a in range(8)], [YWs, IDb], [PS[3]])
                YTb = S.pa("YTb", [128, 8, NS], BF16)
                S.cp("act", YTb[:, :, :], p3s, [PS[3]], [YTb])
                for hf in range(2):
                    S.mm([(PS[hf][0:NS, :], YTb[:, k, :], WOs[:, k, hf * 512:(hf + 1) * 512], k == 0, k == 7) for k in range(8)], [YTb, WOs], [PS[hf]])
                    S.stt(Xs[:, hf * 512:(hf + 1) * 512], PS[hf][0:NS, :], rss_[:, :], Xs[:, hf * 512:(hf + 1) * 512], ALU.mult, ALU.add, [PS[hf], rss_, Xs], [Xs])
            S.pa_release(mark_ssd)

        if phases["mlstm"]:
            phase_norm(1)
            WO_ = D["w_in_odd"]
            mark_ml = S.pa_mark()
            Wif = S.pa("Wif", [128, 8, 16], BF16)
            S.dma(Wif[:, :, :], WO_[:, 8192:8208].rearrange("(k p) n -> p k n", p=128), wbuf=Wif, q="pool")
            Wqk = S.ring("Wqk", 2, [128, 8, 512], BF16)
            Wvoz = S.ring("Wvoz", 2, [128, 8, 768], BF16)
            WOo = S.ring("WOo", 2, [128, 2, 1024], BF16)
            MNWh = S.ring("MNWh", 2, [128, 256], F32)

            def load_head_w(h_):
                i_ = h_ % 2
                for j_, off in enumerate((0, 2048)):
                    S.dma(Wqk[i_][:, :, j_ * 256:(j_ + 1) * 256], WO_[:, off + h_ * 256:off + (h_ + 1) * 256].rearrange("(k p) n -> p k n", p=128), wbuf=Wqk[i_], q="pool")
                for j_, off in enumerate((4096, 6144, 8208)):
                    S.dma(Wvoz[i_][:, :, j_ * 256:(j_ + 1) * 256], WO_[:, off + h_ * 256:off + (h_ + 1) * 256].rearrange("(k p) n -> p k n", p=128), wbuf=Wvoz[i_], q="pool")
                S.dma(WOo[i_][:, :, :], D["w_out_odd"][h_ * 256:(h_ + 1) * 256, :].rearrange("(k p) n -> p k n", p=128), wbuf=WOo[i_], q="pool")
                S.dma(MNWh[i_][:, :], D["mnw"][:, h_ * 256:(h_ + 1) * 256], wbuf=MNWh[i_])

            load_head_w(0)
            MCW = S.pa("MCW", [128, 32, 4], F32)
            MCB = S.pa("MCB", [128, 32], F32)
            IFB = S.pa("IFB", [128, 16], F32)
            S.dma(MCW[:, :, :], D["mcwT"], wbuf=MCW)
            S.dma(MCB[:, :], D["mcbT"], wbuf=MCB)
            S.dma(IFB[:, 0:8], D["igb"], wbuf=IFB)
            S.dma(IFB[:, 8:16], D["fgb"], wbuf=IFB)
            IFt = S.pa("IFt", [128, 16, 16], F32)
            LF = S.pa("LF", [128, 16, 8], F32)
            BCt = S.pa("BCt", [128, 16, 8], F32)
            BLB = S.pa("BLB", [128, 16, 8], F32)
            Gt = S.pa("Gt", [128, 16, 8], F32)
            At = S.pa("At", [128, 16, 8], F32)
            EBt = S.pa("EBt", [128, 16, 8], F32)
            EBL = S.pa("EBL", [128, 16, 8], F32)
            EMF = S.pa("EMF", [128, 8], F32)
            MX = S.pa("MX", [128, 1], F32)
            MXR = S.pa("MXR", [1, 128], F32)
            MF = S.pa("MF", [1, 8], F32)
            for c in range(NCH):
                S.mm([(PS[3][:, c * 16:(c + 1) * 16], hnT[c][:, k, :], Wif[:, k, :], k == 0, k == 7) for k in range(8)], [hnT[c], Wif], [PS[3]])
            S.tt("dve", IFt[:, :, :], PS[3][:, 0:256].rearrange("p (c j) -> p c j", c=16), IFB[:, :].unsqueeze(1).to_broadcast([128, 16, 16]), ALU.add, [PS[3], IFB], [IFt])
            S.act(LF[:, :, :], IFt[:, :, 8:16], AF.Exp, [IFt], [LF], scale=-1.0)
            S.act(LF[:, :, :], LF[:, :, :], AF.Ln, [LF], [LF], bias=1.0)
            S.ts("dve", LF[:, :, :], LF[:, :, :], -1.0, None, ALU.mult, None, [LF], [LF])
            lfl = LF[:, :, :].rearrange("p c h -> p (c h)")
            S.mm([(PS[3][:, 256 + c * 8:256 + (c + 1) * 8], TRI[:, :], LF[:, c, :], True, True) for c in range(NCH)] +
                 [(PS[3][:, 384:512], ONES[:, :], lfl, True, True)], [TRI, ONES, LF], [PS[3]])
            S.cp("dve", BCt[:, :, :].rearrange("p c h -> p (c h)"), PS[3][:, 256:384], [PS[3]], [BCt])
            S.cp("dve", BLB[:, :, :].rearrange("p c h -> p (c h)"), PS[3][:, 384:512], [PS[3]], [BLB])
            S.tt("dve", Gt[:, :, :], IFt[:, :, 0:8], BCt[:, :, :], ALU.subtract, [IFt, BCt], [Gt])
            S.act(At[:, :, :], Gt[:, :, :], AF.Exp, [Gt], [At], bias=float(math.log(1.0 / 16.0)))
            S.act(EBt[:, :, :], BCt[:, :, :], AF.Exp, [BCt], [EBt])
            S.act(EBL[:, :, :], BLB[:, :, :], AF.Exp, [BLB], [EBL])
            S.tr([(PS[4][:, 0:128], Gt[:, :, :].rearrange("p c h -> p (c h)"), IDf[:, :])], [Gt, IDf], [PS[4]])
            S.op("dve", lambda h: h.tensor_reduce(out=MX[:, :], in_=PS[4][:, 0:128], axis=AX.X, op=ALU.max), [PS[4]], [MX])
            S.tr([(PS[4][0:1, 128:256], MX[:, 0:1], IDf[:, :])], [MX, IDf], [PS[4]])
            S.cp("dve", MXR[:, :], PS[4][0:1, 128:256], [PS[4]], [MXR])
            S.memset("dve", MF[:, :], -1.0e30, [MF])
            for c in range(NCH):
                S.tt("dve", MF[:, :], MF[:, :], MXR[:, c * 8:(c + 1) * 8], ALU.max, [MF, MXR], [MF])
                S.tt("dve", MF[:, :], MF[:, :], BLB[0:1, c, :], ALU.add, [MF, BLB], [MF])
            S.dma(D["p_mm"], MF[:, :], rbuf=MF)
            S.mm([(PS[4][:, 256:264], ONES[0:1, :], MF[:, :], True, True)], [ONES, MF], [PS[4]])
            S.cp("dve", EMF[:, :], PS[4][:, 256:264], [PS[4]], [EMF])
            S.act(EMF[:, :], EMF[:, :], AF.Exp, [EMF], [EMF], scale=-1.0)

            if phases["sample"] and phases.get("s_ml", True):
                IFs = S.pa("IFs", [NS, 16], F32)
                LFs = S.pa("LFs", [NS, 8], F32)
                MM0 = S.pa("MM0", [NS, 8], F32)
                INTs = S.pa("INTs", [NS, 8], F32)
                MNs = S.pa("MNs", [NS, 8], F32)
                WINs = S.pa("WINs", [NS, 8], F32)
                WOUs = S.pa("WOUs", [NS, 8], F32)
                EMNs = S.pa("EMNs", [NS, 8], F32)
                BDs = S.pa("BDs", [NS, NS, 8], F32)
                WSB = S.pa("WSB", [128, NS, 8], F32)
                SCB = S.pa("SCB", [128, NS, 8], F32)
                OHr = S.pa("OHr", [NS, NS, NS], F32)
                OH = S.pa("OH", [128, NS, NS], F32)
                S.dma(MM0[:, :], D["mmm"], wbuf=MM0)
                S.mm([(PS[4][0:NS, 300:316], hnTs[:, k, :], Wif[:, k, :], k == 0, k == 7) for k in range(8)], [hnTs, Wif], [PS[4]])
                S.tt("dve", IFs[:, :], PS[4][0:NS, 300:316], IFB[0:NS, :], ALU.add, [PS[4], IFB], [IFs])
                S.act(LFs[:, :], IFs[:, 8:16], AF.Exp, [IFs], [LFs], scale=-1.0)
                S.act(LFs[:, :], LFs[:, :], AF.Ln, [LFs], [LFs], bias=1.0)
                S.ts("dve", LFs[:, :], LFs[:, :], -1.0, None, ALU.mult, None, [LFs], [LFs])
                S.tt("dve", INTs[:, :], LFs[:, :], MM0[:, :], ALU.add, [LFs, MM0], [INTs])
                S.tt("dve", MNs[:, :], INTs[:, :], IFs[:, 0:8], ALU.max, [INTs, IFs], [MNs])
                S.dma(D["s_mm"], MNs[:, :], rbuf=MNs)
                S.tt("dve", WINs[:, :], IFs[:, 0:8], MNs[:, :], ALU.subtract, [IFs, MNs], [WINs])
                S.act(WINs[:, :], WINs[:, :], AF.Exp, [WINs], [WINs])
                S.tt("dve", WOUs[:, :], INTs[:, :], MNs[:, :], ALU.subtract, [INTs, MNs], [WOUs])
                S.act(WOUs[:, :], WOUs[:, :], AF.Exp, [WOUs], [WOUs])
                S.act(EMNs[:, :], MNs[:, :], AF.Exp, [MNs], [EMNs], scale=-1.0)
                idb = IDf[0:NS, 0:NS]
                for src_, dst_ in ((WINs, WSB), (WOUs, SCB)):
                    S.tt("dve", BDs[:, :, :], src_[:, :].unsqueeze(1).to_broadcast([NS, NS, 8]), idb.unsqueeze(2).to_broadcast([NS, NS, 8]), ALU.mult, [src_, IDf], [BDs])
                    S.mm([(PS[4][:, 0:128], ONES[0:NS, :], BDs[:, :, :].rearrange("p a b -> p (a b)"), True, True)], [ONES, BDs], [PS[4]])
                    S.cp("dve", dst_[:, :, :].rearrange("p a b -> p (a b)"), PS[4][:, 0:128], [PS[4]], [dst_])
                S.tt("dve", OHr[:, :, :], idb.unsqueeze(2).to_broadcast([NS, NS, NS]), idb.unsqueeze(1).to_broadcast([NS, NS, NS]), ALU.mult, [IDf], [OHr])
                S.mm([(PS[4][:, 0:256], ONES[0:NS, :], OHr[:, :, :].rearrange("p a b -> p (a b)"), True, True)], [ONES, OHr], [PS[4]])
                S.cp("dve", OH[:, :, :].rearrange("p a b -> p (a b)"), PS[4][:, 0:256], [PS[4]], [OH])

            p2b = PS[2][:, 256:512].bitcast(BF16)
            R2z, R2k, R2h = Buf('R2z', None), Buf('R2k', None), Buf('R2h', None)
            R3s, R3kv = Buf('R3s', None), Buf('R3kv', None)
            for h_ in range(8):
                if h_ + 1 < 8:
                    load_head_w(h_ + 1)
                wqk, wvoz, woo, mnwh = Wqk[h_ % 2], Wvoz[h_ % 2], WOo[h_ % 2], MNWh[h_ % 2]
                mark_w = S.pa_mark()
                PREm_r = S.ring("PREm", 2, [128, 4, 131], F32)
                CARm = S.pa("CARm", [128, 4, 3], F32)
                CVm_r = S.ring("CVm", 2, [128, 4, 128], F32)
                QKc = S.ring("QKc", 2, [128, 4, 128], BF16)
                VAm = S.ring("VAm", 2, [128, 258], BF16)
                SIGO_r = S.ring("SIGO", 2, [128, 256], F32)
                SZm_r = S.ring("SZm", 2, [128, 256], F32)
                ATTm = S.ring("ATTm", 2, [128, 128], BF16)
                KTOK = S.ring("KTOK", 2, [128, 256], BF16)
                Hm_r = S.ring("Hm", 2, [128, 256], F32)
                GZ_r = S.ring("GZ", 2, [128, 256], F32)
                HG_r = S.ring("HG", 2, [128, 256], BF16)
                HGT_r = S.ring("HGT", 2, [128, 2, 128], BF16)
                C32 = S.pa("C32", [128, 2, 257], F32)
                C16 = S.pa("C16", [128, 2, 257], BF16)
                CO = S.pa("CO", [128, 2, 257], F32)
                DQ_r = S.ring("DQ", 2, [128, 1], F32)
                RQ_r = S.ring("RQ", 2, [128, 1], F32)
                BNS_r = S.ring("BNS", 2, [128, 6], F32)
                MV_r = S.ring("MV", 2, [128, 2], F32)
                RSD_r = S.ring("RSD", 2, [128, 1], F32)
                S.memset("pool", C32[:, :, :], 0.0, [C32])
                S.memset("pool", C16[:, :, :], 0.0, [C16])
                S.memset("pool", CARm[:, :, :], 0.0, [CARm])
                cblk = [2 * h_, 2 * h_ + 1, 16 + 2 * h_, 16 + 2 * h_ + 1]
                def early(c):
                    hb = hnT[c]
                    qkc, va, att, ktok = QKc[c % 2], VAm[c % 2], ATTm[c % 2], KTOK[c % 2]
                    PREm = PREm_r[c % 2]
                    CVm = CVm_r[c % 2]
                    SIGO = SIGO_r[c % 2]
                    SZm = SZm_r[c % 2]
                    Hm = Hm_r[c % 2]
                    GZ = GZ_r[c % 2]
                    HG = HG_r[c % 2]
                    HGT = HGT_r[c % 2]
                    DQ = DQ_r[c % 2]
                    RQ = RQ_r[c % 2]
                    BNS = BNS_r[c % 2]
                    MV = MV_r[c % 2]
                    RSD = RSD_r[c % 2]
                    lst = []
                    for bq in range(4):
                        for k in range(8):
                            lst.append((PS[0][:, bq * 128:(bq + 1) * 128], wqk[:, k, bq * 128:(bq + 1) * 128], hb[:, k, :], k == 0, k == 7))
                    S.mm(lst, [wqk, hb], [PS[0]])
                    S.cp("pool", PREm[:, :, 0:3], CARm[:, :, :], [CARm], [PREm])
                    S.cp("act", PREm[:, :, 3:131], PS[0][:, :].rearrange("p (a b) -> p a b", a=4), [PS[0]], [PREm])
                    S.cp("pool", CARm[:, :, :], PREm[:, :, 128:131], [PREm], [CARm])
                    if c == NCH - 1:
                        for bq in range(4):
                            col0 = cblk[bq] * 128
                            for j_ in range(3):
                                S.dma(D["p_mconv"][j_, col0:col0 + 128].rearrange("(p o) -> p o", o=1), PREm[:, bq, 128 + j_:129 + j_], rbuf=PREm)
                    for bq in range(4):
                        S.act(CVm[:, bq, :], PREm[:, bq, 0:128], AF.Identity, [PREm, MCW, MCB], [CVm], bias=MCB[:, cblk[bq]:cblk[bq] + 1], scale=MCW[:, cblk[bq], 0:1])
                    for bq in range(4):
                        for j_ in range(1, 4):
                            S.stt(CVm[:, bq, :], PREm[:, bq, j_:j_ + 128], MCW[:, cblk[bq], j_:j_ + 1], CVm[:, bq, :], ALU.mult, ALU.add, [PREm, MCW, CVm], [CVm])
                    S.act(qkc[:, :, :], CVm[:, :, :], AF.Silu, [CVm], [qkc])
                    lst = []
                    for k in range(8):
                        lst.append((PS[1][:, :], hb[:, k, :], wvoz[:, k, 0:512], k == 0, k == 7))
                        lst.append((PS[2][:, 0:256], hb[:, k, :], wvoz[:, k, 512:768], k == 0, k == 7))
                    S.mm(lst, [hb, wvoz], [PS[1], PS[2]])
                    S.act(va[:, 0:256], PS[1][:, 0:256], AF.Copy, [PS[1], At], [va], scale=At[:, c, h_:h_ + 1])
                    S.cp("pool", va[:, 256:257], At[:, c, h_:h_ + 1], [At], [va])
                    S.act(SIGO[:, :], PS[1][:, 256:512], AF.Sigmoid, [PS[1]], [SIGO])
                    S.act(SZm[:, :], PS[2][:, 0:256], AF.Silu, [PS[2]], [SZm])
                    S.mm([(PS[3][:, 0:128], qkc[:, 2 + db, :], qkc[:, db, :], db == 0, db == 1) for db in range(2)], [qkc], [PS[3]])
                    S.tt("dve", att[:, :], PS[3][:, 0:128], TRI[:, :], ALU.mult, [PS[3], TRI], [att])
                    kv_ = p2b[:, 0:256].rearrange("p (a t) -> p a t", a=2)
                    S.tr([(kv_[:, db, :], qkc[:, 2 + db, :], IDb[:, :]) for db in range(2)], [qkc, IDb], [PS[2]])
                    S.cp("act", ktok[:, :], p2b[:, 0:256], [PS[2]], [ktok])
                def late(c):
                    qkc, va, att, ktok = QKc[c % 2], VAm[c % 2], ATTm[c % 2], KTOK[c % 2]
                    PREm = PREm_r[c % 2]
                    CVm = CVm_r[c % 2]
                    SIGO = SIGO_r[c % 2]
                    SZm = SZm_r[c % 2]
                    Hm = Hm_r[c % 2]
                    GZ = GZ_r[c % 2]
                    HG = HG_r[c % 2]
                    HGT = HGT_r[c % 2]
                    DQ = DQ_r[c % 2]
                    RQ = RQ_r[c % 2]
                    BNS = BNS_r[c % 2]
                    MV = MV_r[c % 2]
                    RSD = RSD_r[c % 2]
                    S.mm([(PS[5][:, 0:257], att[:, :], va[:, 0:257], True, False)] +
                         [(PS[5][:, 0:257], qkc[:, db, :], C16[:, db, :], False, db == 1) for db in range(2)], [att, va, qkc, C16], [PS[5]])
                    S.ts("dve", DQ[:, :], PS[5][:, 256:257], EBt[:, c, h_:h_ + 1], None, ALU.mult, None, [PS[5], EBt], [DQ])
                    S.stt(RQ[:, :], DQ[:, :], -1.0, DQ[:, :], ALU.mult, ALU.max, [DQ], [RQ])
                    S.ts("dve", DQ[:, :], RQ[:, :], 1.0, None, ALU.max, None, [RQ], [DQ])
                    S.op("dve", lambda h, RQ=RQ, DQ=DQ: h.reciprocal(out=RQ[:, :], in_=DQ[:, :]), [DQ], [RQ])
                    S.tt("dve", RQ[:, :], RQ[:, :], EBt[:, c, h_:h_ + 1], ALU.mult, [RQ, EBt], [RQ])
                    S.stt(Hm[:, :], PS[5][:, 0:256], RQ[:, :], SIGO[:, :], ALU.mult, ALU.mult, [PS[5], RQ, SIGO], [Hm])
                    S.op("dve", lambda h, BNS=BNS, Hm=Hm: h.bn_stats(out=BNS[:, :], in_=Hm[:, :]), [Hm], [BNS])
                    S.op("dve", lambda h, MV=MV, BNS=BNS: h.bn_aggr(out=MV[:, :], in_=BNS[:, :]), [BNS], [MV])
                    S.ts("dve", RSD[:, :], MV[:, 1:2], EPS, None, ALU.add, None, [MV], [RSD])
                    S.act(RSD[:, :], RSD[:, :], AF.Sqrt, [RSD], [RSD])
                    S.op("dve", lambda h, RSD=RSD: h.reciprocal(out=RSD[:, :], in_=RSD[:, :]), [RSD], [RSD])
                    S.ts("dve", Hm[:, :], Hm[:, :], MV[:, 0:1], RSD[:, :], ALU.subtract, ALU.mult, [Hm, MV, RSD], [Hm])
                    S.tt("pool", GZ[:, :], SZm[:, :], mnwh[:, :], ALU.mult, [SZm, mnwh], [GZ])
                    S.tt("pool", HG[:, :], Hm[:, :], GZ[:, :], ALU.mult, [Hm, GZ], [HG])
                    hv_ = PS[4][:, 0:128].bitcast(BF16).rearrange("p (a t) -> p a t", a=2)
                    S.tr([(hv_[:, db, :], HG[:, db * 128:(db + 1) * 128], IDb[:, :]) for db in range(2)], [HG, IDb], [PS[4]])
                    S.cp("act", HGT[:, :, :], hv_, [PS[4]], [HGT])
                    for hf in range(2):
                        S.mm([(PS[6 + hf][:, :], HGT[:, db, :], woo[:, db, hf * 512:(hf + 1) * 512], db == 0, db == 1) for db in range(2)], [HGT, woo], [PS[6 + hf]])
                        S.tt("dve", X[c][:, hf * 512:(hf + 1) * 512], X[c][:, hf * 512:(hf + 1) * 512], PS[6 + hf][:, :], ALU.add, [X[c], PS[6 + hf]], [X[c]])
                    S.mm([(PS[4][:, 128:385], ktok[:, 0:128], va[:, 0:257], True, True), (PS[5][:, 0:257], ktok[:, 128:256], va[:, 0:257], True, True)], [ktok, va], [PS[4], PS[5]])
                    S.ts("pool", C32[:, :, :], C32[:, :, :], EBL[:, c, h_:h_ + 1], None, ALU.mult, None, [C32, EBL], [C32])
                    S.stt(C32[:, 0, :], PS[4][:, 128:385], EBL[:, c, h_:h_ + 1], C32[:, 0, :], ALU.mult, ALU.add, [PS[4], EBL, C32], [C32])
                    S.stt(C32[:, 1, :], PS[5][:, 0:257], EBL[:, c, h_:h_ + 1], C32[:, 1, :], ALU.mult, ALU.add, [PS[5], EBL, C32], [C32])
                    S.cp("pool", C16[:, :, :], C32[:, :, :], [C32], [C16])
                early(0)
                for c in range(NCH):
                    la = S.captured(late, c)
                    ea = S.captured(early, c + 1) if c + 1 < NCH else []
                    S.replay_interleaved(ea, la)
                S.ts("dve", CO[:, :, :], C32[:, :, :], EMF[:, h_:h_ + 1], None, ALU.mult, None, [C32, EMF], [CO])
                S.dma(D["p_mC"][h_].rearrange("(a p) e -> p a e", p=128), CO[:, :, 0:256], rbuf=CO)
                for db in range(2):
                    S.dma(D["p_mn"][h_, db * 128:(db + 1) * 128].rearrange("(p o) -> p o", o=1), CO[:, db, 256:257], rbuf=CO)
                S.pa_release(mark_w)
                if phases["sample"] and phases.get("s_ml", True):
                    PREs = S.pa("PREs", [NS, 512], F32)
                    QKs = S.pa("QKs", [NS, 512], F32)
                    VSs = S.pa("VSs", [NS, 256], F32)
                    SIGs = S.pa("SIGs", [NS, 256], F32)
                    SZs2 = S.pa("SZs2", [NS, 256], F32)
                    CWm = S.pa("CWm", [NS, 4, 256], F32)
                    SCVm = S.pa("SCVm", [NS, 3, 256], F32)
                    CBm = S.pa("CBm", [NS, 256], F32)
                    NSin = S.pa("NSin", [NS, 256], F32)
                    NOUT = S.pa("NOUT", [NS, 256], F32)
                    QKT = S.pa("QKTs", [128, 4, NS], F32)
                    NT = S.pa("NT", [128, 2, NS], F32)
                    WK = S.pa("WK", [128, 2, NS], F32)
                    NN = S.pa("NN", [128, 2, NS], F32)
                    QN = S.pa("QN", [128, 2, NS], F32)
                    QZ = S.ring("QZ", 2, [128, 2, NS], F32)
                    Cb_ = S.ring("Cb_", 2, [128, 2, 256], F32)
                    ADs = S.pa("ADs", [NS, 1], F32)
                    RDs = S.pa("RDs", [NS, 1], F32)
                    Hs_ = S.pa("Hs_", [NS, 256], F32)
                    GZs = S.pa("GZs", [NS, 256], F32)
                    HGs = S.pa("HGs", [NS, 256], BF16)
                    HGTs = S.pa("HGTs", [128, 2, NS], BF16)
                    BNs = S.pa("BNs", [NS, 6], F32)
                    MVs = S.pa("MVs", [NS, 2], F32)
                    RSs = S.pa("RSs", [NS, 1], F32)
                    lst = []
                    for k in range(8):
                        lst.append((PS[0][0:NS, :], hnTs[:, k, :], wqk[:, k, :], k == 0, k == 7))
                        lst.append((PS[1][0:NS, :], hnTs[:, k, :], wvoz[:, k, 0:512], k == 0, k == 7))
                        lst.append((PS[2][0:NS, 0:256], hnTs[:, k, :], wvoz[:, k, 512:768], k == 0, k == 7))
                    S.mm(lst, [hnTs, wqk, wvoz], [PS[0], PS[1], PS[2]])
                    S.cp("act", PREs[:, :], PS[0][0:NS, :], [PS[0]], [PREs])
                    S.cp("act", VSs[:, :], PS[1][0:NS, 0:256], [PS[1]], [VSs])
                    S.act(SIGs[:, :], PS[1][0:NS, 256:512], AF.Sigmoid, [PS[1]], [SIGs])
                    S.act(SZs2[:, :], PS[2][0:NS, 0:256], AF.Silu, [PS[2]], [SZs2])
                    for hq in range(2):
                        col0 = hq * 2048 + h_ * 256
                        cs = slice(col0, col0 + 256)
                        ls = slice(hq * 256, (hq + 1) * 256)
                        S.dma(CWm[:, :, :], D["mcw_s"][:, :, cs], wbuf=CWm)
                        S.dma(SCVm[:, :, :], D["mconv"][:, :, cs], wbuf=SCVm)
                        S.dma(CBm[:, :], D["mcb_s"][:, cs], wbuf=CBm)
                        S.dma(D["s_mconv"][:, 0:2, cs], SCVm[:, 1:3, :], rbuf=SCVm)
                        S.dma(D["s_mconv"][:, 2, cs], PREs[:, ls], rbuf=PREs)
                        S.tt("pool", SCVm[:, :, :], SCVm[:, :, :], CWm[:, 0:3, :], ALU.mult, [SCVm, CWm], [SCVm])
                        S.tt("dve", QKs[:, ls], PREs[:, ls], CWm[:, 3, :], ALU.mult, [PREs, CWm], [QKs])
                        for t_ in range(3):
                            S.tt("dve", QKs[:, ls], QKs[:, ls], SCVm[:, t_, :], ALU.add, [QKs, SCVm], [QKs])
                        S.tt("dve", QKs[:, ls], QKs[:, ls], CBm[:, :], ALU.add, [QKs, CBm], [QKs])
                    S.act(QKs[:, :], QKs[:, :], AF.Silu, [QKs], [QKs])
                    S.ts("dve", QKs[:, 256:512], QKs[:, 256:512], 0.0625, None, ALU.mult, None, [QKs], [QKs])
                    if phases.get("s_ml_stage", 9) < 2:
                        S.pa_release(mark_w)
                        continue
                    S.dma(NSin[:, :], D["mn"][:, h_, :], wbuf=NSin)
                    p3q = PS[3][:, 0:4 * NS].rearrange("p (a t) -> p a t", a=4)
                    p3n = PS[3][:, 64:64 + 2 * NS].rearrange("p (a t) -> p a t", a=2)
                    S.tr([(p3q[:, a, :], QKs[:, a * 128:(a + 1) * 128], IDf[0:NS, 0:NS]) for a in range(4)] +
                         [(p3n[:, a, :], NSin[:, a * 128:(a + 1) * 128], IDf[0:NS, 0:NS]) for a in range(2)], [QKs, NSin, IDf], [PS[3]])
                    S.cp("dve", QKT[:, :, :], p3q, [PS[3]], [QKT])
                    S.cp("dve", NT[:, :, :], p3n, [PS[3]], [NT])
                    S.tt("dve", WK[:, :, :], QKT[:, 2:4, :], WSB[:, :, h_].unsqueeze(1).to_broadcast([128, 2, NS]), ALU.mult, [QKT, WSB], [WK])
                    S.tt("dve", NN[:, :, :], NT[:, :, :], SCB[:, :, h_].unsqueeze(1).to_broadcast([128, 2, NS]), ALU.mult, [NT, SCB], [NN])
                    S.tt("dve", NN[:, :, :], NN[:, :, :], WK[:, :, :], ALU.add, [NN, WK], [NN])
                    S.tt("dve", QN[:, :, :], QKT[:, 0:2, :], NN[:, :, :], ALU.mult, [QKT, NN], [QN])
                    S.tr([(PS[3][0:NS, 128 + a * 128:256 + a * 128], NN[:, a, :], IDf[:, :]) for a in range(2)], [NN, IDf], [PS[3]])
                    S.mm([(PS[3][0:NS, 400:402], QN[:, db, :], ONES[:, 0:2], db == 0, db == 1) for db in range(2)], [QN, ONES], [PS[3]])
                    S.cp("dve", NOUT[:, :], PS[3][0:NS, 128:384], [PS[3]], [NOUT])
                    S.dma(D["s_mn"][:, h_, :], NOUT[:, :], rbuf=NOUT)
                    if phases.get("s_ml_stage", 9) < 3:
                        S.pa_release(mark_w)
                        continue
                    def sstep(b):
                        cb_ = Cb_[b % 2]
                        qz = QZ[b % 2]
                        pvb = PS[4 + (b % 2)]
                        S.dma(cb_[:, :, :], D["mC"][b, h_].rearrange("(a p) e -> p a e", p=128), wbuf=cb_)
                        S.mm([(pvb[:, 0:256], IDf[0:NS, b:b + 1].to_broadcast([NS, 128]), VSs[:, :], True, True)], [IDf, VSs], [pvb])
                        S.act(cb_[:, :, :], cb_[:, :, :], AF.Copy, [cb_, SCB], [cb_], scale=SCB[:, b, h_:h_ + 1])
                        for db in range(2):
                            S.stt(cb_[:, db, :], pvb[:, 0:256], WK[:, db, b:b + 1], cb_[:, db, :], ALU.mult, ALU.add, [pvb, WK, cb_], [cb_])
                        S.dma(D["s_mC"][b, h_].rearrange("(a p) e -> p a e", p=128), cb_[:, :, :], rbuf=cb_)
                        S.tt("dve", qz[:, :, :], QKT[:, 0:2, :], OH[:, b, :].unsqueeze(1).to_broadcast([128, 2, NS]), ALU.mult, [QKT, OH], [qz])
                        S.mm([(PS[6][0:NS, 0:256], qz[:, db, :], cb_[:, db, :], (b == 0 and db == 0), (b == NS - 1 and db == 1)) for db in range(2)], [qz, cb_], [PS[6]])
                    for b in range(0, NS, 2):
                        sa = S.captured(sstep, b)
                        sb_ = S.captured(sstep, b + 1)
                        S.replay_interleaved(sa, sb_)
                    if phases.get("s_ml_stage", 9) < 4:
                        S.pa_release(mark_w)
                        continue
                    S.cp("dve", ADs[:, :], PS[3][0:NS, 400:401], [PS[3]], [ADs])
                    S.stt(RDs[:, :], ADs[:, :], -1.0, ADs[:, :], ALU.mult, ALU.max, [ADs], [RDs])
                    S.tt("dve", RDs[:, :], RDs[:, :], EMNs[:, h_:h_ + 1], ALU.max, [RDs, EMNs], [RDs])
                    S.op("dve", lambda h, RDs=RDs: h.reciprocal(out=RDs[:, :], in_=RDs[:, :]), [RDs], [RDs])
                    S.stt(Hs_[:, :], PS[6][0:NS, 0:256], RDs[:, :], SIGs[:, :], ALU.mult, ALU.mult, [PS[6], RDs, SIGs], [Hs_])
                    if phases.get("s_ml_stage", 9) < 5:
                        S.pa_release(mark_w)
                        continue
                    S.op("dve", lambda h, BNs=BNs, Hs_=Hs_: h.bn_stats(out=BNs[:, :], in_=Hs_[:, :]), [Hs_], [BNs])
                    S.op("dve", lambda h, BNs=BNs, MVs=MVs: h.bn_aggr(out=MVs[:, :], in_=BNs[:, :]), [BNs], [MVs])
                    S.ts("dve", RSs[:, :], MVs[:, 1:2], EPS, None, ALU.add, None, [MVs], [RSs])
                    S.act(RSs[:, :], RSs[:, :], AF.Sqrt, [RSs], [RSs])
                    S.op("dve", lambda h, RSs=RSs: h.reciprocal(out=RSs[:, :], in_=RSs[:, :]), [RSs], [RSs])
                    S.ts("dve", Hs_[:, :], Hs_[:, :], MVs[:, 0:1], RSs[:, :], ALU.subtract, ALU.mult, [Hs_, MVs, RSs], [Hs_])
                    S.tt("pool", GZs[:, :], SZs2[:, :], mnwh[0:NS, :], ALU.mult, [SZs2, mnwh], [GZs])
                    S.tt("pool", HGs[:, :], Hs_[:, :], GZs[:, :], ALU.mult, [Hs_, GZs], [HGs])
                    if phases.get("s_ml_stage", 9) < 6:
                        S.pa_release(mark_w)
                        continue
                    hvs = p2b[:, 0:2 * NS].rearrange("p (a t) -> p a t", a=2)
                    S.tr([(hvs[:, db, :], HGs[:, db * 128:(db + 1) * 128], IDb[0:NS, 0:NS]) for db in range(2)], [HGs, IDb], [PS[2]])
                    S.cp("act", HGTs[:, :, :], hvs, [PS[2]], [HGTs])
                    for hf in range(2):
                        S.mm([(PS[6 + hf][0:NS, :], HGTs[:, db, :], woo[:, db, hf * 512:(hf + 1) * 512], db == 0, db == 1) for db in range(2)], [HGTs, woo], [PS[6 + hf]])
                        S.tt("dve", Xs[:, hf * 512:(hf + 1) * 512], Xs[:, hf * 512:(hf + 1) * 512], PS[6 + hf][0:NS, :], ALU.add, [Xs, PS[6 + hf]], [Xs])
                    S.pa_release(mark_w)
            S.pa_release(mark_ml)

        if phases.get("final", True):
            mark_f = S.pa_mark()
            S.dma(NW[:, :], D["normw"][2], wbuf=NW)
            junk = S.ring("fjunk", 2, [128, 1024], BF16)
            yo = S.ring("yo", 2, [128, 1024], F32)
            ss = S.ring("fss", 2, [128, 1], F32)
            rstd = S.ring("frstd", 2, [128, 1], F32)
            for c in range(NCH + 1):
                i = c % 2
                xb, np_ = (X[c], 128) if c < NCH else (Xs, NS)
                rms_rows(xb[0:np_, :], np_, ss[i], rstd[i], junk[i], xb)
                S.stt(yo[i][0:np_, :], xb[0:np_, :], rstd[i][0:np_, :], NW[0:np_, :], ALU.mult, ALU.mult, [xb, rstd[i], NW], [yo[i]])
                if c < NCH:
                    S.dma(D["y_p"][c * 128:(c + 1) * 128, :], yo[i][:, :], rbuf=yo[i])
                else:
                    S.dma(D["y_s"], yo[i][0:NS, :], rbuf=yo[i])
            S.pa_release(mark_f)
        else:
            for c in range(NCH):
                S.dma(D["y_p"][c * 128:(c + 1) * 128, :], X[c][:, :], rbuf=X[c])
            S.dma(D["y_s"], Xs[:, :], rbuf=Xs)
        S.barrier()
        S.emit()
    return nc


def _consts():
    ident = np.eye(128, dtype=np.float32)
    tri = np.triu(np.ones((128, 128), np.float32))
    k = np.arange(128)[:, None]
    q = np.arange(128)[None, :]
    blocks = []
    for Dlt in range(-3, 16):
        d = Dlt * 128 + q - k
        m = ((d >= 0) & (d <= 128)).astype(np.float32)
        m += ((d >= 0) & (d <= 512) & (d % 4 == 0)).astype(np.float32)
        m += ((d >= 0) & (d % 16 == 0)).astype(np.float32)
        blocks.append(m)
    maskmm = np.concatenate(blocks, axis=1).astype(np.float32)
    half = 8
    inv = np.power(np.float32(500000.0), -np.arange(half, dtype=np.float32) * np.float32(2.0 / 16)).astype(np.float32)
    pos = np.arange(2048, dtype=np.float32)
    ang = (pos[:, None] * inv[None, :]).astype(np.float32)
    cos = np.cos(ang).astype(np.float32)
    sin = np.sin(ang).astype(np.float32)
    cc = np.concatenate([cos, cos], axis=1).reshape(16, 128, 16).transpose(1, 0, 2)
    ss = np.concatenate([-sin, sin], axis=1).reshape(16, 128, 16).transpose(1, 0, 2)
    angs = (np.float32(2048.0) * inv).astype(np.float32)
    ccs = np.tile(np.concatenate([np.cos(angs), np.cos(angs)])[None, :], (NS, 1)).astype(np.float32)
    sss = np.tile(np.concatenate([-np.sin(angs), np.sin(angs)])[None, :], (NS, 1)).astype(np.float32)
    return dict(ident=ident, tri=tri, maskmm=maskmm, cct=np.ascontiguousarray(cc, np.float32),
                sst=np.ascontiguousarray(ss, np.float32), ccs=ccs, sss=sss)


def _rep(v, n):
    return np.ascontiguousarray(np.broadcast_to(np.asarray(v, np.float32)[None, ...], (n,) + tuple(np.shape(v))))


_NC_CACHE = {}


def kernel(x_prompt, x_sample, cache_attn_k, cache_attn_v, state_ssd_conv, state_ssd,
           state_mlstm_conv, state_mlstm_c, state_mlstm_n, state_mlstm_m,
           norm_w, final_norm_w, w_in_even, w_out_even, ssd_conv_w, ssd_conv_b,
           ssd_dt_bias, ssd_a_log, ssd_d, ssd_norm_w, w_in_odd, w_out_odd,
           mlstm_conv_w, mlstm_conv_b, mlstm_igate_b, mlstm_fgate_b, mlstm_norm_w):
    f = lambda a: np.ascontiguousarray(np.asarray(a, dtype=np.float32))
    cst = _consts()
    shared = dict(cst)
    shared["normw"] = np.stack([_rep(f(norm_w)[0], 128), _rep(f(norm_w)[1], 128), _rep(f(final_norm_w), 128)])
    shared["w_in_even"] = f(w_in_even)[0]
    shared["w_out_even"] = f(w_out_even)[0]
    shared["w_in_odd"] = f(w_in_odd)[0]
    shared["w_out_odd"] = f(w_out_odd)[0]
    cw = f(ssd_conv_w)[0]
    shared["cwT"] = np.ascontiguousarray(cw.reshape(4, 12, 128).transpose(2, 1, 0))
    shared["cbT"] = np.ascontiguousarray(f(ssd_conv_b)[0].reshape(12, 128).T)
    shared["cw_s"] = _rep(cw, NS)
    shared["cb_s"] = _rep(f(ssd_conv_b)[0], NS)
    shared["dtb"] = _rep(f(ssd_dt_bias)[0], 128)
    shared["alog"] = _rep(f(ssd_a_log)[0], 128)
    shared["dsk"] = _rep(f(ssd_d)[0], 128)
    shared["snw"] = _rep(f(ssd_norm_w)[0], 128)
    mcw = f(mlstm_conv_w)[0]
    shared["mcwT"] = np.ascontiguousarray(mcw.reshape(4, 32, 128).transpose(2, 1, 0))
    shared["mcbT"] = np.ascontiguousarray(f(mlstm_conv_b)[0].reshape(32, 128).T)
    shared["mcw_s"] = _rep(mcw, NS)
    shared["mcb_s"] = _rep(f(mlstm_conv_b)[0], NS)
    shared["igb"] = _rep(f(mlstm_igate_b)[0], 128)
    shared["fgb"] = _rep(f(mlstm_fgate_b)[0], 128)
    shared["mnw"] = _rep(f(mlstm_norm_w)[0], 128)

    xp = f(x_prompt)
    xs = f(x_sample)
    ck = np.asarray(cache_attn_k, np.float32)
    cv = np.asarray(cache_attn_v, np.float32)
    in_maps = []
    for c in range(NCORES):
        sl = slice(c * NS, (c + 1) * NS)
        m = dict(shared)
        m["xp"] = xp[c]
        m["xs"] = np.ascontiguousarray(xs[sl, 0, :])
        m["ck"] = np.ascontiguousarray(ck[0, sl].reshape(NS, 2048, 512)[:NSKV])
        m["cv"] = np.ascontiguousarray(cv[0, sl].reshape(NS, 2048, 512)[:NSKV])
        m["sconv"] = f(state_ssd_conv)[0, sl]
        m["sstate"] = np.ascontiguousarray(f(state_ssd)[0, sl].reshape(NS, 1024, 128))
        m["mconv"] = f(state_mlstm_conv)[0, sl]
        m["mC"] = f(state_mlstm_c)[0, sl]
        m["mn"] = f(state_mlstm_n)[0, sl]
        m["mmm"] = f(state_mlstm_m)[0, sl]
        in_maps.append({k: np.ascontiguousarray(v, dtype=np.float32) for k, v in m.items()})

    if "nc" not in _NC_CACHE:
        _NC_CACHE["nc"] = build_program()
    res = run_bass_kernel_spmd(_NC_CACHE["nc"], in_maps, core_ids=list(range(NCORES)))
    R = res.results

    def cat(name, shape):
        return np.stack([np.asarray(R[c][name], np.float32) for c in range(NCORES)]).reshape(shape)

    def cats(name, shape):
        return np.concatenate([np.asarray(R[c][name], np.float32) for c in range(NCORES)], axis=0).reshape(shape)

    outs = (
        cat("y_p", (8, 2048, 1024)),
        cats("y_s", (128, 1, 1024)),
        cat("p_k", (1, 8, 2048, 8, 64)), cat("p_v", (1, 8, 2048, 8, 64)),
        cat("p_sconv", (1, 8, 3, 1536)), cat("p_ssd", (1, 8, 16, 64, 128)),
        cat("p_mconv", (1, 8, 3, 4096)), cat("p_mC", (1, 8, 8, 256, 256)),
        cat("p_mn", (1, 8, 8, 256)), cat("p_mm", (1, 8, 8)),
        cats("s_k", (1, 128, 1, 8, 64)), cats("s_v", (1, 128, 1, 8, 64)),
        cats("s_sconv", (1, 128, 3, 1536)), cats("s_ssd", (1, 128, 16, 64, 128)),
        cats("s_mconv", (1, 128, 3, 4096)), cats("s_mC", (1, 128, 8, 256, 256)),
        cats("s_mn", (1, 128, 8, 256)), cats("s_mm", (1, 128, 8)),
    )
    return outs
```
